# Optimizing a Trainium2 kernel written in Bass

```python
import math
import jax, jax.numpy as jnp
from jax import lax
import numpy as np

D_MODEL = 1024
BATCH = 4
SEQ = 4096
DEPTH = 2

HEAD_DIM = 64
N_GROUPS = 4
GROUP_WIDTH = D_MODEL // N_GROUPS
N_GROUP_HEADS = GROUP_WIDTH // HEAD_DIM
N_IN_SLICES = 13
D_IN = N_IN_SLICES * GROUP_WIDTH
D_FF = 4 * D_MODEL
CONV_WIDTH = 4
RG_LRU_C = 8.0
RET_CHUNK = 128
HGRN_CHUNK = 64
SB_BLOCK = 128
ROPE_BASE = 10000.0
LN_EPS = 1e-5
NORM_EPS = 1e-6
GATE_FLOOR = 1e-30
DEEPNORM_ALPHA = (2 * DEPTH) ** 0.25
DEEPNORM_BETA = (8 * DEPTH) ** -0.25

kernel_name = "hybrid_rglru_retention_stickbreak_hgrn2"


def layer_norm(x, g, b):
    x = x.astype(jnp.float32)
    mu = jnp.mean(x, axis=-1, keepdims=True)
    var = jnp.mean(jnp.square(x - mu), axis=-1, keepdims=True)
    return (x - mu) * lax.rsqrt(var + LN_EPS) * g + b


def split_heads(t):
    b, s, _ = t.shape
    return t.reshape(b, s, N_GROUP_HEADS, HEAD_DIM).transpose(0, 2, 1, 3)


def merge_heads(t):
    b, h, s, d = t.shape
    return t.transpose(0, 2, 1, 3).reshape(b, s, h * d)


def rotary(t, cos, sin):
    half = HEAD_DIM // 2
    t1, t2 = t[..., :half], t[..., half:]
    return jnp.concatenate([t1 * cos - t2 * sin, t1 * sin + t2 * cos], axis=-1)


def rglru_mixer(xa, ga, conv_w, conv_b, wa, ba, wx, bx, lam):
    b_, s_, _ = xa.shape
    xc = lax.conv_general_dilated(
        xa, conv_w.astype(xa.dtype)[:, None, :], window_strides=(1,),
        padding=[(CONV_WIDTH - 1, 0)], dimension_numbers=("NWC", "WIO", "NWC"),
        feature_group_count=GROUP_WIDTH) + conv_b
    xh = xc.reshape(b_, s_, N_GROUP_HEADS, HEAD_DIM)
    r = jax.nn.sigmoid(jnp.einsum("bshi,hij->bshj", xh, wa) + ba).reshape(b_, s_, GROUP_WIDTH)
    i = jax.nn.sigmoid(jnp.einsum("bshi,hij->bshj", xh, wx) + bx).reshape(b_, s_, GROUP_WIDTH)
    log_a = RG_LRU_C * r * jax.nn.log_sigmoid(lam)
    a = jnp.exp(log_a)
    u = jnp.sqrt(jnp.maximum(-jnp.expm1(2.0 * log_a), 0.0)) * (i * xc)

    def combine(left, right):
        a1, b1 = left
        a2, b2 = right
        return a1 * a2, a2 * b1 + b2

    _, h = lax.associative_scan(combine, (a, u), axis=1)
    return jax.nn.gelu(ga) * h


def retention_mixer(q, k, v, g, norm_g, cos, sin):
    q = rotary(split_heads(q), cos, sin)
    k = rotary(split_heads(k), cos, sin) * HEAD_DIM ** -0.5
    v = split_heads(v)
    b_, h_, s_, d_ = q.shape
    c_ = RET_CHUNK
    n_ = s_ // c_
    log_gamma = jnp.log1p(-jnp.exp2(-5.0 - jnp.arange(N_GROUP_HEADS, dtype=jnp.float32)))
    qc = q.reshape(b_, h_, n_, c_, d_)
    kc = k.reshape(b_, h_, n_, c_, d_)
    vc = v.reshape(b_, h_, n_, c_, d_)
    pos = jnp.arange(c_, dtype=jnp.float32)
    diff = pos[:, None] - pos[None, :]
    decay = jnp.where(diff >= 0, jnp.exp(log_gamma[:, None, None] * jnp.maximum(diff, 0.0)), 0.0)
    scores = jnp.einsum("bhnid,bhnjd->bhnij", qc, kc) * decay[:, None]
    o_intra = jnp.einsum("bhnij,bhnjd->bhnid", scores, vc)
    k_decay = jnp.exp(log_gamma[:, None] * (c_ - 1.0 - pos))
    chunk_kv = jnp.einsum("bhnjd,bhnje->bhnde", kc * k_decay[:, None, :, None], vc)
    chunk_decay = jnp.exp(log_gamma * c_)[None, :, None, None]

    def step(state, kv):
        return chunk_decay * state + kv, state

    init = jnp.zeros((b_, h_, d_, d_), chunk_kv.dtype)
    _, prev = lax.scan(step, init, jnp.moveaxis(chunk_kv, 2, 0))
    prev = jnp.moveaxis(prev, 0, 2)
    q_decay = jnp.exp(log_gamma[:, None] * (pos + 1.0))
    o_inter = jnp.einsum("bhnid,bhnde->bhnie", qc * q_decay[:, None, :, None], prev)
    o = (o_intra + o_inter).reshape(b_, h_, s_, d_).astype(jnp.float32)
    mu = jnp.mean(o, axis=-1, keepdims=True)
    var = jnp.mean(jnp.square(o - mu), axis=-1, keepdims=True)
    o = merge_heads((o - mu) * lax.rsqrt(var + NORM_EPS)) * norm_g
    return jax.nn.silu(g) * o


def stick_breaking_mixer(q, k, v):
    q, k, v = split_heads(q), split_heads(k), split_heads(v)
    b_, h_, s_, d_ = q.shape
    nb = s_ // SB_BLOCK
    q_blocks = jnp.moveaxis(q.reshape(b_, h_, nb, SB_BLOCK, d_), 2, 0)
    key_pos = jnp.arange(s_)
    scale = d_ ** -0.5

    def block(args):
        qb, n = args
        q_pos = n * SB_BLOCK + jnp.arange(SB_BLOCK)
        z = (jnp.einsum("bhqd,bhkd->bhqk", qb, k) * scale).astype(jnp.float32)
        mask = key_pos[None, :] < q_pos[:, None]
        log_keep = jnp.where(mask, jax.nn.log_sigmoid(-z), 0.0)
        later = lax.cumsum(log_keep, axis=3, reverse=True) - log_keep
        log_w = jnp.where(mask, jax.nn.log_sigmoid(z) + later, -1e4)
        w = jnp.where(mask, jnp.exp(log_w), 0.0)
        return jnp.einsum("bhqk,bhkd->bhqd", w.astype(v.dtype), v)

    o = lax.map(block, (q_blocks, jnp.arange(nb)))
    o = jnp.moveaxis(o, 0, 2).reshape(b_, h_, s_, d_)
    return merge_heads(o)


def hgrn2_mixer(q, f_pre, v, g, lower_bound, norm_g):
    q, f_pre, v = split_heads(q), split_heads(f_pre), split_heads(v)
    b_, h_, s_, d_ = q.shape
    c_ = HGRN_CHUNK
    n_ = s_ // c_
    lb = lower_bound.astype(jnp.float32).reshape(h_, d_)[None, :, None, :]
    f_pre = f_pre.astype(jnp.float32)
    f_gate = lb + (1.0 - lb) * jax.nn.sigmoid(f_pre)
    log_f = jnp.log(jnp.maximum(f_gate, GATE_FLOOR))
    k = (1.0 - lb) * jax.nn.sigmoid(-f_pre)
    to_chunks = lambda t: jnp.moveaxis(t.reshape(b_, h_, n_, c_, d_), 2, 0)
    bcum = jnp.cumsum(log_f.reshape(b_, h_, n_, c_, d_), axis=3)
    qs, ks, vs, bs = to_chunks(q), to_chunks(k), to_chunks(v), jnp.moveaxis(bcum, 2, 0)
    causal = jnp.tril(jnp.ones((c_, c_), dtype=bool))[:, :, None]

    def step(state, inp):
        qc, kc, vc, bc = inp
        diff = bc[:, :, :, None, :] - bc[:, :, None, :, :]
        w = jnp.where(causal, jnp.exp(jnp.minimum(diff, 0.0)), 0.0)
        attn = jnp.einsum("bhtd,bhtsd,bhsd->bhts", qc, w, kc)
        o = jnp.einsum("bhts,bhse->bhte", attn, vc) + jnp.einsum("bhtd,bhde->bhte", qc * jnp.exp(bc), state)
        b_last = bc[:, :, -1:, :]
        new_state = jnp.exp(b_last[:, :, 0, :])[..., None] * state + jnp.einsum(
            "bhsd,bhse->bhde", kc * jnp.exp(b_last - bc), vc)
        return new_state, o

    init = jnp.zeros((b_, h_, d_, d_), jnp.float32)
    _, o = lax.scan(step, init, (qs.astype(jnp.float32), ks, vs.astype(jnp.float32), bs))
    o = jnp.moveaxis(o, 0, 2).reshape(b_, h_, s_, d_).astype(jnp.float32)
    o = o * lax.rsqrt(jnp.mean(jnp.square(o), axis=-1, keepdims=True) + NORM_EPS)
    return merge_heads(o) * norm_g * jax.nn.silu(g)


def setup_inputs(seed: int = 0) -> dict:
    key = jax.random.key(seed)
    ks = jax.random.split(key, 24)
    f32 = jnp.float32
    nrm = lambda k, shape, s: jax.random.normal(k, shape, f32) * s
    lam_u = jax.random.uniform(ks[10], (DEPTH, GROUP_WIDTH), f32, minval=0.9, maxval=0.999)
    lam_p = lam_u ** (1.0 / RG_LRU_C)
    return {
        "x": jax.random.normal(ks[0], (BATCH, SEQ, D_MODEL), f32),
        "ln_in_g": 1.0 + nrm(ks[1], (D_MODEL,), 0.02),
        "ln_in_b": nrm(ks[2], (D_MODEL,), 0.02),
        "w_in": nrm(ks[3], (DEPTH, D_MODEL, D_IN), D_MODEL ** -0.5),
        "conv_w": nrm(ks[4], (DEPTH, CONV_WIDTH, GROUP_WIDTH), CONV_WIDTH ** -0.5),
        "conv_b": nrm(ks[5], (DEPTH, GROUP_WIDTH), 0.02),
        "rg_wa": nrm(ks[6], (DEPTH, N_GROUP_HEADS, HEAD_DIM, HEAD_DIM), HEAD_DIM ** -0.5),
        "rg_ba": nrm(ks[7], (DEPTH, N_GROUP_HEADS, HEAD_DIM), 0.1),
        "rg_wx": nrm(ks[8], (DEPTH, N_GROUP_HEADS, HEAD_DIM, HEAD_DIM), HEAD_DIM ** -0.5),
        "rg_bx": nrm(ks[9], (DEPTH, N_GROUP_HEADS, HEAD_DIM), 0.1),
        "rg_lambda": jnp.log(lam_p) - jnp.log1p(-lam_p),
        "ret_norm_g": 1.0 + nrm(ks[11], (DEPTH, GROUP_WIDTH), 0.02),
        "hgrn_lb_logits": nrm(ks[12], (DEPTH, GROUP_WIDTH), 0.5),
        "hgrn_norm_g": 1.0 + nrm(ks[13], (DEPTH, GROUP_WIDTH), 0.02),
        "w_out": nrm(ks[14], (DEPTH, D_MODEL, D_MODEL), D_MODEL ** -0.5 * DEEPNORM_BETA),
        "ln1_g": 1.0 + nrm(ks[15], (DEPTH, D_MODEL), 0.02),
        "ln1_b": nrm(ks[16], (DEPTH, D_MODEL), 0.02),
        "w_up": nrm(ks[17], (DEPTH, D_MODEL, D_FF), D_MODEL ** -0.5),
        "w_down": nrm(ks[18], (DEPTH, D_FF, D_MODEL), D_FF ** -0.5 * DEEPNORM_BETA),
        "ln2_g": 1.0 + nrm(ks[19], (DEPTH, D_MODEL), 0.02),
        "ln2_b": nrm(ks[20], (DEPTH, D_MODEL), 0.02),
    }


def reference(x, ln_in_g, ln_in_b, w_in, conv_w, conv_b, rg_wa, rg_ba, rg_wx, rg_bx, rg_lambda,
              ret_norm_g, hgrn_lb_logits, hgrn_norm_g, w_out, ln1_g, ln1_b, w_up, w_down,
              ln2_g, ln2_b):
    s_ = x.shape[1]
    inv_freq = ROPE_BASE ** (-jnp.arange(0, HEAD_DIM, 2, dtype=jnp.float32) / HEAD_DIM)
    ang = jnp.arange(s_, dtype=jnp.float32)[:, None] * inv_freq[None, :]
    cos, sin = jnp.cos(ang), jnp.sin(ang)
    lb_p = jax.nn.softmax(hgrn_lb_logits.astype(jnp.float32), axis=0)
    lower_bounds = jnp.cumsum(lb_p, axis=0) - lb_p[0]

    h = layer_norm(x, ln_in_g, ln_in_b)
    for l in range(DEPTH):
        proj = h @ w_in[l]
        (a_x, a_g, r_q, r_k, r_v, r_g, s_q, s_k, s_v,
         d_q, d_f, d_v, d_g) = jnp.split(proj, N_IN_SLICES, axis=-1)
        y_a = rglru_mixer(a_x, a_g, conv_w[l], conv_b[l], rg_wa[l], rg_ba[l], rg_wx[l], rg_bx[l], rg_lambda[l])
        y_b = retention_mixer(r_q, r_k, r_v, r_g, ret_norm_g[l], cos, sin)
        y_c = stick_breaking_mixer(s_q, s_k, s_v)
        y_d = hgrn2_mixer(d_q, d_f, d_v, d_g, lower_bounds[l], hgrn_norm_g[l])
        mix = jnp.concatenate([y_a, y_b, y_c, y_d], axis=-1) @ w_out[l]
        h = layer_norm(DEEPNORM_ALPHA * h + mix, ln1_g[l], ln1_b[l])
        ff = jnp.square(jax.nn.relu(h @ w_up[l])) @ w_down[l]
        h = layer_norm(DEEPNORM_ALPHA * h + ff, ln2_g[l], ln2_b[l])
    return h.astype(x.dtype)
```

```python
import numpy as np
import ml_dtypes
import concourse.bass as bass
import concourse.mybir as mybir


F32 = mybir.dt.float32
BF16 = mybir.dt.bfloat16
AF = mybir.ActivationFunctionType
ALU = mybir.AluOpType
AX = mybir.AxisListType

ENGS = ("tensor", "vector", "scalar", "gpsimd", "sync")
EPOCH = 3000


class Res:
    __slots__ = ("name", "w", "r", "excl")

    def __init__(self, name="", excl=False):
        self.name = name
        self.w = None
        self.r = []
        self.excl = excl


class Instr:
    __slots__ = ("eng", "idx", "fn", "deps", "vc", "signal", "is_dma", "sem", "val", "pre", "order", "inc")

    def __init__(self, eng, idx, fn, is_dma=False):
        self.eng = eng
        self.idx = idx
        self.fn = fn
        self.deps = []
        self.vc = {}
        self.signal = False
        self.is_dma = is_dma
        self.sem = None
        self.val = None
        self.pre = None
        self.inc = 16


class Sched:
    def __init__(self, nc, n_dma_sems=32, same_engine_sync=True):
        self.nc = nc
        self.streams = {e: [] for e in ENGS}
        self.known = {e: {} for e in ENGS}
        self.known_dma = {e: set() for e in ENGS}
        self.n_dma_sems = n_dma_sems
        self.dma_count = 0
        self.dma_cnt_by_eng = {}
        self.dma_last = {}
        self.same_engine_sync = same_engine_sync
        self.out_dmas = []
        self.pending = {}
        self.dmas_since_barrier = []

    def add(self, eng, fn, reads=(), writes=(), is_dma=False, extra_deps=(), own_sem=False):
        st = self.streams[eng]
        ins = Instr(eng, len(st), fn, is_dma)
        self.order = getattr(self, 'order', 0) + 1
        ins.order = self.order
        if any(r.excl for r in reads):
            writes = list(writes) + [r for r in reads if r.excl and r not in writes]
            reads = [r for r in reads if not r.excl]
        deps = list(extra_deps) + self.pending.pop(eng, [])
        for r in reads:
            if r.w is not None:
                deps.append(r.w)
        for w in writes:
            if w.w is not None:
                deps.append(w.w)
            deps.extend(w.r)
        known = self.known[eng]
        kd = self.known_dma[eng]
        need = {}
        vc = {}
        for d in deps:
            if d is ins:
                continue
            if d.is_dma:
                if id(d) in kd:
                    continue
                need[("dma", id(d))] = d
            else:
                if d.eng == eng and (eng == "tensor" or not self.same_engine_sync):
                    continue
                if known.get(d.eng, -1) >= d.idx:
                    continue
                k = ("e", d.eng)
                if k not in need or need[k].idx < d.idx:
                    need[k] = d
        for k, d in need.items():
            d.signal = True
            ins.deps.append(d)
            if d.is_dma:
                kd.add(id(d))
            for e2, i2 in d.vc.items():
                if known.get(e2, -1) < i2:
                    known[e2] = i2
            if not d.is_dma:
                if known.get(d.eng, -1) < d.idx:
                    known[d.eng] = d.idx
        ins.vc = dict(known)
        if is_dma and own_sem:
            ins.inc = 1
            ins.sem = "own"
        elif is_dma:
            self.dmas_since_barrier.append(ins)
            half = self.n_dma_sems // 2
            cnt = self.dma_cnt_by_eng.get(eng, 0)
            self.dma_cnt_by_eng[eng] = cnt + 1
            slot = (cnt % half) + (half if eng == "gpsimd" else 0)
            self.dma_count += 1
            prev = self.dma_last.get(slot)
            ins.pre = prev
            self.dma_last[slot] = ins
            ins.sem = slot
        for r in reads:
            r.r.append(ins)
        for w in writes:
            w.w = ins
            w.r = []
        st.append(ins)
        return ins

    def barrier(self, scratch_ap):
        self.flush_cc()
        deps = []
        for e in ("tensor", "scalar", "gpsimd", "vector"):
            st = [i for i in self.streams[e] if not i.is_dma]
            if st:
                deps.append(st[-1])
        deps.extend(d for d in self.dmas_since_barrier if d.inc != 1)
        self.dmas_since_barrier = []
        if not hasattr(self, "bar_res"):
            self.bar_res = Res()
        b = self.add("vector", lambda e: e.memset(scratch_ap, 0.0), writes=[self.bar_res], extra_deps=deps)
        for e in ("scalar", "gpsimd", "sync", "tensor"):
            self.pending[e] = [b]
        return b

    def pe(self, fn, reads=(), writes=()):
        return self.add("tensor", fn, reads, writes)

    def dve(self, fn, reads=(), writes=()):
        return self.add("vector", fn, reads, writes)

    def act(self, fn, reads=(), writes=()):
        return self.add("scalar", fn, reads, writes)

    def pool(self, fn, reads=(), writes=()):
        return self.add("gpsimd", fn, reads, writes)

    def cc(self, kind, ins_, outs, groups, reads=(), writes=()):
        tmp = Res()
        i = self.add("gpsimd", lambda e: e.collective_compute(kind, mybir.AluOpType.bypass, replica_groups=groups, ins=ins_, outs=outs),
                     reads, [tmp], is_dma=True, own_sem=True)
        self.pending_cc = getattr(self, "pending_cc", [])
        self.pending_cc.append((tmp, list(writes)))
        return i

    def flush_cc(self):
        sc = getattr(self, "cc_scratch", None)
        for tmp, writes in getattr(self, "pending_cc", []):
            if not hasattr(self, "cc_res"):
                self.cc_res = Res()
            self.add("gpsimd", lambda e: e.memset(sc, 0.0), reads=[tmp], writes=list(writes) + [self.cc_res])
        self.pending_cc = []

    def dma(self, out, in_, reads=(), writes=(), is_output=False, eng="sync", extra_deps=(), **kw):
        ins = self.add(eng, lambda e: e.dma_start(out=out, in_=in_, **kw), reads, writes, is_dma=True, extra_deps=extra_deps)
        if is_output:
            self.out_dmas.append(ins)
        return ins


def build_and_emit(nc, sched):
    for e in ENGS:
        cnt = 0
        sems = []
        for ins in sched.streams[e]:
            if ins.is_dma or not ins.signal:
                continue
            ep = cnt // EPOCH
            if ep >= len(sems):
                sems.append(nc.alloc_semaphore(f"s_{e}_{ep}"))
            cnt += 1
            ins.sem = sems[ep]
            ins.val = cnt - ep * EPOCH
    n = sched.n_dma_sems
    dma_sems = [nc.alloc_semaphore(f"s_dma_{i}") for i in range(n)]
    dma_vals = [0] * n
    dmas = []
    for e in ENGS:
        for ins in sched.streams[e]:
            if ins.is_dma:
                dmas.append(ins)
    dmas.sort(key=lambda i: i.order)
    for ins in dmas:
        if ins.sem == "own":
            ins.sem = nc.alloc_semaphore(f"s_cc_{ins.order}")
            ins.val = 1
            continue
        slot = ins.sem
        dma_vals[slot] += ins.inc
        ins.sem = dma_sems[slot]
        ins.val = dma_vals[slot]

    final_waits = list(sched.out_dmas)

    def run_stream(ename, eng):
        for ins in sched.streams[ename]:
            for d in ins.deps:
                eng.wait_ge(d.sem, d.val)
            if ins.is_dma and ins.pre is not None:
                eng.wait_ge(ins.pre.sem, ins.pre.val)
            bi = ins.fn(eng)
            if ins.is_dma:
                bi.then_inc(ins.sem, ins.inc)
            elif ins.signal:
                bi.then_inc(ins.sem, 1)
        if ename == "sync":
            for d in final_waits:
                eng.wait_ge(d.sem, d.val)

    with nc.Block() as block:
        @block.tensor
        def _(eng):
            run_stream("tensor", eng)

        @block.vector
        def _(eng):
            run_stream("vector", eng)

        @block.scalar
        def _(eng):
            run_stream("scalar", eng)

        @block.gpsimd
        def _(eng):
            run_stream("gpsimd", eng)

        @block.sync
        def _(eng):
            run_stream("sync", eng)


class Arena:
    def __init__(self, nc, base=16384, limit=229312):
        self.nc, self.off, self.limit = nc, base, limit
        self.n = 0
        self.peak = base

    def alloc(self, name, shape, dtype):
        nbytes = int(np.prod(shape[1:])) * mybir.dt.size(dtype)
        off = (self.off + 63) // 64 * 64
        assert off + nbytes <= self.limit, f"arena overflow allocating {name} {shape}: {off}+{nbytes} > {self.limit}"
        self.off = off + nbytes
        self.peak = max(self.peak, self.off)
        self.n += 1
        return self.nc.alloc_sbuf_tensor_at(f"ar{self.n}_{name}", shape, dtype, offset=off).ap()

    def mark(self):
        return self.off

    def reset(self, m):
        self.off = m


D = 1024
DFF = 4096
ALPHA = 4 ** 0.25
LN_EPS = 1e-5


def ln_tile(S, nc, A, z_ap, rz, g_bc, b_bc, rgb, out_ap, rout, tmp, rtmp, st, rst, tagid):
    if tmp is None:
        tmp, rtmp = z_ap, rz
    S.dve(lambda e: e.bn_stats(out=st[:, 0:6], in_=z_ap[:, 0:512]), reads=[rz], writes=[rst])
    S.dve(lambda e: e.bn_stats(out=st[:, 6:12], in_=z_ap[:, 512:1024]), reads=[rz], writes=[rst])
    S.dve(lambda e: e.bn_aggr(out=st[:, 12:14], in_=st[:, 0:12]), reads=[rst], writes=[rst])
    S.act(lambda e: e.activation(out=st[:, 14:15], in_=st[:, 13:14], func=AF.Sqrt, bias=A["eps_ln"], scale=1.0),
          reads=[rst], writes=[rst])
    S.dve(lambda e: e.reciprocal(out=st[:, 14:15], in_=st[:, 14:15]), reads=[rst], writes=[rst])
    S.dve(lambda e: e.tensor_scalar(out=st[:, 15:16], in0=st[:, 12:13], scalar1=st[:, 14:15], scalar2=-1.0,
                                    op0=ALU.mult, op1=ALU.mult), reads=[rst], writes=[rst])
    S.act(lambda e: e.activation(out=tmp, in_=z_ap, func=AF.Identity, bias=st[:, 15:16], scale=st[:, 14:15]),
          reads=[rz, rst], writes=[rtmp])
    S.dve(lambda e: e.tensor_tensor(out=tmp, in0=tmp, in1=g_bc, op=ALU.mult), reads=[rtmp, rgb], writes=[rtmp])
    S.pool(lambda e: e.tensor_tensor(out=out_ap, in0=tmp, in1=b_bc, op=ALU.add), reads=[rtmp, rgb], writes=[rout])


def phase_F(nc, S, A, T, yT_dram, w_out_d, ln1g_d, ln1b_d, w_up_d, w_down_d, ln2g_d, ln2b_d,
            hres, rh, out_f32_d=None, out_bf16_d=None, y_gather=None, rout32=None, rout16=None, final_out=True, on_tile_done=None, wo_pre=None):
    NT = T // 128
    NB = T // 512
    al = A["alloc"]
    ident = A["ident_bf"]
    yT = al("yT", [128, 8, T], BF16)
    ry = [Res() for _ in range(NT)]
    lnp = al("lnp", [128, 2, 1024], F32)
    rln = Res()
    if y_gather is None:
        S.dma(yT, yT_dram.rearrange("(k p) t -> p k t", p=128), writes=ry)
    else:
        src, idx, ridx, rsrc = y_gather
        for kc in range(8):
            S.add("gpsimd", lambda e, kc=kc: e.indirect_dma_start(out=yT[:, kc, :], out_offset=None, in_=src[kc // 4],
                  in_offset=bass.IndirectOffsetOnAxis(ap=idx[:, kc:kc + 1], axis=0)), reads=[rsrc[kc // 4], ridx], writes=ry, is_dma=True)
    for i, d in enumerate((ln1g_d, ln1b_d)):
        S.dma(lnp[:, i, :], d.partition_broadcast(128), writes=[rln])
    NS = 4
    wu = [al(f"wu{i}", [128, 8, 1024], BF16) for i in range(2)]
    wd = [al(f"wd{i}", [128, 8, 1024], BF16) for i in range(2)]
    rwu = [Res(), Res()]
    rwd = [Res(), Res()]

    def load_ffn(s):
        b = s % 2
        S.dma(wu[b], w_up_d[:, s * 1024:(s + 1) * 1024].rearrange("(k p) f -> p k f", p=128), writes=[rwu[b]], eng="gpsimd")
        S.dma(wd[b], w_down_d[s * 1024:(s + 1) * 1024, :].rearrange("(k p) n -> p k n", p=128), writes=[rwd[b]], eng="gpsimd")

    if wo_pre is None:
        wo = wd[1]
        rwo = rwd[1]
        S.dma(wo, w_out_d.rearrange("(k p) n -> p k n", p=128), writes=[rwo], eng="gpsimd")
        load_ffn(0)
    else:
        wo, rwo = wo_pre
        load_ffn(0)
        load_ffn(1)
    tmp = [None] * 4
    rtmp = [None] * 4
    st = [al(f"st{i}", [128, 16], F32) for i in range(4)]
    rst = [Res() for _ in range(4)]
    ps = A["psum"]
    rps = A["rpsum"]
    psT = ps[7].bitcast(BF16)
    h1T = yT

    hb3 = [al(f"hb3_{i}", [128, 1024], BF16) for i in range(3)]
    rhb3 = [Res() for _ in range(3)]
    hb = [hb3[0], hb3[1]]
    rhb = [rhb3[0], rhb3[1]]
    def ln1_step(t):
        if t < NT:
            p = t % 2
            for half in range(2):
                bank = 2 * p + half
                for kc in range(8):
                    S.pe(lambda e, kc=kc, half=half, bank=bank, t=t: e.matmul(
                        ps[bank], lhsT=yT[:, kc, t * 128:(t + 1) * 128], rhs=wo[:, kc, half * 512:(half + 1) * 512],
                        start=(kc == 0), stop=(kc == 7)), reads=[ry[t], rwo], writes=[rps[bank]])
                S.dve(lambda e, half=half, bank=bank, t=t: e.scalar_tensor_tensor(
                    out=hres[:, t, half * 512:(half + 1) * 512], in0=hres[:, t, half * 512:(half + 1) * 512],
                    scalar=ALPHA, in1=ps[bank], op0=ALU.mult, op1=ALU.add), reads=[rh[t], rps[bank]], writes=[rh[t]])
            ln_tile(S, nc, A, hres[:, t, :], rh[t], lnp[:, 0, :], lnp[:, 1, :], rln, hres[:, t, :], rh[t],
                    None, None, st[t % 4], rst[t % 4], t)
            S.act(lambda e, t=t: e.activation(out=hb3[t % 3], in_=hres[:, t, :], func=AF.Copy), reads=[rh[t]], writes=[rhb3[t % 3]])
        if t >= 2:
            u = t - 2
            for kc in range(8):
                S.pe(lambda e, kc=kc, u=u: e.transpose(out=psT[:, kc * 128:(kc + 1) * 128], in_=hb3[u % 3][:, kc * 128:(kc + 1) * 128],
                                                       identity=ident), reads=[rhb3[u % 3]], writes=[rps[7]])
            S.act(lambda e, u=u: e.activation(out=h1T[:, :, u * 128:(u + 1) * 128],
                                              in_=psT.rearrange("p (k c) -> p k c", k=8), func=AF.Copy),
                  reads=[rps[7]], writes=[ry[u]])


    n_steps = NT + 2
    prologue = min(n_steps, 6)
    for t in range(prologue):
        ln1_step(t)
    ln1_next = [prologue]

    def ln1_more(k):
        for _ in range(k):
            if ln1_next[0] < n_steps:
                ln1_step(ln1_next[0])
                ln1_next[0] += 1

    if NB < 4:
        ln1_more(n_steps)
    aT = [al("aT", [128, 8, 512], BF16)] * 2
    raT = [Res()] * 2
    rl = [al("rl0", [128, 512], BF16)] * 2
    rrl = [Res()] * 2
    cnt = 0
    deferred = []

    def ln2_emit():
        while deferred:
            t = deferred.pop(0)
            p = t % 2
            ln_tile(S, nc, A, hres[:, t, :], rh[t], lnp[:, 0, :], lnp[:, 1, :], rln, hres[:, t, :], rh[t],
                    None, None, st[t % 4], rst[t % 4], t)
            if out_f32_d is not None:
                S.dma(out_f32_d[t * 128:(t + 1) * 128, :], hres[:, t, :], reads=[rh[t]], writes=([rout32[t]] if rout32 else []), is_output=final_out)
            if out_bf16_d is not None:
                S.act(lambda e, t=t, p=p: e.activation(out=hb[p], in_=hres[:, t, :], func=AF.Copy),
                      reads=[rh[t]], writes=[rhb[p]])
                S.dma(out_bf16_d(t) if callable(out_bf16_d) else out_bf16_d[t * 128:(t + 1) * 128, :], hb[p], reads=[rhb[p]],
                      writes=([rout16[t]] if rout16 else []), is_output=final_out)
            if on_tile_done is not None:
                on_tile_done(t)

    for s in range(NS):
        b = s % 2
        for tb in range(NB):
            ab = cnt % 2
            cnt += 1
            for fc in range(8):
                bank = 4 + (fc % 2)
                for kc in range(8):
                    S.pe(lambda e, kc=kc, fc=fc, bank=bank, tb=tb, b=b: e.matmul(
                        ps[bank], lhsT=wu[b][:, kc, fc * 128:(fc + 1) * 128], rhs=h1T[:, kc, tb * 512:(tb + 1) * 512],
                        start=(kc == 0), stop=(kc == 7)),
                        reads=[rwu[b]] + ry[tb * 4:(tb + 1) * 4], writes=[rps[bank]])
                rr = fc % 2
                S.act(lambda e, bank=bank, rr=rr: e.activation(out=rl[rr], in_=ps[bank], func=AF.Relu),
                      reads=[rps[bank]], writes=[rrl[rr]])
                if fc % 2 == 0:
                    S.dve(lambda e, fc=fc, bank=bank, ab=ab, rr=rr: e.tensor_tensor(
                        out=aT[ab][:, fc, :], in0=ps[bank], in1=rl[rr], op=ALU.mult),
                        reads=[rps[bank], rrl[rr]], writes=[raT[ab]])
                else:
                    S.pool(lambda e, fc=fc, ab=ab, rr=rr: e.tensor_tensor(
                        out=aT[ab][:, fc, :], in0=rl[rr], in1=rl[rr], op=ALU.mult),
                        reads=[rrl[rr]], writes=[raT[ab]])
            if s == 0:
                ln1_more(4)
            ln2_emit()
            for tt in range(4):
                t = tb * 4 + tt
                for half in range(2):
                    bank = (tt % 2) * 2 + half
                    for fc in range(8):
                        S.pe(lambda e, fc=fc, half=half, bank=bank, tt=tt, ab=ab, b=b: e.matmul(
                            ps[bank], lhsT=aT[ab][:, fc, tt * 128:(tt + 1) * 128], rhs=wd[b][:, fc, half * 512:(half + 1) * 512],
                            start=(fc == 0), stop=(fc == 7)), reads=[raT[ab], rwd[b]], writes=[rps[bank]])
                    sl = hres[:, t, half * 512:(half + 1) * 512]
                    if s == 0:
                        S.dve(lambda e, sl=sl, bank=bank: e.scalar_tensor_tensor(
                            out=sl, in0=sl, scalar=ALPHA, in1=ps[bank], op0=ALU.mult, op1=ALU.add),
                            reads=[rh[t], rps[bank]], writes=[rh[t]])
                    else:
                        S.dve(lambda e, sl=sl, bank=bank: e.tensor_tensor(out=sl, in0=sl, in1=ps[bank], op=ALU.add),
                              reads=[rh[t], rps[bank]], writes=[rh[t]])
                if s == NS - 1:
                    deferred.append(t)
        if s == 0:
            ln1_more(n_steps)
            if wo_pre is None:
                load_ffn(1)
            for i, d in enumerate((ln2g_d, ln2b_d)):
                S.dma(lnp[:, i, :], d.partition_broadcast(128), writes=[rln])
        if s + 2 < NS:
            load_ffn(s + 2)
    ln2_emit()


NORM_EPS = 1e-6
SL = {n: i for i, n in enumerate(
    ["A_x", "A_g", "B_q", "B_qsw", "B_k", "B_ksw", "B_g", "C_q", "C_k", "D_q", "D_f", "D_g", "B_v", "C_v", "D_v"])}
NSL = 15
PV = {n: i for i, n in enumerate(
    ["cw0", "cw1", "cw2", "cw3", "cb", "ba", "bx", "lam", "retg", "hgg", "lb0", "lbl", "gam64", "nba", "nbx", "c1", "c1x2", "oml", "lnoml", "tmp0", "tmp1"])}
NPV = 24


class Ctx:
    pass


def setup_M(nc, S, A, SEQ, layer, hin_d, win_d, pvec_d, wab_d, consts):
    c = Ctx()
    c.nc, c.S, c.A, c.SEQ, c.layer = nc, S, A, SEQ, layer
    al = A["alloc"]
    c.NB = SEQ // 512
    c.NT = SEQ // 128
    c.ps, c.rps = A["psum"], A["rpsum"]
    c.hT = al("hT", [128, 8, SEQ], BF16)
    c.win = al("win", [128, 8, NSL * 128], BF16)
    c.K = {}
    c.rK = Res()
    for name, (d, shape, dtype) in consts.items():
        t = al("k_" + name, shape, dtype)
        S.dma(t, d, writes=[c.rK])
        c.K[name] = t
    c.rwin = Res()
    S.dma(c.win, win_d.rearrange("(k p) n -> p k n", p=128), writes=[c.rwin], eng="gpsimd")
    c.pv = al("pv", [128, NPV], F32)
    c.rpv = Res()
    S.dma(c.pv[:, 0:13], pvec_d, writes=[c.rpv])
    c.wab = al("wab", [128, 2, 128], BF16)
    c.rwab = Res()
    S.dma(c.wab, wab_d, writes=[c.rwab], eng="gpsimd")
    pieces = hin_d if isinstance(hin_d, (list, tuple)) else [(hin_d, 0, SEQ, list(A.get("rhin", [])))]
    stage = [al(f"hstage{i}", [128, 1024], BF16) for i in range(2)]
    rstage = [Res() for _ in range(2)]
    c.rhT_t = [Res() for _ in range(c.NT)]
    c.rhT = lambda a, b: [c.rhT_t[t] for t in range(a // 128, (b + 127) // 128)]
    c.vtm = al("vtm", [128, c.NT, 384], BF16)
    c.rv = [Res() for _ in range(c.NT)]

    def vproj(t):
        bank = t % 2
        for kc in range(8):
            S.pe(lambda e, kc=kc, t=t, bank=bank: e.matmul(
                c.ps[bank][:, 0:384], lhsT=c.hT[:, kc, t * 128:(t + 1) * 128], rhs=c.win[:, kc, 12 * 128:15 * 128],
                start=(kc == 0), stop=(kc == 7)), reads=c.rhT(t * 128, (t + 1) * 128) + [c.rwin], writes=[c.rps[bank]])
        S.act(lambda e, t=t, bank=bank: e.activation(out=c.vtm[:, t, :], in_=c.ps[bank][:, 0:384], func=AF.Copy),
              reads=[c.rps[bank]], writes=[c.rv[t]])

    tiles = []
    for (src, t0, n, rsrc) in pieces:
        for j in range(n // 128):
            tiles.append((src[j * 128:(j + 1) * 128, :], t0 // 128 + j, rsrc))
    for i, (src_t, t, rsrc) in enumerate(tiles):
        sb = i % 2
        S.dma(stage[sb], src_t, reads=rsrc, writes=[rstage[sb]])
        bank = 6 + i % 2
        psT = c.ps[bank].bitcast(BF16)
        for kc in range(8):
            S.pe(lambda e, kc=kc, sb=sb, psT=psT: e.transpose(out=psT[:, kc * 128:(kc + 1) * 128], in_=stage[sb][:, kc * 128:(kc + 1) * 128],
                                                              identity=c.K["ident"]), reads=[rstage[sb], c.rK], writes=[c.rps[bank]])
        ev = S.act if i % 2 == 0 else S.dve
        if i % 2 == 0:
            S.act(lambda e, t=t, psT=psT: e.activation(out=c.hT[:, :, t * 128:(t + 1) * 128], in_=psT.rearrange("p (k c) -> p k c", k=8), func=AF.Copy),
                  reads=[c.rps[bank]], writes=[c.rhT_t[t]])
        else:
            S.dve(lambda e, t=t, psT=psT: e.tensor_copy(out=c.hT[:, :, t * 128:(t + 1) * 128], in_=psT.rearrange("p (k c) -> p k c", k=8)),
                  reads=[c.rps[bank]], writes=[c.rhT_t[t]])
        if i >= 2:
            vproj(tiles[i - 2][1])
    for (src_t, t, rsrc) in tiles[-2:]:
        vproj(t)
    c.pcount = 0
    c.ry_list = []
    c.ry_by_mixer = {}

    def new_yres():
        r = Res()
        c.ry_list.append(r)
        return r
    c.new_yres = new_yres
    return c


def proj_fm(c, slname, blk, nblk=1):
    S = c.S
    j = SL[slname]
    banks = getattr(c, "proj_banks", (0, 1))
    bank = banks[c.pcount % len(banks)]
    c.pcount += 1
    for kc in range(8):
        c.last_proj = S.pe(lambda e, kc=kc, j=j, blk=blk, bank=bank: e.matmul(
            c.ps[bank], lhsT=c.win[:, kc, j * 128:(j + 1) * 128], rhs=c.hT[:, kc, blk * 512:(blk + 1) * 512],
            start=(kc == 0), stop=(kc == 7)), reads=c.rhT(blk * 512, (blk + 1) * 512) + [c.rwin], writes=[c.rps[bank]])
    return bank


def small_params(c):
    S, pv, rpv = c.S, c.pv, c.rpv
    col = lambda n: pv[:, PV[n]:PV[n] + 1]
    S.act(lambda e: e.activation(out=col("tmp0"), in_=col("lam"), func=AF.Exp, scale=-1.0), reads=[rpv], writes=[rpv])
    S.act(lambda e: e.activation(out=col("tmp0"), in_=col("tmp0"), func=AF.Ln, bias=c.K["one"], scale=1.0), reads=[rpv, c.rK], writes=[rpv])
    S.dve(lambda e: e.tensor_scalar(out=col("c1"), in0=col("tmp0"), scalar1=-8.0, scalar2=None, op0=ALU.mult), reads=[rpv], writes=[rpv])
    S.dve(lambda e: e.tensor_scalar(out=col("c1x2"), in0=col("tmp0"), scalar1=-16.0, scalar2=None, op0=ALU.mult), reads=[rpv], writes=[rpv])
    if c.layer == 0:
        S.dve(lambda e: e.memset(col("oml"), 1.0), writes=[rpv])
        S.dve(lambda e: e.memset(col("lnoml"), 0.0), writes=[rpv])
    else:
        S.dve(lambda e: e.tensor_tensor(out=col("tmp1"), in0=col("lbl"), in1=col("lb0"), op=ALU.subtract), reads=[rpv], writes=[rpv])
        S.act(lambda e: e.activation(out=col("tmp1"), in_=col("tmp1"), func=AF.Exp), reads=[rpv], writes=[rpv])
        S.act(lambda e: e.activation(out=col("lnoml"), in_=col("tmp1"), func=AF.Ln, bias=c.K["one"], scale=1.0), reads=[rpv, c.rK], writes=[rpv])
        S.dve(lambda e: e.tensor_scalar(out=col("lnoml"), in0=col("lnoml"), scalar1=-1.0, scalar2=None, op0=ALU.mult), reads=[rpv], writes=[rpv])
        S.act(lambda e: e.activation(out=col("oml"), in_=col("lnoml"), func=AF.Exp), reads=[rpv], writes=[rpv])


def mixer_A(c, yT_d):
    S, A, SEQ = c.S, c.A, c.SEQ
    al = A["alloc"]
    pv, rpv = c.pv, c.rpv
    col = lambda n: pv[:, PV[n]:PV[n] + 1]
    NH = 2 if SEQ >= 1024 else 1
    L = SEQ // NH
    NBH = L // 512
    xT = al("A_xT", [128, 3 + SEQ], F32)
    rxT = Res()
    XC = al("A_XC", [128, L], F32); rXC = Res()
    R = al("A_R", [128, L], F32); rR = Res()
    TH = al("A_TH", [128, L], F32); rTH = Res()
    AA = al("A_AA", [128, L], F32); rAA = Res()
    XCB = al("A_XCB", [128, L], BF16); rXCB = Res()
    ii = [al(f"A_ii{i}", [128, 512], F32) for i in range(2)]; rii = [Res(), Res()]
    gg = [al(f"A_gg{i}", [128, 512], F32) for i in range(2)]; rgg = [Res(), Res()]
    yst = [al(f"A_y{i}", [128, 512], BF16) for i in range(2)]; ryst = [Res(), Res()]
    carry = al("A_carry", [128, 1], F32); rcar = Res()
    S.dve(lambda e: e.memset(xT[:, 0:3], 0.0), writes=[rxT])
    S.dve(lambda e: e.memset(carry, 0.0), writes=[rcar])
    ps, rps = c.ps, c.rps
    for hf in range(NH):
        t0 = hf * L
        for b in range(NBH):
            blk = hf * NBH + b
            bank = proj_fm(c, "A_x", blk)
            S.act(lambda e, bank=bank, blk=blk: e.activation(out=xT[:, 3 + blk * 512:3 + (blk + 1) * 512], in_=ps[bank], func=AF.Copy),
                  reads=[rps[bank]], writes=[rxT])
        S.dve(lambda e, t0=t0: e.tensor_scalar(out=XC, in0=xT[:, t0 + 3:t0 + 3 + L], scalar1=col("cw3"), scalar2=col("cb"),
                                               op0=ALU.mult, op1=ALU.add), reads=[rxT, rpv], writes=[rXC])
        for w in range(3):
            S.dve(lambda e, t0=t0, w=w: e.scalar_tensor_tensor(out=XC, in0=xT[:, t0 + w:t0 + w + L], scalar=col(f"cw{w}"), in1=XC,
                                                              op0=ALU.mult, op1=ALU.add), reads=[rxT, rpv, rXC], writes=[rXC])
        S.act(lambda e: e.activation(out=XCB, in_=XC, func=AF.Copy), reads=[rXC], writes=[rXCB])
        for b in range(NBH):
            sl = slice(b * 512, (b + 1) * 512)
            S.pe(lambda e, sl=sl: e.matmul(ps[2], lhsT=c.wab[:, 0, :], rhs=XCB[:, sl], start=True, stop=True),
                 reads=[c.rwab, rXCB], writes=[rps[2]])
            S.pe(lambda e, sl=sl: e.matmul(ps[3], lhsT=c.wab[:, 1, :], rhs=XCB[:, sl], start=True, stop=True),
                 reads=[c.rwab, rXCB], writes=[rps[3]])
            S.act(lambda e, sl=sl: e.activation(out=R[:, sl], in_=ps[2], func=AF.Sigmoid, bias=col("ba"), scale=1.0),
                  reads=[rps[2], rpv], writes=[rR])
            S.act(lambda e, b=b: e.activation(out=ii[b % 2], in_=ps[3], func=AF.Sigmoid, bias=col("bx"), scale=1.0),
                  reads=[rps[3], rpv], writes=[rii[b % 2]])
            S.act(lambda e, sl=sl: e.activation(out=TH[:, sl], in_=R[:, sl], func=AF.Tanh, scale=col("c1")),
                  reads=[rR, rpv], writes=[rTH])
            S.dve(lambda e, sl=sl, b=b: e.tensor_tensor(out=XC[:, sl], in0=XC[:, sl], in1=ii[b % 2], op=ALU.mult),
                  reads=[rXC, rii[b % 2]], writes=[rXC])
        S.act(lambda e: e.activation(out=AA, in_=R, func=AF.Exp, scale=col("c1")), reads=[rR, rpv], writes=[rAA])
        S.act(lambda e: e.activation(out=R, in_=R, func=AF.Exp, scale=col("c1x2")), reads=[rR, rpv], writes=[rR])
        S.dve(lambda e: e.scalar_tensor_tensor(out=TH, in0=R, scalar=1.0, in1=TH, op0=ALU.add, op1=ALU.mult),
              reads=[rR, rTH], writes=[rTH])
        S.act(lambda e: e.activation(out=TH, in_=TH, func=AF.Ln, scale=-1.0), reads=[rTH], writes=[rTH])
        S.act(lambda e: e.activation(out=TH, in_=TH, func=AF.Exp, scale=0.5), reads=[rTH], writes=[rTH])
        S.dve(lambda e: e.tensor_tensor(out=XC, in0=XC, in1=TH, op=ALU.mult), reads=[rXC, rTH], writes=[rXC])
        S.dve(lambda e: e.tensor_tensor_scan(out=R, data0=AA, data1=XC, initial=carry, op0=ALU.mult, op1=ALU.add),
              reads=[rAA, rXC, rcar], writes=[rR])
        S.dve(lambda e: e.tensor_copy(out=carry, in_=R[:, L - 1:L]), reads=[rR], writes=[rcar])
        for b in range(NBH):
            blk = hf * NBH + b
            sl = slice(b * 512, (b + 1) * 512)
            bank = proj_fm(c, "A_g", blk)
            S.act(lambda e, bank=bank, b=b: e.activation(out=gg[b % 2], in_=ps[bank], func=AF.Gelu_apprx_tanh),
                  reads=[rps[bank]], writes=[rgg[b % 2]])
            S.dve(lambda e, sl=sl, b=b: e.tensor_tensor(out=yst[b % 2], in0=gg[b % 2], in1=R[:, sl], op=ALU.mult),
                  reads=[rgg[b % 2], rR], writes=[ryst[b % 2]])
            S.dma(yT_d[:, blk * 512:(blk + 1) * 512], yst[b % 2], reads=[ryst[b % 2]], writes=[c.new_yres()], is_output=True)


def gla(c, tag, qt, rqt, kt, rkt, En, rEn, voff, gslice, gcol, groupnorm, yT_d):
    S, A, SEQ = c.S, c.A, c.SEQ
    al = A["alloc"]
    ps, rps = c.ps, c.rps
    NCH = SEQ // 64
    NB = SEQ // 512
    pv, rpv = c.pv, c.rpv
    ident = c.K["ident"]
    psT = ps[6].bitcast(BF16)
    KVs = al(tag + "_KVs", [128, 64, NCH], F32); rKV = Res()
    Ss = al(tag + "_Ss", [128, 64, NCH], BF16); rSs = Res()
    En0 = al(tag + "_En0", [128, NCH], F32); rEn0 = Res()
    ktm4 = al(tag + "_ktm4", [128, 4, 128], BF16); rktm = Res()
    import os
    STOP = int(os.environ.get("GLA_STOP", "99"))
    if STOP == 0:
        S.dma(yT_d, qt, reads=[rqt], is_output=True)
        return
    for tg in range(NB):
        for i in range(4):
            t = tg * 4 + i
            S.pe(lambda e, i=i, t=t: e.transpose(out=psT[:, i * 128:(i + 1) * 128], in_=kt[:, t * 128:(t + 1) * 128], identity=ident),
                 reads=[rkt, c.rK], writes=[rps[6]])
        S.act(lambda e: e.activation(out=ktm4, in_=psT[:, 0:512].rearrange("p (i c) -> p i c", c=128), func=AF.Copy),
              reads=[rps[6]], writes=[rktm])
        for i in range(4):
            t = tg * 4 + i
            S.pe(lambda e, i=i, t=t: e.matmul(ps[2][:, i * 128:(i + 1) * 128], lhsT=ktm4[0:64, i, :], rhs=c.vtm[0:64, t, voff:voff + 128],
                                             start=True, stop=True), reads=[rktm, c.rv[t]], writes=[rps[2]])
            S.pe(lambda e, i=i, t=t: e.matmul(ps[3][:, i * 128:(i + 1) * 128], lhsT=ktm4[64:128, i, :], rhs=c.vtm[64:128, t, voff:voff + 128],
                                             start=True, stop=True), reads=[rktm, c.rv[t]], writes=[rps[3]])
        for par in range(2):
            for hl in range(2):
                rows = slice(hl * 64, (hl + 1) * 64)
                n0 = 8 * tg + par
                S.dve(lambda e, par=par, hl=hl, rows=rows, n0=n0, tg=tg: e.tensor_tensor(
                    out=KVs[rows, :, n0:8 * tg + 8:2].rearrange("p e n -> p n e"),
                    in0=ps[2 + par][rows, :].rearrange("p (i c) -> p i c", c=128)[:, :, hl * 64:(hl + 1) * 64],
                    in1=En[rows, n0:8 * tg + 8:2].unsqueeze(2).broadcast_to([64, 4, 64]), op=ALU.mult),
                    reads=[rps[2 + par], rEn], writes=[rKV])
    if STOP == 1:
        S.dma(yT_d[:, 0:64 * NCH // 2], KVs.rearrange("p e n -> p (e n)").bitcast(BF16)[:, 0:64 * NCH // 2], reads=[rKV], writes=[c.new_yres()], is_output=True)
        return
    S.dve(lambda e: e.tensor_copy(out=En0, in_=En), reads=[rEn], writes=[rEn0])
    S.dve(lambda e: e.memset(En0[:, 0:1], 0.0), writes=[rEn0])
    rSse = [Res() for _ in range(64)]
    for ee in range(64):
        S.dve(lambda e, ee=ee: e.tensor_tensor_scan(out=Ss[:, ee, :], data0=En0, data1=KVs[:, ee, :],
                                                    initial=0.0, op0=ALU.mult, op1=ALU.add), reads=[rEn0, rKV], writes=[rSse[ee]])
    if STOP == 2:
        S.dma(yT_d[:, 0:64 * NCH], Ss.rearrange("p e n -> p (e n)"), reads=rSse, writes=[c.new_yres()], is_output=True)
        return
    atm = [al(tag + f"_atm{i}", [128, 512], BF16) for i in range(2)]; ratm = [Res(), Res()]
    osb = al(tag + "_osb", [128, 512], F32); rosb = Res()
    ob = al(tag + "_ob", [128, 512], BF16); rob = Res()
    rstd = al(tag + "_rstd", [128, 512], F32); rrstd = Res()
    sg = al(tag + "_sg", [128, 512], F32); rsg = Res()
    yst = [al(tag + f"_y{i}", [128, 512], BF16) for i in range(2)]; ryst = [Res(), Res()]
    osb2 = [osb, al(tag + "_osb1", [128, 512], F32)]; rosb2 = [rosb, Res()]
    sg2 = [sg, al(tag + "_sg1", [128, 512], F32)]; rsg2 = [rsg, Res()]

    def front(tb):
        osb_, rosb_, sg_, rsg_ = osb2[tb % 2], rosb2[tb % 2], sg2[tb % 2], rsg2[tb % 2]
        for i in range(4):
            t = tb * 4 + i
            ts_ = slice(t * 128, (t + 1) * 128)
            for hl in range(2):
                rows = slice(hl * 64, (hl + 1) * 64)
                S.pe(lambda e, i=i, hl=hl, rows=rows, ts_=ts_: e.matmul(ps[2 + hl][:, i * 128:(i + 1) * 128], lhsT=kt[rows, ts_], rhs=qt[rows, ts_],
                                                                      start=True, stop=True), reads=[rkt, rqt], writes=[rps[2 + hl]])
        bank = proj_fm(c, gslice, tb)
        S.act(lambda e, bank=bank: e.activation(out=sg_, in_=ps[bank], func=AF.Silu), reads=[rps[bank]], writes=[rsg_])
        for hl in range(2):
            S.dve(lambda e, hl=hl: e.tensor_tensor(out=atm[hl], in0=ps[2 + hl], in1=c.K["glamask"], op=ALU.mult),
                  reads=[rps[2 + hl], c.rK], writes=[ratm[hl]])
        for i in range(4):
            t = tb * 4 + i
            for hl in range(2):
                rows = slice(hl * 64, (hl + 1) * 64)
                ncs = [n for n in (2 * t, 2 * t + 1) if n > 0]
                S.pe(lambda e, i=i, hl=hl, rows=rows, t=t, ncs=ncs: e.matmul(
                    ps[4][rows, i * 128:(i + 1) * 128], lhsT=c.vtm[:, t, voff + hl * 64:voff + (hl + 1) * 64], rhs=atm[hl][:, i * 128:(i + 1) * 128],
                    start=True, stop=(len(ncs) == 0)), reads=[c.rv[t], ratm[hl]], writes=[rps[4]])
                for n in ncs:
                    cc = n - 2 * t
                    S.pe(lambda e, i=i, hl=hl, rows=rows, n=n, cc=cc, ncs=ncs: e.matmul(
                        ps[4][rows, i * 128 + cc * 64:i * 128 + (cc + 1) * 64], lhsT=Ss[rows, :, n - 1], rhs=qt[rows, n * 64:(n + 1) * 64],
                        start=False, stop=(n == ncs[-1])), reads=rSse + [rqt], writes=[rps[4]])
        S.act(lambda e: e.activation(out=osb_, in_=ps[4], func=AF.Copy), reads=[rps[4]], writes=[rosb_])

    def back(tb):
        osb_, rosb_, sg_, rsg_ = osb2[tb % 2], rosb2[tb % 2], sg2[tb % 2], rsg2[tb % 2]
        if groupnorm:
            S.dve(lambda e: e.tensor_copy(out=ob, in_=osb_), reads=[rosb_], writes=[rob])
            S.pe(lambda e: e.matmul(ps[5], lhsT=c.K["blockmean"], rhs=ob, start=True, stop=True), reads=[rob, c.rK], writes=[rps[5]])
            S.dve(lambda e: e.tensor_tensor(out=osb_, in0=osb_, in1=ps[5], op=ALU.subtract), reads=[rosb_, rps[5]], writes=[rosb_])
        S.act(lambda e: e.activation(out=ob, in_=osb_, func=AF.Square), reads=[rosb_], writes=[rob])
        S.pe(lambda e: e.matmul(ps[5], lhsT=c.K["blockmean"], rhs=ob, start=True, stop=True), reads=[rob, c.rK], writes=[rps[5]])
        S.act(lambda e: e.activation(out=rstd, in_=ps[5], func=AF.Ln, bias=c.K["eps_n"], scale=1.0), reads=[rps[5], c.rK], writes=[rrstd])
        S.act(lambda e: e.activation(out=rstd, in_=rstd, func=AF.Exp, scale=-0.5), reads=[rrstd], writes=[rrstd])
        S.dve(lambda e: e.tensor_tensor(out=osb_, in0=osb_, in1=rstd, op=ALU.mult), reads=[rosb_, rrstd], writes=[rosb_])
        S.dve(lambda e: e.scalar_tensor_tensor(out=yst[tb % 2], in0=osb_, scalar=pv[:, PV[gcol]:PV[gcol] + 1], in1=sg_,
                                               op0=ALU.mult, op1=ALU.mult), reads=[rosb_, rsg_, rpv], writes=[ryst[tb % 2]])
        S.dma(yT_d[:, tb * 512:(tb + 1) * 512], yst[tb % 2], reads=[ryst[tb % 2]], writes=[c.new_yres()], is_output=True)

    for tb in range(NB + 1):
        if tb < NB:
            front(tb)
        if tb >= 1:
            back(tb - 1)


def mixer_B(c, yT_d):
    S, A, SEQ = c.S, c.A, c.SEQ
    al = A["alloc"]
    ps, rps = c.ps, c.rps
    NCH = SEQ // 64
    qt = al("B_qt", [128, SEQ], BF16); rqt = Res()
    kt = al("B_kt", [128, SEQ], BF16); rkt = Res()
    En = al("B_En", [128, NCH], F32); rEn = Res()
    tab = al("B_tab", [128, 4, 512], F32); rtab = Res()
    t1s = [al(f"B_t1{i}", [128, 512], F32) for i in range(2)]; rt1s = [Res(), Res()]
    t2s = [al(f"B_t2{i}", [128, 512], F32) for i in range(2)]; rt2s = [Res(), Res()]
    qbs = [al(f"B_qb{i}", [128, 512], BF16) for i in range(2)]; rqbs = [Res(), Res()]
    S.dve(lambda e: e.tensor_copy(out=En, in_=c.pv[:, PV["gam64"]:PV["gam64"] + 1].broadcast_to([128, NCH])), reads=[c.rpv], writes=[rEn])
    c.proj_banks = (0, 1, 2, 3)
    for blk in range(c.NB):
        sl = slice(blk * 512, (blk + 1) * 512)
        S.dma(tab, c.rettab_d[:, :, sl].rearrange("f p t -> p f t"), writes=[rtab])
        for which, dst, rdst, ti in (("B_q", qt, rqt, 0), ("B_k", kt, rkt, 2)):
            t1, rt1, t2, rt2 = t1s[ti // 2], rt1s[ti // 2], t2s[ti // 2], rt2s[ti // 2]
            b1 = proj_fm(c, which, blk)
            qb, rqb = qbs[ti // 2], rqbs[ti // 2]
            S.act(lambda e, b1=b1, qb=qb: e.activation(out=qb, in_=ps[b1], func=AF.Copy), reads=[rps[b1]], writes=[rqb])
            b2 = c.proj_banks[c.pcount % len(c.proj_banks)]
            c.pcount += 1
            S.pe(lambda e, b2=b2, qb=qb: e.matmul(ps[b2], lhsT=c.K["rotperm"], rhs=qb, start=True, stop=True),
                 reads=[rqb, c.rK], writes=[rps[b2]])
            S.dve(lambda e, b1=b1, ti=ti, t1=t1: e.tensor_tensor(out=t1, in0=ps[b1], in1=tab[:, ti, :], op=ALU.mult), reads=[rps[b1], rtab], writes=[rt1])
            S.dve(lambda e, b2=b2, ti=ti, t2=t2: e.tensor_tensor(out=t2, in0=ps[b2], in1=tab[:, ti + 1, :], op=ALU.mult), reads=[rps[b2], rtab], writes=[rt2])
            S.pool(lambda e, dst=dst, sl=sl, t1=t1, t2=t2: e.tensor_tensor(out=dst[:, sl], in0=t1, in1=t2, op=ALU.add), reads=[rt1, rt2], writes=[rdst])
    c.proj_banks = (0, 1)
    gla(c, "B", qt, rqt, kt, rkt, En, rEn, 0, "B_g", "retg", True, yT_d)


def mixer_D(c, yT_d):
    S, A, SEQ = c.S, c.A, c.SEQ
    al = A["alloc"]
    ps, rps = c.ps, c.rps
    pv, rpv = c.pv, c.rpv
    col = lambda n: pv[:, PV[n]:PV[n] + 1]
    NCH = SEQ // 64
    qt = al("D_qt", [128, SEQ], BF16); rqt = Res()
    kt = al("D_kt", [128, SEQ], BF16); rkt = Res()
    En = al("D_En", [128, NCH], F32); rEn = Res()
    T = [al(f"D_T{i}", [128, 512], F32) for i in range(6)]
    rT = [Res() for _ in range(6)]
    one = c.K["one"]
    c.proj_banks = (0, 1, 2, 3)
    for blk in range(c.NB):
        sl = slice(blk * 512, (blk + 1) * 512)
        bf_ = proj_fm(c, "D_f", blk)
        S.act(lambda e, b=bf_: e.activation(out=T[0], in_=ps[b], func=AF.Exp), reads=[rps[bf_]], writes=[rT[0]])
        S.act(lambda e: e.activation(out=T[0], in_=T[0], func=AF.Ln, bias=one, scale=1.0), reads=[rT[0], c.rK], writes=[rT[0]])
        S.act(lambda e: e.activation(out=T[1], in_=T[0], func=AF.Exp, bias=col("lnoml"), scale=-1.0), reads=[rT[0], rpv], writes=[rT[1]])
        S.act(lambda e: e.activation(out=T[2], in_=T[1], func=AF.Ln, bias=one, scale=-1.0), reads=[rT[1], c.rK], writes=[rT[2]])
        S.dve(lambda e: e.tensor_tensor_scan(out=T[3], data0=c.K["resetmask"], data1=T[2], initial=0.0, op0=ALU.mult, op1=ALU.add),
              reads=[rT[2], c.rK], writes=[rT[3]])
        S.act(lambda e: e.activation(out=T[4], in_=T[3], func=AF.Exp), reads=[rT[3]], writes=[rT[4]])
        S.act(lambda e: e.activation(out=T[5], in_=T[3], func=AF.Exp, scale=-1.0), reads=[rT[3]], writes=[rT[5]])
        bq = proj_fm(c, "D_q", blk)
        S.dve(lambda e, b=bq, sl=sl: e.tensor_tensor(out=qt[:, sl], in0=ps[b], in1=T[4], op=ALU.mult), reads=[rps[bq], rT[4]], writes=[rqt])
        S.pool(lambda e, sl=sl: e.tensor_tensor(out=kt[:, sl], in0=T[1], in1=T[5], op=ALU.mult), reads=[rT[1], rT[5]], writes=[rkt])
        S.dve(lambda e, blk=blk: e.tensor_copy(out=En[:, blk * 8:(blk + 1) * 8], in_=T[4][:, 63:512:64]), reads=[rT[4]], writes=[rEn])
    c.proj_banks = (0, 1)
    gla(c, "D", qt, rqt, kt, rkt, En, rEn, 256, "D_g", "hgg", False, yT_d)


def mixer_C(c, yT_d):
    S, A, SEQ = c.S, c.A, c.SEQ
    al = A["alloc"]
    ps, rps = c.ps, c.rps
    NQ = SEQ // 512
    qT = al("C_qT", [128, SEQ], BF16); rqT = Res()
    kT = al("C_kT", [128, SEQ], BF16); rkT = Res()
    c.proj_banks = (0, 1, 2, 3)
    for blk in range(c.NB):
        sl = slice(blk * 512, (blk + 1) * 512)
        b = proj_fm(c, "C_q", blk)
        S.act(lambda e, b=b, sl=sl: e.activation(out=qT[:, sl], in_=ps[b], func=AF.Copy, scale=0.125), reads=[rps[b]], writes=[rqT])
        b = proj_fm(c, "C_k", blk)
        S.dve(lambda e, b=b, sl=sl: e.tensor_copy(out=kT[:, sl], in_=ps[b]), reads=[rps[b]], writes=[rkT])
    c.proj_banks = (0, 1)
    NE = 4
    eb = [al(f"C_e{i}", [128, 512], BF16) for i in range(NE)]; reb = [Res() for _ in range(NE)]
    msp = [al(f"C_msp{i}", [128, 512], BF16) for i in range(NE)]; rmsp = [Res() for _ in range(NE)]
    exr = [al(f"C_exr{i}", [128, 512], BF16) for i in range(2)]; rexr = [Res() for _ in range(2)]
    wT = [al(f"C_w{i}", [128, 512], BF16) for i in range(NE)]; rwT = [Res() for _ in range(NE)]
    chi = [al(f"C_chi{i}", [1, 512], BF16) for i in range(2)]; rchi = [Res(), Res()]
    clo = [al(f"C_clo{i}", [1, 512], BF16) for i in range(2)]; rclo = [Res(), Res()]
    yst = [al(f"C_y{i}", [128, 512], BF16) for i in range(2)]; ryst = [Res(), Res()]
    negtri, ones_row, cmask = c.K["negtri"], c.K["ones_row"], c.K["cmask"]
    steps = []
    for qb in range(NQ):
        for kb in range(4 * qb + 3, -1, -1):
            for h in range(2):
                steps.append((qb, kb, h))
    N = len(steps)

    def cols(i):
        qb, kb, h = steps[i]
        j = kb - 4 * qb
        return slice(max(j, 0) * 128, 512)

    def stage_Z(i):
        qb, kb, h = steps[i]
        rows = slice(h * 64, (h + 1) * 64)
        cs = cols(i)
        q0 = qb * 512
        S.pe(lambda e: e.matmul(ps[2 + h][:, cs], lhsT=kT[rows, kb * 128:(kb + 1) * 128], rhs=qT[rows, q0 + cs.start:q0 + 512],
                                start=True, stop=True), reads=[rkT, rqT], writes=[rps[2 + h]])
        S.act(lambda e: e.activation(out=eb[i % NE][:, cs], in_=ps[2 + h][:, cs], func=AF.Exp), reads=[rps[2 + h]], writes=[reb[i % NE]])
        j = kb - 4 * qb
        if j >= 0:
            dg = slice(j * 128, (j + 1) * 128)
            S.dve(lambda e: e.tensor_tensor(out=eb[i % NE][:, dg], in0=eb[i % NE][:, dg], in1=cmask[:, j, dg], op=ALU.mult),
                  reads=[reb[i % NE], c.rK], writes=[reb[i % NE]])
        S.act(lambda e: e.activation(out=msp[i % NE][:, cs], in_=eb[i % NE][:, cs], func=AF.Ln, bias=1.0, scale=1.0),
              reads=[reb[i % NE]], writes=[rmsp[i % NE]])

    def stage_R(i):
        qb, kb, h = steps[i]
        first = (kb == 4 * qb + 3)
        cs = cols(i)
        S.pe(lambda e: e.matmul(ps[4 + h][:, cs], lhsT=negtri, rhs=msp[i % NE][:, cs], start=first, stop=False, skip_group_check=True),
             reads=[rmsp[i % NE], c.rK], writes=[rps[4 + h]])
        S.act(lambda e: e.activation(out=exr[i % 2][:, cs], in_=ps[4 + h][:, cs], func=AF.Exp), reads=[rps[4 + h]], writes=[rexr[i % 2]])
        S.dve(lambda e: e.tensor_tensor(out=wT[i % NE][:, cs], in0=eb[i % NE][:, cs], in1=exr[i % 2][:, cs], op=ALU.mult),
              reads=[reb[i % NE], rexr[i % 2]], writes=[rwT[i % NE]])

    def stage_O(i):
        qb, kb, h = steps[i]
        rows = slice(h * 64, (h + 1) * 64)
        ob = 6 + qb % 2
        first = (kb == 4 * qb + 3)
        cs = cols(i)
        if kb > 0:
            S.pe(lambda e: e.matmul(ps[4 + h][:, cs], lhsT=c.K["negcompl"], rhs=msp[i % NE][:, cs], start=False, stop=(kb == 1), skip_group_check=True),
                 reads=[rmsp[i % NE], c.rK], writes=[rps[4 + h]])
        S.pe(lambda e: e.matmul(ps[ob][rows, cs], lhsT=c.vtm[:, kb, 128 + h * 64:128 + (h + 1) * 64], rhs=wT[i % NE][:, cs],
                                start=first, stop=(kb == 0), skip_group_check=True), reads=[c.rv[kb], rwT[i % NE]], writes=[rps[ob]])
        if kb == 0 and h == 1:
            S.act(lambda e: e.activation(out=yst[qb % 2], in_=ps[ob], func=AF.Copy), reads=[rps[ob]], writes=[ryst[qb % 2]])
            S.dma(yT_d[:, qb * 512:(qb + 1) * 512], yst[qb % 2], reads=[ryst[qb % 2]], writes=[c.new_yres()], is_output=True)

    for s in range(-2, N):
        if 0 <= s + 2 < N:
            stage_Z(s + 2)
        if 0 <= s + 1 < N:
            stage_R(s + 1)
        if 0 <= s < N:
            stage_O(s)


def phase_M(nc, S, A, SEQ, layer, hin_d, win_d, pvec_d, wab_d, consts, rettab_d, yT_d, which="ABDC"):
    ar = A["arena"]
    c = setup_M(nc, S, A, SEQ, layer, hin_d, win_d, pvec_d, wab_d, consts)
    c.rettab_d = rettab_d
    small_params(c)
    m = ar.mark()
    fns = {"A": (mixer_A, 0), "B": (mixer_B, 1), "C": (mixer_C, 2), "D": (mixer_D, 3)}
    for i, ch in enumerate(which):
        if i > 0:
            S.barrier(A["bar_scratch"])
            ar.reset(m)
        fn, slot = fns[ch]
        fn(c, yT_d[slot] if isinstance(yT_d, (list, tuple)) else yT_d[slot * 128:(slot + 1) * 128, :])
        c.ry_by_mixer[ch] = c.ry_list
        c.ry_list = []
    return c


BF = ml_dtypes.bfloat16
REF_SLICE = {"A_x": 0, "A_g": 1, "B_q": 2, "B_k": 3, "B_v": 4, "B_g": 5, "C_q": 6, "C_k": 7, "C_v": 8,
             "D_q": 9, "D_f": 10, "D_v": 11, "D_g": 12}
MY_SLICES = ["A_x", "A_g", "B_q", "B_qsw", "B_k", "B_ksw", "B_g", "C_q", "C_k", "D_q", "D_f", "D_g", "B_v", "C_v", "D_v"]


def core_cols(hh):
    p = np.arange(128)
    partner = (p // 64) * 64 + ((p % 64) + 32) % 64
    cols = []
    for n in MY_SLICES:
        sw = n.endswith("sw")
        base = REF_SLICE[n[:-2] if sw else n] * 256 + hh * 128
        cols.append(base + (partner if sw else p))
    return np.concatenate(cols)


def prep_layer_core(inp, l, hh):
    ch = slice(hh * 128, (hh + 1) * 128)
    out = {}
    out["win"] = np.ascontiguousarray(np.asarray(inp["w_in"][l])[:, core_cols(hh)])
    pv = np.zeros((128, 13), np.float32)
    cw = np.asarray(inp["conv_w"][l])
    for w in range(4):
        pv[:, w] = cw[w, ch]
    pv[:, 4] = np.asarray(inp["conv_b"][l])[ch]
    pv[:, 5] = np.asarray(inp["rg_ba"][l]).reshape(-1)[ch]
    pv[:, 6] = np.asarray(inp["rg_bx"][l]).reshape(-1)[ch]
    pv[:, 7] = np.asarray(inp["rg_lambda"][l])[ch]
    pv[:, 8] = np.asarray(inp["ret_norm_g"][l])[ch]
    pv[:, 9] = np.asarray(inp["hgrn_norm_g"][l])[ch]
    pv[:, 10] = np.asarray(inp["hgrn_lb_logits"][0])[ch]
    pv[:, 11] = np.asarray(inp["hgrn_lb_logits"][l])[ch]
    for hl in range(2):
        gam = 1.0 - 2.0 ** (-5.0 - (2 * hh + hl))
        pv[hl * 64:(hl + 1) * 64, 12] = gam ** 64
    out["pvec"] = pv
    wab = np.zeros((128, 2, 128), np.float32)
    for hl in range(2):
        s = slice(hl * 64, (hl + 1) * 64)
        wab[s, 0, s] = np.asarray(inp["rg_wa"][l])[2 * hh + hl]
        wab[s, 1, s] = np.asarray(inp["rg_wx"][l])[2 * hh + hl]
    out["wab"] = wab
    return out


def const_tables(hh, SEQ):
    K = {}
    K["ident"] = np.eye(128).astype(BF)
    j = np.arange(128)
    K["negtri"] = (-(j[:, None] >= j[None, :]).astype(np.float32)).astype(BF)
    K["negcompl"] = (-(j[:, None] < j[None, :]).astype(np.float32)).astype(BF)
    partner = (j // 64) * 64 + ((j % 64) + 32) % 64
    rp = np.zeros((128, 128), np.float32)
    rp[partner, j] = 1.0
    K["rotperm"] = rp.astype(BF)
    K["ones_row"] = np.ones((1, 128), np.float32).astype(BF)
    K["one"] = np.ones((128, 1), np.float32)
    K["eps_n"] = np.full((128, 1), 1e-6, np.float32)
    q = np.arange(512)
    cm = np.zeros((128, 4, 512), np.float32)
    for jb in range(4):
        cm[:, jb, :] = ((jb * 128 + j)[:, None] < q[None, :])
    K["cmask"] = cm.astype(BF)
    s = np.arange(128)
    gm = ((s[:, None] // 64 == s[None, :] // 64) & (s[:, None] <= s[None, :])).astype(np.float32)
    K["glamask"] = np.tile(gm, (1, 4)).astype(BF)
    K["blockmean"] = ((s[:, None] // 64 == s[None, :] // 64) / 64.0).astype(np.float32).astype(BF)
    rm = np.ones((128, 512), np.float32)
    rm[:, ::64] = 0.0
    K["resetmask"] = rm
    d = np.arange(64)
    inv_freq = (10000.0 ** (-np.arange(0, 64, 2, dtype=np.float32) / 64)).astype(np.float32)
    t = np.arange(SEQ, dtype=np.float32)
    ang = (t[:, None] * inv_freq[None, :]).astype(np.float32)
    cos, sin = np.cos(ang).T, np.sin(ang).T
    tl = (np.arange(SEQ) % 64 + 1).astype(np.float64)
    tabs = np.zeros((4, 128, SEQ), np.float32)
    for hl in range(2):
        gam = 1.0 - 2.0 ** (-5.0 - (2 * hh + hl))
        lg = np.log1p(-2.0 ** (-5.0 - (2 * hh + hl)))
        dq = np.exp(lg * tl)
        dk = np.exp(-lg * tl) / 8.0
        for dd in range(64):
            p = hl * 64 + dd
            c_, s_ = cos[dd % 32], sin[dd % 32]
            sg = -1.0 if dd < 32 else 1.0
            tabs[0, p] = c_ * dq
            tabs[1, p] = sg * s_ * dq
            tabs[2, p] = c_ * dk
            tabs[3, p] = sg * s_ * dk
    K["rettab"] = tabs
    return K


from concourse.bass_utils import run_bass_kernel_spmd

SEQ_FULL = 4096
T_OWN = 2048
SMALLK = ["ident", "negtri", "negcompl", "rotperm", "ones_row", "one", "eps_n", "cmask", "glamask", "blockmean", "resetmask"]


def _mk_A(nc):
    A = {}
    ar = Arena(nc)
    A["arena"] = ar
    A["alloc"] = ar.alloc
    A["psum"] = [nc.alloc_psum_tensor(f"ps{i}", [128, 512], F32).ap() for i in range(8)]
    A["rpsum"] = [Res(excl=True) for _ in range(8)]
    A["bar_scratch"] = ar.alloc("bar", [128, 1], F32)
    return A


def _np_dt(a):
    return BF16 if a.dtype == BF else F32


def build_pre():
    nc = bass.Bass("TRN2", target_bir_lowering=False)
    S = Sched(nc)
    dt = lambda n, s, d, k="ExternalInput": nc.dram_tensor(n, s, d, kind=k).ap()
    x_d = dt("x", [T_OWN, 1024], F32)
    g_d = dt("ln_g", [1024], F32)
    b_d = dt("ln_b", [1024], F32)
    h32 = dt("h32", [T_OWN, 1024], F32, "ExternalOutput")
    h16 = dt("h16", [T_OWN, 1024], BF16, "ExternalOutput")
    A = _mk_A(nc)
    al = A["alloc"]
    eps = al("eps", [128, 1], F32)
    reps = Res()
    S.dve(lambda e: e.memset(eps, LN_EPS), writes=[reps])
    A["eps_ln"] = eps
    lnp = al("lnp", [128, 2, 1024], F32); rln = Res()
    if True:
        dmy = al("dmy", [128, 128], BF16); rd = Res()
        S.dve(lambda e: e.memset(dmy, 0.0), writes=[rd])
        S.pe(lambda e: e.matmul(A["psum"][0][:, 0:128], lhsT=dmy, rhs=dmy, start=True, stop=True), reads=[rd], writes=[A["rpsum"][0]])
    S.dma(lnp[:, 0, :], g_d.partition_broadcast(128), writes=[rln])
    S.dma(lnp[:, 1, :], b_d.partition_broadcast(128), writes=[rln])
    NT = T_OWN // 128
    xt = [al(f"xt{i}", [128, 1024], F32) for i in range(2)]; rxt = [Res(), Res()]
    tmp = [al(f"tmp{i}", [128, 1024], F32) for i in range(2)]; rtmp = [Res(), Res()]
    hb = [al(f"hb{i}", [128, 1024], BF16) for i in range(2)]; rhb = [Res(), Res()]
    st = [al(f"st{i}", [128, 16], F32) for i in range(2)]; rst = [Res(), Res()]
    for t in range(NT):
        p = t % 2
        S.dma(xt[p], x_d[t * 128:(t + 1) * 128, :], writes=[rxt[p]])
        ln_tile(S, nc, A, xt[p], rxt[p], lnp[:, 0, :], lnp[:, 1, :], rln, xt[p], rxt[p], tmp[p], rtmp[p], st[p], rst[p], t)
        S.dma(h32[t * 128:(t + 1) * 128, :], xt[p], reads=[rxt[p]], is_output=True)
        S.act(lambda e, p=p: e.activation(out=hb[p], in_=xt[p], func=AF.Copy), reads=[rxt[p]], writes=[rhb[p]])
        S.dma(h16[t * 128:(t + 1) * 128, :], hb[p], reads=[rhb[p]], is_output=True)
    build_and_emit(nc, S)
    return nc


def build_M(layer, Kh):
    nc = bass.Bass("TRN2", target_bir_lowering=False)
    S = Sched(nc)
    dt = lambda n, s, d, k="ExternalInput": nc.dram_tensor(n, s, d, kind=k).ap()
    SEQ = SEQ_FULL
    hin = dt("hin", [SEQ, 1024], BF16)
    win = dt("win", [1024, 1920], F32)
    pvec = dt("pvec", [128, 13], F32)
    wab = dt("wab", [128, 2, 128], F32)
    consts = {}
    for n in SMALLK:
        a = Kh[n]
        consts[n] = (dt("k_" + n, list(a.shape), _np_dt(a)), list(a.shape), _np_dt(a))
    rettab = dt("rettab", [4, 128, SEQ], F32)
    yT = dt("yT", [512, SEQ], BF16, "ExternalOutput")
    A = _mk_A(nc)
    phase_M(nc, S, A, SEQ, layer, hin, win, pvec, wab, consts, rettab, yT)
    build_and_emit(nc, S)
    return nc


def build_F():
    nc = bass.Bass("TRN2", target_bir_lowering=False)
    S = Sched(nc)
    dt = lambda n, s, d, k="ExternalInput": nc.dram_tensor(n, s, d, kind=k).ap()
    T = T_OWN
    yT_d = dt("yT", [1024, T], BF16)
    h_d = dt("h", [T, 1024], F32)
    wout = dt("w_out", [1024, 1024], F32)
    wup = dt("w_up", [1024, 4096], F32)
    wdn = dt("w_down", [4096, 1024], F32)
    l1g, l1b, l2g, l2b = [dt(n, [1024], F32) for n in ("l1g", "l1b", "l2g", "l2b")]
    ident_d = dt("ident", [128, 128], BF16)
    out = dt("h32", [T, 1024], F32, "ExternalOutput")
    outb = dt("h16", [T, 1024], BF16, "ExternalOutput")
    A = _mk_A(nc)
    al = A["alloc"]
    A["ident_bf"] = al("ident", [128, 128], BF16)
    rid = Res()
    S.dma(A["ident_bf"], ident_d, writes=[rid])
    eps = al("eps", [128, 1], F32)
    S.dve(lambda e: e.memset(eps, LN_EPS), writes=[rid])
    A["eps_ln"] = eps
    NT = T // 128
    hres = al("hres", [128, NT, 1024], F32)
    rh = [Res() for _ in range(NT)]
    for t in range(NT):
        S.dma(hres[:, t, :], h_d[t * 128:(t + 1) * 128, :], writes=[rh[t]])
    phase_F(nc, S, A, T, yT_d, wout, l1g, l1b, wup, wdn, l2g, l2b, hres, rh, out_f32_d=out, out_bf16_d=outb)
    build_and_emit(nc, S)
    return nc


def wout_perm():
    idx = []
    for hh in range(2):
        for m in range(4):
            idx.append(m * 256 + hh * 128 + np.arange(128))
    return np.concatenate(idx)


def kernel_unfused(**inputs):
    inp = {k: np.asarray(v) for k, v in inputs.items()}
    x = inp["x"]
    NC = 8
    cores = list(range(NC))
    f32 = np.float32
    Kh = [const_tables(hh, SEQ_FULL) for hh in range(2)]
    nc_pre = build_pre()
    im = []
    for c in cores:
        b, hh = c // 2, c % 2
        im.append({"x": np.ascontiguousarray(x[b, hh * T_OWN:(hh + 1) * T_OWN, :]), "ln_g": inp["ln_in_g"], "ln_b": inp["ln_in_b"]})
    res = run_bass_kernel_spmd(nc_pre, im, core_ids=cores).results
    h32 = [r["h32"] for r in res]
    h16 = [r["h16"] for r in res]
    nc_F = build_F()
    perm = wout_perm()
    for l in range(2):
        nc_M = build_M(l, Kh[0])
        im = []
        for c in cores:
            b, hh = c // 2, c % 2
            pc = prep_layer_core(inp, l, hh)
            d = {"hin": np.concatenate([h16[2 * b], h16[2 * b + 1]], axis=0), "win": pc["win"], "pvec": pc["pvec"], "wab": pc["wab"],
                 "rettab": Kh[hh]["rettab"]}
            for n in SMALLK:
                d["k_" + n] = Kh[hh][n]
            im.append(d)
        res = run_bass_kernel_spmd(nc_M, im, core_ids=cores).results
        yT = [r["yT"] for r in res]
        im = []
        wo = np.ascontiguousarray(inp["w_out"][l][perm, :])
        for c in cores:
            b, hh = c // 2, c % 2
            ya = np.concatenate([yT[2 * b], yT[2 * b + 1]], axis=0)[:, hh * T_OWN:(hh + 1) * T_OWN]
            im.append({"yT": np.ascontiguousarray(ya), "h": h32[c], "w_out": wo, "w_up": inp["w_up"][l], "w_down": inp["w_down"][l],
                       "l1g": inp["ln1_g"][l], "l1b": inp["ln1_b"][l], "l2g": inp["ln2_g"][l], "l2b": inp["ln2_b"][l],
                       "ident": Kh[0]["ident"]})
        res = run_bass_kernel_spmd(nc_F, im, core_ids=cores).results
        h32 = [r["h32"] for r in res]
        h16 = [r["h16"] for r in res]
    out = np.zeros((4, SEQ_FULL, 1024), f32)
    for c in cores:
        b, hh = c // 2, c % 2
        out[b, hh * T_OWN:(hh + 1) * T_OWN, :] = h32[c]
    return out


GROUPS = [[0, 1], [2, 3], [4, 5], [6, 7]]
U32 = mybir.dt.uint32


def build_fused(Kh):
    nc = bass.Bass("TRN2", target_bir_lowering=False)
    S = Sched(nc)
    dt = lambda n, s, d, k="ExternalInput": nc.dram_tensor(n, s, d, kind=k).ap()
    SEQ, T = SEQ_FULL, T_OWN
    NT = T // 128
    x_d = dt("x", [T, 1024], F32)
    g_d = dt("ln_g", [1024], F32)
    b_d = dt("ln_b", [1024], F32)
    gidx_d = dt("gidx", [128, 8], U32)
    L = []
    for l in range(2):
        L.append(dict(
            win=dt(f"win{l}", [1024, 1920], F32), pvec=dt(f"pvec{l}", [128, 13], F32), wab=dt(f"wab{l}", [128, 2, 128], F32),
            wout=dt(f"w_out{l}", [1024, 1024], F32), wup=dt(f"w_up{l}", [1024, 4096], F32), wdn=dt(f"w_down{l}", [4096, 1024], F32),
            l1g=dt(f"l1g{l}", [1024], F32), l1b=dt(f"l1b{l}", [1024], F32), l2g=dt(f"l2g{l}", [1024], F32), l2b=dt(f"l2b{l}", [1024], F32)))
    kd = {}
    for n in SMALLK:
        a = Kh[n]
        kd[n] = (dt("k_" + n, list(a.shape), _np_dt(a)), list(a.shape), _np_dt(a))
    rettab = dt("rettab", [4, 128, SEQ], F32)
    out_d = dt("out", [T, 1024], F32, "ExternalOutput")
    H = T // 2
    hx_loc = [nc.dram_tensor(f"hx_loc{i}", [H, 1024], BF16).ap() for i in range(2)]
    hx_all = [nc.dram_tensor(f"hx_all{i}", [2 * H, 1024], BF16).ap() for i in range(2)]
    y_loc = [nc.dram_tensor(f"y_loc{i}", [256, SEQ], BF16).ap() for i in range(2)]
    y_all = [nc.dram_tensor(f"y_all{i}", [512, SEQ], BF16).ap() for i in range(2)]
    hsp = nc.dram_tensor("hspill", [T, 1024], F32).ap()

    def hx_dst(t):
        i, r = divmod(t * 128, H)
        return hx_loc[i][r:r + 128, :]

    hx_rr = [Res(), Res()]

    def gather_half(i, rhxl):
        S.cc("AllGather", [hx_loc[i]], [hx_all[i]], GROUPS, reads=rhxl[i * (NT // 2):(i + 1) * (NT // 2)], writes=[hx_rr[i]])

    def gather_hx(rhxl, done=()):
        rr = hx_rr
        for i in range(2):
            if i not in done:
                gather_half(i, rhxl)
        return [(hx_all[0][0:H, :], 0, H, [rr[0]]), (hx_all[1][0:H, :], H, H, [rr[1]]),
                (hx_all[0][H:2 * H, :], 2 * H, H, [rr[0]]), (hx_all[1][H:2 * H, :], 3 * H, H, [rr[1]])]
    A = _mk_A(nc)
    ar = A["arena"]
    al = A["alloc"]
    S.cc_scratch = al("ccs", [128, 1], F32)
    eps = al("eps", [128, 1], F32)
    rconst = Res()
    S.dve(lambda e: e.memset(eps, LN_EPS), writes=[rconst])
    A["eps_ln"] = eps
    A["ident_bf"] = al("identF", [128, 128], BF16)
    S.dma(A["ident_bf"], kd["ident"][0], writes=[rconst])
    gidx = al("gidx", [128, 8], U32); rgidx = Res()
    S.dma(gidx, gidx_d, writes=[rgidx])
    base = ar.mark()
    rhsp = [Res() for _ in range(NT)]
    rhxl = [Res() for _ in range(NT)]
    lnp = al("lnp", [128, 2, 1024], F32); rln = Res()
    S.dma(lnp[:, 0, :], g_d.partition_broadcast(128), writes=[rln])
    S.dma(lnp[:, 1, :], b_d.partition_broadcast(128), writes=[rln])
    xt = [al(f"xt{i}", [128, 1024], F32) for i in range(NT)]; rxt = [Res() for _ in range(NT)]
    tmp = [al(f"tmp{i}", [128, 1024], F32) for i in range(4)]; rtmp = [Res() for _ in range(4)]
    hb = [al(f"hb{i}", [128, 1024], BF16) for i in range(4)]; rhb = [Res() for _ in range(4)]
    st = [al(f"st{i}", [128, 16], F32) for i in range(4)]; rst = [Res() for _ in range(4)]
    for t in range(NT):
        S.dma(xt[t], x_d[t * 128:(t + 1) * 128, :], writes=[rxt[t]])
    for t in range(NT):
        p = t % 4
        ln_tile(S, nc, A, xt[t], rxt[t], lnp[:, 0, :], lnp[:, 1, :], rln, xt[t], rxt[t], None, None, st[p], rst[p], t)
        S.act(lambda e, t=t: e.activation(out=hb[t % 4], in_=xt[t], func=AF.Copy), reads=[rxt[t]], writes=[rhb[t % 4]])
        S.dma(hsp[t * 128:(t + 1) * 128, :], xt[t], reads=[rxt[t]], writes=[rhsp[t]])
        S.dma(hx_dst(t), hb[t % 4], reads=[rhb[t % 4]], writes=[rhxl[t]])
        if t == NT // 2 - 1:
            gather_half(0, rhxl)
    hin_pieces = gather_hx(rhxl, done=(0,))
    import os
    NL = int(os.environ.get("FUSE_LAYERS", "2"))
    PH = os.environ.get("FUSE_PHASES", "MF")
    for l in range(NL):
        W = L[l]
        S.barrier(A["bar_scratch"])
        ar.reset(base)
        yslots = [y_loc[0][0:128, :], y_loc[0][128:256, :], y_loc[1][0:128, :], y_loc[1][128:256, :]]
        c = phase_M(nc, S, A, SEQ, l, hin_pieces, W["win"], W["pvec"], W["wab"], kd, rettab, yslots)
        ar.reset(base)
        hres = al("hres", [128, NT, 1024], F32)
        rh = [Res() for _ in range(NT)]
        wo_pre = al("wo_pre", [128, 8, 1024], BF16); rwo_pre = Res()
        S.dma(wo_pre, W["wout"].rearrange("(k p) n -> p k n", p=128), writes=[rwo_pre], eng="gpsimd", extra_deps=[c.last_proj])
        for t in range(NT):
            S.dma(hres[:, t, :], hsp[t * 128:(t + 1) * 128, :], reads=[rhsp[t]], writes=[rh[t]], eng="gpsimd", extra_deps=[c.last_proj])
        ry_all = [Res(), Res()]
        S.cc("AllGather", [y_loc[0]], [y_all[0]], GROUPS, reads=c.ry_by_mixer["A"] + c.ry_by_mixer["B"], writes=[ry_all[0]])
        S.cc("AllGather", [y_loc[1]], [y_all[1]], GROUPS, reads=c.ry_by_mixer["C"] + c.ry_by_mixer["D"], writes=[ry_all[1]])
        if PH == "M":
            continue
        S.barrier(A["bar_scratch"])
        last = (l == NL - 1)
        src = [ya.rearrange("c (h t) -> (c h) t", h=2) for ya in y_all]
        phase_F(nc, S, A, T, None, W["wout"], W["l1g"], W["l1b"], W["wup"], W["wdn"], W["l2g"], W["l2b"], hres, rh,
                out_f32_d=(out_d if last else hsp), out_bf16_d=(None if last else hx_dst),
                y_gather=(src, gidx, rgidx, ry_all), rout32=(None if last else rhsp), rout16=(None if last else rhxl),
                final_out=last, wo_pre=(wo_pre, rwo_pre),
                on_tile_done=(None if last else (lambda t: gather_half(0, rhxl) if t == NT // 2 - 1 else None)))
        if not last:
            hin_pieces = gather_hx(rhxl, done=(0,))
    if NL == 0 or PH == "M":
        S.barrier(A["bar_scratch"])
        ar.reset(base)
        tt = al("tt", [128, 1024], F32); rtt = Res()
        S.dma(tt, hsp[0:128, :], reads=rhsp, writes=[rtt])
        S.dma(out_d[0:128, :], tt, reads=[rtt], is_output=True)
    build_and_emit(nc, S)
    print("fused instr counts", {e: len(S.streams[e]) for e in ENGS}, "arena peak", ar.peak)
    return nc


def wout_perm_fused():
    idx = []
    for grp in ((0, 1), (2, 3)):
        for hh in range(2):
            for m in grp:
                idx.append(m * 256 + hh * 128 + np.arange(128))
    return np.concatenate(idx)


def kernel(**inputs):
    inp = {k: np.asarray(v) for k, v in inputs.items()}
    x = inp["x"]
    cores = list(range(8))
    Kh = [const_tables(hh, SEQ_FULL) for hh in range(2)]
    nc = build_fused(Kh[0])
    perm = wout_perm_fused()
    wo = [np.ascontiguousarray(inp["w_out"][l][perm, :]) for l in range(2)]
    im = []
    for c in cores:
        b, hh = c // 2, c % 2
        d = {"x": np.ascontiguousarray(x[b, hh * T_OWN:(hh + 1) * T_OWN, :]), "ln_g": inp["ln_in_g"], "ln_b": inp["ln_in_b"],
             "rettab": Kh[hh]["rettab"]}
        gi = np.zeros((128, 8), np.uint32)
        for kc in range(8):
            gi[:, kc] = ((kc % 4) * 128 + np.arange(128)) * 2 + hh
        d["gidx"] = gi
        for n in SMALLK:
            d["k_" + n] = Kh[hh][n]
        for l in range(2):
            pc = prep_layer_core(inp, l, hh)
            d[f"win{l}"] = pc["win"]; d[f"pvec{l}"] = pc["pvec"]; d[f"wab{l}"] = pc["wab"]
            d[f"w_out{l}"] = wo[l]; d[f"w_up{l}"] = inp["w_up"][l]; d[f"w_down{l}"] = inp["w_down"][l]
            d[f"l1g{l}"] = inp["ln1_g"][l]; d[f"l1b{l}"] = inp["ln1_b"][l]; d[f"l2g{l}"] = inp["ln2_g"][l]; d[f"l2b{l}"] = inp["ln2_b"][l]
        im.append(d)
    import os
    rr = run_bass_kernel_spmd(nc, im, core_ids=cores, trace=bool(os.environ.get("KERNEL_TRACE")))
    if os.environ.get("KERNEL_TRACE"):
        print("exec_time_ns", rr.exec_time_ns)
    res = rr.results
    out = np.zeros((4, SEQ_FULL, 1024), np.float32)
    for c in cores:
        b, hh = c // 2, c % 2
        out[b, hh * T_OWN:(hh + 1) * T_OWN, :] = res[c]["out"]
    return out
```

```python
import numpy as np
import ml_dtypes
import concourse.bass as bass
import concourse.mybir as mybir


F32 = mybir.dt.float32
BF16 = mybir.dt.bfloat16
AF = mybir.ActivationFunctionType
ALU = mybir.AluOpType
AX = mybir.AxisListType

ENGS = ("tensor", "vector", "scalar", "gpsimd", "sync")
EPOCH = 3000


class Res:
    __slots__ = ("name", "w", "r", "excl")

    def __init__(self, name="", excl=False):
        self.name = name
        self.w = None
        self.r = []
        self.excl = excl


class Instr:
    __slots__ = ("eng", "idx", "fn", "deps", "vc", "signal", "is_dma", "sem", "val", "pre", "order", "inc")

    def __init__(self, eng, idx, fn, is_dma=False):
        self.eng = eng
        self.idx = idx
        self.fn = fn
        self.deps = []
        self.vc = {}
        self.signal = False
        self.is_dma = is_dma
        self.sem = None
        self.val = None
        self.pre = None
        self.inc = 16


class Sched:
    def __init__(self, nc, n_dma_sems=32, same_engine_sync=True):
        self.nc = nc
        self.streams = {e: [] for e in ENGS}
        self.known = {e: {} for e in ENGS}
        self.known_dma = {e: set() for e in ENGS}
        self.n_dma_sems = n_dma_sems
        self.dma_count = 0
        self.dma_cnt_by_eng = {}
        self.dma_last = {}
        self.same_engine_sync = same_engine_sync
        self.out_dmas = []
        self.pending = {}
        self.dmas_since_barrier = []

    def add(self, eng, fn, reads=(), writes=(), is_dma=False, extra_deps=(), own_sem=False):
        st = self.streams[eng]
        ins = Instr(eng, len(st), fn, is_dma)
        self.order = getattr(self, 'order', 0) + 1
        ins.order = self.order
        if any(r.excl for r in reads):
            writes = list(writes) + [r for r in reads if r.excl and r not in writes]
            reads = [r for r in reads if not r.excl]
        deps = list(extra_deps) + self.pending.pop(eng, [])
        for r in reads:
            if r.w is not None:
                deps.append(r.w)
        for w in writes:
            if w.w is not None:
                deps.append(w.w)
            deps.extend(w.r)
        known = self.known[eng]
        kd = self.known_dma[eng]
        need = {}
        vc = {}
        for d in deps:
            if d is ins:
                continue
            if d.is_dma:
                if id(d) in kd:
                    continue
                need[("dma", id(d))] = d
            else:
                if d.eng == eng and (eng == "tensor" or not self.same_engine_sync):
                    continue
                if known.get(d.eng, -1) >= d.idx:
                    continue
                k = ("e", d.eng)
                if k not in need or need[k].idx < d.idx:
                    need[k] = d
        for k, d in need.items():
            d.signal = True
            ins.deps.append(d)
            if d.is_dma:
                kd.add(id(d))
            for e2, i2 in d.vc.items():
                if known.get(e2, -1) < i2:
                    known[e2] = i2
            if not d.is_dma:
                if known.get(d.eng, -1) < d.idx:
                    known[d.eng] = d.idx
        ins.vc = dict(known)
        if is_dma and own_sem:
            ins.inc = 1
            ins.sem = "own"
        elif is_dma:
            self.dmas_since_barrier.append(ins)
            half = self.n_dma_sems // 2
            cnt = self.dma_cnt_by_eng.get(eng, 0)
            self.dma_cnt_by_eng[eng] = cnt + 1
            slot = (cnt % half) + (half if eng == "gpsimd" else 0)
            self.dma_count += 1
            prev = self.dma_last.get(slot)
            ins.pre = prev
            self.dma_last[slot] = ins
            ins.sem = slot
        for r in reads:
            r.r.append(ins)
        for w in writes:
            w.w = ins
            w.r = []
        st.append(ins)
        return ins

    def barrier(self, scratch_ap):
        self.flush_cc()
        deps = []
        for e in ("tensor", "scalar", "gpsimd", "vector"):
            st = [i for i in self.streams[e] if not i.is_dma]
            if st:
                deps.append(st[-1])
        deps.extend(d for d in self.dmas_since_barrier if d.inc != 1)
        self.dmas_since_barrier = []
        if not hasattr(self, "bar_res"):
            self.bar_res = Res()
        b = self.add("vector", lambda e: e.memset(scratch_ap, 0.0), writes=[self.bar_res], extra_deps=deps)
        for e in ("scalar", "gpsimd", "sync", "tensor"):
            self.pending[e] = [b]
        return b

    def pe(self, fn, reads=(), writes=()):
        return self.add("tensor", fn, reads, writes)

    def dve(self, fn, reads=(), writes=()):
        return self.add("vector", fn, reads, writes)

    def act(self, fn, reads=(), writes=()):
        return self.add("scalar", fn, reads, writes)

    def pool(self, fn, reads=(), writes=()):
        return self.add("gpsimd", fn, reads, writes)

    def cc(self, kind, ins_, outs, groups, reads=(), writes=()):
        tmp = Res()
        i = self.add("gpsimd", lambda e: e.collective_compute(kind, mybir.AluOpType.bypass, replica_groups=groups, ins=ins_, outs=outs),
                     reads, [tmp], is_dma=True, own_sem=True)
        self.pending_cc = getattr(self, "pending_cc", [])
        self.pending_cc.append((tmp, list(writes)))
        return i

    def flush_cc(self):
        sc = getattr(self, "cc_scratch", None)
        for tmp, writes in getattr(self, "pending_cc", []):
            if not hasattr(self, "cc_res"):
                self.cc_res = Res()
            self.add("gpsimd", lambda e: e.memset(sc, 0.0), reads=[tmp], writes=list(writes) + [self.cc_res])
        self.pending_cc = []

    def dma(self, out, in_, reads=(), writes=(), is_output=False, eng="sync", extra_deps=(), **kw):
        ins = self.add(eng, lambda e: e.dma_start(out=out, in_=in_, **kw), reads, writes, is_dma=True, extra_deps=extra_deps)
        if is_output:
            self.out_dmas.append(ins)
        return ins


def build_and_emit(nc, sched):
    for e in ENGS:
        cnt = 0
        sems = []
        for ins in sched.streams[e]:
            if ins.is_dma or not ins.signal:
                continue
            ep = cnt // EPOCH
            if ep >= len(sems):
                sems.append(nc.alloc_semaphore(f"s_{e}_{ep}"))
            cnt += 1
            ins.sem = sems[ep]
            ins.val = cnt - ep * EPOCH
    n = sched.n_dma_sems
    dma_sems = [nc.alloc_semaphore(f"s_dma_{i}") for i in range(n)]
    dma_vals = [0] * n
    dmas = []
    for e in ENGS:
        for ins in sched.streams[e]:
            if ins.is_dma:
                dmas.append(ins)
    dmas.sort(key=lambda i: i.order)
    for ins in dmas:
        if ins.sem == "own":
            ins.sem = nc.alloc_semaphore(f"s_cc_{ins.order}")
            ins.val = 1
            continue
        slot = ins.sem
        dma_vals[slot] += ins.inc
        ins.sem = dma_sems[slot]
        ins.val = dma_vals[slot]

    final_waits = list(sched.out_dmas)

    def run_stream(ename, eng):
        for ins in sched.streams[ename]:
            for d in ins.deps:
                eng.wait_ge(d.sem, d.val)
            if ins.is_dma and ins.pre is not None:
                eng.wait_ge(ins.pre.sem, ins.pre.val)
            bi = ins.fn(eng)
            if ins.is_dma:
                bi.then_inc(ins.sem, ins.inc)
            elif ins.signal:
                bi.then_inc(ins.sem, 1)
        if ename == "sync":
            for d in final_waits:
                eng.wait_ge(d.sem, d.val)

    with nc.Block() as block:
        @block.tensor
        def _(eng):
            run_stream("tensor", eng)

        @block.vector
        def _(eng):
            run_stream("vector", eng)

        @block.scalar
        def _(eng):
            run_stream("scalar", eng)

        @block.gpsimd
        def _(eng):
            run_stream("gpsimd", eng)

        @block.sync
        def _(eng):
            run_stream("sync", eng)


class Arena:
    def __init__(self, nc, base=16384, limit=229312):
        self.nc, self.off, self.limit = nc, base, limit
        self.n = 0
        self.peak = base

    def alloc(self, name, shape, dtype):
        nbytes = int(np.prod(shape[1:])) * mybir.dt.size(dtype)
        off = (self.off + 63) // 64 * 64
        assert off + nbytes <= self.limit, f"arena overflow allocating {name} {shape}: {off}+{nbytes} > {self.limit}"
        self.off = off + nbytes
        self.peak = max(self.peak, self.off)
        self.n += 1
        return self.nc.alloc_sbuf_tensor_at(f"ar{self.n}_{name}", shape, dtype, offset=off).ap()

    def mark(self):
        return self.off

    def reset(self, m):
        self.off = m


D = 1024
DFF = 4096
ALPHA = 4 ** 0.25
LN_EPS = 1e-5


def ln_tile(S, nc, A, z_ap, rz, g_bc, b_bc, rgb, out_ap, rout, tmp, rtmp, st, rst, tagid):
    if tmp is None:
        tmp, rtmp = z_ap, rz
    S.dve(lambda e: e.bn_stats(out=st[:, 0:6], in_=z_ap[:, 0:512]), reads=[rz], writes=[rst])
    S.dve(lambda e: e.bn_stats(out=st[:, 6:12], in_=z_ap[:, 512:1024]), reads=[rz], writes=[rst])
    S.dve(lambda e: e.bn_aggr(out=st[:, 12:14], in_=st[:, 0:12]), reads=[rst], writes=[rst])
    S.act(lambda e: e.activation(out=st[:, 14:15], in_=st[:, 13:14], func=AF.Sqrt, bias=A["eps_ln"], scale=1.0),
          reads=[rst], writes=[rst])
    S.dve(lambda e: e.reciprocal(out=st[:, 14:15], in_=st[:, 14:15]), reads=[rst], writes=[rst])
    S.dve(lambda e: e.tensor_scalar(out=st[:, 15:16], in0=st[:, 12:13], scalar1=st[:, 14:15], scalar2=-1.0,
                                    op0=ALU.mult, op1=ALU.mult), reads=[rst], writes=[rst])
    S.act(lambda e: e.activation(out=tmp, in_=z_ap, func=AF.Identity, bias=st[:, 15:16], scale=st[:, 14:15]),
          reads=[rz, rst], writes=[rtmp])
    S.dve(lambda e: e.tensor_tensor(out=tmp, in0=tmp, in1=g_bc, op=ALU.mult), reads=[rtmp, rgb], writes=[rtmp])
    S.pool(lambda e: e.tensor_tensor(out=out_ap, in0=tmp, in1=b_bc, op=ALU.add), reads=[rtmp, rgb], writes=[rout])


def phase_F(nc, S, A, T, yT_dram, w_out_d, ln1g_d, ln1b_d, w_up_d, w_down_d, ln2g_d, ln2b_d,
            hres, rh, out_f32_d=None, out_bf16_d=None, y_gather=None, rout32=None, rout16=None, final_out=True, on_tile_done=None, wo_pre=None):
    NT = T // 128
    NB = T // 512
    al = A["alloc"]
    ident = A["ident_bf"]
    yT = al("yT", [128, 8, T], BF16)
    ry = [Res() for _ in range(NT)]
    lnp = al("lnp", [128, 2, 1024], F32)
    rln = Res()
    if y_gather is None:
        S.dma(yT, yT_dram.rearrange("(k p) t -> p k t", p=128), writes=ry)
    else:
        src, idx, ridx, rsrc = y_gather
        for kc in range(8):
            S.add("gpsimd", lambda e, kc=kc: e.indirect_dma_start(out=yT[:, kc, :], out_offset=None, in_=src[kc // 4],
                  in_offset=bass.IndirectOffsetOnAxis(ap=idx[:, kc:kc + 1], axis=0)), reads=[rsrc[kc // 4], ridx], writes=ry, is_dma=True)
    for i, d in enumerate((ln1g_d, ln1b_d)):
        S.dma(lnp[:, i, :], d.partition_broadcast(128), writes=[rln])
    NS = 4
    wu = [al(f"wu{i}", [128, 8, 1024], BF16) for i in range(2)]
    wd = [al(f"wd{i}", [128, 8, 1024], BF16) for i in range(2)]
    rwu = [Res(), Res()]
    rwd = [Res(), Res()]

    def load_ffn(s):
        b = s % 2
        S.dma(wu[b], w_up_d[:, s * 1024:(s + 1) * 1024].rearrange("(k p) f -> p k f", p=128), writes=[rwu[b]], eng="gpsimd")
        S.dma(wd[b], w_down_d[s * 1024:(s + 1) * 1024, :].rearrange("(k p) n -> p k n", p=128), writes=[rwd[b]], eng="gpsimd")

    if wo_pre is None:
        wo = wd[1]
        rwo = rwd[1]
        S.dma(wo, w_out_d.rearrange("(k p) n -> p k n", p=128), writes=[rwo], eng="gpsimd")
        load_ffn(0)
    else:
        wo, rwo = wo_pre
        load_ffn(0)
        load_ffn(1)
    tmp = [None] * 4
    rtmp = [None] * 4
    st = [al(f"st{i}", [128, 16], F32) for i in range(4)]
    rst = [Res() for _ in range(4)]
    ps = A["psum"]
    rps = A["rpsum"]
    psT = ps[7].bitcast(BF16)
    h1T = yT

    hb3 = [al(f"hb3_{i}", [128, 1024], BF16) for i in range(3)]
    rhb3 = [Res() for _ in range(3)]
    hb = [hb3[0], hb3[1]]
    rhb = [rhb3[0], rhb3[1]]
    def ln1_step(t):
        if t < NT:
            p = t % 2
            for half in range(2):
                bank = 2 * p + half
                for kc in range(8):
                    S.pe(lambda e, kc=kc, half=half, bank=bank, t=t: e.matmul(
                        ps[bank], lhsT=yT[:, kc, t * 128:(t + 1) * 128], rhs=wo[:, kc, half * 512:(half + 1) * 512],
                        start=(kc == 0), stop=(kc == 7)), reads=[ry[t], rwo], writes=[rps[bank]])
                S.dve(lambda e, half=half, bank=bank, t=t: e.scalar_tensor_tensor(
                    out=hres[:, t, half * 512:(half + 1) * 512], in0=hres[:, t, half * 512:(half + 1) * 512],
                    scalar=ALPHA, in1=ps[bank], op0=ALU.mult, op1=ALU.add), reads=[rh[t], rps[bank]], writes=[rh[t]])
            ln_tile(S, nc, A, hres[:, t, :], rh[t], lnp[:, 0, :], lnp[:, 1, :], rln, hres[:, t, :], rh[t],
                    None, None, st[t % 4], rst[t % 4], t)
            S.act(lambda e, t=t: e.activation(out=hb3[t % 3], in_=hres[:, t, :], func=AF.Copy), reads=[rh[t]], writes=[rhb3[t % 3]])
        if t >= 2:
            u = t - 2
            for kc in range(8):
                S.pe(lambda e, kc=kc, u=u: e.transpose(out=psT[:, kc * 128:(kc + 1) * 128], in_=hb3[u % 3][:, kc * 128:(kc + 1) * 128],
                                                       identity=ident), reads=[rhb3[u % 3]], writes=[rps[7]])
            S.act(lambda e, u=u: e.activation(out=h1T[:, :, u * 128:(u + 1) * 128],
                                              in_=psT.rearrange("p (k c) -> p k c", k=8), func=AF.Copy),
                  reads=[rps[7]], writes=[ry[u]])


    n_steps = NT + 2
    prologue = min(n_steps, 6)
    for t in range(prologue):
        ln1_step(t)
    ln1_next = [prologue]

    def ln1_more(k):
        for _ in range(k):
            if ln1_next[0] < n_steps:
                ln1_step(ln1_next[0])
                ln1_next[0] += 1

    if NB < 4:
        ln1_more(n_steps)
    aT = [al("aT", [128, 8, 512], BF16)] * 2
    raT = [Res()] * 2
    rl = [al("rl0", [128, 512], BF16)] * 2
    rrl = [Res()] * 2
    cnt = 0
    deferred = []

    def ln2_emit():
        while deferred:
            t = deferred.pop(0)
            p = t % 2
            ln_tile(S, nc, A, hres[:, t, :], rh[t], lnp[:, 0, :], lnp[:, 1, :], rln, hres[:, t, :], rh[t],
                    None, None, st[t % 4], rst[t % 4], t)
            if out_f32_d is not None:
                S.dma(out_f32_d[t * 128:(t + 1) * 128, :], hres[:, t, :], reads=[rh[t]], writes=([rout32[t]] if rout32 else []), is_output=final_out)
            if out_bf16_d is not None:
                S.act(lambda e, t=t, p=p: e.activation(out=hb[p], in_=hres[:, t, :], func=AF.Copy),
                      reads=[rh[t]], writes=[rhb[p]])
                S.dma(out_bf16_d(t) if callable(out_bf16_d) else out_bf16_d[t * 128:(t + 1) * 128, :], hb[p], reads=[rhb[p]],
                      writes=([rout16[t]] if rout16 else []), is_output=final_out)
            if on_tile_done is not None:
                on_tile_done(t)

    for s in range(NS):
        b = s % 2
        for tb in range(NB):
            ab = cnt % 2
            cnt += 1
            for fc in range(8):
                bank = 4 + (fc % 2)
                for kc in range(8):
                    S.pe(lambda e, kc=kc, fc=fc, bank=bank, tb=tb, b=b: e.matmul(
                        ps[bank], lhsT=wu[b][:, kc, fc * 128:(fc + 1) * 128], rhs=h1T[:, kc, tb * 512:(tb + 1) * 512],
                        start=(kc == 0), stop=(kc == 7)),
                        reads=[rwu[b]] + ry[tb * 4:(tb + 1) * 4], writes=[rps[bank]])
                rr = fc % 2
                S.act(lambda e, bank=bank, rr=rr: e.activation(out=rl[rr], in_=ps[bank], func=AF.Relu),
                      reads=[rps[bank]], writes=[rrl[rr]])
                if fc % 2 == 0:
                    S.dve(lambda e, fc=fc, bank=bank, ab=ab, rr=rr: e.tensor_tensor(
                        out=aT[ab][:, fc, :], in0=ps[bank], in1=rl[rr], op=ALU.mult),
                        reads=[rps[bank], rrl[rr]], writes=[raT[ab]])
                else:
                    S.pool(lambda e, fc=fc, ab=ab, rr=rr: e.tensor_tensor(
                        out=aT[ab][:, fc, :], in0=rl[rr], in1=rl[rr], op=ALU.mult),
                        reads=[rrl[rr]], writes=[raT[ab]])
            ln2_emit()
            for tt in range(4):
                t = tb * 4 + tt
                if s == 0:
                    ln1_more(1)
                for half in range(2):
                    bank = (tt % 2) * 2 + half
                    for fc in range(8):
                        S.pe(lambda e, fc=fc, half=half, bank=bank, tt=tt, ab=ab, b=b: e.matmul(
                            ps[bank], lhsT=aT[ab][:, fc, tt * 128:(tt + 1) * 128], rhs=wd[b][:, fc, half * 512:(half + 1) * 512],
                            start=(fc == 0), stop=(fc == 7)), reads=[raT[ab], rwd[b]], writes=[rps[bank]])
                    sl = hres[:, t, half * 512:(half + 1) * 512]
                    if s == 0:
                        S.dve(lambda e, sl=sl, bank=bank: e.scalar_tensor_tensor(
                            out=sl, in0=sl, scalar=ALPHA, in1=ps[bank], op0=ALU.mult, op1=ALU.add),
                            reads=[rh[t], rps[bank]], writes=[rh[t]])
                    else:
                        S.dve(lambda e, sl=sl, bank=bank: e.tensor_tensor(out=sl, in0=sl, in1=ps[bank], op=ALU.add),
                              reads=[rh[t], rps[bank]], writes=[rh[t]])
                if s == NS - 1:
                    deferred.append(t)
        if s == 0:
            ln1_more(n_steps)
            if wo_pre is None:
                load_ffn(1)
            for i, d in enumerate((ln2g_d, ln2b_d)):
                S.dma(lnp[:, i, :], d.partition_broadcast(128), writes=[rln])
        if s + 2 < NS:
            load_ffn(s + 2)
    ln2_emit()


NORM_EPS = 1e-6
SL = {n: i for i, n in enumerate(
    ["A_x", "A_g", "B_q", "B_qsw", "B_k", "B_ksw", "B_g", "C_q", "C_k", "D_q", "D_f", "D_g", "B_v", "C_v", "D_v"])}
NSL = 15
PV = {n: i for i, n in enumerate(
    ["cw0", "cw1", "cw2", "cw3", "cb", "ba", "bx", "lam", "retg", "hgg", "lb0", "lbl", "gam64", "nba", "nbx", "c1", "c1x2", "oml", "lnoml", "tmp0", "tmp1"])}
NPV = 24


class Ctx:
    pass


def setup_M(nc, S, A, SEQ, layer, hin_d, win_d, pvec_d, wab_d, consts):
    c = Ctx()
    c.nc, c.S, c.A, c.SEQ, c.layer = nc, S, A, SEQ, layer
    al = A["alloc"]
    c.NB = SEQ // 512
    c.NT = SEQ // 128
    c.ps, c.rps = A["psum"], A["rpsum"]
    c.hT = al("hT", [128, 8, SEQ], BF16)
    c.win = al("win", [128, 8, NSL * 128], BF16)
    c.K = {}
    c.rK = Res()
    for name, (d, shape, dtype) in consts.items():
        t = al("k_" + name, shape, dtype)
        S.dma(t, d, writes=[c.rK])
        c.K[name] = t
    c.rwin = Res()
    S.dma(c.win, win_d.rearrange("(k p) n -> p k n", p=128), writes=[c.rwin], eng="gpsimd")
    c.pv = al("pv", [128, NPV], F32)
    c.rpv = Res()
    S.dma(c.pv[:, 0:13], pvec_d, writes=[c.rpv])
    c.wab = al("wab", [128, 2, 128], BF16)
    c.rwab = Res()
    S.dma(c.wab, wab_d, writes=[c.rwab], eng="gpsimd")
    pieces = hin_d if isinstance(hin_d, (list, tuple)) else [(hin_d, 0, SEQ, list(A.get("rhin", [])))]
    stage = [al(f"hstage{i}", [128, 1024], BF16) for i in range(3)]
    rstage = [Res() for _ in range(3)]
    c.rhT_t = [Res() for _ in range(c.NT)]
    c.rhT = lambda a, b: [c.rhT_t[t] for t in range(a // 128, (b + 127) // 128)]
    c.vtm = al("vtm", [128, c.NT, 384], BF16)
    c.rv = [Res() for _ in range(c.NT)]

    def vproj(t):
        bank = t % 2
        for kc in range(8):
            S.pe(lambda e, kc=kc, t=t, bank=bank: e.matmul(
                c.ps[bank][:, 0:384], lhsT=c.hT[:, kc, t * 128:(t + 1) * 128], rhs=c.win[:, kc, 12 * 128:15 * 128],
                start=(kc == 0), stop=(kc == 7)), reads=c.rhT(t * 128, (t + 1) * 128) + [c.rwin], writes=[c.rps[bank]])
        S.act(lambda e, t=t, bank=bank: e.activation(out=c.vtm[:, t, :], in_=c.ps[bank][:, 0:384], func=AF.Copy),
              reads=[c.rps[bank]], writes=[c.rv[t]])

    tiles = []
    for (src, t0, n, rsrc) in pieces:
        for j in range(n // 128):
            tiles.append((src[j * 128:(j + 1) * 128, :], t0 // 128 + j, rsrc))
    for i, (src_t, t, rsrc) in enumerate(tiles):
        sb = i % 3
        S.dma(stage[sb], src_t, reads=rsrc, writes=[rstage[sb]])
        bank = 6 + i % 2
        psT = c.ps[bank].bitcast(BF16)
        for kc in range(8):
            S.pe(lambda e, kc=kc, sb=sb, psT=psT: e.transpose(out=psT[:, kc * 128:(kc + 1) * 128], in_=stage[sb][:, kc * 128:(kc + 1) * 128],
                                                              identity=c.K["ident"]), reads=[rstage[sb], c.rK], writes=[c.rps[bank]])
        ev = S.act if i % 2 == 0 else S.dve
        if i % 2 == 0:
            S.act(lambda e, t=t, psT=psT: e.activation(out=c.hT[:, :, t * 128:(t + 1) * 128], in_=psT.rearrange("p (k c) -> p k c", k=8), func=AF.Copy),
                  reads=[c.rps[bank]], writes=[c.rhT_t[t]])
        else:
            S.dve(lambda e, t=t, psT=psT: e.tensor_copy(out=c.hT[:, :, t * 128:(t + 1) * 128], in_=psT.rearrange("p (k c) -> p k c", k=8)),
                  reads=[c.rps[bank]], writes=[c.rhT_t[t]])
        if i >= 2:
            vproj(tiles[i - 2][1])
    for (src_t, t, rsrc) in tiles[-2:]:
        vproj(t)
    c.pcount = 0
    c.ry_list = []
    c.ry_by_mixer = {}

    def new_yres():
        r = Res()
        c.ry_list.append(r)
        return r
    c.new_yres = new_yres
    return c


def proj_fm(c, slname, blk, nblk=1):
    S = c.S
    j = SL[slname]
    banks = getattr(c, "proj_banks", (0, 1))
    bank = banks[c.pcount % len(banks)]
    c.pcount += 1
    for kc in range(8):
        c.last_proj = S.pe(lambda e, kc=kc, j=j, blk=blk, bank=bank: e.matmul(
            c.ps[bank], lhsT=c.win[:, kc, j * 128:(j + 1) * 128], rhs=c.hT[:, kc, blk * 512:(blk + 1) * 512],
            start=(kc == 0), stop=(kc == 7)), reads=c.rhT(blk * 512, (blk + 1) * 512) + [c.rwin], writes=[c.rps[bank]])
    return bank


def small_params(c):
    S, pv, rpv = c.S, c.pv, c.rpv
    col = lambda n: pv[:, PV[n]:PV[n] + 1]
    S.act(lambda e: e.activation(out=col("tmp0"), in_=col("lam"), func=AF.Exp, scale=-1.0), reads=[rpv], writes=[rpv])
    S.act(lambda e: e.activation(out=col("tmp0"), in_=col("tmp0"), func=AF.Ln, bias=c.K["one"], scale=1.0), reads=[rpv, c.rK], writes=[rpv])
    S.dve(lambda e: e.tensor_scalar(out=col("c1"), in0=col("tmp0"), scalar1=-8.0, scalar2=None, op0=ALU.mult), reads=[rpv], writes=[rpv])
    S.dve(lambda e: e.tensor_scalar(out=col("c1x2"), in0=col("tmp0"), scalar1=-16.0, scalar2=None, op0=ALU.mult), reads=[rpv], writes=[rpv])
    if c.layer == 0:
        S.dve(lambda e: e.memset(col("oml"), 1.0), writes=[rpv])
        S.dve(lambda e: e.memset(col("lnoml"), 0.0), writes=[rpv])
    else:
        S.dve(lambda e: e.tensor_tensor(out=col("tmp1"), in0=col("lbl"), in1=col("lb0"), op=ALU.subtract), reads=[rpv], writes=[rpv])
        S.act(lambda e: e.activation(out=col("tmp1"), in_=col("tmp1"), func=AF.Exp), reads=[rpv], writes=[rpv])
        S.act(lambda e: e.activation(out=col("lnoml"), in_=col("tmp1"), func=AF.Ln, bias=c.K["one"], scale=1.0), reads=[rpv, c.rK], writes=[rpv])
        S.dve(lambda e: e.tensor_scalar(out=col("lnoml"), in0=col("lnoml"), scalar1=-1.0, scalar2=None, op0=ALU.mult), reads=[rpv], writes=[rpv])
        S.act(lambda e: e.activation(out=col("oml"), in_=col("lnoml"), func=AF.Exp), reads=[rpv], writes=[rpv])


def mixer_A(c, yT_d):
    S, A, SEQ = c.S, c.A, c.SEQ
    al = A["alloc"]
    pv, rpv = c.pv, c.rpv
    col = lambda n: pv[:, PV[n]:PV[n] + 1]
    NH = 2 if SEQ >= 1024 else 1
    L = SEQ // NH
    NBH = L // 512
    xT = al("A_xT", [128, 3 + SEQ], F32)
    rxT = Res()
    XC = al("A_XC", [128, L], F32); rXC = Res()
    R = al("A_R", [128, L], F32); rR = Res()
    TH = al("A_TH", [128, L], F32); rTH = Res()
    AA = al("A_AA", [128, L], F32); rAA = Res()
    XCB = al("A_XCB", [128, L], BF16); rXCB = Res()
    ii = [al(f"A_ii{i}", [128, 512], F32) for i in range(2)]; rii = [Res(), Res()]
    gg = [al(f"A_gg{i}", [128, 512], F32) for i in range(2)]; rgg = [Res(), Res()]
    yst = [al(f"A_y{i}", [128, 512], BF16) for i in range(2)]; ryst = [Res(), Res()]
    carry = al("A_carry", [128, 1], F32); rcar = Res()
    S.dve(lambda e: e.memset(xT[:, 0:3], 0.0), writes=[rxT])
    S.dve(lambda e: e.memset(carry, 0.0), writes=[rcar])
    ps, rps = c.ps, c.rps
    for hf in range(NH):
        t0 = hf * L
        for b in range(NBH):
            blk = hf * NBH + b
            bank = proj_fm(c, "A_x", blk)
            S.act(lambda e, bank=bank, blk=blk: e.activation(out=xT[:, 3 + blk * 512:3 + (blk + 1) * 512], in_=ps[bank], func=AF.Copy),
                  reads=[rps[bank]], writes=[rxT])
        S.dve(lambda e, t0=t0: e.tensor_scalar(out=XC, in0=xT[:, t0 + 3:t0 + 3 + L], scalar1=col("cw3"), scalar2=col("cb"),
                                               op0=ALU.mult, op1=ALU.add), reads=[rxT, rpv], writes=[rXC])
        for w in range(3):
            S.dve(lambda e, t0=t0, w=w: e.scalar_tensor_tensor(out=XC, in0=xT[:, t0 + w:t0 + w + L], scalar=col(f"cw{w}"), in1=XC,
                                                              op0=ALU.mult, op1=ALU.add), reads=[rxT, rpv, rXC], writes=[rXC])
        S.act(lambda e: e.activation(out=XCB, in_=XC, func=AF.Copy), reads=[rXC], writes=[rXCB])
        for b in range(NBH):
            sl = slice(b * 512, (b + 1) * 512)
            S.pe(lambda e, sl=sl: e.matmul(ps[2], lhsT=c.wab[:, 0, :], rhs=XCB[:, sl], start=True, stop=True),
                 reads=[c.rwab, rXCB], writes=[rps[2]])
            S.pe(lambda e, sl=sl: e.matmul(ps[3], lhsT=c.wab[:, 1, :], rhs=XCB[:, sl], start=True, stop=True),
                 reads=[c.rwab, rXCB], writes=[rps[3]])
            S.act(lambda e, sl=sl: e.activation(out=R[:, sl], in_=ps[2], func=AF.Sigmoid, bias=col("ba"), scale=1.0),
                  reads=[rps[2], rpv], writes=[rR])
            S.act(lambda e, b=b: e.activation(out=ii[b % 2], in_=ps[3], func=AF.Sigmoid, bias=col("bx"), scale=1.0),
                  reads=[rps[3], rpv], writes=[rii[b % 2]])
            S.act(lambda e, sl=sl: e.activation(out=TH[:, sl], in_=R[:, sl], func=AF.Tanh, scale=col("c1")),
                  reads=[rR, rpv], writes=[rTH])
            S.dve(lambda e, sl=sl, b=b: e.tensor_tensor(out=XC[:, sl], in0=XC[:, sl], in1=ii[b % 2], op=ALU.mult),
                  reads=[rXC, rii[b % 2]], writes=[rXC])
        S.act(lambda e: e.activation(out=AA, in_=R, func=AF.Exp, scale=col("c1")), reads=[rR, rpv], writes=[rAA])
        S.act(lambda e: e.activation(out=R, in_=R, func=AF.Exp, scale=col("c1x2")), reads=[rR, rpv], writes=[rR])
        S.dve(lambda e: e.scalar_tensor_tensor(out=TH, in0=R, scalar=1.0, in1=TH, op0=ALU.add, op1=ALU.mult),
              reads=[rR, rTH], writes=[rTH])
        S.act(lambda e: e.activation(out=TH, in_=TH, func=AF.Ln, scale=-1.0), reads=[rTH], writes=[rTH])
        S.act(lambda e: e.activation(out=TH, in_=TH, func=AF.Exp, scale=0.5), reads=[rTH], writes=[rTH])
        S.dve(lambda e: e.tensor_tensor(out=XC, in0=XC, in1=TH, op=ALU.mult), reads=[rXC, rTH], writes=[rXC])
        S.dve(lambda e: e.tensor_tensor_scan(out=R, data0=AA, data1=XC, initial=carry, op0=ALU.mult, op1=ALU.add),
              reads=[rAA, rXC, rcar], writes=[rR])
        S.dve(lambda e: e.tensor_copy(out=carry, in_=R[:, L - 1:L]), reads=[rR], writes=[rcar])
        for b in range(NBH):
            blk = hf * NBH + b
            sl = slice(b * 512, (b + 1) * 512)
            bank = proj_fm(c, "A_g", blk)
            S.act(lambda e, bank=bank, b=b: e.activation(out=gg[b % 2], in_=ps[bank], func=AF.Gelu_apprx_tanh),
                  reads=[rps[bank]], writes=[rgg[b % 2]])
            S.dve(lambda e, sl=sl, b=b: e.tensor_tensor(out=yst[b % 2], in0=gg[b % 2], in1=R[:, sl], op=ALU.mult),
                  reads=[rgg[b % 2], rR], writes=[ryst[b % 2]])
            S.dma(yT_d[:, blk * 512:(blk + 1) * 512], yst[b % 2], reads=[ryst[b % 2]], writes=[c.new_yres()], is_output=True)


def gla(c, tag, qt, rqt, kt, rkt, En, rEn, voff, gslice, gcol, groupnorm, yT_d):
    S, A, SEQ = c.S, c.A, c.SEQ
    al = A["alloc"]
    ps, rps = c.ps, c.rps
    NCH = SEQ // 64
    NB = SEQ // 512
    pv, rpv = c.pv, c.rpv
    ident = c.K["ident"]
    psT = ps[6].bitcast(BF16)
    KVs = al(tag + "_KVs", [128, 64, NCH], F32); rKV = Res()
    Ss = al(tag + "_Ss", [128, 64, NCH], BF16); rSs = Res()
    En0 = al(tag + "_En0", [128, NCH], F32); rEn0 = Res()
    ktm4 = al(tag + "_ktm4", [128, 4, 128], BF16); rktm = Res()
    import os
    STOP = int(os.environ.get("GLA_STOP", "99"))
    if STOP == 0:
        S.dma(yT_d, qt, reads=[rqt], is_output=True)
        return
    for tg in range(NB):
        for i in range(4):
            t = tg * 4 + i
            S.pe(lambda e, i=i, t=t: e.transpose(out=psT[:, i * 128:(i + 1) * 128], in_=kt[:, t * 128:(t + 1) * 128], identity=ident),
                 reads=[rkt, c.rK], writes=[rps[6]])
        S.act(lambda e: e.activation(out=ktm4, in_=psT[:, 0:512].rearrange("p (i c) -> p i c", c=128), func=AF.Copy),
              reads=[rps[6]], writes=[rktm])
        for i in range(4):
            t = tg * 4 + i
            S.pe(lambda e, i=i, t=t: e.matmul(ps[2][:, i * 128:(i + 1) * 128], lhsT=ktm4[0:64, i, :], rhs=c.vtm[0:64, t, voff:voff + 128],
                                             start=True, stop=True), reads=[rktm, c.rv[t]], writes=[rps[2]])
            S.pe(lambda e, i=i, t=t: e.matmul(ps[3][:, i * 128:(i + 1) * 128], lhsT=ktm4[64:128, i, :], rhs=c.vtm[64:128, t, voff:voff + 128],
                                             start=True, stop=True), reads=[rktm, c.rv[t]], writes=[rps[3]])
        for par in range(2):
            for hl in range(2):
                rows = slice(hl * 64, (hl + 1) * 64)
                n0 = 8 * tg + par
                S.dve(lambda e, par=par, hl=hl, rows=rows, n0=n0, tg=tg: e.tensor_tensor(
                    out=KVs[rows, :, n0:8 * tg + 8:2].rearrange("p e n -> p n e"),
                    in0=ps[2 + par][rows, :].rearrange("p (i c) -> p i c", c=128)[:, :, hl * 64:(hl + 1) * 64],
                    in1=En[rows, n0:8 * tg + 8:2].unsqueeze(2).broadcast_to([64, 4, 64]), op=ALU.mult),
                    reads=[rps[2 + par], rEn], writes=[rKV])
    if STOP == 1:
        S.dma(yT_d[:, 0:64 * NCH // 2], KVs.rearrange("p e n -> p (e n)").bitcast(BF16)[:, 0:64 * NCH // 2], reads=[rKV], writes=[c.new_yres()], is_output=True)
        return
    S.dve(lambda e: e.tensor_copy(out=En0, in_=En), reads=[rEn], writes=[rEn0])
    S.dve(lambda e: e.memset(En0[:, 0:1], 0.0), writes=[rEn0])
    rSse = [Res() for _ in range(64)]
    for ee in range(64):
        S.dve(lambda e, ee=ee: e.tensor_tensor_scan(out=Ss[:, ee, :], data0=En0, data1=KVs[:, ee, :],
                                                    initial=0.0, op0=ALU.mult, op1=ALU.add), reads=[rEn0, rKV], writes=[rSse[ee]])
    if STOP == 2:
        S.dma(yT_d[:, 0:64 * NCH], Ss.rearrange("p e n -> p (e n)"), reads=rSse, writes=[c.new_yres()], is_output=True)
        return
    atm = [al(tag + f"_atm{i}", [128, 512], BF16) for i in range(2)]; ratm = [Res(), Res()]
    osb = al(tag + "_osb", [128, 512], F32); rosb = Res()
    ob = al(tag + "_ob", [128, 512], BF16); rob = Res()
    rstd = al(tag + "_rstd", [128, 512], F32); rrstd = Res()
    sg = al(tag + "_sg", [128, 512], F32); rsg = Res()
    yst = [al(tag + f"_y{i}", [128, 512], BF16) for i in range(2)]; ryst = [Res(), Res()]
    osb2 = [osb, al(tag + "_osb1", [128, 512], F32)]; rosb2 = [rosb, Res()]
    sg2 = [sg, al(tag + "_sg1", [128, 512], F32)]; rsg2 = [rsg, Res()]

    def front(tb):
        osb_, rosb_, sg_, rsg_ = osb2[tb % 2], rosb2[tb % 2], sg2[tb % 2], rsg2[tb % 2]
        for i in range(4):
            t = tb * 4 + i
            ts_ = slice(t * 128, (t + 1) * 128)
            for hl in range(2):
                rows = slice(hl * 64, (hl + 1) * 64)
                S.pe(lambda e, i=i, hl=hl, rows=rows, ts_=ts_: e.matmul(ps[2 + hl][:, i * 128:(i + 1) * 128], lhsT=kt[rows, ts_], rhs=qt[rows, ts_],
                                                                      start=True, stop=True), reads=[rkt, rqt], writes=[rps[2 + hl]])
        bank = proj_fm(c, gslice, tb)
        S.act(lambda e, bank=bank: e.activation(out=sg_, in_=ps[bank], func=AF.Silu), reads=[rps[bank]], writes=[rsg_])
        for hl in range(2):
            S.dve(lambda e, hl=hl: e.tensor_tensor(out=atm[hl], in0=ps[2 + hl], in1=c.K["glamask"], op=ALU.mult),
                  reads=[rps[2 + hl], c.rK], writes=[ratm[hl]])
        for i in range(4):
            t = tb * 4 + i
            for hl in range(2):
                rows = slice(hl * 64, (hl + 1) * 64)
                ncs = [n for n in (2 * t, 2 * t + 1) if n > 0]
                S.pe(lambda e, i=i, hl=hl, rows=rows, t=t, ncs=ncs: e.matmul(
                    ps[4][rows, i * 128:(i + 1) * 128], lhsT=c.vtm[:, t, voff + hl * 64:voff + (hl + 1) * 64], rhs=atm[hl][:, i * 128:(i + 1) * 128],
                    start=True, stop=(len(ncs) == 0)), reads=[c.rv[t], ratm[hl]], writes=[rps[4]])
                for n in ncs:
                    cc = n - 2 * t
                    S.pe(lambda e, i=i, hl=hl, rows=rows, n=n, cc=cc, ncs=ncs: e.matmul(
                        ps[4][rows, i * 128 + cc * 64:i * 128 + (cc + 1) * 64], lhsT=Ss[rows, :, n - 1], rhs=qt[rows, n * 64:(n + 1) * 64],
                        start=False, stop=(n == ncs[-1])), reads=rSse + [rqt], writes=[rps[4]])
        S.act(lambda e: e.activation(out=osb_, in_=ps[4], func=AF.Copy), reads=[rps[4]], writes=[rosb_])

    def back(tb):
        osb_, rosb_, sg_, rsg_ = osb2[tb % 2], rosb2[tb % 2], sg2[tb % 2], rsg2[tb % 2]
        if groupnorm:
            S.dve(lambda e: e.tensor_copy(out=ob, in_=osb_), reads=[rosb_], writes=[rob])
            S.pe(lambda e: e.matmul(ps[5], lhsT=c.K["blockmean"], rhs=ob, start=True, stop=True), reads=[rob, c.rK], writes=[rps[5]])
            S.dve(lambda e: e.tensor_tensor(out=osb_, in0=osb_, in1=ps[5], op=ALU.subtract), reads=[rosb_, rps[5]], writes=[rosb_])
        S.act(lambda e: e.activation(out=ob, in_=osb_, func=AF.Square), reads=[rosb_], writes=[rob])
        S.pe(lambda e: e.matmul(ps[5], lhsT=c.K["blockmean"], rhs=ob, start=True, stop=True), reads=[rob, c.rK], writes=[rps[5]])
        S.act(lambda e: e.activation(out=rstd, in_=ps[5], func=AF.Ln, bias=c.K["eps_n"], scale=1.0), reads=[rps[5], c.rK], writes=[rrstd])
        S.act(lambda e: e.activation(out=rstd, in_=rstd, func=AF.Exp, scale=-0.5), reads=[rrstd], writes=[rrstd])
        S.dve(lambda e: e.tensor_tensor(out=osb_, in0=osb_, in1=rstd, op=ALU.mult), reads=[rosb_, rrstd], writes=[rosb_])
        S.dve(lambda e: e.scalar_tensor_tensor(out=yst[tb % 2], in0=osb_, scalar=pv[:, PV[gcol]:PV[gcol] + 1], in1=sg_,
                                               op0=ALU.mult, op1=ALU.mult), reads=[rosb_, rsg_, rpv], writes=[ryst[tb % 2]])
        S.dma(yT_d[:, tb * 512:(tb + 1) * 512], yst[tb % 2], reads=[ryst[tb % 2]], writes=[c.new_yres()], is_output=True)

    for tb in range(NB + 1):
        if tb < NB:
            front(tb)
        if tb >= 1:
            back(tb - 1)


def mixer_B(c, yT_d):
    S, A, SEQ = c.S, c.A, c.SEQ
    al = A["alloc"]
    ps, rps = c.ps, c.rps
    NCH = SEQ // 64
    qt = al("B_qt", [128, SEQ], BF16); rqt = Res()
    kt = al("B_kt", [128, SEQ], BF16); rkt = Res()
    En = al("B_En", [128, NCH], F32); rEn = Res()
    tab = al("B_tab", [128, 4, 512], F32); rtab = Res()
    t1s = [al(f"B_t1{i}", [128, 512], F32) for i in range(2)]; rt1s = [Res(), Res()]
    t2s = [al(f"B_t2{i}", [128, 512], F32) for i in range(2)]; rt2s = [Res(), Res()]
    S.dve(lambda e: e.tensor_copy(out=En, in_=c.pv[:, PV["gam64"]:PV["gam64"] + 1].broadcast_to([128, NCH])), reads=[c.rpv], writes=[rEn])
    c.proj_banks = (0, 1, 2, 3)
    for blk in range(c.NB):
        sl = slice(blk * 512, (blk + 1) * 512)
        S.dma(tab, c.rettab_d[:, :, sl].rearrange("f p t -> p f t"), writes=[rtab])
        for which, dst, rdst, ti in (("B_q", qt, rqt, 0), ("B_k", kt, rkt, 2)):
            t1, rt1, t2, rt2 = t1s[ti // 2], rt1s[ti // 2], t2s[ti // 2], rt2s[ti // 2]
            b1 = proj_fm(c, which, blk)
            b2 = proj_fm(c, which + "sw", blk)
            S.dve(lambda e, b1=b1, ti=ti, t1=t1: e.tensor_tensor(out=t1, in0=ps[b1], in1=tab[:, ti, :], op=ALU.mult), reads=[rps[b1], rtab], writes=[rt1])
            S.dve(lambda e, b2=b2, ti=ti, t2=t2: e.tensor_tensor(out=t2, in0=ps[b2], in1=tab[:, ti + 1, :], op=ALU.mult), reads=[rps[b2], rtab], writes=[rt2])
            S.pool(lambda e, dst=dst, sl=sl, t1=t1, t2=t2: e.tensor_tensor(out=dst[:, sl], in0=t1, in1=t2, op=ALU.add), reads=[rt1, rt2], writes=[rdst])
    c.proj_banks = (0, 1)
    gla(c, "B", qt, rqt, kt, rkt, En, rEn, 0, "B_g", "retg", True, yT_d)


def mixer_D(c, yT_d):
    S, A, SEQ = c.S, c.A, c.SEQ
    al = A["alloc"]
    ps, rps = c.ps, c.rps
    pv, rpv = c.pv, c.rpv
    col = lambda n: pv[:, PV[n]:PV[n] + 1]
    NCH = SEQ // 64
    qt = al("D_qt", [128, SEQ], BF16); rqt = Res()
    kt = al("D_kt", [128, SEQ], BF16); rkt = Res()
    En = al("D_En", [128, NCH], F32); rEn = Res()
    T = [al(f"D_T{i}", [128, 512], F32) for i in range(6)]
    rT = [Res() for _ in range(6)]
    one = c.K["one"]
    c.proj_banks = (0, 1, 2, 3)
    for blk in range(c.NB):
        sl = slice(blk * 512, (blk + 1) * 512)
        bf_ = proj_fm(c, "D_f", blk)
        S.act(lambda e, b=bf_: e.activation(out=T[0], in_=ps[b], func=AF.Exp), reads=[rps[bf_]], writes=[rT[0]])
        S.act(lambda e: e.activation(out=T[0], in_=T[0], func=AF.Ln, bias=one, scale=1.0), reads=[rT[0], c.rK], writes=[rT[0]])
        S.act(lambda e: e.activation(out=T[1], in_=T[0], func=AF.Exp, bias=col("lnoml"), scale=-1.0), reads=[rT[0], rpv], writes=[rT[1]])
        S.act(lambda e: e.activation(out=T[2], in_=T[1], func=AF.Ln, bias=one, scale=-1.0), reads=[rT[1], c.rK], writes=[rT[2]])
        S.dve(lambda e: e.tensor_tensor_scan(out=T[3], data0=c.K["resetmask"], data1=T[2], initial=0.0, op0=ALU.mult, op1=ALU.add),
              reads=[rT[2], c.rK], writes=[rT[3]])
        S.act(lambda e: e.activation(out=T[4], in_=T[3], func=AF.Exp), reads=[rT[3]], writes=[rT[4]])
        S.act(lambda e: e.activation(out=T[5], in_=T[3], func=AF.Exp, scale=-1.0), reads=[rT[3]], writes=[rT[5]])
        bq = proj_fm(c, "D_q", blk)
        S.dve(lambda e, b=bq, sl=sl: e.tensor_tensor(out=qt[:, sl], in0=ps[b], in1=T[4], op=ALU.mult), reads=[rps[bq], rT[4]], writes=[rqt])
        S.pool(lambda e, sl=sl: e.tensor_tensor(out=kt[:, sl], in0=T[1], in1=T[5], op=ALU.mult), reads=[rT[1], rT[5]], writes=[rkt])
        S.dve(lambda e, blk=blk: e.tensor_copy(out=En[:, blk * 8:(blk + 1) * 8], in_=T[4][:, 63:512:64]), reads=[rT[4]], writes=[rEn])
    c.proj_banks = (0, 1)
    gla(c, "D", qt, rqt, kt, rkt, En, rEn, 256, "D_g", "hgg", False, yT_d)


def mixer_C(c, yT_d):
    S, A, SEQ = c.S, c.A, c.SEQ
    al = A["alloc"]
    ps, rps = c.ps, c.rps
    NQ = SEQ // 512
    qT = al("C_qT", [128, SEQ], BF16); rqT = Res()
    kT = al("C_kT", [128, SEQ], BF16); rkT = Res()
    c.proj_banks = (0, 1, 2, 3)
    for blk in range(c.NB):
        sl = slice(blk * 512, (blk + 1) * 512)
        b = proj_fm(c, "C_q", blk)
        S.act(lambda e, b=b, sl=sl: e.activation(out=qT[:, sl], in_=ps[b], func=AF.Copy, scale=0.125), reads=[rps[b]], writes=[rqT])
        b = proj_fm(c, "C_k", blk)
        S.dve(lambda e, b=b, sl=sl: e.tensor_copy(out=kT[:, sl], in_=ps[b]), reads=[rps[b]], writes=[rkT])
    c.proj_banks = (0, 1)
    NE = 4
    eb = [al(f"C_e{i}", [128, 512], BF16) for i in range(NE)]; reb = [Res() for _ in range(NE)]
    msp = [al(f"C_msp{i}", [128, 512], BF16) for i in range(NE)]; rmsp = [Res() for _ in range(NE)]
    exr = [al(f"C_exr{i}", [128, 512], BF16) for i in range(2)]; rexr = [Res() for _ in range(2)]
    wT = [al(f"C_w{i}", [128, 512], BF16) for i in range(NE)]; rwT = [Res() for _ in range(NE)]
    chi = [al(f"C_chi{i}", [1, 512], BF16) for i in range(2)]; rchi = [Res(), Res()]
    clo = [al(f"C_clo{i}", [1, 512], BF16) for i in range(2)]; rclo = [Res(), Res()]
    yst = [al(f"C_y{i}", [128, 512], BF16) for i in range(2)]; ryst = [Res(), Res()]
    negtri, ones_row, cmask = c.K["negtri"], c.K["ones_row"], c.K["cmask"]
    steps = []
    for qb in range(NQ):
        for kb in range(4 * qb + 3, -1, -1):
            for h in range(2):
                steps.append((qb, kb, h))
    N = len(steps)

    def cols(i):
        qb, kb, h = steps[i]
        j = kb - 4 * qb
        return slice(max(j, 0) * 128, 512)

    def stage_Z(i):
        qb, kb, h = steps[i]
        rows = slice(h * 64, (h + 1) * 64)
        cs = cols(i)
        q0 = qb * 512
        S.pe(lambda e: e.matmul(ps[2 + h][:, cs], lhsT=kT[rows, kb * 128:(kb + 1) * 128], rhs=qT[rows, q0 + cs.start:q0 + 512],
                                start=True, stop=True), reads=[rkT, rqT], writes=[rps[2 + h]])
        S.act(lambda e: e.activation(out=eb[i % NE][:, cs], in_=ps[2 + h][:, cs], func=AF.Exp), reads=[rps[2 + h]], writes=[reb[i % NE]])
        j = kb - 4 * qb
        if j >= 0:
            dg = slice(j * 128, (j + 1) * 128)
            S.dve(lambda e: e.tensor_tensor(out=eb[i % NE][:, dg], in0=eb[i % NE][:, dg], in1=cmask[:, j, dg], op=ALU.mult),
                  reads=[reb[i % NE], c.rK], writes=[reb[i % NE]])
        S.act(lambda e: e.activation(out=msp[i % NE][:, cs], in_=eb[i % NE][:, cs], func=AF.Ln, bias=1.0, scale=1.0),
              reads=[reb[i % NE]], writes=[rmsp[i % NE]])

    def stage_R(i):
        qb, kb, h = steps[i]
        first = (kb == 4 * qb + 3)
        cs = cols(i)
        S.pe(lambda e: e.matmul(ps[4 + h][:, cs], lhsT=negtri, rhs=msp[i % NE][:, cs], start=first, stop=False, skip_group_check=True),
             reads=[rmsp[i % NE], c.rK], writes=[rps[4 + h]])
        S.act(lambda e: e.activation(out=exr[i % 2][:, cs], in_=ps[4 + h][:, cs], func=AF.Exp), reads=[rps[4 + h]], writes=[rexr[i % 2]])
        S.dve(lambda e: e.tensor_tensor(out=wT[i % NE][:, cs], in0=eb[i % NE][:, cs], in1=exr[i % 2][:, cs], op=ALU.mult),
              reads=[reb[i % NE], rexr[i % 2]], writes=[rwT[i % NE]])

    def stage_O(i):
        qb, kb, h = steps[i]
        rows = slice(h * 64, (h + 1) * 64)
        ob = 6 + qb % 2
        first = (kb == 4 * qb + 3)
        cs = cols(i)
        if kb > 0:
            S.pe(lambda e: e.matmul(ps[4 + h][:, cs], lhsT=c.K["negcompl"], rhs=msp[i % NE][:, cs], start=False, stop=(kb == 1), skip_group_check=True),
                 reads=[rmsp[i % NE], c.rK], writes=[rps[4 + h]])
        S.pe(lambda e: e.matmul(ps[ob][rows, cs], lhsT=c.vtm[:, kb, 128 + h * 64:128 + (h + 1) * 64], rhs=wT[i % NE][:, cs],
                                start=first, stop=(kb == 0), skip_group_check=True), reads=[c.rv[kb], rwT[i % NE]], writes=[rps[ob]])
        if kb == 0 and h == 1:
            S.act(lambda e: e.activation(out=yst[qb % 2], in_=ps[ob], func=AF.Copy), reads=[rps[ob]], writes=[ryst[qb % 2]])
            S.dma(yT_d[:, qb * 512:(qb + 1) * 512], yst[qb % 2], reads=[ryst[qb % 2]], writes=[c.new_yres()], is_output=True)

    for s in range(-2, N):
        if 0 <= s + 2 < N:
            stage_Z(s + 2)
        if 0 <= s + 1 < N:
            stage_R(s + 1)
        if 0 <= s < N:
            stage_O(s)


def phase_M(nc, S, A, SEQ, layer, hin_d, win_d, pvec_d, wab_d, consts, rettab_d, yT_d, which="ABDC"):
    ar = A["arena"]
    c = setup_M(nc, S, A, SEQ, layer, hin_d, win_d, pvec_d, wab_d, consts)
    c.rettab_d = rettab_d
    small_params(c)
    m = ar.mark()
    fns = {"A": (mixer_A, 0), "B": (mixer_B, 1), "C": (mixer_C, 2), "D": (mixer_D, 3)}
    for i, ch in enumerate(which):
        if i > 0:
            S.barrier(A["bar_scratch"])
            ar.reset(m)
        fn, slot = fns[ch]
        fn(c, yT_d[slot] if isinstance(yT_d, (list, tuple)) else yT_d[slot * 128:(slot + 1) * 128, :])
        c.ry_by_mixer[ch] = c.ry_list
        c.ry_list = []
    return c


BF = ml_dtypes.bfloat16
REF_SLICE = {"A_x": 0, "A_g": 1, "B_q": 2, "B_k": 3, "B_v": 4, "B_g": 5, "C_q": 6, "C_k": 7, "C_v": 8,
             "D_q": 9, "D_f": 10, "D_v": 11, "D_g": 12}
MY_SLICES = ["A_x", "A_g", "B_q", "B_qsw", "B_k", "B_ksw", "B_g", "C_q", "C_k", "D_q", "D_f", "D_g", "B_v", "C_v", "D_v"]


def core_cols(hh):
    p = np.arange(128)
    partner = (p // 64) * 64 + ((p % 64) + 32) % 64
    cols = []
    for n in MY_SLICES:
        sw = n.endswith("sw")
        base = REF_SLICE[n[:-2] if sw else n] * 256 + hh * 128
        cols.append(base + (partner if sw else p))
    return np.concatenate(cols)


def prep_layer_core(inp, l, hh):
    ch = slice(hh * 128, (hh + 1) * 128)
    out = {}
    out["win"] = np.ascontiguousarray(np.asarray(inp["w_in"][l])[:, core_cols(hh)])
    pv = np.zeros((128, 13), np.float32)
    cw = np.asarray(inp["conv_w"][l])
    for w in range(4):
        pv[:, w] = cw[w, ch]
    pv[:, 4] = np.asarray(inp["conv_b"][l])[ch]
    pv[:, 5] = np.asarray(inp["rg_ba"][l]).reshape(-1)[ch]
    pv[:, 6] = np.asarray(inp["rg_bx"][l]).reshape(-1)[ch]
    pv[:, 7] = np.asarray(inp["rg_lambda"][l])[ch]
    pv[:, 8] = np.asarray(inp["ret_norm_g"][l])[ch]
    pv[:, 9] = np.asarray(inp["hgrn_norm_g"][l])[ch]
    pv[:, 10] = np.asarray(inp["hgrn_lb_logits"][0])[ch]
    pv[:, 11] = np.asarray(inp["hgrn_lb_logits"][l])[ch]
    for hl in range(2):
        gam = 1.0 - 2.0 ** (-5.0 - (2 * hh + hl))
        pv[hl * 64:(hl + 1) * 64, 12] = gam ** 64
    out["pvec"] = pv
    wab = np.zeros((128, 2, 128), np.float32)
    for hl in range(2):
        s = slice(hl * 64, (hl + 1) * 64)
        wab[s, 0, s] = np.asarray(inp["rg_wa"][l])[2 * hh + hl]
        wab[s, 1, s] = np.asarray(inp["rg_wx"][l])[2 * hh + hl]
    out["wab"] = wab
    return out


def const_tables(hh, SEQ):
    K = {}
    K["ident"] = np.eye(128).astype(BF)
    j = np.arange(128)
    K["negtri"] = (-(j[:, None] >= j[None, :]).astype(np.float32)).astype(BF)
    K["negcompl"] = (-(j[:, None] < j[None, :]).astype(np.float32)).astype(BF)
    K["ones_row"] = np.ones((1, 128), np.float32).astype(BF)
    K["one"] = np.ones((128, 1), np.float32)
    K["eps_n"] = np.full((128, 1), 1e-6, np.float32)
    q = np.arange(512)
    cm = np.zeros((128, 4, 512), np.float32)
    for jb in range(4):
        cm[:, jb, :] = ((jb * 128 + j)[:, None] < q[None, :])
    K["cmask"] = cm.astype(BF)
    s = np.arange(128)
    gm = ((s[:, None] // 64 == s[None, :] // 64) & (s[:, None] <= s[None, :])).astype(np.float32)
    K["glamask"] = np.tile(gm, (1, 4)).astype(BF)
    K["blockmean"] = ((s[:, None] // 64 == s[None, :] // 64) / 64.0).astype(np.float32).astype(BF)
    rm = np.ones((128, 512), np.float32)
    rm[:, ::64] = 0.0
    K["resetmask"] = rm
    d = np.arange(64)
    inv_freq = (10000.0 ** (-np.arange(0, 64, 2, dtype=np.float32) / 64)).astype(np.float32)
    t = np.arange(SEQ, dtype=np.float32)
    ang = (t[:, None] * inv_freq[None, :]).astype(np.float32)
    cos, sin = np.cos(ang).T, np.sin(ang).T
    tl = (np.arange(SEQ) % 64 + 1).astype(np.float64)
    tabs = np.zeros((4, 128, SEQ), np.float32)
    for hl in range(2):
        gam = 1.0 - 2.0 ** (-5.0 - (2 * hh + hl))
        lg = np.log1p(-2.0 ** (-5.0 - (2 * hh + hl)))
        dq = np.exp(lg * tl)
        dk = np.exp(-lg * tl) / 8.0
        for dd in range(64):
            p = hl * 64 + dd
            c_, s_ = cos[dd % 32], sin[dd % 32]
            sg = -1.0 if dd < 32 else 1.0
            tabs[0, p] = c_ * dq
            tabs[1, p] = sg * s_ * dq
            tabs[2, p] = c_ * dk
            tabs[3, p] = sg * s_ * dk
    K["rettab"] = tabs
    return K


from concourse.bass_utils import run_bass_kernel_spmd

SEQ_FULL = 4096
T_OWN = 2048
SMALLK = ["ident", "negtri", "negcompl", "ones_row", "one", "eps_n", "cmask", "glamask", "blockmean", "resetmask"]


def _mk_A(nc):
    A = {}
    ar = Arena(nc)
    A["arena"] = ar
    A["alloc"] = ar.alloc
    A["psum"] = [nc.alloc_psum_tensor(f"ps{i}", [128, 512], F32).ap() for i in range(8)]
    A["rpsum"] = [Res(excl=True) for _ in range(8)]
    A["bar_scratch"] = ar.alloc("bar", [128, 1], F32)
    return A


def _np_dt(a):
    return BF16 if a.dtype == BF else F32


def build_pre():
    nc = bass.Bass("TRN2", target_bir_lowering=False)
    S = Sched(nc)
    dt = lambda n, s, d, k="ExternalInput": nc.dram_tensor(n, s, d, kind=k).ap()
    x_d = dt("x", [T_OWN, 1024], F32)
    g_d = dt("ln_g", [1024], F32)
    b_d = dt("ln_b", [1024], F32)
    h32 = dt("h32", [T_OWN, 1024], F32, "ExternalOutput")
    h16 = dt("h16", [T_OWN, 1024], BF16, "ExternalOutput")
    A = _mk_A(nc)
    al = A["alloc"]
    eps = al("eps", [128, 1], F32)
    reps = Res()
    S.dve(lambda e: e.memset(eps, LN_EPS), writes=[reps])
    A["eps_ln"] = eps
    lnp = al("lnp", [128, 2, 1024], F32); rln = Res()
    if True:
        dmy = al("dmy", [128, 128], BF16); rd = Res()
        S.dve(lambda e: e.memset(dmy, 0.0), writes=[rd])
        S.pe(lambda e: e.matmul(A["psum"][0][:, 0:128], lhsT=dmy, rhs=dmy, start=True, stop=True), reads=[rd], writes=[A["rpsum"][0]])
    S.dma(lnp[:, 0, :], g_d.partition_broadcast(128), writes=[rln])
    S.dma(lnp[:, 1, :], b_d.partition_broadcast(128), writes=[rln])
    NT = T_OWN // 128
    xt = [al(f"xt{i}", [128, 1024], F32) for i in range(2)]; rxt = [Res(), Res()]
    tmp = [al(f"tmp{i}", [128, 1024], F32) for i in range(2)]; rtmp = [Res(), Res()]
    hb = [al(f"hb{i}", [128, 1024], BF16) for i in range(2)]; rhb = [Res(), Res()]
    st = [al(f"st{i}", [128, 16], F32) for i in range(2)]; rst = [Res(), Res()]
    for t in range(NT):
        p = t % 2
        S.dma(xt[p], x_d[t * 128:(t + 1) * 128, :], writes=[rxt[p]])
        ln_tile(S, nc, A, xt[p], rxt[p], lnp[:, 0, :], lnp[:, 1, :], rln, xt[p], rxt[p], tmp[p], rtmp[p], st[p], rst[p], t)
        S.dma(h32[t * 128:(t + 1) * 128, :], xt[p], reads=[rxt[p]], is_output=True)
        S.act(lambda e, p=p: e.activation(out=hb[p], in_=xt[p], func=AF.Copy), reads=[rxt[p]], writes=[rhb[p]])
        S.dma(h16[t * 128:(t + 1) * 128, :], hb[p], reads=[rhb[p]], is_output=True)
    build_and_emit(nc, S)
    return nc


def build_M(layer, Kh):
    nc = bass.Bass("TRN2", target_bir_lowering=False)
    S = Sched(nc)
    dt = lambda n, s, d, k="ExternalInput": nc.dram_tensor(n, s, d, kind=k).ap()
    SEQ = SEQ_FULL
    hin = dt("hin", [SEQ, 1024], BF16)
    win = dt("win", [1024, 1920], F32)
    pvec = dt("pvec", [128, 13], F32)
    wab = dt("wab", [128, 2, 128], F32)
    consts = {}
    for n in SMALLK:
        a = Kh[n]
        consts[n] = (dt("k_" + n, list(a.shape), _np_dt(a)), list(a.shape), _np_dt(a))
    rettab = dt("rettab", [4, 128, SEQ], F32)
    yT = dt("yT", [512, SEQ], BF16, "ExternalOutput")
    A = _mk_A(nc)
    phase_M(nc, S, A, SEQ, layer, hin, win, pvec, wab, consts, rettab, yT)
    build_and_emit(nc, S)
    return nc


def build_F():
    nc = bass.Bass("TRN2", target_bir_lowering=False)
    S = Sched(nc)
    dt = lambda n, s, d, k="ExternalInput": nc.dram_tensor(n, s, d, kind=k).ap()
    T = T_OWN
    yT_d = dt("yT", [1024, T], BF16)
    h_d = dt("h", [T, 1024], F32)
    wout = dt("w_out", [1024, 1024], F32)
    wup = dt("w_up", [1024, 4096], F32)
    wdn = dt("w_down", [4096, 1024], F32)
    l1g, l1b, l2g, l2b = [dt(n, [1024], F32) for n in ("l1g", "l1b", "l2g", "l2b")]
    ident_d = dt("ident", [128, 128], BF16)
    out = dt("h32", [T, 1024], F32, "ExternalOutput")
    outb = dt("h16", [T, 1024], BF16, "ExternalOutput")
    A = _mk_A(nc)
    al = A["alloc"]
    A["ident_bf"] = al("ident", [128, 128], BF16)
    rid = Res()
    S.dma(A["ident_bf"], ident_d, writes=[rid])
    eps = al("eps", [128, 1], F32)
    S.dve(lambda e: e.memset(eps, LN_EPS), writes=[rid])
    A["eps_ln"] = eps
    NT = T // 128
    hres = al("hres", [128, NT, 1024], F32)
    rh = [Res() for _ in range(NT)]
    for t in range(NT):
        S.dma(hres[:, t, :], h_d[t * 128:(t + 1) * 128, :], writes=[rh[t]])
    phase_F(nc, S, A, T, yT_d, wout, l1g, l1b, wup, wdn, l2g, l2b, hres, rh, out_f32_d=out, out_bf16_d=outb)
    build_and_emit(nc, S)
    return nc


def wout_perm():
    idx = []
    for hh in range(2):
        for m in range(4):
            idx.append(m * 256 + hh * 128 + np.arange(128))
    return np.concatenate(idx)


def kernel_unfused(**inputs):
    inp = {k: np.asarray(v) for k, v in inputs.items()}
    x = inp["x"]
    NC = 8
    cores = list(range(NC))
    f32 = np.float32
    Kh = [const_tables(hh, SEQ_FULL) for hh in range(2)]
    nc_pre = build_pre()
    im = []
    for c in cores:
        b, hh = c // 2, c % 2
        im.append({"x": np.ascontiguousarray(x[b, hh * T_OWN:(hh + 1) * T_OWN, :]), "ln_g": inp["ln_in_g"], "ln_b": inp["ln_in_b"]})
    res = run_bass_kernel_spmd(nc_pre, im, core_ids=cores).results
    h32 = [r["h32"] for r in res]
    h16 = [r["h16"] for r in res]
    nc_F = build_F()
    perm = wout_perm()
    for l in range(2):
        nc_M = build_M(l, Kh[0])
        im = []
        for c in cores:
            b, hh = c // 2, c % 2
            pc = prep_layer_core(inp, l, hh)
            d = {"hin": np.concatenate([h16[2 * b], h16[2 * b + 1]], axis=0), "win": pc["win"], "pvec": pc["pvec"], "wab": pc["wab"],
                 "rettab": Kh[hh]["rettab"]}
            for n in SMALLK:
                d["k_" + n] = Kh[hh][n]
            im.append(d)
        res = run_bass_kernel_spmd(nc_M, im, core_ids=cores).results
        yT = [r["yT"] for r in res]
        im = []
        wo = np.ascontiguousarray(inp["w_out"][l][perm, :])
        for c in cores:
            b, hh = c // 2, c % 2
            ya = np.concatenate([yT[2 * b], yT[2 * b + 1]], axis=0)[:, hh * T_OWN:(hh + 1) * T_OWN]
            im.append({"yT": np.ascontiguousarray(ya), "h": h32[c], "w_out": wo, "w_up": inp["w_up"][l], "w_down": inp["w_down"][l],
                       "l1g": inp["ln1_g"][l], "l1b": inp["ln1_b"][l], "l2g": inp["ln2_g"][l], "l2b": inp["ln2_b"][l],
                       "ident": Kh[0]["ident"]})
        res = run_bass_kernel_spmd(nc_F, im, core_ids=cores).results
        h32 = [r["h32"] for r in res]
        h16 = [r["h16"] for r in res]
    out = np.zeros((4, SEQ_FULL, 1024), f32)
    for c in cores:
        b, hh = c // 2, c % 2
        out[b, hh * T_OWN:(hh + 1) * T_OWN, :] = h32[c]
    return out


GROUPS = [[0, 1], [2, 3], [4, 5], [6, 7]]
U32 = mybir.dt.uint32


def build_fused(Kh):
    nc = bass.Bass("TRN2", target_bir_lowering=False)
    S = Sched(nc)
    dt = lambda n, s, d, k="ExternalInput": nc.dram_tensor(n, s, d, kind=k).ap()
    SEQ, T = SEQ_FULL, T_OWN
    NT = T // 128
    x_d = dt("x", [T, 1024], F32)
    g_d = dt("ln_g", [1024], F32)
    b_d = dt("ln_b", [1024], F32)
    gidx_d = dt("gidx", [128, 8], U32)
    L = []
    for l in range(2):
        L.append(dict(
            win=dt(f"win{l}", [1024, 1920], F32), pvec=dt(f"pvec{l}", [128, 13], F32), wab=dt(f"wab{l}", [128, 2, 128], F32),
            wout=dt(f"w_out{l}", [1024, 1024], F32), wup=dt(f"w_up{l}", [1024, 4096], F32), wdn=dt(f"w_down{l}", [4096, 1024], F32),
            l1g=dt(f"l1g{l}", [1024], F32), l1b=dt(f"l1b{l}", [1024], F32), l2g=dt(f"l2g{l}", [1024], F32), l2b=dt(f"l2b{l}", [1024], F32)))
    kd = {}
    for n in SMALLK:
        a = Kh[n]
        kd[n] = (dt("k_" + n, list(a.shape), _np_dt(a)), list(a.shape), _np_dt(a))
    rettab = dt("rettab", [4, 128, SEQ], F32)
    out_d = dt("out", [T, 1024], F32, "ExternalOutput")
    H = T // 2
    hx_loc = [nc.dram_tensor(f"hx_loc{i}", [H, 1024], BF16).ap() for i in range(2)]
    hx_all = [nc.dram_tensor(f"hx_all{i}", [2 * H, 1024], BF16).ap() for i in range(2)]
    y_loc = [nc.dram_tensor(f"y_loc{i}", [256, SEQ], BF16).ap() for i in range(2)]
    y_all = [nc.dram_tensor(f"y_all{i}", [512, SEQ], BF16).ap() for i in range(2)]
    hsp = nc.dram_tensor("hspill", [T, 1024], F32).ap()

    def hx_dst(t):
        i, r = divmod(t * 128, H)
        return hx_loc[i][r:r + 128, :]

    hx_rr = [Res(), Res()]

    def gather_half(i, rhxl):
        S.cc("AllGather", [hx_loc[i]], [hx_all[i]], GROUPS, reads=rhxl[i * (NT // 2):(i + 1) * (NT // 2)], writes=[hx_rr[i]])

    def gather_hx(rhxl, done=()):
        rr = hx_rr
        for i in range(2):
            if i not in done:
                gather_half(i, rhxl)
        return [(hx_all[0][0:H, :], 0, H, [rr[0]]), (hx_all[1][0:H, :], H, H, [rr[1]]),
                (hx_all[0][H:2 * H, :], 2 * H, H, [rr[0]]), (hx_all[1][H:2 * H, :], 3 * H, H, [rr[1]])]
    A = _mk_A(nc)
    ar = A["arena"]
    al = A["alloc"]
    S.cc_scratch = al("ccs", [128, 1], F32)
    eps = al("eps", [128, 1], F32)
    rconst = Res()
    S.dve(lambda e: e.memset(eps, LN_EPS), writes=[rconst])
    A["eps_ln"] = eps
    A["ident_bf"] = al("identF", [128, 128], BF16)
    S.dma(A["ident_bf"], kd["ident"][0], writes=[rconst])
    gidx = al("gidx", [128, 8], U32); rgidx = Res()
    S.dma(gidx, gidx_d, writes=[rgidx])
    base = ar.mark()
    rhsp = [Res() for _ in range(NT)]
    rhxl = [Res() for _ in range(NT)]
    lnp = al("lnp", [128, 2, 1024], F32); rln = Res()
    S.dma(lnp[:, 0, :], g_d.partition_broadcast(128), writes=[rln])
    S.dma(lnp[:, 1, :], b_d.partition_broadcast(128), writes=[rln])
    xt = [al(f"xt{i}", [128, 1024], F32) for i in range(NT)]; rxt = [Res() for _ in range(NT)]
    tmp = [al(f"tmp{i}", [128, 1024], F32) for i in range(4)]; rtmp = [Res() for _ in range(4)]
    hb = [al(f"hb{i}", [128, 1024], BF16) for i in range(8)]; rhb = [Res() for _ in range(8)]
    st = [al(f"st{i}", [128, 16], F32) for i in range(8)]; rst = [Res() for _ in range(8)]
    for t in range(NT):
        S.dma(xt[t], x_d[t * 128:(t + 1) * 128, :], writes=[rxt[t]])
    for t in range(NT):
        p = t % 8
        ln_tile(S, nc, A, xt[t], rxt[t], lnp[:, 0, :], lnp[:, 1, :], rln, xt[t], rxt[t], None, None, st[p], rst[p], t)
        S.act(lambda e, t=t: e.activation(out=hb[t % 8], in_=xt[t], func=AF.Copy), reads=[rxt[t]], writes=[rhb[t % 8]])
        S.dma(hsp[t * 128:(t + 1) * 128, :], xt[t], reads=[rxt[t]], writes=[rhsp[t]])
        S.dma(hx_dst(t), hb[t % 8], reads=[rhb[t % 8]], writes=[rhxl[t]])
        if t == NT // 2 - 1:
            gather_half(0, rhxl)
    hin_pieces = gather_hx(rhxl, done=(0,))
    import os
    NL = int(os.environ.get("FUSE_LAYERS", "2"))
    PH = os.environ.get("FUSE_PHASES", "MF")
    for l in range(NL):
        W = L[l]
        S.barrier(A["bar_scratch"])
        ar.reset(base)
        yslots = [y_loc[0][0:128, :], y_loc[0][128:256, :], y_loc[1][0:128, :], y_loc[1][128:256, :]]
        c = phase_M(nc, S, A, SEQ, l, hin_pieces, W["win"], W["pvec"], W["wab"], kd, rettab, yslots)
        ar.reset(base)
        hres = al("hres", [128, NT, 1024], F32)
        rh = [Res() for _ in range(NT)]
        wo_pre = al("wo_pre", [128, 8, 1024], BF16); rwo_pre = Res()
        S.dma(wo_pre, W["wout"].rearrange("(k p) n -> p k n", p=128), writes=[rwo_pre], eng="gpsimd", extra_deps=[c.last_proj])
        for t in range(NT):
            S.dma(hres[:, t, :], hsp[t * 128:(t + 1) * 128, :], reads=[rhsp[t]], writes=[rh[t]], eng="gpsimd", extra_deps=[c.last_proj])
        ry_all = [Res(), Res()]
        S.cc("AllGather", [y_loc[0]], [y_all[0]], GROUPS, reads=c.ry_by_mixer["A"] + c.ry_by_mixer["B"], writes=[ry_all[0]])
        S.cc("AllGather", [y_loc[1]], [y_all[1]], GROUPS, reads=c.ry_by_mixer["C"] + c.ry_by_mixer["D"], writes=[ry_all[1]])
        if PH == "M":
            continue
        S.barrier(A["bar_scratch"])
        last = (l == NL - 1)
        src = [ya.rearrange("c (h t) -> (c h) t", h=2) for ya in y_all]
        phase_F(nc, S, A, T, None, W["wout"], W["l1g"], W["l1b"], W["wup"], W["wdn"], W["l2g"], W["l2b"], hres, rh,
                out_f32_d=(out_d if last else hsp), out_bf16_d=(None if last else hx_dst),
                y_gather=(src, gidx, rgidx, ry_all), rout32=(None if last else rhsp), rout16=(None if last else rhxl),
                final_out=last, wo_pre=(wo_pre, rwo_pre),
                on_tile_done=(None if last else (lambda t: gather_half(0, rhxl) if t == NT // 2 - 1 else None)))
        if not last:
            hin_pieces = gather_hx(rhxl, done=(0,))
    if NL == 0 or PH == "M":
        S.barrier(A["bar_scratch"])
        ar.reset(base)
        tt = al("tt", [128, 1024], F32); rtt = Res()
        S.dma(tt, hsp[0:128, :], reads=rhsp, writes=[rtt])
        S.dma(out_d[0:128, :], tt, reads=[rtt], is_output=True)
    build_and_emit(nc, S)
    print("fused instr counts", {e: len(S.streams[e]) for e in ENGS}, "arena peak", ar.peak)
    return nc


def wout_perm_fused():
    idx = []
    for grp in ((0, 1), (2, 3)):
        for hh in range(2):
            for m in grp:
                idx.append(m * 256 + hh * 128 + np.arange(128))
    return np.concatenate(idx)


def kernel(**inputs):
    inp = {k: np.asarray(v) for k, v in inputs.items()}
    x = inp["x"]
    cores = list(range(8))
    Kh = [const_tables(hh, SEQ_FULL) for hh in range(2)]
    nc = build_fused(Kh[0])
    perm = wout_perm_fused()
    wo = [np.ascontiguousarray(inp["w_out"][l][perm, :]) for l in range(2)]
    im = []
    for c in cores:
        b, hh = c // 2, c % 2
        d = {"x": np.ascontiguousarray(x[b, hh * T_OWN:(hh + 1) * T_OWN, :]), "ln_g": inp["ln_in_g"], "ln_b": inp["ln_in_b"],
             "rettab": Kh[hh]["rettab"]}
        gi = np.zeros((128, 8), np.uint32)
        for kc in range(8):
            gi[:, kc] = ((kc % 4) * 128 + np.arange(128)) * 2 + hh
        d["gidx"] = gi
        for n in SMALLK:
            d["k_" + n] = Kh[hh][n]
        for l in range(2):
            pc = prep_layer_core(inp, l, hh)
            d[f"win{l}"] = pc["win"]; d[f"pvec{l}"] = pc["pvec"]; d[f"wab{l}"] = pc["wab"]
            d[f"w_out{l}"] = wo[l]; d[f"w_up{l}"] = inp["w_up"][l]; d[f"w_down{l}"] = inp["w_down"][l]
            d[f"l1g{l}"] = inp["ln1_g"][l]; d[f"l1b{l}"] = inp["ln1_b"][l]; d[f"l2g{l}"] = inp["ln2_g"][l]; d[f"l2b{l}"] = inp["ln2_b"][l]
        im.append(d)
    import os
    rr = run_bass_kernel_spmd(nc, im, core_ids=cores, trace=bool(os.environ.get("KERNEL_TRACE")))
    if os.environ.get("KERNEL_TRACE"):
        print("exec_time_ns", rr.exec_time_ns)
    res = rr.results
    out = np.zeros((4, SEQ_FULL, 1024), np.float32)
    for c in cores:
        b, hh = c // 2, c % 2
        out[b, hh * T_OWN:(hh + 1) * T_OWN, :] = res[c]["out"]
    return out
```

```python
import numpy as np
import ml_dtypes
import concourse.bass as bass
import concourse.mybir as mybir


F32 = mybir.dt.float32
BF16 = mybir.dt.bfloat16
AF = mybir.ActivationFunctionType
ALU = mybir.AluOpType
AX = mybir.AxisListType

ENGS = ("tensor", "vector", "scalar", "gpsimd", "sync")
EPOCH = 3000


class Res:
    __slots__ = ("name", "w", "r", "excl")

    def __init__(self, name="", excl=False):
        self.name = name
        self.w = None
        self.r = []
        self.excl = excl


class Instr:
    __slots__ = ("eng", "idx", "fn", "deps", "vc", "signal", "is_dma", "sem", "val", "pre", "order", "inc")

    def __init__(self, eng, idx, fn, is_dma=False):
        self.eng = eng
        self.idx = idx
        self.fn = fn
        self.deps = []
        self.vc = {}
        self.signal = False
        self.is_dma = is_dma
        self.sem = None
        self.val = None
        self.pre = None
        self.inc = 16


class Sched:
    def __init__(self, nc, n_dma_sems=32, same_engine_sync=True):
        self.nc = nc
        self.streams = {e: [] for e in ENGS}
        self.known = {e: {} for e in ENGS}
        self.known_dma = {e: set() for e in ENGS}
        self.n_dma_sems = n_dma_sems
        self.dma_count = 0
        self.dma_cnt_by_eng = {}
        self.dma_last = {}
        self.same_engine_sync = same_engine_sync
        self.out_dmas = []
        self.pending = {}
        self.dmas_since_barrier = []

    def add(self, eng, fn, reads=(), writes=(), is_dma=False, extra_deps=(), own_sem=False):
        st = self.streams[eng]
        ins = Instr(eng, len(st), fn, is_dma)
        self.order = getattr(self, 'order', 0) + 1
        ins.order = self.order
        if any(r.excl for r in reads):
            writes = list(writes) + [r for r in reads if r.excl and r not in writes]
            reads = [r for r in reads if not r.excl]
        deps = list(extra_deps) + self.pending.pop(eng, [])
        for r in reads:
            if r.w is not None:
                deps.append(r.w)
        for w in writes:
            if w.w is not None:
                deps.append(w.w)
            deps.extend(w.r)
        known = self.known[eng]
        kd = self.known_dma[eng]
        need = {}
        vc = {}
        for d in deps:
            if d is ins:
                continue
            if d.is_dma:
                if id(d) in kd:
                    continue
                need[("dma", id(d))] = d
            else:
                if d.eng == eng and (eng == "tensor" or not self.same_engine_sync):
                    continue
                if known.get(d.eng, -1) >= d.idx:
                    continue
                k = ("e", d.eng)
                if k not in need or need[k].idx < d.idx:
                    need[k] = d
        for k, d in need.items():
            d.signal = True
            ins.deps.append(d)
            if d.is_dma:
                kd.add(id(d))
            for e2, i2 in d.vc.items():
                if known.get(e2, -1) < i2:
                    known[e2] = i2
            if not d.is_dma:
                if known.get(d.eng, -1) < d.idx:
                    known[d.eng] = d.idx
        ins.vc = dict(known)
        if is_dma and own_sem:
            ins.inc = 1
            ins.sem = "own"
        elif is_dma:
            self.dmas_since_barrier.append(ins)
            half = self.n_dma_sems // 2
            cnt = self.dma_cnt_by_eng.get(eng, 0)
            self.dma_cnt_by_eng[eng] = cnt + 1
            slot = (cnt % half) + (half if eng == "gpsimd" else 0)
            self.dma_count += 1
            prev = self.dma_last.get(slot)
            ins.pre = prev
            self.dma_last[slot] = ins
            ins.sem = slot
        for r in reads:
            r.r.append(ins)
        for w in writes:
            w.w = ins
            w.r = []
        st.append(ins)
        return ins

    def barrier(self, scratch_ap):
        self.flush_cc()
        deps = []
        for e in ("tensor", "scalar", "gpsimd", "vector"):
            st = [i for i in self.streams[e] if not i.is_dma]
            if st:
                deps.append(st[-1])
        deps.extend(d for d in self.dmas_since_barrier if d.inc != 1)
        self.dmas_since_barrier = []
        if not hasattr(self, "bar_res"):
            self.bar_res = Res()
        b = self.add("vector", lambda e: e.memset(scratch_ap, 0.0), writes=[self.bar_res], extra_deps=deps)
        for e in ("scalar", "gpsimd", "sync", "tensor"):
            self.pending[e] = [b]
        return b

    def pe(self, fn, reads=(), writes=()):
        return self.add("tensor", fn, reads, writes)

    def dve(self, fn, reads=(), writes=()):
        return self.add("vector", fn, reads, writes)

    def act(self, fn, reads=(), writes=()):
        return self.add("scalar", fn, reads, writes)

    def pool(self, fn, reads=(), writes=()):
        return self.add("gpsimd", fn, reads, writes)

    def cc(self, kind, ins_, outs, groups, reads=(), writes=()):
        tmp = Res()
        i = self.add("gpsimd", lambda e: e.collective_compute(kind, mybir.AluOpType.bypass, replica_groups=groups, ins=ins_, outs=outs),
                     reads, [tmp], is_dma=True, own_sem=True)
        self.pending_cc = getattr(self, "pending_cc", [])
        self.pending_cc.append((tmp, list(writes)))
        return i

    def flush_cc(self):
        sc = getattr(self, "cc_scratch", None)
        for tmp, writes in getattr(self, "pending_cc", []):
            if not hasattr(self, "cc_res"):
                self.cc_res = Res()
            self.add("gpsimd", lambda e: e.memset(sc, 0.0), reads=[tmp], writes=list(writes) + [self.cc_res])
        self.pending_cc = []

    def dma(self, out, in_, reads=(), writes=(), is_output=False, eng="sync", extra_deps=(), **kw):
        ins = self.add(eng, lambda e: e.dma_start(out=out, in_=in_, **kw), reads, writes, is_dma=True, extra_deps=extra_deps)
        if is_output:
            self.out_dmas.append(ins)
        return ins


def build_and_emit(nc, sched):
    for e in ENGS:
        cnt = 0
        sems = []
        for ins in sched.streams[e]:
            if ins.is_dma or not ins.signal:
                continue
            ep = cnt // EPOCH
            if ep >= len(sems):
                sems.append(nc.alloc_semaphore(f"s_{e}_{ep}"))
            cnt += 1
            ins.sem = sems[ep]
            ins.val = cnt - ep * EPOCH
    n = sched.n_dma_sems
    dma_sems = [nc.alloc_semaphore(f"s_dma_{i}") for i in range(n)]
    dma_vals = [0] * n
    dmas = []
    for e in ENGS:
        for ins in sched.streams[e]:
            if ins.is_dma:
                dmas.append(ins)
    dmas.sort(key=lambda i: i.order)
    for ins in dmas:
        if ins.sem == "own":
            ins.sem = nc.alloc_semaphore(f"s_cc_{ins.order}")
            ins.val = 1
            continue
        slot = ins.sem
        dma_vals[slot] += ins.inc
        ins.sem = dma_sems[slot]
        ins.val = dma_vals[slot]

    final_waits = list(sched.out_dmas)

    def run_stream(ename, eng):
        for ins in sched.streams[ename]:
            for d in ins.deps:
                eng.wait_ge(d.sem, d.val)
            if ins.is_dma and ins.pre is not None:
                eng.wait_ge(ins.pre.sem, ins.pre.val)
            bi = ins.fn(eng)
            if ins.is_dma:
                bi.then_inc(ins.sem, ins.inc)
            elif ins.signal:
                bi.then_inc(ins.sem, 1)
        if ename == "sync":
            for d in final_waits:
                eng.wait_ge(d.sem, d.val)

    with nc.Block() as block:
        @block.tensor
        def _(eng):
            run_stream("tensor", eng)

        @block.vector
        def _(eng):
            run_stream("vector", eng)

        @block.scalar
        def _(eng):
            run_stream("scalar", eng)

        @block.gpsimd
        def _(eng):
            run_stream("gpsimd", eng)

        @block.sync
        def _(eng):
            run_stream("sync", eng)


class Arena:
    def __init__(self, nc, base=16384, limit=229312):
        self.nc, self.off, self.limit = nc, base, limit
        self.n = 0
        self.peak = base

    def alloc(self, name, shape, dtype):
        nbytes = int(np.prod(shape[1:])) * mybir.dt.size(dtype)
        off = (self.off + 63) // 64 * 64
        assert off + nbytes <= self.limit, f"arena overflow allocating {name} {shape}: {off}+{nbytes} > {self.limit}"
        self.off = off + nbytes
        self.peak = max(self.peak, self.off)
        self.n += 1
        return self.nc.alloc_sbuf_tensor_at(f"ar{self.n}_{name}", shape, dtype, offset=off).ap()

    def mark(self):
        return self.off

    def reset(self, m):
        self.off = m


D = 1024
DFF = 4096
ALPHA = 4 ** 0.25
LN_EPS = 1e-5


def ln_tile(S, nc, A, z_ap, rz, g_bc, b_bc, rgb, out_ap, rout, tmp, rtmp, st, rst, tagid):
    if tmp is None:
        tmp, rtmp = z_ap, rz
    S.dve(lambda e: e.bn_stats(out=st[:, 0:6], in_=z_ap[:, 0:512]), reads=[rz], writes=[rst])
    S.dve(lambda e: e.bn_stats(out=st[:, 6:12], in_=z_ap[:, 512:1024]), reads=[rz], writes=[rst])
    S.dve(lambda e: e.bn_aggr(out=st[:, 12:14], in_=st[:, 0:12]), reads=[rst], writes=[rst])
    S.act(lambda e: e.activation(out=st[:, 14:15], in_=st[:, 13:14], func=AF.Sqrt, bias=A["eps_ln"], scale=1.0),
          reads=[rst], writes=[rst])
    S.dve(lambda e: e.reciprocal(out=st[:, 14:15], in_=st[:, 14:15]), reads=[rst], writes=[rst])
    S.dve(lambda e: e.tensor_scalar(out=st[:, 15:16], in0=st[:, 12:13], scalar1=st[:, 14:15], scalar2=-1.0,
                                    op0=ALU.mult, op1=ALU.mult), reads=[rst], writes=[rst])
    S.act(lambda e: e.activation(out=tmp, in_=z_ap, func=AF.Identity, bias=st[:, 15:16], scale=st[:, 14:15]),
          reads=[rz, rst], writes=[rtmp])
    S.dve(lambda e: e.tensor_tensor(out=tmp, in0=tmp, in1=g_bc, op=ALU.mult), reads=[rtmp, rgb], writes=[rtmp])
    S.pool(lambda e: e.tensor_tensor(out=out_ap, in0=tmp, in1=b_bc, op=ALU.add), reads=[rtmp, rgb], writes=[rout])


def phase_F(nc, S, A, T, yT_dram, w_out_d, ln1g_d, ln1b_d, w_up_d, w_down_d, ln2g_d, ln2b_d,
            hres, rh, out_f32_d=None, out_bf16_d=None, y_gather=None, rout32=None, rout16=None, final_out=True, on_tile_done=None, wo_pre=None):
    NT = T // 128
    NB = T // 512
    al = A["alloc"]
    ident = A["ident_bf"]
    yT = al("yT", [128, 8, T], BF16)
    ry = [Res() for _ in range(NT)]
    lnp = al("lnp", [128, 2, 1024], F32)
    rln = Res()
    if y_gather is None:
        S.dma(yT, yT_dram.rearrange("(k p) t -> p k t", p=128), writes=ry)
    else:
        src, idx, ridx, rsrc = y_gather
        for kc in range(8):
            S.add("gpsimd", lambda e, kc=kc: e.indirect_dma_start(out=yT[:, kc, :], out_offset=None, in_=src[kc // 4],
                  in_offset=bass.IndirectOffsetOnAxis(ap=idx[:, kc:kc + 1], axis=0)), reads=[rsrc[kc // 4], ridx], writes=ry, is_dma=True)
    for i, d in enumerate((ln1g_d, ln1b_d)):
        S.dma(lnp[:, i, :], d.partition_broadcast(128), writes=[rln])
    NS = 4
    wu = [al(f"wu{i}", [128, 8, 1024], BF16) for i in range(2)]
    wd = [al(f"wd{i}", [128, 8, 1024], BF16) for i in range(2)]
    rwu = [Res(), Res()]
    rwd = [Res(), Res()]

    def load_ffn(s):
        b = s % 2
        S.dma(wu[b], w_up_d[:, s * 1024:(s + 1) * 1024].rearrange("(k p) f -> p k f", p=128), writes=[rwu[b]], eng="gpsimd")
        S.dma(wd[b], w_down_d[s * 1024:(s + 1) * 1024, :].rearrange("(k p) n -> p k n", p=128), writes=[rwd[b]], eng="gpsimd")

    if wo_pre is None:
        wo = wd[1]
        rwo = rwd[1]
        S.dma(wo, w_out_d.rearrange("(k p) n -> p k n", p=128), writes=[rwo], eng="gpsimd")
        load_ffn(0)
    else:
        wo, rwo = wo_pre
        load_ffn(0)
        load_ffn(1)
    tmp = [None] * 4
    rtmp = [None] * 4
    st = [al(f"st{i}", [128, 16], F32) for i in range(4)]
    rst = [Res() for _ in range(4)]
    ps = A["psum"]
    rps = A["rpsum"]
    psT = ps[7].bitcast(BF16)
    h1T = yT

    hb3 = [al(f"hb3_{i}", [128, 1024], BF16) for i in range(3)]
    rhb3 = [Res() for _ in range(3)]
    hb = [hb3[0], hb3[1]]
    rhb = [rhb3[0], rhb3[1]]
    def ln1_step(t):
        if t < NT:
            p = t % 2
            for half in range(2):
                bank = 2 * p + half
                for kc in range(8):
                    S.pe(lambda e, kc=kc, half=half, bank=bank, t=t: e.matmul(
                        ps[bank], lhsT=yT[:, kc, t * 128:(t + 1) * 128], rhs=wo[:, kc, half * 512:(half + 1) * 512],
                        start=(kc == 0), stop=(kc == 7)), reads=[ry[t], rwo], writes=[rps[bank]])
                S.dve(lambda e, half=half, bank=bank, t=t: e.scalar_tensor_tensor(
                    out=hres[:, t, half * 512:(half + 1) * 512], in0=hres[:, t, half * 512:(half + 1) * 512],
                    scalar=ALPHA, in1=ps[bank], op0=ALU.mult, op1=ALU.add), reads=[rh[t], rps[bank]], writes=[rh[t]])
            ln_tile(S, nc, A, hres[:, t, :], rh[t], lnp[:, 0, :], lnp[:, 1, :], rln, hres[:, t, :], rh[t],
                    None, None, st[t % 4], rst[t % 4], t)
            S.act(lambda e, t=t: e.activation(out=hb3[t % 3], in_=hres[:, t, :], func=AF.Copy), reads=[rh[t]], writes=[rhb3[t % 3]])
        if t >= 2:
            u = t - 2
            for kc in range(8):
                S.pe(lambda e, kc=kc, u=u: e.transpose(out=psT[:, kc * 128:(kc + 1) * 128], in_=hb3[u % 3][:, kc * 128:(kc + 1) * 128],
                                                       identity=ident), reads=[rhb3[u % 3]], writes=[rps[7]])
            S.act(lambda e, u=u: e.activation(out=h1T[:, :, u * 128:(u + 1) * 128],
                                              in_=psT.rearrange("p (k c) -> p k c", k=8), func=AF.Copy),
                  reads=[rps[7]], writes=[ry[u]])


    n_steps = NT + 2
    prologue = min(n_steps, 6)
    for t in range(prologue):
        ln1_step(t)
    ln1_next = [prologue]

    def ln1_more(k):
        for _ in range(k):
            if ln1_next[0] < n_steps:
                ln1_step(ln1_next[0])
                ln1_next[0] += 1

    if NB < 4:
        ln1_more(n_steps)
    aT = [al("aT", [128, 8, 512], BF16)] * 2
    raT = [Res()] * 2
    rl = [al("rl0", [128, 512], BF16)] * 2
    rrl = [Res()] * 2
    cnt = 0
    deferred = []

    def ln2_emit():
        while deferred:
            t = deferred.pop(0)
            p = t % 2
            ln_tile(S, nc, A, hres[:, t, :], rh[t], lnp[:, 0, :], lnp[:, 1, :], rln, hres[:, t, :], rh[t],
                    None, None, st[t % 4], rst[t % 4], t)
            if out_f32_d is not None:
                S.dma(out_f32_d[t * 128:(t + 1) * 128, :], hres[:, t, :], reads=[rh[t]], writes=([rout32[t]] if rout32 else []), is_output=final_out)
            if out_bf16_d is not None:
                S.act(lambda e, t=t, p=p: e.activation(out=hb[p], in_=hres[:, t, :], func=AF.Copy),
                      reads=[rh[t]], writes=[rhb[p]])
                S.dma(out_bf16_d(t) if callable(out_bf16_d) else out_bf16_d[t * 128:(t + 1) * 128, :], hb[p], reads=[rhb[p]],
                      writes=([rout16[t]] if rout16 else []), is_output=final_out)
            if on_tile_done is not None:
                on_tile_done(t)

    for s in range(NS):
        b = s % 2
        for tb in range(NB):
            ab = cnt % 2
            cnt += 1
            for fc in range(8):
                bank = 4 + (fc % 2)
                for kc in range(8):
                    S.pe(lambda e, kc=kc, fc=fc, bank=bank, tb=tb, b=b: e.matmul(
                        ps[bank], lhsT=wu[b][:, kc, fc * 128:(fc + 1) * 128], rhs=h1T[:, kc, tb * 512:(tb + 1) * 512],
                        start=(kc == 0), stop=(kc == 7)),
                        reads=[rwu[b]] + ry[tb * 4:(tb + 1) * 4], writes=[rps[bank]])
                rr = fc % 2
                S.act(lambda e, bank=bank, rr=rr: e.activation(out=rl[rr], in_=ps[bank], func=AF.Relu),
                      reads=[rps[bank]], writes=[rrl[rr]])
                if fc % 2 == 0:
                    S.dve(lambda e, fc=fc, bank=bank, ab=ab, rr=rr: e.tensor_tensor(
                        out=aT[ab][:, fc, :], in0=ps[bank], in1=rl[rr], op=ALU.mult),
                        reads=[rps[bank], rrl[rr]], writes=[raT[ab]])
                else:
                    S.pool(lambda e, fc=fc, ab=ab, rr=rr: e.tensor_tensor(
                        out=aT[ab][:, fc, :], in0=rl[rr], in1=rl[rr], op=ALU.mult),
                        reads=[rrl[rr]], writes=[raT[ab]])
            ln2_emit()
            for tt in range(4):
                t = tb * 4 + tt
                if s == 0:
                    ln1_more(1)
                for half in range(2):
                    bank = (tt % 2) * 2 + half
                    for fc in range(8):
                        S.pe(lambda e, fc=fc, half=half, bank=bank, tt=tt, ab=ab, b=b: e.matmul(
                            ps[bank], lhsT=aT[ab][:, fc, tt * 128:(tt + 1) * 128], rhs=wd[b][:, fc, half * 512:(half + 1) * 512],
                            start=(fc == 0), stop=(fc == 7)), reads=[raT[ab], rwd[b]], writes=[rps[bank]])
                    sl = hres[:, t, half * 512:(half + 1) * 512]
                    if s == 0:
                        S.dve(lambda e, sl=sl, bank=bank: e.scalar_tensor_tensor(
                            out=sl, in0=sl, scalar=ALPHA, in1=ps[bank], op0=ALU.mult, op1=ALU.add),
                            reads=[rh[t], rps[bank]], writes=[rh[t]])
                    else:
                        S.dve(lambda e, sl=sl, bank=bank: e.tensor_tensor(out=sl, in0=sl, in1=ps[bank], op=ALU.add),
                              reads=[rh[t], rps[bank]], writes=[rh[t]])
                if s == NS - 1:
                    deferred.append(t)
        if s == 0:
            ln1_more(n_steps)
            if wo_pre is None:
                load_ffn(1)
            for i, d in enumerate((ln2g_d, ln2b_d)):
                S.dma(lnp[:, i, :], d.partition_broadcast(128), writes=[rln])
        if s + 2 < NS:
            load_ffn(s + 2)
    ln2_emit()


NORM_EPS = 1e-6
SL = {n: i for i, n in enumerate(
    ["A_x", "A_g", "B_q", "B_qsw", "B_k", "B_ksw", "B_g", "C_q", "C_k", "D_q", "D_f", "D_g", "B_v", "C_v", "D_v"])}
NSL = 15
PV = {n: i for i, n in enumerate(
    ["cw0", "cw1", "cw2", "cw3", "cb", "ba", "bx", "lam", "retg", "hgg", "lb0", "lbl", "gam64", "nba", "nbx", "c1", "c1x2", "oml", "lnoml", "tmp0", "tmp1"])}
NPV = 24


class Ctx:
    pass


def setup_M(nc, S, A, SEQ, layer, hin_d, win_d, pvec_d, wab_d, consts):
    c = Ctx()
    c.nc, c.S, c.A, c.SEQ, c.layer = nc, S, A, SEQ, layer
    al = A["alloc"]
    c.NB = SEQ // 512
    c.NT = SEQ // 128
    c.ps, c.rps = A["psum"], A["rpsum"]
    c.hT = al("hT", [128, 8, SEQ], BF16)
    c.win = al("win", [128, 8, NSL * 128], BF16)
    c.K = {}
    c.rK = Res()
    for name, (d, shape, dtype) in consts.items():
        t = al("k_" + name, shape, dtype)
        S.dma(t, d, writes=[c.rK])
        c.K[name] = t
    c.rwin = Res()
    S.dma(c.win, win_d.rearrange("(k p) n -> p k n", p=128), writes=[c.rwin], eng="gpsimd")
    c.pv = al("pv", [128, NPV], F32)
    c.rpv = Res()
    S.dma(c.pv[:, 0:13], pvec_d, writes=[c.rpv])
    c.wab = al("wab", [128, 2, 128], BF16)
    c.rwab = Res()
    S.dma(c.wab, wab_d, writes=[c.rwab], eng="gpsimd")
    pieces = hin_d if isinstance(hin_d, (list, tuple)) else [(hin_d, 0, SEQ, list(A.get("rhin", [])))]
    c.rhT_t = [Res() for _ in range(c.NT)]
    c.rhT = lambda a, b: [c.rhT_t[t] for t in range(a // 128, (b + 127) // 128)]
    c.vtm = al("vtm", [128, c.NT, 384], BF16)
    c.rv = [Res() for _ in range(c.NT)]
    c.persist_mark = A["arena"].mark()
    NSTG = 12
    stage = [al(f"hstage{i}", [128, 1024], BF16) for i in range(NSTG)]
    rstage = [Res() for _ in range(NSTG)]

    def vproj(t):
        bank = t % 2
        for kc in range(8):
            S.pe(lambda e, kc=kc, t=t, bank=bank: e.matmul(
                c.ps[bank][:, 0:384], lhsT=c.hT[:, kc, t * 128:(t + 1) * 128], rhs=c.win[:, kc, 12 * 128:15 * 128],
                start=(kc == 0), stop=(kc == 7)), reads=c.rhT(t * 128, (t + 1) * 128) + [c.rwin], writes=[c.rps[bank]])
        S.act(lambda e, t=t, bank=bank: e.activation(out=c.vtm[:, t, :], in_=c.ps[bank][:, 0:384], func=AF.Copy),
              reads=[c.rps[bank]], writes=[c.rv[t]])

    tiles = []
    for (src, t0, n, rsrc) in pieces:
        for j in range(n // 128):
            tiles.append((src[j * 128:(j + 1) * 128, :], t0 // 128 + j, rsrc))
    for i, (src_t, t, rsrc) in enumerate(tiles):
        sb = i % NSTG
        S.dma(stage[sb], src_t, reads=rsrc, writes=[rstage[sb]])
        bank = 6 + i % 2
        psT = c.ps[bank].bitcast(BF16)
        for kc in range(8):
            S.pe(lambda e, kc=kc, sb=sb, psT=psT: e.transpose(out=psT[:, kc * 128:(kc + 1) * 128], in_=stage[sb][:, kc * 128:(kc + 1) * 128],
                                                              identity=c.K["ident"]), reads=[rstage[sb], c.rK], writes=[c.rps[bank]])
        ev = S.act if i % 2 == 0 else S.dve
        if i % 2 == 0:
            S.act(lambda e, t=t, psT=psT: e.activation(out=c.hT[:, :, t * 128:(t + 1) * 128], in_=psT.rearrange("p (k c) -> p k c", k=8), func=AF.Copy),
                  reads=[c.rps[bank]], writes=[c.rhT_t[t]])
        else:
            S.dve(lambda e, t=t, psT=psT: e.tensor_copy(out=c.hT[:, :, t * 128:(t + 1) * 128], in_=psT.rearrange("p (k c) -> p k c", k=8)),
                  reads=[c.rps[bank]], writes=[c.rhT_t[t]])
        if i >= 2:
            vproj(tiles[i - 2][1])
    for (src_t, t, rsrc) in tiles[-2:]:
        vproj(t)
    c.pcount = 0
    c.ry_list = []
    c.ry_by_mixer = {}

    def new_yres():
        r = Res()
        c.ry_list.append(r)
        return r
    c.new_yres = new_yres
    return c


def proj_fm(c, slname, blk, nblk=1):
    S = c.S
    j = SL[slname]
    banks = getattr(c, "proj_banks", (0, 1))
    bank = banks[c.pcount % len(banks)]
    c.pcount += 1
    for kc in range(8):
        c.last_proj = S.pe(lambda e, kc=kc, j=j, blk=blk, bank=bank: e.matmul(
            c.ps[bank], lhsT=c.win[:, kc, j * 128:(j + 1) * 128], rhs=c.hT[:, kc, blk * 512:(blk + 1) * 512],
            start=(kc == 0), stop=(kc == 7)), reads=c.rhT(blk * 512, (blk + 1) * 512) + [c.rwin], writes=[c.rps[bank]])
    return bank


def small_params(c):
    S, pv, rpv = c.S, c.pv, c.rpv
    col = lambda n: pv[:, PV[n]:PV[n] + 1]
    S.act(lambda e: e.activation(out=col("tmp0"), in_=col("lam"), func=AF.Exp, scale=-1.0), reads=[rpv], writes=[rpv])
    S.act(lambda e: e.activation(out=col("tmp0"), in_=col("tmp0"), func=AF.Ln, bias=c.K["one"], scale=1.0), reads=[rpv, c.rK], writes=[rpv])
    S.dve(lambda e: e.tensor_scalar(out=col("c1"), in0=col("tmp0"), scalar1=-8.0, scalar2=None, op0=ALU.mult), reads=[rpv], writes=[rpv])
    S.dve(lambda e: e.tensor_scalar(out=col("c1x2"), in0=col("tmp0"), scalar1=-16.0, scalar2=None, op0=ALU.mult), reads=[rpv], writes=[rpv])
    if c.layer == 0:
        S.dve(lambda e: e.memset(col("oml"), 1.0), writes=[rpv])
        S.dve(lambda e: e.memset(col("lnoml"), 0.0), writes=[rpv])
    else:
        S.dve(lambda e: e.tensor_tensor(out=col("tmp1"), in0=col("lbl"), in1=col("lb0"), op=ALU.subtract), reads=[rpv], writes=[rpv])
        S.act(lambda e: e.activation(out=col("tmp1"), in_=col("tmp1"), func=AF.Exp), reads=[rpv], writes=[rpv])
        S.act(lambda e: e.activation(out=col("lnoml"), in_=col("tmp1"), func=AF.Ln, bias=c.K["one"], scale=1.0), reads=[rpv, c.rK], writes=[rpv])
        S.dve(lambda e: e.tensor_scalar(out=col("lnoml"), in0=col("lnoml"), scalar1=-1.0, scalar2=None, op0=ALU.mult), reads=[rpv], writes=[rpv])
        S.act(lambda e: e.activation(out=col("oml"), in_=col("lnoml"), func=AF.Exp), reads=[rpv], writes=[rpv])


def mixer_A(c, yT_d):
    S, A, SEQ = c.S, c.A, c.SEQ
    al = A["alloc"]
    pv, rpv = c.pv, c.rpv
    col = lambda n: pv[:, PV[n]:PV[n] + 1]
    NH = 2 if SEQ >= 1024 else 1
    L = SEQ // NH
    NBH = L // 512
    xT = al("A_xT", [128, 3 + SEQ], F32)
    rxT = Res()
    XC = al("A_XC", [128, L], F32); rXC = Res()
    R = al("A_R", [128, L], F32); rR = Res()
    TH = al("A_TH", [128, L], F32); rTH = Res()
    AA = al("A_AA", [128, L], F32); rAA = Res()
    XCB = al("A_XCB", [128, L], BF16); rXCB = Res()
    ii = [al(f"A_ii{i}", [128, 512], F32) for i in range(2)]; rii = [Res(), Res()]
    gg = [al(f"A_gg{i}", [128, 512], F32) for i in range(2)]; rgg = [Res(), Res()]
    yst = [al(f"A_y{i}", [128, 512], BF16) for i in range(2)]; ryst = [Res(), Res()]
    carry = al("A_carry", [128, 1], F32); rcar = Res()
    S.dve(lambda e: e.memset(xT[:, 0:3], 0.0), writes=[rxT])
    S.dve(lambda e: e.memset(carry, 0.0), writes=[rcar])
    ps, rps = c.ps, c.rps
    for hf in range(NH):
        t0 = hf * L
        for b in range(NBH):
            blk = hf * NBH + b
            bank = proj_fm(c, "A_x", blk)
            S.act(lambda e, bank=bank, blk=blk: e.activation(out=xT[:, 3 + blk * 512:3 + (blk + 1) * 512], in_=ps[bank], func=AF.Copy),
                  reads=[rps[bank]], writes=[rxT])
        S.dve(lambda e, t0=t0: e.tensor_scalar(out=XC, in0=xT[:, t0 + 3:t0 + 3 + L], scalar1=col("cw3"), scalar2=col("cb"),
                                               op0=ALU.mult, op1=ALU.add), reads=[rxT, rpv], writes=[rXC])
        for w in range(3):
            S.dve(lambda e, t0=t0, w=w: e.scalar_tensor_tensor(out=XC, in0=xT[:, t0 + w:t0 + w + L], scalar=col(f"cw{w}"), in1=XC,
                                                              op0=ALU.mult, op1=ALU.add), reads=[rxT, rpv, rXC], writes=[rXC])
        S.act(lambda e: e.activation(out=XCB, in_=XC, func=AF.Copy), reads=[rXC], writes=[rXCB])
        for b in range(NBH):
            sl = slice(b * 512, (b + 1) * 512)
            S.pe(lambda e, sl=sl: e.matmul(ps[2], lhsT=c.wab[:, 0, :], rhs=XCB[:, sl], start=True, stop=True),
                 reads=[c.rwab, rXCB], writes=[rps[2]])
            S.pe(lambda e, sl=sl: e.matmul(ps[3], lhsT=c.wab[:, 1, :], rhs=XCB[:, sl], start=True, stop=True),
                 reads=[c.rwab, rXCB], writes=[rps[3]])
            S.act(lambda e, sl=sl: e.activation(out=R[:, sl], in_=ps[2], func=AF.Sigmoid, bias=col("ba"), scale=1.0),
                  reads=[rps[2], rpv], writes=[rR])
            S.act(lambda e, b=b: e.activation(out=ii[b % 2], in_=ps[3], func=AF.Sigmoid, bias=col("bx"), scale=1.0),
                  reads=[rps[3], rpv], writes=[rii[b % 2]])
            S.act(lambda e, sl=sl: e.activation(out=TH[:, sl], in_=R[:, sl], func=AF.Tanh, scale=col("c1")),
                  reads=[rR, rpv], writes=[rTH])
            S.dve(lambda e, sl=sl, b=b: e.tensor_tensor(out=XC[:, sl], in0=XC[:, sl], in1=ii[b % 2], op=ALU.mult),
                  reads=[rXC, rii[b % 2]], writes=[rXC])
        S.act(lambda e: e.activation(out=AA, in_=R, func=AF.Exp, scale=col("c1")), reads=[rR, rpv], writes=[rAA])
        S.act(lambda e: e.activation(out=R, in_=R, func=AF.Exp, scale=col("c1x2")), reads=[rR, rpv], writes=[rR])
        S.dve(lambda e: e.scalar_tensor_tensor(out=TH, in0=R, scalar=1.0, in1=TH, op0=ALU.add, op1=ALU.mult),
              reads=[rR, rTH], writes=[rTH])
        S.act(lambda e: e.activation(out=TH, in_=TH, func=AF.Ln, scale=-1.0), reads=[rTH], writes=[rTH])
        S.act(lambda e: e.activation(out=TH, in_=TH, func=AF.Exp, scale=0.5), reads=[rTH], writes=[rTH])
        S.dve(lambda e: e.tensor_tensor(out=XC, in0=XC, in1=TH, op=ALU.mult), reads=[rXC, rTH], writes=[rXC])
        S.dve(lambda e: e.tensor_tensor_scan(out=R, data0=AA, data1=XC, initial=carry, op0=ALU.mult, op1=ALU.add),
              reads=[rAA, rXC, rcar], writes=[rR])
        S.dve(lambda e: e.tensor_copy(out=carry, in_=R[:, L - 1:L]), reads=[rR], writes=[rcar])
        for b in range(NBH):
            blk = hf * NBH + b
            sl = slice(b * 512, (b + 1) * 512)
            bank = proj_fm(c, "A_g", blk)
            S.act(lambda e, bank=bank, b=b: e.activation(out=gg[b % 2], in_=ps[bank], func=AF.Gelu_apprx_tanh),
                  reads=[rps[bank]], writes=[rgg[b % 2]])
            S.dve(lambda e, sl=sl, b=b: e.tensor_tensor(out=yst[b % 2], in0=gg[b % 2], in1=R[:, sl], op=ALU.mult),
                  reads=[rgg[b % 2], rR], writes=[ryst[b % 2]])
            S.dma(yT_d[:, blk * 512:(blk + 1) * 512], yst[b % 2], reads=[ryst[b % 2]], writes=[c.new_yres()], is_output=True)


def gla(c, tag, qt, rqt, kt, rkt, En, rEn, voff, gslice, gcol, groupnorm, yT_d):
    S, A, SEQ = c.S, c.A, c.SEQ
    al = A["alloc"]
    ps, rps = c.ps, c.rps
    NCH = SEQ // 64
    NB = SEQ // 512
    pv, rpv = c.pv, c.rpv
    ident = c.K["ident"]
    psT = ps[6].bitcast(BF16)
    KVs = al(tag + "_KVs", [128, 64, NCH], F32); rKV = Res()
    Ss = al(tag + "_Ss", [128, 64, NCH], BF16); rSs = Res()
    En0 = al(tag + "_En0", [128, NCH], F32); rEn0 = Res()
    ktm4 = al(tag + "_ktm4", [128, 4, 128], BF16); rktm = Res()
    import os
    STOP = int(os.environ.get("GLA_STOP", "99"))
    if STOP == 0:
        S.dma(yT_d, qt, reads=[rqt], is_output=True)
        return
    for tg in range(NB):
        for i in range(4):
            t = tg * 4 + i
            S.pe(lambda e, i=i, t=t: e.transpose(out=psT[:, i * 128:(i + 1) * 128], in_=kt[:, t * 128:(t + 1) * 128], identity=ident),
                 reads=[rkt, c.rK], writes=[rps[6]])
        S.act(lambda e: e.activation(out=ktm4, in_=psT[:, 0:512].rearrange("p (i c) -> p i c", c=128), func=AF.Copy),
              reads=[rps[6]], writes=[rktm])
        for i in range(4):
            t = tg * 4 + i
            S.pe(lambda e, i=i, t=t: e.matmul(ps[2][:, i * 128:(i + 1) * 128], lhsT=ktm4[0:64, i, :], rhs=c.vtm[0:64, t, voff:voff + 128],
                                             start=True, stop=True), reads=[rktm, c.rv[t]], writes=[rps[2]])
            S.pe(lambda e, i=i, t=t: e.matmul(ps[3][:, i * 128:(i + 1) * 128], lhsT=ktm4[64:128, i, :], rhs=c.vtm[64:128, t, voff:voff + 128],
                                             start=True, stop=True), reads=[rktm, c.rv[t]], writes=[rps[3]])
        for par in range(2):
            for hl in range(2):
                rows = slice(hl * 64, (hl + 1) * 64)
                n0 = 8 * tg + par
                S.dve(lambda e, par=par, hl=hl, rows=rows, n0=n0, tg=tg: e.tensor_tensor(
                    out=KVs[rows, :, n0:8 * tg + 8:2].rearrange("p e n -> p n e"),
                    in0=ps[2 + par][rows, :].rearrange("p (i c) -> p i c", c=128)[:, :, hl * 64:(hl + 1) * 64],
                    in1=En[rows, n0:8 * tg + 8:2].unsqueeze(2).broadcast_to([64, 4, 64]), op=ALU.mult),
                    reads=[rps[2 + par], rEn], writes=[rKV])
    if STOP == 1:
        S.dma(yT_d[:, 0:64 * NCH // 2], KVs.rearrange("p e n -> p (e n)").bitcast(BF16)[:, 0:64 * NCH // 2], reads=[rKV], writes=[c.new_yres()], is_output=True)
        return
    S.dve(lambda e: e.tensor_copy(out=En0, in_=En), reads=[rEn], writes=[rEn0])
    S.dve(lambda e: e.memset(En0[:, 0:1], 0.0), writes=[rEn0])
    rSse = [Res() for _ in range(64)]
    for ee in range(64):
        S.dve(lambda e, ee=ee: e.tensor_tensor_scan(out=Ss[:, ee, :], data0=En0, data1=KVs[:, ee, :],
                                                    initial=0.0, op0=ALU.mult, op1=ALU.add), reads=[rEn0, rKV], writes=[rSse[ee]])
    if STOP == 2:
        S.dma(yT_d[:, 0:64 * NCH], Ss.rearrange("p e n -> p (e n)"), reads=rSse, writes=[c.new_yres()], is_output=True)
        return
    atm = [al(tag + f"_atm{i}", [128, 512], BF16) for i in range(2)]; ratm = [Res(), Res()]
    osb = al(tag + "_osb", [128, 512], F32); rosb = Res()
    ob = al(tag + "_ob", [128, 512], BF16); rob = Res()
    rstd = al(tag + "_rstd", [128, 512], F32); rrstd = Res()
    sg = al(tag + "_sg", [128, 512], F32); rsg = Res()
    yst = [al(tag + f"_y{i}", [128, 512], BF16) for i in range(2)]; ryst = [Res(), Res()]
    osb2 = [osb, al(tag + "_osb1", [128, 512], F32)]; rosb2 = [rosb, Res()]
    sg2 = [sg, al(tag + "_sg1", [128, 512], F32)]; rsg2 = [rsg, Res()]

    def front(tb):
        osb_, rosb_, sg_, rsg_ = osb2[tb % 2], rosb2[tb % 2], sg2[tb % 2], rsg2[tb % 2]
        for i in range(4):
            t = tb * 4 + i
            ts_ = slice(t * 128, (t + 1) * 128)
            for hl in range(2):
                rows = slice(hl * 64, (hl + 1) * 64)
                S.pe(lambda e, i=i, hl=hl, rows=rows, ts_=ts_: e.matmul(ps[2 + hl][:, i * 128:(i + 1) * 128], lhsT=kt[rows, ts_], rhs=qt[rows, ts_],
                                                                      start=True, stop=True), reads=[rkt, rqt], writes=[rps[2 + hl]])
        bank = proj_fm(c, gslice, tb)
        S.act(lambda e, bank=bank: e.activation(out=sg_, in_=ps[bank], func=AF.Silu), reads=[rps[bank]], writes=[rsg_])
        for hl in range(2):
            S.dve(lambda e, hl=hl: e.tensor_tensor(out=atm[hl], in0=ps[2 + hl], in1=c.K["glamask"], op=ALU.mult),
                  reads=[rps[2 + hl], c.rK], writes=[ratm[hl]])
        for i in range(4):
            t = tb * 4 + i
            for hl in range(2):
                rows = slice(hl * 64, (hl + 1) * 64)
                ncs = [n for n in (2 * t, 2 * t + 1) if n > 0]
                S.pe(lambda e, i=i, hl=hl, rows=rows, t=t, ncs=ncs: e.matmul(
                    ps[4][rows, i * 128:(i + 1) * 128], lhsT=c.vtm[:, t, voff + hl * 64:voff + (hl + 1) * 64], rhs=atm[hl][:, i * 128:(i + 1) * 128],
                    start=True, stop=(len(ncs) == 0)), reads=[c.rv[t], ratm[hl]], writes=[rps[4]])
                for n in ncs:
                    cc = n - 2 * t
                    S.pe(lambda e, i=i, hl=hl, rows=rows, n=n, cc=cc, ncs=ncs: e.matmul(
                        ps[4][rows, i * 128 + cc * 64:i * 128 + (cc + 1) * 64], lhsT=Ss[rows, :, n - 1], rhs=qt[rows, n * 64:(n + 1) * 64],
                        start=False, stop=(n == ncs[-1])), reads=rSse + [rqt], writes=[rps[4]])
        S.act(lambda e: e.activation(out=osb_, in_=ps[4], func=AF.Copy), reads=[rps[4]], writes=[rosb_])

    def back(tb):
        osb_, rosb_, sg_, rsg_ = osb2[tb % 2], rosb2[tb % 2], sg2[tb % 2], rsg2[tb % 2]
        if groupnorm:
            S.dve(lambda e: e.tensor_copy(out=ob, in_=osb_), reads=[rosb_], writes=[rob])
            S.pe(lambda e: e.matmul(ps[5], lhsT=c.K["blockmean"], rhs=ob, start=True, stop=True), reads=[rob, c.rK], writes=[rps[5]])
            S.dve(lambda e: e.tensor_tensor(out=osb_, in0=osb_, in1=ps[5], op=ALU.subtract), reads=[rosb_, rps[5]], writes=[rosb_])
        S.act(lambda e: e.activation(out=ob, in_=osb_, func=AF.Square), reads=[rosb_], writes=[rob])
        S.pe(lambda e: e.matmul(ps[5], lhsT=c.K["blockmean"], rhs=ob, start=True, stop=True), reads=[rob, c.rK], writes=[rps[5]])
        S.act(lambda e: e.activation(out=rstd, in_=ps[5], func=AF.Ln, bias=c.K["eps_n"], scale=1.0), reads=[rps[5], c.rK], writes=[rrstd])
        S.act(lambda e: e.activation(out=rstd, in_=rstd, func=AF.Exp, scale=-0.5), reads=[rrstd], writes=[rrstd])
        S.dve(lambda e: e.tensor_tensor(out=osb_, in0=osb_, in1=rstd, op=ALU.mult), reads=[rosb_, rrstd], writes=[rosb_])
        S.dve(lambda e: e.scalar_tensor_tensor(out=yst[tb % 2], in0=osb_, scalar=pv[:, PV[gcol]:PV[gcol] + 1], in1=sg_,
                                               op0=ALU.mult, op1=ALU.mult), reads=[rosb_, rsg_, rpv], writes=[ryst[tb % 2]])
        S.dma(yT_d[:, tb * 512:(tb + 1) * 512], yst[tb % 2], reads=[ryst[tb % 2]], writes=[c.new_yres()], is_output=True)

    for tb in range(NB + 1):
        if tb < NB:
            front(tb)
        if tb >= 1:
            back(tb - 1)


def mixer_B(c, yT_d):
    S, A, SEQ = c.S, c.A, c.SEQ
    al = A["alloc"]
    ps, rps = c.ps, c.rps
    NCH = SEQ // 64
    qt = al("B_qt", [128, SEQ], BF16); rqt = Res()
    kt = al("B_kt", [128, SEQ], BF16); rkt = Res()
    En = al("B_En", [128, NCH], F32); rEn = Res()
    tab = al("B_tab", [128, 4, 512], F32); rtab = Res()
    t1s = [al(f"B_t1{i}", [128, 512], F32) for i in range(2)]; rt1s = [Res(), Res()]
    t2s = [al(f"B_t2{i}", [128, 512], F32) for i in range(2)]; rt2s = [Res(), Res()]
    S.dve(lambda e: e.tensor_copy(out=En, in_=c.pv[:, PV["gam64"]:PV["gam64"] + 1].broadcast_to([128, NCH])), reads=[c.rpv], writes=[rEn])
    c.proj_banks = (0, 1, 2, 3)
    for blk in range(c.NB):
        sl = slice(blk * 512, (blk + 1) * 512)
        S.dma(tab, c.rettab_d[:, :, sl].rearrange("f p t -> p f t"), writes=[rtab])
        for which, dst, rdst, ti in (("B_q", qt, rqt, 0), ("B_k", kt, rkt, 2)):
            t1, rt1, t2, rt2 = t1s[ti // 2], rt1s[ti // 2], t2s[ti // 2], rt2s[ti // 2]
            b1 = proj_fm(c, which, blk)
            b2 = proj_fm(c, which + "sw", blk)
            S.dve(lambda e, b1=b1, ti=ti, t1=t1: e.tensor_tensor(out=t1, in0=ps[b1], in1=tab[:, ti, :], op=ALU.mult), reads=[rps[b1], rtab], writes=[rt1])
            S.dve(lambda e, b2=b2, ti=ti, t2=t2: e.tensor_tensor(out=t2, in0=ps[b2], in1=tab[:, ti + 1, :], op=ALU.mult), reads=[rps[b2], rtab], writes=[rt2])
            S.pool(lambda e, dst=dst, sl=sl, t1=t1, t2=t2: e.tensor_tensor(out=dst[:, sl], in0=t1, in1=t2, op=ALU.add), reads=[rt1, rt2], writes=[rdst])
    c.proj_banks = (0, 1)
    gla(c, "B", qt, rqt, kt, rkt, En, rEn, 0, "B_g", "retg", True, yT_d)


def mixer_D(c, yT_d):
    S, A, SEQ = c.S, c.A, c.SEQ
    al = A["alloc"]
    ps, rps = c.ps, c.rps
    pv, rpv = c.pv, c.rpv
    col = lambda n: pv[:, PV[n]:PV[n] + 1]
    NCH = SEQ // 64
    qt = al("D_qt", [128, SEQ], BF16); rqt = Res()
    kt = al("D_kt", [128, SEQ], BF16); rkt = Res()
    En = al("D_En", [128, NCH], F32); rEn = Res()
    T = [al(f"D_T{i}", [128, 512], F32) for i in range(6)]
    rT = [Res() for _ in range(6)]
    one = c.K["one"]
    c.proj_banks = (0, 1, 2, 3)
    for blk in range(c.NB):
        sl = slice(blk * 512, (blk + 1) * 512)
        bf_ = proj_fm(c, "D_f", blk)
        S.act(lambda e, b=bf_: e.activation(out=T[0], in_=ps[b], func=AF.Exp), reads=[rps[bf_]], writes=[rT[0]])
        S.act(lambda e: e.activation(out=T[0], in_=T[0], func=AF.Ln, bias=one, scale=1.0), reads=[rT[0], c.rK], writes=[rT[0]])
        S.act(lambda e: e.activation(out=T[1], in_=T[0], func=AF.Exp, bias=col("lnoml"), scale=-1.0), reads=[rT[0], rpv], writes=[rT[1]])
        S.act(lambda e: e.activation(out=T[2], in_=T[1], func=AF.Ln, bias=one, scale=-1.0), reads=[rT[1], c.rK], writes=[rT[2]])
        S.dve(lambda e: e.tensor_tensor_scan(out=T[3], data0=c.K["resetmask"], data1=T[2], initial=0.0, op0=ALU.mult, op1=ALU.add),
              reads=[rT[2], c.rK], writes=[rT[3]])
        S.act(lambda e: e.activation(out=T[4], in_=T[3], func=AF.Exp), reads=[rT[3]], writes=[rT[4]])
        S.act(lambda e: e.activation(out=T[5], in_=T[3], func=AF.Exp, scale=-1.0), reads=[rT[3]], writes=[rT[5]])
        bq = proj_fm(c, "D_q", blk)
        S.dve(lambda e, b=bq, sl=sl: e.tensor_tensor(out=qt[:, sl], in0=ps[b], in1=T[4], op=ALU.mult), reads=[rps[bq], rT[4]], writes=[rqt])
        S.pool(lambda e, sl=sl: e.tensor_tensor(out=kt[:, sl], in0=T[1], in1=T[5], op=ALU.mult), reads=[rT[1], rT[5]], writes=[rkt])
        S.dve(lambda e, blk=blk: e.tensor_copy(out=En[:, blk * 8:(blk + 1) * 8], in_=T[4][:, 63:512:64]), reads=[rT[4]], writes=[rEn])
    c.proj_banks = (0, 1)
    gla(c, "D", qt, rqt, kt, rkt, En, rEn, 256, "D_g", "hgg", False, yT_d)


def mixer_C(c, yT_d):
    S, A, SEQ = c.S, c.A, c.SEQ
    al = A["alloc"]
    ps, rps = c.ps, c.rps
    NQ = SEQ // 512
    qT = al("C_qT", [128, SEQ], BF16); rqT = Res()
    kT = al("C_kT", [128, SEQ], BF16); rkT = Res()
    c.proj_banks = (0, 1, 2, 3)
    for blk in range(c.NB):
        sl = slice(blk * 512, (blk + 1) * 512)
        b = proj_fm(c, "C_q", blk)
        S.act(lambda e, b=b, sl=sl: e.activation(out=qT[:, sl], in_=ps[b], func=AF.Copy, scale=0.125), reads=[rps[b]], writes=[rqT])
        b = proj_fm(c, "C_k", blk)
        S.dve(lambda e, b=b, sl=sl: e.tensor_copy(out=kT[:, sl], in_=ps[b]), reads=[rps[b]], writes=[rkT])
    c.proj_banks = (0, 1)
    NE = 4
    eb = [al(f"C_e{i}", [128, 512], BF16) for i in range(NE)]; reb = [Res() for _ in range(NE)]
    msp = [al(f"C_msp{i}", [128, 512], BF16) for i in range(NE)]; rmsp = [Res() for _ in range(NE)]
    exr = [al(f"C_exr{i}", [128, 512], BF16) for i in range(2)]; rexr = [Res() for _ in range(2)]
    wT = [al(f"C_w{i}", [128, 512], BF16) for i in range(NE)]; rwT = [Res() for _ in range(NE)]
    chi = [al(f"C_chi{i}", [1, 512], BF16) for i in range(2)]; rchi = [Res(), Res()]
    clo = [al(f"C_clo{i}", [1, 512], BF16) for i in range(2)]; rclo = [Res(), Res()]
    yst = [al(f"C_y{i}", [128, 512], BF16) for i in range(2)]; ryst = [Res(), Res()]
    negtri, ones_row, cmask = c.K["negtri"], c.K["ones_row"], c.K["cmask"]
    steps = []
    for qb in range(NQ):
        for kb in range(4 * qb + 3, -1, -1):
            for h in range(2):
                steps.append((qb, kb, h))
    N = len(steps)

    def cols(i):
        qb, kb, h = steps[i]
        j = kb - 4 * qb
        return slice(max(j, 0) * 128, 512)

    def stage_Z(i):
        qb, kb, h = steps[i]
        rows = slice(h * 64, (h + 1) * 64)
        cs = cols(i)
        q0 = qb * 512
        S.pe(lambda e: e.matmul(ps[2 + h][:, cs], lhsT=kT[rows, kb * 128:(kb + 1) * 128], rhs=qT[rows, q0 + cs.start:q0 + 512],
                                start=True, stop=True), reads=[rkT, rqT], writes=[rps[2 + h]])
        S.act(lambda e: e.activation(out=eb[i % NE][:, cs], in_=ps[2 + h][:, cs], func=AF.Exp), reads=[rps[2 + h]], writes=[reb[i % NE]])
        j = kb - 4 * qb
        if j >= 0:
            dg = slice(j * 128, (j + 1) * 128)
            S.dve(lambda e: e.tensor_tensor(out=eb[i % NE][:, dg], in0=eb[i % NE][:, dg], in1=cmask[:, j, dg], op=ALU.mult),
                  reads=[reb[i % NE], c.rK], writes=[reb[i % NE]])
        S.act(lambda e: e.activation(out=msp[i % NE][:, cs], in_=eb[i % NE][:, cs], func=AF.Ln, bias=1.0, scale=1.0),
              reads=[reb[i % NE]], writes=[rmsp[i % NE]])

    def stage_R(i):
        qb, kb, h = steps[i]
        first = (kb == 4 * qb + 3)
        cs = cols(i)
        S.pe(lambda e: e.matmul(ps[4 + h][:, cs], lhsT=negtri, rhs=msp[i % NE][:, cs], start=first, stop=False, skip_group_check=True),
             reads=[rmsp[i % NE], c.rK], writes=[rps[4 + h]])
        S.act(lambda e: e.activation(out=exr[i % 2][:, cs], in_=ps[4 + h][:, cs], func=AF.Exp), reads=[rps[4 + h]], writes=[rexr[i % 2]])
        S.dve(lambda e: e.tensor_tensor(out=wT[i % NE][:, cs], in0=eb[i % NE][:, cs], in1=exr[i % 2][:, cs], op=ALU.mult),
              reads=[reb[i % NE], rexr[i % 2]], writes=[rwT[i % NE]])

    def stage_O(i):
        qb, kb, h = steps[i]
        rows = slice(h * 64, (h + 1) * 64)
        ob = 6 + qb % 2
        first = (kb == 4 * qb + 3)
        cs = cols(i)
        if kb > 0:
            S.pe(lambda e: e.matmul(ps[4 + h][:, cs], lhsT=c.K["negcompl"], rhs=msp[i % NE][:, cs], start=False, stop=(kb == 1), skip_group_check=True),
                 reads=[rmsp[i % NE], c.rK], writes=[rps[4 + h]])
        S.pe(lambda e: e.matmul(ps[ob][rows, cs], lhsT=c.vtm[:, kb, 128 + h * 64:128 + (h + 1) * 64], rhs=wT[i % NE][:, cs],
                                start=first, stop=(kb == 0), skip_group_check=True), reads=[c.rv[kb], rwT[i % NE]], writes=[rps[ob]])
        if kb == 0 and h == 1:
            S.act(lambda e: e.activation(out=yst[qb % 2], in_=ps[ob], func=AF.Copy), reads=[rps[ob]], writes=[ryst[qb % 2]])
            S.dma(yT_d[:, qb * 512:(qb + 1) * 512], yst[qb % 2], reads=[ryst[qb % 2]], writes=[c.new_yres()], is_output=True)

    for s in range(-2, N):
        if 0 <= s + 2 < N:
            stage_Z(s + 2)
        if 0 <= s + 1 < N:
            stage_R(s + 1)
        if 0 <= s < N:
            stage_O(s)


def phase_M(nc, S, A, SEQ, layer, hin_d, win_d, pvec_d, wab_d, consts, rettab_d, yT_d, which="ABDC"):
    ar = A["arena"]
    c = setup_M(nc, S, A, SEQ, layer, hin_d, win_d, pvec_d, wab_d, consts)
    c.rettab_d = rettab_d
    small_params(c)
    m = c.persist_mark
    fns = {"A": (mixer_A, 0), "B": (mixer_B, 1), "C": (mixer_C, 2), "D": (mixer_D, 3)}
    for i, ch in enumerate(which):
        S.barrier(A["bar_scratch"])
        ar.reset(m)
        fn, slot = fns[ch]
        fn(c, yT_d[slot] if isinstance(yT_d, (list, tuple)) else yT_d[slot * 128:(slot + 1) * 128, :])
        c.ry_by_mixer[ch] = c.ry_list
        c.ry_list = []
    return c


BF = ml_dtypes.bfloat16
REF_SLICE = {"A_x": 0, "A_g": 1, "B_q": 2, "B_k": 3, "B_v": 4, "B_g": 5, "C_q": 6, "C_k": 7, "C_v": 8,
             "D_q": 9, "D_f": 10, "D_v": 11, "D_g": 12}
MY_SLICES = ["A_x", "A_g", "B_q", "B_qsw", "B_k", "B_ksw", "B_g", "C_q", "C_k", "D_q", "D_f", "D_g", "B_v", "C_v", "D_v"]


def core_cols(hh):
    p = np.arange(128)
    partner = (p // 64) * 64 + ((p % 64) + 32) % 64
    cols = []
    for n in MY_SLICES:
        sw = n.endswith("sw")
        base = REF_SLICE[n[:-2] if sw else n] * 256 + hh * 128
        cols.append(base + (partner if sw else p))
    return np.concatenate(cols)


def prep_layer_core(inp, l, hh):
    ch = slice(hh * 128, (hh + 1) * 128)
    out = {}
    out["win"] = np.ascontiguousarray(np.asarray(inp["w_in"][l])[:, core_cols(hh)])
    pv = np.zeros((128, 13), np.float32)
    cw = np.asarray(inp["conv_w"][l])
    for w in range(4):
        pv[:, w] = cw[w, ch]
    pv[:, 4] = np.asarray(inp["conv_b"][l])[ch]
    pv[:, 5] = np.asarray(inp["rg_ba"][l]).reshape(-1)[ch]
    pv[:, 6] = np.asarray(inp["rg_bx"][l]).reshape(-1)[ch]
    pv[:, 7] = np.asarray(inp["rg_lambda"][l])[ch]
    pv[:, 8] = np.asarray(inp["ret_norm_g"][l])[ch]
    pv[:, 9] = np.asarray(inp["hgrn_norm_g"][l])[ch]
    pv[:, 10] = np.asarray(inp["hgrn_lb_logits"][0])[ch]
    pv[:, 11] = np.asarray(inp["hgrn_lb_logits"][l])[ch]
    for hl in range(2):
        gam = 1.0 - 2.0 ** (-5.0 - (2 * hh + hl))
        pv[hl * 64:(hl + 1) * 64, 12] = gam ** 64
    out["pvec"] = pv
    wab = np.zeros((128, 2, 128), np.float32)
    for hl in range(2):
        s = slice(hl * 64, (hl + 1) * 64)
        wab[s, 0, s] = np.asarray(inp["rg_wa"][l])[2 * hh + hl]
        wab[s, 1, s] = np.asarray(inp["rg_wx"][l])[2 * hh + hl]
    out["wab"] = wab
    return out


def const_tables(hh, SEQ):
    K = {}
    K["ident"] = np.eye(128).astype(BF)
    j = np.arange(128)
    K["negtri"] = (-(j[:, None] >= j[None, :]).astype(np.float32)).astype(BF)
    K["negcompl"] = (-(j[:, None] < j[None, :]).astype(np.float32)).astype(BF)
    K["ones_row"] = np.ones((1, 128), np.float32).astype(BF)
    K["one"] = np.ones((128, 1), np.float32)
    K["eps_n"] = np.full((128, 1), 1e-6, np.float32)
    q = np.arange(512)
    cm = np.zeros((128, 4, 512), np.float32)
    for jb in range(4):
        cm[:, jb, :] = ((jb * 128 + j)[:, None] < q[None, :])
    K["cmask"] = cm.astype(BF)
    s = np.arange(128)
    gm = ((s[:, None] // 64 == s[None, :] // 64) & (s[:, None] <= s[None, :])).astype(np.float32)
    K["glamask"] = np.tile(gm, (1, 4)).astype(BF)
    K["blockmean"] = ((s[:, None] // 64 == s[None, :] // 64) / 64.0).astype(np.float32).astype(BF)
    rm = np.ones((128, 512), np.float32)
    rm[:, ::64] = 0.0
    K["resetmask"] = rm
    d = np.arange(64)
    inv_freq = (10000.0 ** (-np.arange(0, 64, 2, dtype=np.float32) / 64)).astype(np.float32)
    t = np.arange(SEQ, dtype=np.float32)
    ang = (t[:, None] * inv_freq[None, :]).astype(np.float32)
    cos, sin = np.cos(ang).T, np.sin(ang).T
    tl = (np.arange(SEQ) % 64 + 1).astype(np.float64)
    tabs = np.zeros((4, 128, SEQ), np.float32)
    for hl in range(2):
        gam = 1.0 - 2.0 ** (-5.0 - (2 * hh + hl))
        lg = np.log1p(-2.0 ** (-5.0 - (2 * hh + hl)))
        dq = np.exp(lg * tl)
        dk = np.exp(-lg * tl) / 8.0
        for dd in range(64):
            p = hl * 64 + dd
            c_, s_ = cos[dd % 32], sin[dd % 32]
            sg = -1.0 if dd < 32 else 1.0
            tabs[0, p] = c_ * dq
            tabs[1, p] = sg * s_ * dq
            tabs[2, p] = c_ * dk
            tabs[3, p] = sg * s_ * dk
    K["rettab"] = tabs
    return K


from concourse.bass_utils import run_bass_kernel_spmd

SEQ_FULL = 4096
T_OWN = 2048
SMALLK = ["ident", "negtri", "negcompl", "ones_row", "one", "eps_n", "cmask", "glamask", "blockmean", "resetmask"]


def _mk_A(nc):
    A = {}
    ar = Arena(nc)
    A["arena"] = ar
    A["alloc"] = ar.alloc
    A["psum"] = [nc.alloc_psum_tensor(f"ps{i}", [128, 512], F32).ap() for i in range(8)]
    A["rpsum"] = [Res(excl=True) for _ in range(8)]
    A["bar_scratch"] = ar.alloc("bar", [128, 1], F32)
    return A


def _np_dt(a):
    return BF16 if a.dtype == BF else F32


def build_pre():
    nc = bass.Bass("TRN2", target_bir_lowering=False)
    S = Sched(nc)
    dt = lambda n, s, d, k="ExternalInput": nc.dram_tensor(n, s, d, kind=k).ap()
    x_d = dt("x", [T_OWN, 1024], F32)
    g_d = dt("ln_g", [1024], F32)
    b_d = dt("ln_b", [1024], F32)
    h32 = dt("h32", [T_OWN, 1024], F32, "ExternalOutput")
    h16 = dt("h16", [T_OWN, 1024], BF16, "ExternalOutput")
    A = _mk_A(nc)
    al = A["alloc"]
    eps = al("eps", [128, 1], F32)
    reps = Res()
    S.dve(lambda e: e.memset(eps, LN_EPS), writes=[reps])
    A["eps_ln"] = eps
    lnp = al("lnp", [128, 2, 1024], F32); rln = Res()
    if True:
        dmy = al("dmy", [128, 128], BF16); rd = Res()
        S.dve(lambda e: e.memset(dmy, 0.0), writes=[rd])
        S.pe(lambda e: e.matmul(A["psum"][0][:, 0:128], lhsT=dmy, rhs=dmy, start=True, stop=True), reads=[rd], writes=[A["rpsum"][0]])
    S.dma(lnp[:, 0, :], g_d.partition_broadcast(128), writes=[rln])
    S.dma(lnp[:, 1, :], b_d.partition_broadcast(128), writes=[rln])
    NT = T_OWN // 128
    xt = [al(f"xt{i}", [128, 1024], F32) for i in range(2)]; rxt = [Res(), Res()]
    tmp = [al(f"tmp{i}", [128, 1024], F32) for i in range(2)]; rtmp = [Res(), Res()]
    hb = [al(f"hb{i}", [128, 1024], BF16) for i in range(2)]; rhb = [Res(), Res()]
    st = [al(f"st{i}", [128, 16], F32) for i in range(2)]; rst = [Res(), Res()]
    for t in range(NT):
        p = t % 2
        S.dma(xt[p], x_d[t * 128:(t + 1) * 128, :], writes=[rxt[p]])
        ln_tile(S, nc, A, xt[p], rxt[p], lnp[:, 0, :], lnp[:, 1, :], rln, xt[p], rxt[p], tmp[p], rtmp[p], st[p], rst[p], t)
        S.dma(h32[t * 128:(t + 1) * 128, :], xt[p], reads=[rxt[p]], is_output=True)
        S.act(lambda e, p=p: e.activation(out=hb[p], in_=xt[p], func=AF.Copy), reads=[rxt[p]], writes=[rhb[p]])
        S.dma(h16[t * 128:(t + 1) * 128, :], hb[p], reads=[rhb[p]], is_output=True)
    build_and_emit(nc, S)
    return nc


def build_M(layer, Kh):
    nc = bass.Bass("TRN2", target_bir_lowering=False)
    S = Sched(nc)
    dt = lambda n, s, d, k="ExternalInput": nc.dram_tensor(n, s, d, kind=k).ap()
    SEQ = SEQ_FULL
    hin = dt("hin", [SEQ, 1024], BF16)
    win = dt("win", [1024, 1920], F32)
    pvec = dt("pvec", [128, 13], F32)
    wab = dt("wab", [128, 2, 128], F32)
    consts = {}
    for n in SMALLK:
        a = Kh[n]
        consts[n] = (dt("k_" + n, list(a.shape), _np_dt(a)), list(a.shape), _np_dt(a))
    rettab = dt("rettab", [4, 128, SEQ], F32)
    yT = dt("yT", [512, SEQ], BF16, "ExternalOutput")
    A = _mk_A(nc)
    phase_M(nc, S, A, SEQ, layer, hin, win, pvec, wab, consts, rettab, yT)
    build_and_emit(nc, S)
    return nc


def build_F():
    nc = bass.Bass("TRN2", target_bir_lowering=False)
    S = Sched(nc)
    dt = lambda n, s, d, k="ExternalInput": nc.dram_tensor(n, s, d, kind=k).ap()
    T = T_OWN
    yT_d = dt("yT", [1024, T], BF16)
    h_d = dt("h", [T, 1024], F32)
    wout = dt("w_out", [1024, 1024], F32)
    wup = dt("w_up", [1024, 4096], F32)
    wdn = dt("w_down", [4096, 1024], F32)
    l1g, l1b, l2g, l2b = [dt(n, [1024], F32) for n in ("l1g", "l1b", "l2g", "l2b")]
    ident_d = dt("ident", [128, 128], BF16)
    out = dt("h32", [T, 1024], F32, "ExternalOutput")
    outb = dt("h16", [T, 1024], BF16, "ExternalOutput")
    A = _mk_A(nc)
    al = A["alloc"]
    A["ident_bf"] = al("ident", [128, 128], BF16)
    rid = Res()
    S.dma(A["ident_bf"], ident_d, writes=[rid])
    eps = al("eps", [128, 1], F32)
    S.dve(lambda e: e.memset(eps, LN_EPS), writes=[rid])
    A["eps_ln"] = eps
    NT = T // 128
    hres = al("hres", [128, NT, 1024], F32)
    rh = [Res() for _ in range(NT)]
    for t in range(NT):
        S.dma(hres[:, t, :], h_d[t * 128:(t + 1) * 128, :], writes=[rh[t]])
    phase_F(nc, S, A, T, yT_d, wout, l1g, l1b, wup, wdn, l2g, l2b, hres, rh, out_f32_d=out, out_bf16_d=outb)
    build_and_emit(nc, S)
    return nc


def wout_perm():
    idx = []
    for hh in range(2):
        for m in range(4):
            idx.append(m * 256 + hh * 128 + np.arange(128))
    return np.concatenate(idx)


def kernel_unfused(**inputs):
    inp = {k: np.asarray(v) for k, v in inputs.items()}
    x = inp["x"]
    NC = 8
    cores = list(range(NC))
    f32 = np.float32
    Kh = [const_tables(hh, SEQ_FULL) for hh in range(2)]
    nc_pre = build_pre()
    im = []
    for c in cores:
        b, hh = c // 2, c % 2
        im.append({"x": np.ascontiguousarray(x[b, hh * T_OWN:(hh + 1) * T_OWN, :]), "ln_g": inp["ln_in_g"], "ln_b": inp["ln_in_b"]})
    res = run_bass_kernel_spmd(nc_pre, im, core_ids=cores).results
    h32 = [r["h32"] for r in res]
    h16 = [r["h16"] for r in res]
    nc_F = build_F()
    perm = wout_perm()
    for l in range(2):
        nc_M = build_M(l, Kh[0])
        im = []
        for c in cores:
            b, hh = c // 2, c % 2
            pc = prep_layer_core(inp, l, hh)
            d = {"hin": np.concatenate([h16[2 * b], h16[2 * b + 1]], axis=0), "win": pc["win"], "pvec": pc["pvec"], "wab": pc["wab"],
                 "rettab": Kh[hh]["rettab"]}
            for n in SMALLK:
                d["k_" + n] = Kh[hh][n]
            im.append(d)
        res = run_bass_kernel_spmd(nc_M, im, core_ids=cores).results
        yT = [r["yT"] for r in res]
        im = []
        wo = np.ascontiguousarray(inp["w_out"][l][perm, :])
        for c in cores:
            b, hh = c // 2, c % 2
            ya = np.concatenate([yT[2 * b], yT[2 * b + 1]], axis=0)[:, hh * T_OWN:(hh + 1) * T_OWN]
            im.append({"yT": np.ascontiguousarray(ya), "h": h32[c], "w_out": wo, "w_up": inp["w_up"][l], "w_down": inp["w_down"][l],
                       "l1g": inp["ln1_g"][l], "l1b": inp["ln1_b"][l], "l2g": inp["ln2_g"][l], "l2b": inp["ln2_b"][l],
                       "ident": Kh[0]["ident"]})
        res = run_bass_kernel_spmd(nc_F, im, core_ids=cores).results
        h32 = [r["h32"] for r in res]
        h16 = [r["h16"] for r in res]
    out = np.zeros((4, SEQ_FULL, 1024), f32)
    for c in cores:
        b, hh = c // 2, c % 2
        out[b, hh * T_OWN:(hh + 1) * T_OWN, :] = h32[c]
    return out


GROUPS = [[0, 1], [2, 3], [4, 5], [6, 7]]
U32 = mybir.dt.uint32


def build_fused(Kh):
    nc = bass.Bass("TRN2", target_bir_lowering=False)
    S = Sched(nc)
    dt = lambda n, s, d, k="ExternalInput": nc.dram_tensor(n, s, d, kind=k).ap()
    SEQ, T = SEQ_FULL, T_OWN
    NT = T // 128
    x_d = dt("x", [T, 1024], F32)
    g_d = dt("ln_g", [1024], F32)
    b_d = dt("ln_b", [1024], F32)
    gidx_d = dt("gidx", [128, 8], U32)
    L = []
    for l in range(2):
        L.append(dict(
            win=dt(f"win{l}", [1024, 1920], F32), pvec=dt(f"pvec{l}", [128, 13], F32), wab=dt(f"wab{l}", [128, 2, 128], F32),
            wout=dt(f"w_out{l}", [1024, 1024], F32), wup=dt(f"w_up{l}", [1024, 4096], F32), wdn=dt(f"w_down{l}", [4096, 1024], F32),
            l1g=dt(f"l1g{l}", [1024], F32), l1b=dt(f"l1b{l}", [1024], F32), l2g=dt(f"l2g{l}", [1024], F32), l2b=dt(f"l2b{l}", [1024], F32)))
    kd = {}
    for n in SMALLK:
        a = Kh[n]
        kd[n] = (dt("k_" + n, list(a.shape), _np_dt(a)), list(a.shape), _np_dt(a))
    rettab = dt("rettab", [4, 128, SEQ], F32)
    out_d = dt("out", [T, 1024], F32, "ExternalOutput")
    H = T // 2
    hx_loc = [nc.dram_tensor(f"hx_loc{i}", [H, 1024], BF16).ap() for i in range(2)]
    hx_all = [nc.dram_tensor(f"hx_all{i}", [2 * H, 1024], BF16).ap() for i in range(2)]
    y_loc = [nc.dram_tensor(f"y_loc{i}", [256, SEQ], BF16).ap() for i in range(2)]
    y_all = [nc.dram_tensor(f"y_all{i}", [512, SEQ], BF16).ap() for i in range(2)]
    hsp = nc.dram_tensor("hspill", [T, 1024], F32).ap()

    def hx_dst(t):
        i, r = divmod(t * 128, H)
        return hx_loc[i][r:r + 128, :]

    hx_rr = [Res(), Res()]

    def gather_half(i, rhxl):
        S.cc("AllGather", [hx_loc[i]], [hx_all[i]], GROUPS, reads=rhxl[i * (NT // 2):(i + 1) * (NT // 2)], writes=[hx_rr[i]])

    def gather_hx(rhxl, done=()):
        rr = hx_rr
        for i in range(2):
            if i not in done:
                gather_half(i, rhxl)
        return [(hx_all[0][0:H, :], 0, H, [rr[0]]), (hx_all[1][0:H, :], H, H, [rr[1]]),
                (hx_all[0][H:2 * H, :], 2 * H, H, [rr[0]]), (hx_all[1][H:2 * H, :], 3 * H, H, [rr[1]])]
    A = _mk_A(nc)
    ar = A["arena"]
    al = A["alloc"]
    S.cc_scratch = al("ccs", [128, 1], F32)
    eps = al("eps", [128, 1], F32)
    rconst = Res()
    S.dve(lambda e: e.memset(eps, LN_EPS), writes=[rconst])
    A["eps_ln"] = eps
    A["ident_bf"] = al("identF", [128, 128], BF16)
    S.dma(A["ident_bf"], kd["ident"][0], writes=[rconst])
    gidx = al("gidx", [128, 8], U32); rgidx = Res()
    S.dma(gidx, gidx_d, writes=[rgidx])
    base = ar.mark()
    rhsp = [Res() for _ in range(NT)]
    rhxl = [Res() for _ in range(NT)]
    lnp = al("lnp", [128, 2, 1024], F32); rln = Res()
    S.dma(lnp[:, 0, :], g_d.partition_broadcast(128), writes=[rln])
    S.dma(lnp[:, 1, :], b_d.partition_broadcast(128), writes=[rln])
    xt = [al(f"xt{i}", [128, 1024], F32) for i in range(NT)]; rxt = [Res() for _ in range(NT)]
    tmp = [al(f"tmp{i}", [128, 1024], F32) for i in range(4)]; rtmp = [Res() for _ in range(4)]
    hb = [al(f"hb{i}", [128, 1024], BF16) for i in range(8)]; rhb = [Res() for _ in range(8)]
    st = [al(f"st{i}", [128, 16], F32) for i in range(8)]; rst = [Res() for _ in range(8)]
    for t in range(NT):
        S.dma(xt[t], x_d[t * 128:(t + 1) * 128, :], writes=[rxt[t]])
    for t in range(NT):
        p = t % 8
        ln_tile(S, nc, A, xt[t], rxt[t], lnp[:, 0, :], lnp[:, 1, :], rln, xt[t], rxt[t], None, None, st[p], rst[p], t)
        S.act(lambda e, t=t: e.activation(out=hb[t % 8], in_=xt[t], func=AF.Copy), reads=[rxt[t]], writes=[rhb[t % 8]])
        S.dma(hsp[t * 128:(t + 1) * 128, :], xt[t], reads=[rxt[t]], writes=[rhsp[t]])
        S.dma(hx_dst(t), hb[t % 8], reads=[rhb[t % 8]], writes=[rhxl[t]])
        if t == NT // 2 - 1:
            gather_half(0, rhxl)
    hin_pieces = gather_hx(rhxl, done=(0,))
    import os
    NL = int(os.environ.get("FUSE_LAYERS", "2"))
    PH = os.environ.get("FUSE_PHASES", "MF")
    for l in range(NL):
        W = L[l]
        S.barrier(A["bar_scratch"])
        ar.reset(base)
        yslots = [y_loc[0][0:128, :], y_loc[0][128:256, :], y_loc[1][0:128, :], y_loc[1][128:256, :]]
        c = phase_M(nc, S, A, SEQ, l, hin_pieces, W["win"], W["pvec"], W["wab"], kd, rettab, yslots)
        ar.reset(base)
        hres = al("hres", [128, NT, 1024], F32)
        rh = [Res() for _ in range(NT)]
        wo_pre = al("wo_pre", [128, 8, 1024], BF16); rwo_pre = Res()
        S.dma(wo_pre, W["wout"].rearrange("(k p) n -> p k n", p=128), writes=[rwo_pre], eng="gpsimd", extra_deps=[c.last_proj])
        for t in range(NT):
            S.dma(hres[:, t, :], hsp[t * 128:(t + 1) * 128, :], reads=[rhsp[t]], writes=[rh[t]], eng="gpsimd", extra_deps=[c.last_proj])
        ry_all = [Res(), Res()]
        S.cc("AllGather", [y_loc[0]], [y_all[0]], GROUPS, reads=c.ry_by_mixer["A"] + c.ry_by_mixer["B"], writes=[ry_all[0]])
        S.cc("AllGather", [y_loc[1]], [y_all[1]], GROUPS, reads=c.ry_by_mixer["C"] + c.ry_by_mixer["D"], writes=[ry_all[1]])
        if PH == "M":
            continue
        S.barrier(A["bar_scratch"])
        last = (l == NL - 1)
        src = [ya.rearrange("c (h t) -> (c h) t", h=2) for ya in y_all]
        phase_F(nc, S, A, T, None, W["wout"], W["l1g"], W["l1b"], W["wup"], W["wdn"], W["l2g"], W["l2b"], hres, rh,
                out_f32_d=(out_d if last else hsp), out_bf16_d=(None if last else hx_dst),
                y_gather=(src, gidx, rgidx, ry_all), rout32=(None if last else rhsp), rout16=(None if last else rhxl),
                final_out=last, wo_pre=(wo_pre, rwo_pre),
                on_tile_done=(None if last else (lambda t: gather_half(0, rhxl) if t == NT // 2 - 1 else None)))
        if not last:
            hin_pieces = gather_hx(rhxl, done=(0,))
    if NL == 0 or PH == "M":
        S.barrier(A["bar_scratch"])
        ar.reset(base)
        tt = al("tt", [128, 1024], F32); rtt = Res()
        S.dma(tt, hsp[0:128, :], reads=rhsp, writes=[rtt])
        S.dma(out_d[0:128, :], tt, reads=[rtt], is_output=True)
    build_and_emit(nc, S)
    print("fused instr counts", {e: len(S.streams[e]) for e in ENGS}, "arena peak", ar.peak)
    return nc


def wout_perm_fused():
    idx = []
    for grp in ((0, 1), (2, 3)):
        for hh in range(2):
            for m in grp:
                idx.append(m * 256 + hh * 128 + np.arange(128))
    return np.concatenate(idx)


def kernel(**inputs):
    inp = {k: np.asarray(v) for k, v in inputs.items()}
    x = inp["x"]
    cores = list(range(8))
    Kh = [const_tables(hh, SEQ_FULL) for hh in range(2)]
    nc = build_fused(Kh[0])
    perm = wout_perm_fused()
    wo = [np.ascontiguousarray(inp["w_out"][l][perm, :]) for l in range(2)]
    im = []
    for c in cores:
        b, hh = c // 2, c % 2
        d = {"x": np.ascontiguousarray(x[b, hh * T_OWN:(hh + 1) * T_OWN, :]), "ln_g": inp["ln_in_g"], "ln_b": inp["ln_in_b"],
             "rettab": Kh[hh]["rettab"]}
        gi = np.zeros((128, 8), np.uint32)
        for kc in range(8):
            gi[:, kc] = ((kc % 4) * 128 + np.arange(128)) * 2 + hh
        d["gidx"] = gi
        for n in SMALLK:
            d["k_" + n] = Kh[hh][n]
        for l in range(2):
            pc = prep_layer_core(inp, l, hh)
            d[f"win{l}"] = pc["win"]; d[f"pvec{l}"] = pc["pvec"]; d[f"wab{l}"] = pc["wab"]
            d[f"w_out{l}"] = wo[l]; d[f"w_up{l}"] = inp["w_up"][l]; d[f"w_down{l}"] = inp["w_down"][l]
            d[f"l1g{l}"] = inp["ln1_g"][l]; d[f"l1b{l}"] = inp["ln1_b"][l]; d[f"l2g{l}"] = inp["ln2_g"][l]; d[f"l2b{l}"] = inp["ln2_b"][l]
        im.append(d)
    import os
    rr = run_bass_kernel_spmd(nc, im, core_ids=cores, trace=bool(os.environ.get("KERNEL_TRACE")))
    if os.environ.get("KERNEL_TRACE"):
        print("exec_time_ns", rr.exec_time_ns)
    res = rr.results
    out = np.zeros((4, SEQ_FULL, 1024), np.float32)
    for c in cores:
        b, hh = c // 2, c % 2
        out[b, hh * T_OWN:(hh + 1) * T_OWN, :] = res[c]["out"]
    return out
```

```python
import numpy as np
import ml_dtypes
import concourse.bass as bass
import concourse.mybir as mybir


F32 = mybir.dt.float32
BF16 = mybir.dt.bfloat16
AF = mybir.ActivationFunctionType
ALU = mybir.AluOpType
AX = mybir.AxisListType

ENGS = ("tensor", "vector", "scalar", "gpsimd", "sync")
EPOCH = 3000


class Res:
    __slots__ = ("name", "w", "r", "excl")

    def __init__(self, name="", excl=False):
        self.name = name
        self.w = None
        self.r = []
        self.excl = excl


class Instr:
    __slots__ = ("eng", "idx", "fn", "deps", "vc", "signal", "is_dma", "sem", "val", "pre", "order", "inc")

    def __init__(self, eng, idx, fn, is_dma=False):
        self.eng = eng
        self.idx = idx
        self.fn = fn
        self.deps = []
        self.vc = {}
        self.signal = False
        self.is_dma = is_dma
        self.sem = None
        self.val = None
        self.pre = None
        self.inc = 16


class Sched:
    def __init__(self, nc, n_dma_sems=32, same_engine_sync=True):
        self.nc = nc
        self.streams = {e: [] for e in ENGS}
        self.known = {e: {} for e in ENGS}
        self.known_dma = {e: set() for e in ENGS}
        self.n_dma_sems = n_dma_sems
        self.dma_count = 0
        self.dma_cnt_by_eng = {}
        self.dma_last = {}
        self.same_engine_sync = same_engine_sync
        self.out_dmas = []
        self.pending = {}
        self.dmas_since_barrier = []

    def add(self, eng, fn, reads=(), writes=(), is_dma=False, extra_deps=(), own_sem=False):
        st = self.streams[eng]
        ins = Instr(eng, len(st), fn, is_dma)
        self.order = getattr(self, 'order', 0) + 1
        ins.order = self.order
        if any(r.excl for r in reads):
            writes = list(writes) + [r for r in reads if r.excl and r not in writes]
            reads = [r for r in reads if not r.excl]
        deps = list(extra_deps) + self.pending.pop(eng, [])
        for r in reads:
            if r.w is not None:
                deps.append(r.w)
        for w in writes:
            if w.w is not None:
                deps.append(w.w)
            deps.extend(w.r)
        known = self.known[eng]
        kd = self.known_dma[eng]
        need = {}
        vc = {}
        for d in deps:
            if d is ins:
                continue
            if d.is_dma:
                if id(d) in kd:
                    continue
                need[("dma", id(d))] = d
            else:
                if d.eng == eng and (eng == "tensor" or not self.same_engine_sync):
                    continue
                if known.get(d.eng, -1) >= d.idx:
                    continue
                k = ("e", d.eng)
                if k not in need or need[k].idx < d.idx:
                    need[k] = d
        for k, d in need.items():
            d.signal = True
            ins.deps.append(d)
            if d.is_dma:
                kd.add(id(d))
            for e2, i2 in d.vc.items():
                if known.get(e2, -1) < i2:
                    known[e2] = i2
            if not d.is_dma:
                if known.get(d.eng, -1) < d.idx:
                    known[d.eng] = d.idx
        ins.vc = dict(known)
        if is_dma and own_sem:
            ins.inc = 1
            ins.sem = "own"
        elif is_dma:
            self.dmas_since_barrier.append(ins)
            half = self.n_dma_sems // 2
            cnt = self.dma_cnt_by_eng.get(eng, 0)
            self.dma_cnt_by_eng[eng] = cnt + 1
            slot = (cnt % half) + (half if eng == "gpsimd" else 0)
            self.dma_count += 1
            prev = self.dma_last.get(slot)
            ins.pre = prev
            self.dma_last[slot] = ins
            ins.sem = slot
        for r in reads:
            r.r.append(ins)
        for w in writes:
            w.w = ins
            w.r = []
        st.append(ins)
        return ins

    def barrier(self, scratch_ap):
        self.flush_cc()
        deps = []
        for e in ("tensor", "scalar", "gpsimd", "vector"):
            st = [i for i in self.streams[e] if not i.is_dma]
            if st:
                deps.append(st[-1])
        deps.extend(d for d in self.dmas_since_barrier if d.inc != 1)
        self.dmas_since_barrier = []
        if not hasattr(self, "bar_res"):
            self.bar_res = Res()
        b = self.add("vector", lambda e: e.memset(scratch_ap, 0.0), writes=[self.bar_res], extra_deps=deps)
        for e in ("scalar", "gpsimd", "sync", "tensor"):
            self.pending[e] = [b]
        return b

    def pe(self, fn, reads=(), writes=()):
        return self.add("tensor", fn, reads, writes)

    def dve(self, fn, reads=(), writes=()):
        return self.add("vector", fn, reads, writes)

    def act(self, fn, reads=(), writes=()):
        return self.add("scalar", fn, reads, writes)

    def pool(self, fn, reads=(), writes=()):
        return self.add("gpsimd", fn, reads, writes)

    def cc(self, kind, ins_, outs, groups, reads=(), writes=()):
        tmp = Res()
        i = self.add("gpsimd", lambda e: e.collective_compute(kind, mybir.AluOpType.bypass, replica_groups=groups, ins=ins_, outs=outs),
                     reads, [tmp], is_dma=True, own_sem=True)
        self.pending_cc = getattr(self, "pending_cc", [])
        self.pending_cc.append((tmp, list(writes)))
        return i

    def flush_cc(self):
        sc = getattr(self, "cc_scratch", None)
        for tmp, writes in getattr(self, "pending_cc", []):
            if not hasattr(self, "cc_res"):
                self.cc_res = Res()
            self.add("gpsimd", lambda e: e.memset(sc, 0.0), reads=[tmp], writes=list(writes) + [self.cc_res])
        self.pending_cc = []

    def dma(self, out, in_, reads=(), writes=(), is_output=False, eng="sync", extra_deps=(), **kw):
        ins = self.add(eng, lambda e: e.dma_start(out=out, in_=in_, **kw), reads, writes, is_dma=True, extra_deps=extra_deps)
        if is_output:
            self.out_dmas.append(ins)
        return ins


def build_and_emit(nc, sched):
    for e in ENGS:
        cnt = 0
        sems = []
        for ins in sched.streams[e]:
            if ins.is_dma or not ins.signal:
                continue
            ep = cnt // EPOCH
            if ep >= len(sems):
                sems.append(nc.alloc_semaphore(f"s_{e}_{ep}"))
            cnt += 1
            ins.sem = sems[ep]
            ins.val = cnt - ep * EPOCH
    n = sched.n_dma_sems
    dma_sems = [nc.alloc_semaphore(f"s_dma_{i}") for i in range(n)]
    dma_vals = [0] * n
    dmas = []
    for e in ENGS:
        for ins in sched.streams[e]:
            if ins.is_dma:
                dmas.append(ins)
    dmas.sort(key=lambda i: i.order)
    for ins in dmas:
        if ins.sem == "own":
            ins.sem = nc.alloc_semaphore(f"s_cc_{ins.order}")
            ins.val = 1
            continue
        slot = ins.sem
        dma_vals[slot] += ins.inc
        ins.sem = dma_sems[slot]
        ins.val = dma_vals[slot]

    final_waits = list(sched.out_dmas)

    def run_stream(ename, eng):
        for ins in sched.streams[ename]:
            for d in ins.deps:
                eng.wait_ge(d.sem, d.val)
            if ins.is_dma and ins.pre is not None:
                eng.wait_ge(ins.pre.sem, ins.pre.val)
            bi = ins.fn(eng)
            if ins.is_dma:
                bi.then_inc(ins.sem, ins.inc)
            elif ins.signal:
                bi.then_inc(ins.sem, 1)
        if ename == "sync":
            for d in final_waits:
                eng.wait_ge(d.sem, d.val)

    with nc.Block() as block:
        @block.tensor
        def _(eng):
            run_stream("tensor", eng)

        @block.vector
        def _(eng):
            run_stream("vector", eng)

        @block.scalar
        def _(eng):
            run_stream("scalar", eng)

        @block.gpsimd
        def _(eng):
            run_stream("gpsimd", eng)

        @block.sync
        def _(eng):
            run_stream("sync", eng)


class Arena:
    def __init__(self, nc, base=16384, limit=229312):
        self.nc, self.off, self.limit = nc, base, limit
        self.n = 0
        self.peak = base

    def alloc(self, name, shape, dtype):
        nbytes = int(np.prod(shape[1:])) * mybir.dt.size(dtype)
        off = (self.off + 63) // 64 * 64
        assert off + nbytes <= self.limit, f"arena overflow allocating {name} {shape}: {off}+{nbytes} > {self.limit}"
        self.off = off + nbytes
        self.peak = max(self.peak, self.off)
        self.n += 1
        return self.nc.alloc_sbuf_tensor_at(f"ar{self.n}_{name}", shape, dtype, offset=off).ap()

    def mark(self):
        return self.off

    def reset(self, m):
        self.off = m


D = 1024
DFF = 4096
ALPHA = 4 ** 0.25
LN_EPS = 1e-5


def ln_tile(S, nc, A, z_ap, rz, g_bc, b_bc, rgb, out_ap, rout, tmp, rtmp, st, rst, tagid):
    if tmp is None:
        tmp, rtmp = z_ap, rz
    S.dve(lambda e: e.bn_stats(out=st[:, 0:6], in_=z_ap[:, 0:512]), reads=[rz], writes=[rst])
    S.dve(lambda e: e.bn_stats(out=st[:, 6:12], in_=z_ap[:, 512:1024]), reads=[rz], writes=[rst])
    S.dve(lambda e: e.bn_aggr(out=st[:, 12:14], in_=st[:, 0:12]), reads=[rst], writes=[rst])
    S.act(lambda e: e.activation(out=st[:, 14:15], in_=st[:, 13:14], func=AF.Sqrt, bias=A["eps_ln"], scale=1.0),
          reads=[rst], writes=[rst])
    S.dve(lambda e: e.reciprocal(out=st[:, 14:15], in_=st[:, 14:15]), reads=[rst], writes=[rst])
    S.dve(lambda e: e.tensor_scalar(out=st[:, 15:16], in0=st[:, 12:13], scalar1=st[:, 14:15], scalar2=-1.0,
                                    op0=ALU.mult, op1=ALU.mult), reads=[rst], writes=[rst])
    S.act(lambda e: e.activation(out=tmp, in_=z_ap, func=AF.Identity, bias=st[:, 15:16], scale=st[:, 14:15]),
          reads=[rz, rst], writes=[rtmp])
    S.dve(lambda e: e.tensor_tensor(out=tmp, in0=tmp, in1=g_bc, op=ALU.mult), reads=[rtmp, rgb], writes=[rtmp])
    S.pool(lambda e: e.tensor_tensor(out=out_ap, in0=tmp, in1=b_bc, op=ALU.add), reads=[rtmp, rgb], writes=[rout])


def phase_F(nc, S, A, T, yT_dram, w_out_d, ln1g_d, ln1b_d, w_up_d, w_down_d, ln2g_d, ln2b_d,
            hres, rh, out_f32_d=None, out_bf16_d=None, y_gather=None, rout32=None, rout16=None, final_out=True, on_tile_done=None, wo_pre=None):
    NT = T // 128
    NB = T // 512
    al = A["alloc"]
    ident = A["ident_bf"]
    yT = al("yT", [128, 8, T], BF16)
    ry = [Res() for _ in range(NT)]
    lnp = al("lnp", [128, 2, 1024], F32)
    rln = Res()
    if y_gather is None:
        S.dma(yT, yT_dram.rearrange("(k p) t -> p k t", p=128), writes=ry)
    else:
        src, idx, ridx, rsrc = y_gather
        for kc in range(8):
            S.add("gpsimd", lambda e, kc=kc: e.indirect_dma_start(out=yT[:, kc, :], out_offset=None, in_=src[kc // 4],
                  in_offset=bass.IndirectOffsetOnAxis(ap=idx[:, kc:kc + 1], axis=0)), reads=[rsrc[kc // 4], ridx], writes=ry, is_dma=True)
    for i, d in enumerate((ln1g_d, ln1b_d)):
        S.dma(lnp[:, i, :], d.partition_broadcast(128), writes=[rln])
    NS = 4
    wu = [al(f"wu{i}", [128, 8, 1024], BF16) for i in range(2)]
    wd = [al(f"wd{i}", [128, 8, 1024], BF16) for i in range(2)]
    rwu = [Res(), Res()]
    rwd = [Res(), Res()]

    def load_ffn(s):
        b = s % 2
        S.dma(wu[b], w_up_d[:, s * 1024:(s + 1) * 1024].rearrange("(k p) f -> p k f", p=128), writes=[rwu[b]], eng="gpsimd")
        S.dma(wd[b], w_down_d[s * 1024:(s + 1) * 1024, :].rearrange("(k p) n -> p k n", p=128), writes=[rwd[b]], eng="gpsimd")

    if wo_pre is None:
        wo = wd[1]
        rwo = rwd[1]
        S.dma(wo, w_out_d.rearrange("(k p) n -> p k n", p=128), writes=[rwo], eng="gpsimd")
        load_ffn(0)
    else:
        wo, rwo = wo_pre
        load_ffn(0)
        load_ffn(1)
    tmp = [None] * 4
    rtmp = [None] * 4
    st = [al(f"st{i}", [128, 16], F32) for i in range(4)]
    rst = [Res() for _ in range(4)]
    ps = A["psum"]
    rps = A["rpsum"]
    psT = ps[7].bitcast(BF16)
    h1T = yT

    hb3 = [al(f"hb3_{i}", [128, 1024], BF16) for i in range(3)]
    rhb3 = [Res() for _ in range(3)]
    hb = [hb3[0], hb3[1]]
    rhb = [rhb3[0], rhb3[1]]
    def ln1_step(t):
        if t < NT:
            p = t % 2
            for half in range(2):
                bank = 2 * p + half
                for kc in range(8):
                    S.pe(lambda e, kc=kc, half=half, bank=bank, t=t: e.matmul(
                        ps[bank], lhsT=yT[:, kc, t * 128:(t + 1) * 128], rhs=wo[:, kc, half * 512:(half + 1) * 512],
                        start=(kc == 0), stop=(kc == 7)), reads=[ry[t], rwo], writes=[rps[bank]])
                S.dve(lambda e, half=half, bank=bank, t=t: e.scalar_tensor_tensor(
                    out=hres[:, t, half * 512:(half + 1) * 512], in0=hres[:, t, half * 512:(half + 1) * 512],
                    scalar=ALPHA, in1=ps[bank], op0=ALU.mult, op1=ALU.add), reads=[rh[t], rps[bank]], writes=[rh[t]])
            ln_tile(S, nc, A, hres[:, t, :], rh[t], lnp[:, 0, :], lnp[:, 1, :], rln, hres[:, t, :], rh[t],
                    None, None, st[t % 4], rst[t % 4], t)
            S.act(lambda e, t=t: e.activation(out=hb3[t % 3], in_=hres[:, t, :], func=AF.Copy), reads=[rh[t]], writes=[rhb3[t % 3]])
        if t >= 2:
            u = t - 2
            for kc in range(8):
                S.pe(lambda e, kc=kc, u=u: e.transpose(out=psT[:, kc * 128:(kc + 1) * 128], in_=hb3[u % 3][:, kc * 128:(kc + 1) * 128],
                                                       identity=ident), reads=[rhb3[u % 3]], writes=[rps[7]])
            S.act(lambda e, u=u: e.activation(out=h1T[:, :, u * 128:(u + 1) * 128],
                                              in_=psT.rearrange("p (k c) -> p k c", k=8), func=AF.Copy),
                  reads=[rps[7]], writes=[ry[u]])


    n_steps = NT + 2
    prologue = min(n_steps, 6)
    for t in range(prologue):
        ln1_step(t)
    ln1_next = [prologue]

    def ln1_more(k):
        for _ in range(k):
            if ln1_next[0] < n_steps:
                ln1_step(ln1_next[0])
                ln1_next[0] += 1

    if NB < 4:
        ln1_more(n_steps)
    aT = [al("aT", [128, 8, 512], BF16)] * 2
    raT = [Res()] * 2
    rl = [al("rl0", [128, 512], BF16)] * 2
    rrl = [Res()] * 2
    cnt = 0
    deferred = []

    def ln2_emit():
        while deferred:
            t = deferred.pop(0)
            p = t % 2
            ln_tile(S, nc, A, hres[:, t, :], rh[t], lnp[:, 0, :], lnp[:, 1, :], rln, hres[:, t, :], rh[t],
                    None, None, st[t % 4], rst[t % 4], t)
            if out_f32_d is not None:
                S.dma(out_f32_d[t * 128:(t + 1) * 128, :], hres[:, t, :], reads=[rh[t]], writes=([rout32[t]] if rout32 else []), is_output=final_out)
            if out_bf16_d is not None:
                S.act(lambda e, t=t, p=p: e.activation(out=hb[p], in_=hres[:, t, :], func=AF.Copy),
                      reads=[rh[t]], writes=[rhb[p]])
                S.dma(out_bf16_d(t) if callable(out_bf16_d) else out_bf16_d[t * 128:(t + 1) * 128, :], hb[p], reads=[rhb[p]],
                      writes=([rout16[t]] if rout16 else []), is_output=final_out)
            if on_tile_done is not None:
                on_tile_done(t)

    for s in range(NS):
        b = s % 2
        for tb in range(NB):
            ab = cnt % 2
            cnt += 1
            for fc in range(8):
                bank = 4 + (fc % 2)
                for kc in range(8):
                    S.pe(lambda e, kc=kc, fc=fc, bank=bank, tb=tb, b=b: e.matmul(
                        ps[bank], lhsT=wu[b][:, kc, fc * 128:(fc + 1) * 128], rhs=h1T[:, kc, tb * 512:(tb + 1) * 512],
                        start=(kc == 0), stop=(kc == 7)),
                        reads=[rwu[b]] + ry[tb * 4:(tb + 1) * 4], writes=[rps[bank]])
                rr = fc % 2
                S.act(lambda e, bank=bank, rr=rr: e.activation(out=rl[rr], in_=ps[bank], func=AF.Relu),
                      reads=[rps[bank]], writes=[rrl[rr]])
                if fc % 2 == 0:
                    S.dve(lambda e, fc=fc, bank=bank, ab=ab, rr=rr: e.tensor_tensor(
                        out=aT[ab][:, fc, :], in0=ps[bank], in1=rl[rr], op=ALU.mult),
                        reads=[rps[bank], rrl[rr]], writes=[raT[ab]])
                else:
                    S.pool(lambda e, fc=fc, ab=ab, rr=rr: e.tensor_tensor(
                        out=aT[ab][:, fc, :], in0=rl[rr], in1=rl[rr], op=ALU.mult),
                        reads=[rrl[rr]], writes=[raT[ab]])
            ln2_emit()
            for tt in range(4):
                t = tb * 4 + tt
                if s == 0:
                    ln1_more(1)
                for half in range(2):
                    bank = (tt % 2) * 2 + half
                    for fc in range(8):
                        S.pe(lambda e, fc=fc, half=half, bank=bank, tt=tt, ab=ab, b=b: e.matmul(
                            ps[bank], lhsT=aT[ab][:, fc, tt * 128:(tt + 1) * 128], rhs=wd[b][:, fc, half * 512:(half + 1) * 512],
                            start=(fc == 0), stop=(fc == 7)), reads=[raT[ab], rwd[b]], writes=[rps[bank]])
                    sl = hres[:, t, half * 512:(half + 1) * 512]
                    if s == 0:
                        S.dve(lambda e, sl=sl, bank=bank: e.scalar_tensor_tensor(
                            out=sl, in0=sl, scalar=ALPHA, in1=ps[bank], op0=ALU.mult, op1=ALU.add),
                            reads=[rh[t], rps[bank]], writes=[rh[t]])
                    else:
                        S.dve(lambda e, sl=sl, bank=bank: e.tensor_tensor(out=sl, in0=sl, in1=ps[bank], op=ALU.add),
                              reads=[rh[t], rps[bank]], writes=[rh[t]])
                if s == NS - 1:
                    deferred.append(t)
        if s == 0:
            ln1_more(n_steps)
            if wo_pre is None:
                load_ffn(1)
            for i, d in enumerate((ln2g_d, ln2b_d)):
                S.dma(lnp[:, i, :], d.partition_broadcast(128), writes=[rln])
        if s + 2 < NS:
            load_ffn(s + 2)
    ln2_emit()


NORM_EPS = 1e-6
SL = {n: i for i, n in enumerate(
    ["A_x", "A_g", "B_q", "B_qsw", "B_k", "B_ksw", "B_g", "C_q", "C_k", "D_q", "D_f", "D_g", "B_v", "C_v", "D_v"])}
NSL = 15
PV = {n: i for i, n in enumerate(
    ["cw0", "cw1", "cw2", "cw3", "cb", "ba", "bx", "lam", "retg", "hgg", "lb0", "lbl", "gam64", "nba", "nbx", "c1", "c1x2", "oml", "lnoml", "tmp0", "tmp1"])}
NPV = 24


class Ctx:
    pass


def setup_M(nc, S, A, SEQ, layer, hin_d, win_d, pvec_d, wab_d, consts):
    c = Ctx()
    c.nc, c.S, c.A, c.SEQ, c.layer = nc, S, A, SEQ, layer
    al = A["alloc"]
    c.NB = SEQ // 512
    c.NT = SEQ // 128
    c.ps, c.rps = A["psum"], A["rpsum"]
    c.hT = al("hT", [128, 8, SEQ], BF16)
    c.win = al("win", [128, 8, NSL * 128], BF16)
    c.K = {}
    c.rK = Res()
    for name, (d, shape, dtype) in consts.items():
        t = al("k_" + name, shape, dtype)
        S.dma(t, d, writes=[c.rK])
        c.K[name] = t
    c.rwin = Res()
    S.dma(c.win, win_d.rearrange("(k p) n -> p k n", p=128), writes=[c.rwin], eng="gpsimd")
    c.pv = al("pv", [128, NPV], F32)
    c.rpv = Res()
    S.dma(c.pv[:, 0:13], pvec_d, writes=[c.rpv])
    c.wab = al("wab", [128, 2, 128], BF16)
    c.rwab = Res()
    S.dma(c.wab, wab_d, writes=[c.rwab], eng="gpsimd")
    pieces = hin_d if isinstance(hin_d, (list, tuple)) else [(hin_d, 0, SEQ, list(A.get("rhin", [])))]
    stage = [al(f"hstage{i}", [128, 1024], BF16) for i in range(3)]
    rstage = [Res() for _ in range(3)]
    c.rhT_t = [Res() for _ in range(c.NT)]
    c.rhT = lambda a, b: [c.rhT_t[t] for t in range(a // 128, (b + 127) // 128)]
    c.vtm = al("vtm", [128, c.NT, 384], BF16)
    c.rv = [Res() for _ in range(c.NT)]

    def vproj(t):
        bank = t % 2
        for kc in range(8):
            S.pe(lambda e, kc=kc, t=t, bank=bank: e.matmul(
                c.ps[bank][:, 0:384], lhsT=c.hT[:, kc, t * 128:(t + 1) * 128], rhs=c.win[:, kc, 12 * 128:15 * 128],
                start=(kc == 0), stop=(kc == 7)), reads=c.rhT(t * 128, (t + 1) * 128) + [c.rwin], writes=[c.rps[bank]])
        S.act(lambda e, t=t, bank=bank: e.activation(out=c.vtm[:, t, :], in_=c.ps[bank][:, 0:384], func=AF.Copy),
              reads=[c.rps[bank]], writes=[c.rv[t]])

    tiles = []
    for (src, t0, n, rsrc) in pieces:
        for j in range(n // 128):
            tiles.append((src[j * 128:(j + 1) * 128, :], t0 // 128 + j, rsrc))
    for i, (src_t, t, rsrc) in enumerate(tiles):
        sb = i % 3
        S.dma(stage[sb], src_t, reads=rsrc, writes=[rstage[sb]])
        bank = 6 + i % 2
        psT = c.ps[bank].bitcast(BF16)
        for kc in range(8):
            S.pe(lambda e, kc=kc, sb=sb, psT=psT: e.transpose(out=psT[:, kc * 128:(kc + 1) * 128], in_=stage[sb][:, kc * 128:(kc + 1) * 128],
                                                              identity=c.K["ident"]), reads=[rstage[sb], c.rK], writes=[c.rps[bank]])
        ev = S.act if i % 2 == 0 else S.dve
        if i % 2 == 0:
            S.act(lambda e, t=t, psT=psT: e.activation(out=c.hT[:, :, t * 128:(t + 1) * 128], in_=psT.rearrange("p (k c) -> p k c", k=8), func=AF.Copy),
                  reads=[c.rps[bank]], writes=[c.rhT_t[t]])
        else:
            S.dve(lambda e, t=t, psT=psT: e.tensor_copy(out=c.hT[:, :, t * 128:(t + 1) * 128], in_=psT.rearrange("p (k c) -> p k c", k=8)),
                  reads=[c.rps[bank]], writes=[c.rhT_t[t]])
        if i >= 2:
            vproj(tiles[i - 2][1])
    for (src_t, t, rsrc) in tiles[-2:]:
        vproj(t)
    c.pcount = 0
    c.ry_list = []
    c.ry_by_mixer = {}

    def new_yres():
        r = Res()
        c.ry_list.append(r)
        return r
    c.new_yres = new_yres
    return c


def proj_fm(c, slname, blk, nblk=1):
    S = c.S
    j = SL[slname]
    banks = getattr(c, "proj_banks", (0, 1))
    bank = banks[c.pcount % len(banks)]
    c.pcount += 1
    for kc in range(8):
        c.last_proj = S.pe(lambda e, kc=kc, j=j, blk=blk, bank=bank: e.matmul(
            c.ps[bank], lhsT=c.win[:, kc, j * 128:(j + 1) * 128], rhs=c.hT[:, kc, blk * 512:(blk + 1) * 512],
            start=(kc == 0), stop=(kc == 7)), reads=c.rhT(blk * 512, (blk + 1) * 512) + [c.rwin], writes=[c.rps[bank]])
    return bank


def small_params(c):
    S, pv, rpv = c.S, c.pv, c.rpv
    col = lambda n: pv[:, PV[n]:PV[n] + 1]
    S.act(lambda e: e.activation(out=col("tmp0"), in_=col("lam"), func=AF.Exp, scale=-1.0), reads=[rpv], writes=[rpv])
    S.act(lambda e: e.activation(out=col("tmp0"), in_=col("tmp0"), func=AF.Ln, bias=c.K["one"], scale=1.0), reads=[rpv, c.rK], writes=[rpv])
    S.dve(lambda e: e.tensor_scalar(out=col("c1"), in0=col("tmp0"), scalar1=-8.0, scalar2=None, op0=ALU.mult), reads=[rpv], writes=[rpv])
    S.dve(lambda e: e.tensor_scalar(out=col("c1x2"), in0=col("tmp0"), scalar1=-16.0, scalar2=None, op0=ALU.mult), reads=[rpv], writes=[rpv])
    if c.layer == 0:
        S.dve(lambda e: e.memset(col("oml"), 1.0), writes=[rpv])
        S.dve(lambda e: e.memset(col("lnoml"), 0.0), writes=[rpv])
    else:
        S.dve(lambda e: e.tensor_tensor(out=col("tmp1"), in0=col("lbl"), in1=col("lb0"), op=ALU.subtract), reads=[rpv], writes=[rpv])
        S.act(lambda e: e.activation(out=col("tmp1"), in_=col("tmp1"), func=AF.Exp), reads=[rpv], writes=[rpv])
        S.act(lambda e: e.activation(out=col("lnoml"), in_=col("tmp1"), func=AF.Ln, bias=c.K["one"], scale=1.0), reads=[rpv, c.rK], writes=[rpv])
        S.dve(lambda e: e.tensor_scalar(out=col("lnoml"), in0=col("lnoml"), scalar1=-1.0, scalar2=None, op0=ALU.mult), reads=[rpv], writes=[rpv])
        S.act(lambda e: e.activation(out=col("oml"), in_=col("lnoml"), func=AF.Exp), reads=[rpv], writes=[rpv])


def mixer_A(c, yT_d):
    S, A, SEQ = c.S, c.A, c.SEQ
    al = A["alloc"]
    pv, rpv = c.pv, c.rpv
    col = lambda n: pv[:, PV[n]:PV[n] + 1]
    NH = 2 if SEQ >= 1024 else 1
    L = SEQ // NH
    NBH = L // 512
    xT = al("A_xT", [128, 3 + SEQ], F32)
    rxT = Res()
    XC = al("A_XC", [128, L], F32); rXC = Res()
    R = al("A_R", [128, L], F32); rR = Res()
    TH = al("A_TH", [128, L], F32); rTH = Res()
    AA = al("A_AA", [128, L], F32); rAA = Res()
    XCB = al("A_XCB", [128, L], BF16); rXCB = Res()
    ii = [al(f"A_ii{i}", [128, 512], F32) for i in range(2)]; rii = [Res(), Res()]
    gg = [al(f"A_gg{i}", [128, 512], F32) for i in range(2)]; rgg = [Res(), Res()]
    yst = [al(f"A_y{i}", [128, 512], BF16) for i in range(2)]; ryst = [Res(), Res()]
    carry = al("A_carry", [128, 1], F32); rcar = Res()
    S.dve(lambda e: e.memset(xT[:, 0:3], 0.0), writes=[rxT])
    S.dve(lambda e: e.memset(carry, 0.0), writes=[rcar])
    ps, rps = c.ps, c.rps
    for hf in range(NH):
        t0 = hf * L
        for b in range(NBH):
            blk = hf * NBH + b
            bank = proj_fm(c, "A_x", blk)
            S.act(lambda e, bank=bank, blk=blk: e.activation(out=xT[:, 3 + blk * 512:3 + (blk + 1) * 512], in_=ps[bank], func=AF.Copy),
                  reads=[rps[bank]], writes=[rxT])
        S.dve(lambda e, t0=t0: e.tensor_scalar(out=XC, in0=xT[:, t0 + 3:t0 + 3 + L], scalar1=col("cw3"), scalar2=col("cb"),
                                               op0=ALU.mult, op1=ALU.add), reads=[rxT, rpv], writes=[rXC])
        for w in range(3):
            S.dve(lambda e, t0=t0, w=w: e.scalar_tensor_tensor(out=XC, in0=xT[:, t0 + w:t0 + w + L], scalar=col(f"cw{w}"), in1=XC,
                                                              op0=ALU.mult, op1=ALU.add), reads=[rxT, rpv, rXC], writes=[rXC])
        S.act(lambda e: e.activation(out=XCB, in_=XC, func=AF.Copy), reads=[rXC], writes=[rXCB])
        for b in range(NBH):
            sl = slice(b * 512, (b + 1) * 512)
            S.pe(lambda e, sl=sl: e.matmul(ps[2], lhsT=c.wab[:, 0, :], rhs=XCB[:, sl], start=True, stop=True),
                 reads=[c.rwab, rXCB], writes=[rps[2]])
            S.pe(lambda e, sl=sl: e.matmul(ps[3], lhsT=c.wab[:, 1, :], rhs=XCB[:, sl], start=True, stop=True),
                 reads=[c.rwab, rXCB], writes=[rps[3]])
            S.act(lambda e, sl=sl: e.activation(out=R[:, sl], in_=ps[2], func=AF.Sigmoid, bias=col("ba"), scale=1.0),
                  reads=[rps[2], rpv], writes=[rR])
            S.act(lambda e, b=b: e.activation(out=ii[b % 2], in_=ps[3], func=AF.Sigmoid, bias=col("bx"), scale=1.0),
                  reads=[rps[3], rpv], writes=[rii[b % 2]])
            S.act(lambda e, sl=sl: e.activation(out=TH[:, sl], in_=R[:, sl], func=AF.Tanh, scale=col("c1")),
                  reads=[rR, rpv], writes=[rTH])
            S.dve(lambda e, sl=sl, b=b: e.tensor_tensor(out=XC[:, sl], in0=XC[:, sl], in1=ii[b % 2], op=ALU.mult),
                  reads=[rXC, rii[b % 2]], writes=[rXC])
        S.act(lambda e: e.activation(out=AA, in_=R, func=AF.Exp, scale=col("c1")), reads=[rR, rpv], writes=[rAA])
        S.act(lambda e: e.activation(out=R, in_=R, func=AF.Exp, scale=col("c1x2")), reads=[rR, rpv], writes=[rR])
        S.dve(lambda e: e.scalar_tensor_tensor(out=TH, in0=R, scalar=1.0, in1=TH, op0=ALU.add, op1=ALU.mult),
              reads=[rR, rTH], writes=[rTH])
        S.act(lambda e: e.activation(out=TH, in_=TH, func=AF.Ln, scale=-1.0), reads=[rTH], writes=[rTH])
        S.act(lambda e: e.activation(out=TH, in_=TH, func=AF.Exp, scale=0.5), reads=[rTH], writes=[rTH])
        S.dve(lambda e: e.tensor_tensor(out=XC, in0=XC, in1=TH, op=ALU.mult), reads=[rXC, rTH], writes=[rXC])
        S.dve(lambda e: e.tensor_tensor_scan(out=R, data0=AA, data1=XC, initial=carry, op0=ALU.mult, op1=ALU.add),
              reads=[rAA, rXC, rcar], writes=[rR])
        S.dve(lambda e: e.tensor_copy(out=carry, in_=R[:, L - 1:L]), reads=[rR], writes=[rcar])
        for b in range(NBH):
            blk = hf * NBH + b
            sl = slice(b * 512, (b + 1) * 512)
            bank = proj_fm(c, "A_g", blk)
            S.act(lambda e, bank=bank, b=b: e.activation(out=gg[b % 2], in_=ps[bank], func=AF.Gelu_apprx_tanh),
                  reads=[rps[bank]], writes=[rgg[b % 2]])
            S.dve(lambda e, sl=sl, b=b: e.tensor_tensor(out=yst[b % 2], in0=gg[b % 2], in1=R[:, sl], op=ALU.mult),
                  reads=[rgg[b % 2], rR], writes=[ryst[b % 2]])
            S.dma(yT_d[:, blk * 512:(blk + 1) * 512], yst[b % 2], reads=[ryst[b % 2]], writes=[c.new_yres()], is_output=True)


def gla(c, tag, qt, rqt, kt, rkt, En, rEn, voff, gslice, gcol, groupnorm, yT_d):
    S, A, SEQ = c.S, c.A, c.SEQ
    al = A["alloc"]
    ps, rps = c.ps, c.rps
    NCH = SEQ // 64
    NB = SEQ // 512
    pv, rpv = c.pv, c.rpv
    ident = c.K["ident"]
    psT = ps[6].bitcast(BF16)
    KVs = al(tag + "_KVs", [128, 64, NCH], F32); rKV = Res()
    Ss = al(tag + "_Ss", [128, 64, NCH], BF16); rSs = Res()
    En0 = al(tag + "_En0", [128, NCH], F32); rEn0 = Res()
    ktm4 = al(tag + "_ktm4", [128, 4, 128], BF16); rktm = Res()
    import os
    STOP = int(os.environ.get("GLA_STOP", "99"))
    if STOP == 0:
        S.dma(yT_d, qt, reads=[rqt], is_output=True)
        return
    for tg in range(NB):
        for i in range(4):
            t = tg * 4 + i
            S.pe(lambda e, i=i, t=t: e.transpose(out=psT[:, i * 128:(i + 1) * 128], in_=kt[:, t * 128:(t + 1) * 128], identity=ident),
                 reads=[rkt, c.rK], writes=[rps[6]])
        S.act(lambda e: e.activation(out=ktm4, in_=psT[:, 0:512].rearrange("p (i c) -> p i c", c=128), func=AF.Copy),
              reads=[rps[6]], writes=[rktm])
        for i in range(4):
            t = tg * 4 + i
            S.pe(lambda e, i=i, t=t: e.matmul(ps[2][:, i * 128:(i + 1) * 128], lhsT=ktm4[0:64, i, :], rhs=c.vtm[0:64, t, voff:voff + 128],
                                             start=True, stop=True), reads=[rktm, c.rv[t]], writes=[rps[2]])
            S.pe(lambda e, i=i, t=t: e.matmul(ps[3][:, i * 128:(i + 1) * 128], lhsT=ktm4[64:128, i, :], rhs=c.vtm[64:128, t, voff:voff + 128],
                                             start=True, stop=True), reads=[rktm, c.rv[t]], writes=[rps[3]])
        for par in range(2):
            for hl in range(2):
                rows = slice(hl * 64, (hl + 1) * 64)
                n0 = 8 * tg + par
                S.dve(lambda e, par=par, hl=hl, rows=rows, n0=n0, tg=tg: e.tensor_tensor(
                    out=KVs[rows, :, n0:8 * tg + 8:2].rearrange("p e n -> p n e"),
                    in0=ps[2 + par][rows, :].rearrange("p (i c) -> p i c", c=128)[:, :, hl * 64:(hl + 1) * 64],
                    in1=En[rows, n0:8 * tg + 8:2].unsqueeze(2).broadcast_to([64, 4, 64]), op=ALU.mult),
                    reads=[rps[2 + par], rEn], writes=[rKV])
    if STOP == 1:
        S.dma(yT_d[:, 0:64 * NCH // 2], KVs.rearrange("p e n -> p (e n)").bitcast(BF16)[:, 0:64 * NCH // 2], reads=[rKV], writes=[c.new_yres()], is_output=True)
        return
    S.dve(lambda e: e.tensor_copy(out=En0, in_=En), reads=[rEn], writes=[rEn0])
    S.dve(lambda e: e.memset(En0[:, 0:1], 0.0), writes=[rEn0])
    rSse = [Res() for _ in range(64)]
    for ee in range(64):
        S.dve(lambda e, ee=ee: e.tensor_tensor_scan(out=Ss[:, ee, :], data0=En0, data1=KVs[:, ee, :],
                                                    initial=0.0, op0=ALU.mult, op1=ALU.add), reads=[rEn0, rKV], writes=[rSse[ee]])
    if STOP == 2:
        S.dma(yT_d[:, 0:64 * NCH], Ss.rearrange("p e n -> p (e n)"), reads=rSse, writes=[c.new_yres()], is_output=True)
        return
    atm = [al(tag + f"_atm{i}", [128, 512], BF16) for i in range(2)]; ratm = [Res(), Res()]
    osb = al(tag + "_osb", [128, 512], F32); rosb = Res()
    ob = al(tag + "_ob", [128, 512], BF16); rob = Res()
    rstd = al(tag + "_rstd", [128, 512], F32); rrstd = Res()
    sg = al(tag + "_sg", [128, 512], F32); rsg = Res()
    yst = [al(tag + f"_y{i}", [128, 512], BF16) for i in range(2)]; ryst = [Res(), Res()]
    osb2 = [osb, al(tag + "_osb1", [128, 512], F32)]; rosb2 = [rosb, Res()]
    sg2 = [sg, al(tag + "_sg1", [128, 512], F32)]; rsg2 = [rsg, Res()]

    def front(tb):
        osb_, rosb_, sg_, rsg_ = osb2[tb % 2], rosb2[tb % 2], sg2[tb % 2], rsg2[tb % 2]
        for i in range(4):
            t = tb * 4 + i
            ts_ = slice(t * 128, (t + 1) * 128)
            for hl in range(2):
                rows = slice(hl * 64, (hl + 1) * 64)
                S.pe(lambda e, i=i, hl=hl, rows=rows, ts_=ts_: e.matmul(ps[2 + hl][:, i * 128:(i + 1) * 128], lhsT=kt[rows, ts_], rhs=qt[rows, ts_],
                                                                      start=True, stop=True), reads=[rkt, rqt], writes=[rps[2 + hl]])
        bank = proj_fm(c, gslice, tb)
        S.act(lambda e, bank=bank: e.activation(out=sg_, in_=ps[bank], func=AF.Silu), reads=[rps[bank]], writes=[rsg_])
        for hl in range(2):
            S.dve(lambda e, hl=hl: e.tensor_tensor(out=atm[hl], in0=ps[2 + hl], in1=c.K["glamask"], op=ALU.mult),
                  reads=[rps[2 + hl], c.rK], writes=[ratm[hl]])
        for i in range(4):
            t = tb * 4 + i
            for hl in range(2):
                rows = slice(hl * 64, (hl + 1) * 64)
                ncs = [n for n in (2 * t, 2 * t + 1) if n > 0]
                S.pe(lambda e, i=i, hl=hl, rows=rows, t=t, ncs=ncs: e.matmul(
                    ps[4][rows, i * 128:(i + 1) * 128], lhsT=c.vtm[:, t, voff + hl * 64:voff + (hl + 1) * 64], rhs=atm[hl][:, i * 128:(i + 1) * 128],
                    start=True, stop=(len(ncs) == 0)), reads=[c.rv[t], ratm[hl]], writes=[rps[4]])
                for n in ncs:
                    cc = n - 2 * t
                    S.pe(lambda e, i=i, hl=hl, rows=rows, n=n, cc=cc, ncs=ncs: e.matmul(
                        ps[4][rows, i * 128 + cc * 64:i * 128 + (cc + 1) * 64], lhsT=Ss[rows, :, n - 1], rhs=qt[rows, n * 64:(n + 1) * 64],
                        start=False, stop=(n == ncs[-1])), reads=rSse + [rqt], writes=[rps[4]])
        S.act(lambda e: e.activation(out=osb_, in_=ps[4], func=AF.Copy), reads=[rps[4]], writes=[rosb_])

    def back(tb):
        osb_, rosb_, sg_, rsg_ = osb2[tb % 2], rosb2[tb % 2], sg2[tb % 2], rsg2[tb % 2]
        if groupnorm:
            S.dve(lambda e: e.tensor_copy(out=ob, in_=osb_), reads=[rosb_], writes=[rob])
            S.pe(lambda e: e.matmul(ps[5], lhsT=c.K["blockmean"], rhs=ob, start=True, stop=True), reads=[rob, c.rK], writes=[rps[5]])
            S.dve(lambda e: e.tensor_tensor(out=osb_, in0=osb_, in1=ps[5], op=ALU.subtract), reads=[rosb_, rps[5]], writes=[rosb_])
        S.act(lambda e: e.activation(out=ob, in_=osb_, func=AF.Square), reads=[rosb_], writes=[rob])
        S.pe(lambda e: e.matmul(ps[5], lhsT=c.K["blockmean"], rhs=ob, start=True, stop=True), reads=[rob, c.rK], writes=[rps[5]])
        S.act(lambda e: e.activation(out=rstd, in_=ps[5], func=AF.Ln, bias=c.K["eps_n"], scale=1.0), reads=[rps[5], c.rK], writes=[rrstd])
        S.act(lambda e: e.activation(out=rstd, in_=rstd, func=AF.Exp, scale=-0.5), reads=[rrstd], writes=[rrstd])
        S.dve(lambda e: e.tensor_tensor(out=osb_, in0=osb_, in1=rstd, op=ALU.mult), reads=[rosb_, rrstd], writes=[rosb_])
        S.dve(lambda e: e.scalar_tensor_tensor(out=yst[tb % 2], in0=osb_, scalar=pv[:, PV[gcol]:PV[gcol] + 1], in1=sg_,
                                               op0=ALU.mult, op1=ALU.mult), reads=[rosb_, rsg_, rpv], writes=[ryst[tb % 2]])
        S.dma(yT_d[:, tb * 512:(tb + 1) * 512], yst[tb % 2], reads=[ryst[tb % 2]], writes=[c.new_yres()], is_output=True)

    for tb in range(NB + 1):
        if tb < NB:
            front(tb)
        if tb >= 1:
            back(tb - 1)


def mixer_B(c, yT_d):
    S, A, SEQ = c.S, c.A, c.SEQ
    al = A["alloc"]
    ps, rps = c.ps, c.rps
    NCH = SEQ // 64
    qt = al("B_qt", [128, SEQ], BF16); rqt = Res()
    kt = al("B_kt", [128, SEQ], BF16); rkt = Res()
    En = al("B_En", [128, NCH], F32); rEn = Res()
    tab = al("B_tab", [128, 4, 512], F32); rtab = Res()
    t1s = [al(f"B_t1{i}", [128, 512], F32) for i in range(2)]; rt1s = [Res(), Res()]
    t2s = [al(f"B_t2{i}", [128, 512], F32) for i in range(2)]; rt2s = [Res(), Res()]
    S.dve(lambda e: e.tensor_copy(out=En, in_=c.pv[:, PV["gam64"]:PV["gam64"] + 1].broadcast_to([128, NCH])), reads=[c.rpv], writes=[rEn])
    c.proj_banks = (0, 1, 2, 3)
    for blk in range(c.NB):
        sl = slice(blk * 512, (blk + 1) * 512)
        S.dma(tab, c.rettab_d[:, :, sl].rearrange("f p t -> p f t"), writes=[rtab])
        for which, dst, rdst, ti in (("B_q", qt, rqt, 0), ("B_k", kt, rkt, 2)):
            t1, rt1, t2, rt2 = t1s[ti // 2], rt1s[ti // 2], t2s[ti // 2], rt2s[ti // 2]
            b1 = proj_fm(c, which, blk)
            b2 = proj_fm(c, which + "sw", blk)
            S.dve(lambda e, b1=b1, ti=ti, t1=t1: e.tensor_tensor(out=t1, in0=ps[b1], in1=tab[:, ti, :], op=ALU.mult), reads=[rps[b1], rtab], writes=[rt1])
            S.dve(lambda e, b2=b2, ti=ti, t2=t2: e.tensor_tensor(out=t2, in0=ps[b2], in1=tab[:, ti + 1, :], op=ALU.mult), reads=[rps[b2], rtab], writes=[rt2])
            S.pool(lambda e, dst=dst, sl=sl, t1=t1, t2=t2: e.tensor_tensor(out=dst[:, sl], in0=t1, in1=t2, op=ALU.add), reads=[rt1, rt2], writes=[rdst])
    c.proj_banks = (0, 1)
    gla(c, "B", qt, rqt, kt, rkt, En, rEn, 0, "B_g", "retg", True, yT_d)


def mixer_D(c, yT_d):
    S, A, SEQ = c.S, c.A, c.SEQ
    al = A["alloc"]
    ps, rps = c.ps, c.rps
    pv, rpv = c.pv, c.rpv
    col = lambda n: pv[:, PV[n]:PV[n] + 1]
    NCH = SEQ // 64
    qt = al("D_qt", [128, SEQ], BF16); rqt = Res()
    kt = al("D_kt", [128, SEQ], BF16); rkt = Res()
    En = al("D_En", [128, NCH], F32); rEn = Res()
    T = [al(f"D_T{i}", [128, 512], F32) for i in range(6)]
    rT = [Res() for _ in range(6)]
    one = c.K["one"]
    c.proj_banks = (0, 1, 2, 3)
    for blk in range(c.NB):
        sl = slice(blk * 512, (blk + 1) * 512)
        bf_ = proj_fm(c, "D_f", blk)
        S.act(lambda e, b=bf_: e.activation(out=T[0], in_=ps[b], func=AF.Exp), reads=[rps[bf_]], writes=[rT[0]])
        S.act(lambda e: e.activation(out=T[0], in_=T[0], func=AF.Ln, bias=one, scale=1.0), reads=[rT[0], c.rK], writes=[rT[0]])
        S.act(lambda e: e.activation(out=T[1], in_=T[0], func=AF.Exp, bias=col("lnoml"), scale=-1.0), reads=[rT[0], rpv], writes=[rT[1]])
        S.act(lambda e: e.activation(out=T[2], in_=T[1], func=AF.Ln, bias=one, scale=-1.0), reads=[rT[1], c.rK], writes=[rT[2]])
        S.dve(lambda e: e.tensor_tensor_scan(out=T[3], data0=c.K["resetmask"], data1=T[2], initial=0.0, op0=ALU.mult, op1=ALU.add),
              reads=[rT[2], c.rK], writes=[rT[3]])
        S.act(lambda e: e.activation(out=T[4], in_=T[3], func=AF.Exp), reads=[rT[3]], writes=[rT[4]])
        S.act(lambda e: e.activation(out=T[5], in_=T[3], func=AF.Exp, scale=-1.0), reads=[rT[3]], writes=[rT[5]])
        bq = proj_fm(c, "D_q", blk)
        S.dve(lambda e, b=bq, sl=sl: e.tensor_tensor(out=qt[:, sl], in0=ps[b], in1=T[4], op=ALU.mult), reads=[rps[bq], rT[4]], writes=[rqt])
        S.pool(lambda e, sl=sl: e.tensor_tensor(out=kt[:, sl], in0=T[1], in1=T[5], op=ALU.mult), reads=[rT[1], rT[5]], writes=[rkt])
        S.dve(lambda e, blk=blk: e.tensor_copy(out=En[:, blk * 8:(blk + 1) * 8], in_=T[4][:, 63:512:64]), reads=[rT[4]], writes=[rEn])
    c.proj_banks = (0, 1)
    gla(c, "D", qt, rqt, kt, rkt, En, rEn, 256, "D_g", "hgg", False, yT_d)


def mixer_C(c, yT_d):
    S, A, SEQ = c.S, c.A, c.SEQ
    al = A["alloc"]
    ps, rps = c.ps, c.rps
    NQ = SEQ // 512
    qT = al("C_qT", [128, SEQ], BF16); rqT = Res()
    kT = al("C_kT", [128, SEQ], BF16); rkT = Res()
    c.proj_banks = (0, 1, 2, 3)
    for blk in range(c.NB):
        sl = slice(blk * 512, (blk + 1) * 512)
        b = proj_fm(c, "C_q", blk)
        S.act(lambda e, b=b, sl=sl: e.activation(out=qT[:, sl], in_=ps[b], func=AF.Copy, scale=0.125), reads=[rps[b]], writes=[rqT])
        b = proj_fm(c, "C_k", blk)
        S.dve(lambda e, b=b, sl=sl: e.tensor_copy(out=kT[:, sl], in_=ps[b]), reads=[rps[b]], writes=[rkT])
    c.proj_banks = (0, 1)
    NE = 4
    eb = [al(f"C_e{i}", [128, 512], BF16) for i in range(NE)]; reb = [Res() for _ in range(NE)]
    msp = [al(f"C_msp{i}", [128, 512], BF16) for i in range(NE)]; rmsp = [Res() for _ in range(NE)]
    exr = [al(f"C_exr{i}", [128, 512], BF16) for i in range(2)]; rexr = [Res() for _ in range(2)]
    wT = [al(f"C_w{i}", [128, 512], BF16) for i in range(NE)]; rwT = [Res() for _ in range(NE)]
    chi = [al(f"C_chi{i}", [1, 512], BF16) for i in range(2)]; rchi = [Res(), Res()]
    clo = [al(f"C_clo{i}", [1, 512], BF16) for i in range(2)]; rclo = [Res(), Res()]
    yst = [al(f"C_y{i}", [128, 512], BF16) for i in range(2)]; ryst = [Res(), Res()]
    negtri, ones_row, cmask = c.K["negtri"], c.K["ones_row"], c.K["cmask"]
    steps = []
    for qb in range(NQ):
        for kb in range(4 * qb + 3, -1, -1):
            for h in range(2):
                steps.append((qb, kb, h))
    N = len(steps)

    def cols(i):
        qb, kb, h = steps[i]
        j = kb - 4 * qb
        return slice(max(j, 0) * 128, 512)

    def stage_Z(i):
        qb, kb, h = steps[i]
        rows = slice(h * 64, (h + 1) * 64)
        cs = cols(i)
        q0 = qb * 512
        S.pe(lambda e: e.matmul(ps[2 + h][:, cs], lhsT=kT[rows, kb * 128:(kb + 1) * 128], rhs=qT[rows, q0 + cs.start:q0 + 512],
                                start=True, stop=True), reads=[rkT, rqT], writes=[rps[2 + h]])
        S.act(lambda e: e.activation(out=eb[i % NE][:, cs], in_=ps[2 + h][:, cs], func=AF.Exp), reads=[rps[2 + h]], writes=[reb[i % NE]])
        j = kb - 4 * qb
        if j >= 0:
            dg = slice(j * 128, (j + 1) * 128)
            S.dve(lambda e: e.tensor_tensor(out=eb[i % NE][:, dg], in0=eb[i % NE][:, dg], in1=cmask[:, j, dg], op=ALU.mult),
                  reads=[reb[i % NE], c.rK], writes=[reb[i % NE]])
        S.act(lambda e: e.activation(out=msp[i % NE][:, cs], in_=eb[i % NE][:, cs], func=AF.Ln, bias=1.0, scale=1.0),
              reads=[reb[i % NE]], writes=[rmsp[i % NE]])

    def stage_R(i):
        qb, kb, h = steps[i]
        first = (kb == 4 * qb + 3)
        cs = cols(i)
        S.pe(lambda e: e.matmul(ps[4 + h][:, cs], lhsT=negtri, rhs=msp[i % NE][:, cs], start=first, stop=False, skip_group_check=True),
             reads=[rmsp[i % NE], c.rK], writes=[rps[4 + h]])
        S.act(lambda e: e.activation(out=exr[i % 2][:, cs], in_=ps[4 + h][:, cs], func=AF.Exp), reads=[rps[4 + h]], writes=[rexr[i % 2]])
        S.dve(lambda e: e.tensor_tensor(out=wT[i % NE][:, cs], in0=eb[i % NE][:, cs], in1=exr[i % 2][:, cs], op=ALU.mult),
              reads=[reb[i % NE], rexr[i % 2]], writes=[rwT[i % NE]])

    def stage_O(i):
        qb, kb, h = steps[i]
        rows = slice(h * 64, (h + 1) * 64)
        ob = 6 + qb % 2
        first = (kb == 4 * qb + 3)
        cs = cols(i)
        if kb > 0:
            S.pe(lambda e: e.matmul(ps[4 + h][:, cs], lhsT=c.K["negcompl"], rhs=msp[i % NE][:, cs], start=False, stop=(kb == 1), skip_group_check=True),
                 reads=[rmsp[i % NE], c.rK], writes=[rps[4 + h]])
        S.pe(lambda e: e.matmul(ps[ob][rows, cs], lhsT=c.vtm[:, kb, 128 + h * 64:128 + (h + 1) * 64], rhs=wT[i % NE][:, cs],
                                start=first, stop=(kb == 0), skip_group_check=True), reads=[c.rv[kb], rwT[i % NE]], writes=[rps[ob]])
        if kb == 0 and h == 1:
            S.act(lambda e: e.activation(out=yst[qb % 2], in_=ps[ob], func=AF.Copy), reads=[rps[ob]], writes=[ryst[qb % 2]])
            S.dma(yT_d[:, qb * 512:(qb + 1) * 512], yst[qb % 2], reads=[ryst[qb % 2]], writes=[c.new_yres()], is_output=True)

    for s in range(-2, N):
        if 0 <= s + 2 < N:
            stage_Z(s + 2)
        if 0 <= s + 1 < N:
            stage_R(s + 1)
        if 0 <= s < N:
            stage_O(s)


def mixer_C2(c, yT_d):
    S, A, SEQ = c.S, c.A, c.SEQ
    al = A["alloc"]
    ps, rps = c.ps, c.rps
    big = A["psum_all"]
    NQ = SEQ // 512
    qT = al("C_qT", [128, SEQ], BF16); rqT = Res()
    kT = al("C_kT", [128, SEQ], BF16); rkT = Res()
    c.proj_banks = (0, 1, 2, 3)
    for blk in range(c.NB):
        sl = slice(blk * 512, (blk + 1) * 512)
        b = proj_fm(c, "C_q", blk)
        S.act(lambda e, b=b, sl=sl: e.activation(out=qT[:, sl], in_=ps[b], func=AF.Copy, scale=0.125), reads=[rps[b]], writes=[rqT])
        b = proj_fm(c, "C_k", blk)
        S.dve(lambda e, b=b, sl=sl: e.tensor_copy(out=kT[:, sl], in_=ps[b]), reads=[rps[b]], writes=[rkT])
    c.proj_banks = (0, 1)
    NE = 4
    eb = [al(f"C_e{i}", [128, 2, 512], BF16) for i in range(NE)]; reb = [Res() for _ in range(NE)]
    msp = [al(f"C_msp{i}", [128, 2, 512], BF16) for i in range(NE)]; rmsp = [Res() for _ in range(NE)]
    exr = [al(f"C_exr{i}", [128, 2, 512], BF16) for i in range(2)]; rexr = [Res() for _ in range(2)]
    wT = [al(f"C_w{i}", [128, 2, 512], BF16) for i in range(NE)]; rwT = [Res() for _ in range(NE)]
    yst = [al(f"C_y{i}", [128, 512], BF16) for i in range(2)]; ryst = [Res(), Res()]
    negtri, negcompl, cmask = c.K["negtri"], c.K["negcompl"], c.K["cmask"]

    def pair(b0):
        return big[:, b0 * 512:(b0 + 2) * 512].rearrange("p (h n) -> p h n", h=2)

    steps = []
    for qb in range(NQ):
        for kb in range(4 * qb + 3, -1, -1):
            steps.append((qb, kb))
    N = len(steps)

    def cols(i):
        qb, kb = steps[i]
        return slice(max(kb - 4 * qb, 0) * 128, 512)

    def stage_Z(i):
        qb, kb = steps[i]
        cs = cols(i)
        q0 = qb * 512
        zb = 0 if i % 2 == 0 else 2
        for h in range(2):
            rows = slice(h * 64, (h + 1) * 64)
            S.pe(lambda e, h=h, rows=rows: e.matmul(ps[zb + h][:, cs], lhsT=kT[rows, kb * 128:(kb + 1) * 128], rhs=qT[rows, q0 + cs.start:q0 + 512],
                                                    start=True, stop=True), reads=[rkT, rqT], writes=[rps[zb + h]])
        S.act(lambda e: e.activation(out=eb[i % NE][:, :, cs], in_=pair(zb)[:, :, cs], func=AF.Exp),
              reads=[rps[zb], rps[zb + 1]], writes=[reb[i % NE]])
        j = kb - 4 * qb
        if j >= 0:
            dg = slice(j * 128, (j + 1) * 128)
            S.dve(lambda e: e.tensor_tensor(out=eb[i % NE][:, :, dg], in0=eb[i % NE][:, :, dg],
                                            in1=cmask[:, j:j + 1, dg].broadcast_to([128, 2, 128]), op=ALU.mult),
                  reads=[reb[i % NE], c.rK], writes=[reb[i % NE]])
        S.act(lambda e: e.activation(out=msp[i % NE][:, :, cs], in_=eb[i % NE][:, :, cs], func=AF.Ln, bias=1.0, scale=1.0),
              reads=[reb[i % NE]], writes=[rmsp[i % NE]])

    def stage_R(i):
        qb, kb = steps[i]
        first = (kb == 4 * qb + 3)
        cs = cols(i)
        for h in range(2):
            S.pe(lambda e, h=h: e.matmul(ps[4 + h][:, cs], lhsT=negtri, rhs=msp[i % NE][:, h, cs], start=first, stop=False, skip_group_check=True),
                 reads=[rmsp[i % NE], c.rK], writes=[rps[4 + h]])
        S.act(lambda e: e.activation(out=exr[i % 2][:, :, cs], in_=pair(4)[:, :, cs], func=AF.Exp),
              reads=[rps[4], rps[5]], writes=[rexr[i % 2]])
        S.dve(lambda e: e.tensor_tensor(out=wT[i % NE][:, :, cs], in0=eb[i % NE][:, :, cs], in1=exr[i % 2][:, :, cs], op=ALU.mult),
              reads=[reb[i % NE], rexr[i % 2]], writes=[rwT[i % NE]])

    def stage_Cmp(i):
        qb, kb = steps[i]
        cs = cols(i)
        if kb > 0:
            for h in range(2):
                S.pe(lambda e, h=h: e.matmul(ps[4 + h][:, cs], lhsT=negcompl, rhs=msp[i % NE][:, h, cs], start=False, stop=(kb == 1), skip_group_check=True),
                     reads=[rmsp[i % NE], c.rK], writes=[rps[4 + h]])

    def stage_O(i):
        qb, kb = steps[i]
        ob = 6 + qb % 2
        first = (kb == 4 * qb + 3)
        cs = cols(i)
        for h in range(2):
            rows = slice(h * 64, (h + 1) * 64)
            S.pe(lambda e, h=h, rows=rows: e.matmul(ps[ob][rows, cs], lhsT=c.vtm[:, kb, 128 + h * 64:128 + (h + 1) * 64], rhs=wT[i % NE][:, h, cs],
                                                    start=first, stop=(kb == 0), skip_group_check=True), reads=[c.rv[kb], rwT[i % NE]], writes=[rps[ob]])
        if kb == 0:
            S.act(lambda e: e.activation(out=yst[qb % 2], in_=ps[ob], func=AF.Copy), reads=[rps[ob]], writes=[ryst[qb % 2]])
            S.dma(yT_d[:, qb * 512:(qb + 1) * 512], yst[qb % 2], reads=[ryst[qb % 2]], writes=[c.new_yres()], is_output=True)

    for s in range(-2, N):
        if 0 <= s + 2 < N:
            stage_Z(s + 2)
        if 0 <= s < N:
            stage_Cmp(s)
        if 0 <= s + 1 < N:
            stage_R(s + 1)
        if 0 <= s < N:
            stage_O(s)


def phase_M(nc, S, A, SEQ, layer, hin_d, win_d, pvec_d, wab_d, consts, rettab_d, yT_d, which="ABDC"):
    ar = A["arena"]
    c = setup_M(nc, S, A, SEQ, layer, hin_d, win_d, pvec_d, wab_d, consts)
    c.rettab_d = rettab_d
    small_params(c)
    m = ar.mark()
    fns = {"A": (mixer_A, 0), "B": (mixer_B, 1), "C": (mixer_C2 if "psum_all" in A else mixer_C, 2), "D": (mixer_D, 3)}
    for i, ch in enumerate(which):
        if i > 0:
            S.barrier(A["bar_scratch"])
            ar.reset(m)
        fn, slot = fns[ch]
        fn(c, yT_d[slot] if isinstance(yT_d, (list, tuple)) else yT_d[slot * 128:(slot + 1) * 128, :])
        c.ry_by_mixer[ch] = c.ry_list
        c.ry_list = []
    return c


BF = ml_dtypes.bfloat16
REF_SLICE = {"A_x": 0, "A_g": 1, "B_q": 2, "B_k": 3, "B_v": 4, "B_g": 5, "C_q": 6, "C_k": 7, "C_v": 8,
             "D_q": 9, "D_f": 10, "D_v": 11, "D_g": 12}
MY_SLICES = ["A_x", "A_g", "B_q", "B_qsw", "B_k", "B_ksw", "B_g", "C_q", "C_k", "D_q", "D_f", "D_g", "B_v", "C_v", "D_v"]


def core_cols(hh):
    p = np.arange(128)
    partner = (p // 64) * 64 + ((p % 64) + 32) % 64
    cols = []
    for n in MY_SLICES:
        sw = n.endswith("sw")
        base = REF_SLICE[n[:-2] if sw else n] * 256 + hh * 128
        cols.append(base + (partner if sw else p))
    return np.concatenate(cols)


def prep_layer_core(inp, l, hh):
    ch = slice(hh * 128, (hh + 1) * 128)
    out = {}
    out["win"] = np.ascontiguousarray(np.asarray(inp["w_in"][l])[:, core_cols(hh)])
    pv = np.zeros((128, 13), np.float32)
    cw = np.asarray(inp["conv_w"][l])
    for w in range(4):
        pv[:, w] = cw[w, ch]
    pv[:, 4] = np.asarray(inp["conv_b"][l])[ch]
    pv[:, 5] = np.asarray(inp["rg_ba"][l]).reshape(-1)[ch]
    pv[:, 6] = np.asarray(inp["rg_bx"][l]).reshape(-1)[ch]
    pv[:, 7] = np.asarray(inp["rg_lambda"][l])[ch]
    pv[:, 8] = np.asarray(inp["ret_norm_g"][l])[ch]
    pv[:, 9] = np.asarray(inp["hgrn_norm_g"][l])[ch]
    pv[:, 10] = np.asarray(inp["hgrn_lb_logits"][0])[ch]
    pv[:, 11] = np.asarray(inp["hgrn_lb_logits"][l])[ch]
    for hl in range(2):
        gam = 1.0 - 2.0 ** (-5.0 - (2 * hh + hl))
        pv[hl * 64:(hl + 1) * 64, 12] = gam ** 64
    out["pvec"] = pv
    wab = np.zeros((128, 2, 128), np.float32)
    for hl in range(2):
        s = slice(hl * 64, (hl + 1) * 64)
        wab[s, 0, s] = np.asarray(inp["rg_wa"][l])[2 * hh + hl]
        wab[s, 1, s] = np.asarray(inp["rg_wx"][l])[2 * hh + hl]
    out["wab"] = wab
    return out


def const_tables(hh, SEQ):
    K = {}
    K["ident"] = np.eye(128).astype(BF)
    j = np.arange(128)
    K["negtri"] = (-(j[:, None] >= j[None, :]).astype(np.float32)).astype(BF)
    K["negcompl"] = (-(j[:, None] < j[None, :]).astype(np.float32)).astype(BF)
    K["ones_row"] = np.ones((1, 128), np.float32).astype(BF)
    K["one"] = np.ones((128, 1), np.float32)
    K["eps_n"] = np.full((128, 1), 1e-6, np.float32)
    q = np.arange(512)
    cm = np.zeros((128, 4, 512), np.float32)
    for jb in range(4):
        cm[:, jb, :] = ((jb * 128 + j)[:, None] < q[None, :])
    K["cmask"] = cm.astype(BF)
    s = np.arange(128)
    gm = ((s[:, None] // 64 == s[None, :] // 64) & (s[:, None] <= s[None, :])).astype(np.float32)
    K["glamask"] = np.tile(gm, (1, 4)).astype(BF)
    K["blockmean"] = ((s[:, None] // 64 == s[None, :] // 64) / 64.0).astype(np.float32).astype(BF)
    rm = np.ones((128, 512), np.float32)
    rm[:, ::64] = 0.0
    K["resetmask"] = rm
    d = np.arange(64)
    inv_freq = (10000.0 ** (-np.arange(0, 64, 2, dtype=np.float32) / 64)).astype(np.float32)
    t = np.arange(SEQ, dtype=np.float32)
    ang = (t[:, None] * inv_freq[None, :]).astype(np.float32)
    cos, sin = np.cos(ang).T, np.sin(ang).T
    tl = (np.arange(SEQ) % 64 + 1).astype(np.float64)
    tabs = np.zeros((4, 128, SEQ), np.float32)
    for hl in range(2):
        gam = 1.0 - 2.0 ** (-5.0 - (2 * hh + hl))
        lg = np.log1p(-2.0 ** (-5.0 - (2 * hh + hl)))
        dq = np.exp(lg * tl)
        dk = np.exp(-lg * tl) / 8.0
        for dd in range(64):
            p = hl * 64 + dd
            c_, s_ = cos[dd % 32], sin[dd % 32]
            sg = -1.0 if dd < 32 else 1.0
            tabs[0, p] = c_ * dq
            tabs[1, p] = sg * s_ * dq
            tabs[2, p] = c_ * dk
            tabs[3, p] = sg * s_ * dk
    K["rettab"] = tabs
    return K


from concourse.bass_utils import run_bass_kernel_spmd

SEQ_FULL = 4096
T_OWN = 2048
SMALLK = ["ident", "negtri", "negcompl", "ones_row", "one", "eps_n", "cmask", "glamask", "blockmean", "resetmask"]


def _mk_A(nc):
    A = {}
    ar = Arena(nc)
    A["arena"] = ar
    A["alloc"] = ar.alloc
    A["psum_all"] = nc.alloc_psum_tensor("psum_all", [128, 6 * 512], F32).ap()
    A["psum"] = [A["psum_all"][:, i * 512:(i + 1) * 512] for i in range(6)] + [nc.alloc_psum_tensor(f"ps{i}", [128, 512], F32).ap() for i in (6, 7)]
    A["rpsum"] = [Res(excl=True) for _ in range(8)]
    A["bar_scratch"] = ar.alloc("bar", [128, 1], F32)
    return A


def _np_dt(a):
    return BF16 if a.dtype == BF else F32


def build_pre():
    nc = bass.Bass("TRN2", target_bir_lowering=False)
    S = Sched(nc)
    dt = lambda n, s, d, k="ExternalInput": nc.dram_tensor(n, s, d, kind=k).ap()
    x_d = dt("x", [T_OWN, 1024], F32)
    g_d = dt("ln_g", [1024], F32)
    b_d = dt("ln_b", [1024], F32)
    h32 = dt("h32", [T_OWN, 1024], F32, "ExternalOutput")
    h16 = dt("h16", [T_OWN, 1024], BF16, "ExternalOutput")
    A = _mk_A(nc)
    al = A["alloc"]
    eps = al("eps", [128, 1], F32)
    reps = Res()
    S.dve(lambda e: e.memset(eps, LN_EPS), writes=[reps])
    A["eps_ln"] = eps
    lnp = al("lnp", [128, 2, 1024], F32); rln = Res()
    if True:
        dmy = al("dmy", [128, 128], BF16); rd = Res()
        S.dve(lambda e: e.memset(dmy, 0.0), writes=[rd])
        S.pe(lambda e: e.matmul(A["psum"][0][:, 0:128], lhsT=dmy, rhs=dmy, start=True, stop=True), reads=[rd], writes=[A["rpsum"][0]])
    S.dma(lnp[:, 0, :], g_d.partition_broadcast(128), writes=[rln])
    S.dma(lnp[:, 1, :], b_d.partition_broadcast(128), writes=[rln])
    NT = T_OWN // 128
    xt = [al(f"xt{i}", [128, 1024], F32) for i in range(2)]; rxt = [Res(), Res()]
    tmp = [al(f"tmp{i}", [128, 1024], F32) for i in range(2)]; rtmp = [Res(), Res()]
    hb = [al(f"hb{i}", [128, 1024], BF16) for i in range(2)]; rhb = [Res(), Res()]
    st = [al(f"st{i}", [128, 16], F32) for i in range(2)]; rst = [Res(), Res()]
    for t in range(NT):
        p = t % 2
        S.dma(xt[p], x_d[t * 128:(t + 1) * 128, :], writes=[rxt[p]])
        ln_tile(S, nc, A, xt[p], rxt[p], lnp[:, 0, :], lnp[:, 1, :], rln, xt[p], rxt[p], tmp[p], rtmp[p], st[p], rst[p], t)
        S.dma(h32[t * 128:(t + 1) * 128, :], xt[p], reads=[rxt[p]], is_output=True)
        S.act(lambda e, p=p: e.activation(out=hb[p], in_=xt[p], func=AF.Copy), reads=[rxt[p]], writes=[rhb[p]])
        S.dma(h16[t * 128:(t + 1) * 128, :], hb[p], reads=[rhb[p]], is_output=True)
    build_and_emit(nc, S)
    return nc


def build_M(layer, Kh):
    nc = bass.Bass("TRN2", target_bir_lowering=False)
    S = Sched(nc)
    dt = lambda n, s, d, k="ExternalInput": nc.dram_tensor(n, s, d, kind=k).ap()
    SEQ = SEQ_FULL
    hin = dt("hin", [SEQ, 1024], BF16)
    win = dt("win", [1024, 1920], F32)
    pvec = dt("pvec", [128, 13], F32)
    wab = dt("wab", [128, 2, 128], F32)
    consts = {}
    for n in SMALLK:
        a = Kh[n]
        consts[n] = (dt("k_" + n, list(a.shape), _np_dt(a)), list(a.shape), _np_dt(a))
    rettab = dt("rettab", [4, 128, SEQ], F32)
    yT = dt("yT", [512, SEQ], BF16, "ExternalOutput")
    A = _mk_A(nc)
    phase_M(nc, S, A, SEQ, layer, hin, win, pvec, wab, consts, rettab, yT)
    build_and_emit(nc, S)
    return nc


def build_F():
    nc = bass.Bass("TRN2", target_bir_lowering=False)
    S = Sched(nc)
    dt = lambda n, s, d, k="ExternalInput": nc.dram_tensor(n, s, d, kind=k).ap()
    T = T_OWN
    yT_d = dt("yT", [1024, T], BF16)
    h_d = dt("h", [T, 1024], F32)
    wout = dt("w_out", [1024, 1024], F32)
    wup = dt("w_up", [1024, 4096], F32)
    wdn = dt("w_down", [4096, 1024], F32)
    l1g, l1b, l2g, l2b = [dt(n, [1024], F32) for n in ("l1g", "l1b", "l2g", "l2b")]
    ident_d = dt("ident", [128, 128], BF16)
    out = dt("h32", [T, 1024], F32, "ExternalOutput")
    outb = dt("h16", [T, 1024], BF16, "ExternalOutput")
    A = _mk_A(nc)
    al = A["alloc"]
    A["ident_bf"] = al("ident", [128, 128], BF16)
    rid = Res()
    S.dma(A["ident_bf"], ident_d, writes=[rid])
    eps = al("eps", [128, 1], F32)
    S.dve(lambda e: e.memset(eps, LN_EPS), writes=[rid])
    A["eps_ln"] = eps
    NT = T // 128
    hres = al("hres", [128, NT, 1024], F32)
    rh = [Res() for _ in range(NT)]
    for t in range(NT):
        S.dma(hres[:, t, :], h_d[t * 128:(t + 1) * 128, :], writes=[rh[t]])
    phase_F(nc, S, A, T, yT_d, wout, l1g, l1b, wup, wdn, l2g, l2b, hres, rh, out_f32_d=out, out_bf16_d=outb)
    build_and_emit(nc, S)
    return nc


def wout_perm():
    idx = []
    for hh in range(2):
        for m in range(4):
            idx.append(m * 256 + hh * 128 + np.arange(128))
    return np.concatenate(idx)


def kernel_unfused(**inputs):
    inp = {k: np.asarray(v) for k, v in inputs.items()}
    x = inp["x"]
    NC = 8
    cores = list(range(NC))
    f32 = np.float32
    Kh = [const_tables(hh, SEQ_FULL) for hh in range(2)]
    nc_pre = build_pre()
    im = []
    for c in cores:
        b, hh = c // 2, c % 2
        im.append({"x": np.ascontiguousarray(x[b, hh * T_OWN:(hh + 1) * T_OWN, :]), "ln_g": inp["ln_in_g"], "ln_b": inp["ln_in_b"]})
    res = run_bass_kernel_spmd(nc_pre, im, core_ids=cores).results
    h32 = [r["h32"] for r in res]
    h16 = [r["h16"] for r in res]
    nc_F = build_F()
    perm = wout_perm()
    for l in range(2):
        nc_M = build_M(l, Kh[0])
        im = []
        for c in cores:
            b, hh = c // 2, c % 2
            pc = prep_layer_core(inp, l, hh)
            d = {"hin": np.concatenate([h16[2 * b], h16[2 * b + 1]], axis=0), "win": pc["win"], "pvec": pc["pvec"], "wab": pc["wab"],
                 "rettab": Kh[hh]["rettab"]}
            for n in SMALLK:
                d["k_" + n] = Kh[hh][n]
            im.append(d)
        res = run_bass_kernel_spmd(nc_M, im, core_ids=cores).results
        yT = [r["yT"] for r in res]
        im = []
        wo = np.ascontiguousarray(inp["w_out"][l][perm, :])
        for c in cores:
            b, hh = c // 2, c % 2
            ya = np.concatenate([yT[2 * b], yT[2 * b + 1]], axis=0)[:, hh * T_OWN:(hh + 1) * T_OWN]
            im.append({"yT": np.ascontiguousarray(ya), "h": h32[c], "w_out": wo, "w_up": inp["w_up"][l], "w_down": inp["w_down"][l],
                       "l1g": inp["ln1_g"][l], "l1b": inp["ln1_b"][l], "l2g": inp["ln2_g"][l], "l2b": inp["ln2_b"][l],
                       "ident": Kh[0]["ident"]})
        res = run_bass_kernel_spmd(nc_F, im, core_ids=cores).results
        h32 = [r["h32"] for r in res]
        h16 = [r["h16"] for r in res]
    out = np.zeros((4, SEQ_FULL, 1024), f32)
    for c in cores:
        b, hh = c // 2, c % 2
        out[b, hh * T_OWN:(hh + 1) * T_OWN, :] = h32[c]
    return out


GROUPS = [[0, 1], [2, 3], [4, 5], [6, 7]]
U32 = mybir.dt.uint32


def build_fused(Kh):
    nc = bass.Bass("TRN2", target_bir_lowering=False)
    S = Sched(nc)
    dt = lambda n, s, d, k="ExternalInput": nc.dram_tensor(n, s, d, kind=k).ap()
    SEQ, T = SEQ_FULL, T_OWN
    NT = T // 128
    x_d = dt("x", [T, 1024], F32)
    g_d = dt("ln_g", [1024], F32)
    b_d = dt("ln_b", [1024], F32)
    gidx_d = dt("gidx", [128, 8], U32)
    L = []
    for l in range(2):
        L.append(dict(
            win=dt(f"win{l}", [1024, 1920], F32), pvec=dt(f"pvec{l}", [128, 13], F32), wab=dt(f"wab{l}", [128, 2, 128], F32),
            wout=dt(f"w_out{l}", [1024, 1024], F32), wup=dt(f"w_up{l}", [1024, 4096], F32), wdn=dt(f"w_down{l}", [4096, 1024], F32),
            l1g=dt(f"l1g{l}", [1024], F32), l1b=dt(f"l1b{l}", [1024], F32), l2g=dt(f"l2g{l}", [1024], F32), l2b=dt(f"l2b{l}", [1024], F32)))
    kd = {}
    for n in SMALLK:
        a = Kh[n]
        kd[n] = (dt("k_" + n, list(a.shape), _np_dt(a)), list(a.shape), _np_dt(a))
    rettab = dt("rettab", [4, 128, SEQ], F32)
    out_d = dt("out", [T, 1024], F32, "ExternalOutput")
    H = T // 2
    hx_loc = [nc.dram_tensor(f"hx_loc{i}", [H, 1024], BF16).ap() for i in range(2)]
    hx_all = [nc.dram_tensor(f"hx_all{i}", [2 * H, 1024], BF16).ap() for i in range(2)]
    y_loc = [nc.dram_tensor(f"y_loc{i}", [256, SEQ], BF16).ap() for i in range(2)]
    y_all = [nc.dram_tensor(f"y_all{i}", [512, SEQ], BF16).ap() for i in range(2)]
    hsp = nc.dram_tensor("hspill", [T, 1024], F32).ap()

    def hx_dst(t):
        i, r = divmod(t * 128, H)
        return hx_loc[i][r:r + 128, :]

    hx_rr = [Res(), Res()]

    def gather_half(i, rhxl):
        S.cc("AllGather", [hx_loc[i]], [hx_all[i]], GROUPS, reads=rhxl[i * (NT // 2):(i + 1) * (NT // 2)], writes=[hx_rr[i]])

    def gather_hx(rhxl, done=()):
        rr = hx_rr
        for i in range(2):
            if i not in done:
                gather_half(i, rhxl)
        return [(hx_all[0][0:H, :], 0, H, [rr[0]]), (hx_all[1][0:H, :], H, H, [rr[1]]),
                (hx_all[0][H:2 * H, :], 2 * H, H, [rr[0]]), (hx_all[1][H:2 * H, :], 3 * H, H, [rr[1]])]
    A = _mk_A(nc)
    ar = A["arena"]
    al = A["alloc"]
    S.cc_scratch = al("ccs", [128, 1], F32)
    eps = al("eps", [128, 1], F32)
    rconst = Res()
    S.dve(lambda e: e.memset(eps, LN_EPS), writes=[rconst])
    A["eps_ln"] = eps
    A["ident_bf"] = al("identF", [128, 128], BF16)
    S.dma(A["ident_bf"], kd["ident"][0], writes=[rconst])
    gidx = al("gidx", [128, 8], U32); rgidx = Res()
    S.dma(gidx, gidx_d, writes=[rgidx])
    base = ar.mark()
    rhsp = [Res() for _ in range(NT)]
    rhxl = [Res() for _ in range(NT)]
    lnp = al("lnp", [128, 2, 1024], F32); rln = Res()
    S.dma(lnp[:, 0, :], g_d.partition_broadcast(128), writes=[rln])
    S.dma(lnp[:, 1, :], b_d.partition_broadcast(128), writes=[rln])
    xt = [al(f"xt{i}", [128, 1024], F32) for i in range(NT)]; rxt = [Res() for _ in range(NT)]
    tmp = [al(f"tmp{i}", [128, 1024], F32) for i in range(4)]; rtmp = [Res() for _ in range(4)]
    hb = [al(f"hb{i}", [128, 1024], BF16) for i in range(8)]; rhb = [Res() for _ in range(8)]
    st = [al(f"st{i}", [128, 16], F32) for i in range(8)]; rst = [Res() for _ in range(8)]
    for t in range(NT):
        S.dma(xt[t], x_d[t * 128:(t + 1) * 128, :], writes=[rxt[t]])
    for t in range(NT):
        p = t % 8
        ln_tile(S, nc, A, xt[t], rxt[t], lnp[:, 0, :], lnp[:, 1, :], rln, xt[t], rxt[t], None, None, st[p], rst[p], t)
        S.act(lambda e, t=t: e.activation(out=hb[t % 8], in_=xt[t], func=AF.Copy), reads=[rxt[t]], writes=[rhb[t % 8]])
        S.dma(hsp[t * 128:(t + 1) * 128, :], xt[t], reads=[rxt[t]], writes=[rhsp[t]])
        S.dma(hx_dst(t), hb[t % 8], reads=[rhb[t % 8]], writes=[rhxl[t]])
        if t == NT // 2 - 1:
            gather_half(0, rhxl)
    hin_pieces = gather_hx(rhxl, done=(0,))
    import os
    NL = int(os.environ.get("FUSE_LAYERS", "2"))
    PH = os.environ.get("FUSE_PHASES", "MF")
    for l in range(NL):
        W = L[l]
        S.barrier(A["bar_scratch"])
        ar.reset(base)
        yslots = [y_loc[0][0:128, :], y_loc[0][128:256, :], y_loc[1][0:128, :], y_loc[1][128:256, :]]
        c = phase_M(nc, S, A, SEQ, l, hin_pieces, W["win"], W["pvec"], W["wab"], kd, rettab, yslots)
        ar.reset(base)
        hres = al("hres", [128, NT, 1024], F32)
        rh = [Res() for _ in range(NT)]
        wo_pre = al("wo_pre", [128, 8, 1024], BF16); rwo_pre = Res()
        S.dma(wo_pre, W["wout"].rearrange("(k p) n -> p k n", p=128), writes=[rwo_pre], eng="gpsimd", extra_deps=[c.last_proj])
        for t in range(NT):
            S.dma(hres[:, t, :], hsp[t * 128:(t + 1) * 128, :], reads=[rhsp[t]], writes=[rh[t]], eng="gpsimd", extra_deps=[c.last_proj])
        ry_all = [Res(), Res()]
        S.cc("AllGather", [y_loc[0]], [y_all[0]], GROUPS, reads=c.ry_by_mixer["A"] + c.ry_by_mixer["B"], writes=[ry_all[0]])
        S.cc("AllGather", [y_loc[1]], [y_all[1]], GROUPS, reads=c.ry_by_mixer["C"] + c.ry_by_mixer["D"], writes=[ry_all[1]])
        if PH == "M":
            continue
        S.barrier(A["bar_scratch"])
        last = (l == NL - 1)
        src = [ya.rearrange("c (h t) -> (c h) t", h=2) for ya in y_all]
        phase_F(nc, S, A, T, None, W["wout"], W["l1g"], W["l1b"], W["wup"], W["wdn"], W["l2g"], W["l2b"], hres, rh,
                out_f32_d=(out_d if last else hsp), out_bf16_d=(None if last else hx_dst),
                y_gather=(src, gidx, rgidx, ry_all), rout32=(None if last else rhsp), rout16=(None if last else rhxl),
                final_out=last, wo_pre=(wo_pre, rwo_pre),
                on_tile_done=(None if last else (lambda t: gather_half(0, rhxl) if t == NT // 2 - 1 else None)))
        if not last:
            hin_pieces = gather_hx(rhxl, done=(0,))
    if NL == 0 or PH == "M":
        S.barrier(A["bar_scratch"])
        ar.reset(base)
        tt = al("tt", [128, 1024], F32); rtt = Res()
        S.dma(tt, hsp[0:128, :], reads=rhsp, writes=[rtt])
        S.dma(out_d[0:128, :], tt, reads=[rtt], is_output=True)
    build_and_emit(nc, S)
    print("fused instr counts", {e: len(S.streams[e]) for e in ENGS}, "arena peak", ar.peak)
    return nc


def wout_perm_fused():
    idx = []
    for grp in ((0, 1), (2, 3)):
        for hh in range(2):
            for m in grp:
                idx.append(m * 256 + hh * 128 + np.arange(128))
    return np.concatenate(idx)


def kernel(**inputs):
    inp = {k: np.asarray(v) for k, v in inputs.items()}
    x = inp["x"]
    cores = list(range(8))
    Kh = [const_tables(hh, SEQ_FULL) for hh in range(2)]
    nc = build_fused(Kh[0])
    perm = wout_perm_fused()
    wo = [np.ascontiguousarray(inp["w_out"][l][perm, :]) for l in range(2)]
    im = []
    for c in cores:
        b, hh = c // 2, c % 2
        d = {"x": np.ascontiguousarray(x[b, hh * T_OWN:(hh + 1) * T_OWN, :]), "ln_g": inp["ln_in_g"], "ln_b": inp["ln_in_b"],
             "rettab": Kh[hh]["rettab"]}
        gi = np.zeros((128, 8), np.uint32)
        for kc in range(8):
            gi[:, kc] = ((kc % 4) * 128 + np.arange(128)) * 2 + hh
        d["gidx"] = gi
        for n in SMALLK:
            d["k_" + n] = Kh[hh][n]
        for l in range(2):
            pc = prep_layer_core(inp, l, hh)
            d[f"win{l}"] = pc["win"]; d[f"pvec{l}"] = pc["pvec"]; d[f"wab{l}"] = pc["wab"]
            d[f"w_out{l}"] = wo[l]; d[f"w_up{l}"] = inp["w_up"][l]; d[f"w_down{l}"] = inp["w_down"][l]
            d[f"l1g{l}"] = inp["ln1_g"][l]; d[f"l1b{l}"] = inp["ln1_b"][l]; d[f"l2g{l}"] = inp["ln2_g"][l]; d[f"l2b{l}"] = inp["ln2_b"][l]
        im.append(d)
    import os
    rr = run_bass_kernel_spmd(nc, im, core_ids=cores, trace=bool(os.environ.get("KERNEL_TRACE")))
    if os.environ.get("KERNEL_TRACE"):
        print("exec_time_ns", rr.exec_time_ns)
    res = rr.results
    out = np.zeros((4, SEQ_FULL, 1024), np.float32)
    for c in cores:
        b, hh = c // 2, c % 2
        out[b, hh * T_OWN:(hh + 1) * T_OWN, :] = res[c]["out"]
    return out
```

```python
import numpy as np
import ml_dtypes
import concourse.bass as bass
import concourse.mybir as mybir


F32 = mybir.dt.float32
BF16 = mybir.dt.bfloat16
AF = mybir.ActivationFunctionType
ALU = mybir.AluOpType
AX = mybir.AxisListType

ENGS = ("tensor", "vector", "scalar", "gpsimd", "sync")
EPOCH = 3000


class Res:
    __slots__ = ("name", "w", "r", "excl")

    def __init__(self, name="", excl=False):
        self.name = name
        self.w = None
        self.r = []
        self.excl = excl


class Instr:
    __slots__ = ("eng", "idx", "fn", "deps", "vc", "signal", "is_dma", "sem", "val", "pre", "order", "inc")

    def __init__(self, eng, idx, fn, is_dma=False):
        self.eng = eng
        self.idx = idx
        self.fn = fn
        self.deps = []
        self.vc = {}
        self.signal = False
        self.is_dma = is_dma
        self.sem = None
        self.val = None
        self.pre = None
        self.inc = 16


class Sched:
    def __init__(self, nc, n_dma_sems=32, same_engine_sync=True):
        self.nc = nc
        self.streams = {e: [] for e in ENGS}
        self.known = {e: {} for e in ENGS}
        self.known_dma = {e: set() for e in ENGS}
        self.n_dma_sems = n_dma_sems
        self.dma_count = 0
        self.dma_cnt_by_eng = {}
        self.dma_last = {}
        self.same_engine_sync = same_engine_sync
        self.out_dmas = []
        self.pending = {}
        self.dmas_since_barrier = []

    def add(self, eng, fn, reads=(), writes=(), is_dma=False, extra_deps=(), own_sem=False):
        st = self.streams[eng]
        ins = Instr(eng, len(st), fn, is_dma)
        self.order = getattr(self, 'order', 0) + 1
        ins.order = self.order
        if any(r.excl for r in reads):
            writes = list(writes) + [r for r in reads if r.excl and r not in writes]
            reads = [r for r in reads if not r.excl]
        deps = list(extra_deps) + self.pending.pop(eng, [])
        for r in reads:
            if r.w is not None:
                deps.append(r.w)
        for w in writes:
            if w.w is not None:
                deps.append(w.w)
            deps.extend(w.r)
        known = self.known[eng]
        kd = self.known_dma[eng]
        need = {}
        vc = {}
        for d in deps:
            if d is ins:
                continue
            if d.is_dma:
                if id(d) in kd:
                    continue
                need[("dma", id(d))] = d
            else:
                if d.eng == eng and (eng == "tensor" or not self.same_engine_sync):
                    continue
                if known.get(d.eng, -1) >= d.idx:
                    continue
                k = ("e", d.eng)
                if k not in need or need[k].idx < d.idx:
                    need[k] = d
        for k, d in need.items():
            d.signal = True
            ins.deps.append(d)
            if d.is_dma:
                kd.add(id(d))
            for e2, i2 in d.vc.items():
                if known.get(e2, -1) < i2:
                    known[e2] = i2
            if not d.is_dma:
                if known.get(d.eng, -1) < d.idx:
                    known[d.eng] = d.idx
        ins.vc = dict(known)
        if is_dma and own_sem:
            ins.inc = 1
            ins.sem = "own"
        elif is_dma:
            self.dmas_since_barrier.append(ins)
            half = self.n_dma_sems // 2
            cnt = self.dma_cnt_by_eng.get(eng, 0)
            self.dma_cnt_by_eng[eng] = cnt + 1
            slot = (cnt % half) + (half if eng == "gpsimd" else 0)
            self.dma_count += 1
            prev = self.dma_last.get(slot)
            ins.pre = prev
            self.dma_last[slot] = ins
            ins.sem = slot
        for r in reads:
            r.r.append(ins)
        for w in writes:
            w.w = ins
            w.r = []
        st.append(ins)
        return ins

    def barrier(self, scratch_ap):
        self.flush_cc()
        deps = []
        for e in ("tensor", "scalar", "gpsimd", "vector"):
            st = [i for i in self.streams[e] if not i.is_dma]
            if st:
                deps.append(st[-1])
        deps.extend(d for d in self.dmas_since_barrier if d.inc != 1)
        self.dmas_since_barrier = []
        if not hasattr(self, "bar_res"):
            self.bar_res = Res()
        b = self.add("vector", lambda e: e.memset(scratch_ap, 0.0), writes=[self.bar_res], extra_deps=deps)
        for e in ("scalar", "gpsimd", "sync", "tensor"):
            self.pending[e] = [b]
        return b

    def pe(self, fn, reads=(), writes=()):
        return self.add("tensor", fn, reads, writes)

    def dve(self, fn, reads=(), writes=()):
        return self.add("vector", fn, reads, writes)

    def act(self, fn, reads=(), writes=()):
        return self.add("scalar", fn, reads, writes)

    def pool(self, fn, reads=(), writes=()):
        return self.add("gpsimd", fn, reads, writes)

    def cc(self, kind, ins_, outs, groups, reads=(), writes=()):
        tmp = Res()
        i = self.add("gpsimd", lambda e: e.collective_compute(kind, mybir.AluOpType.bypass, replica_groups=groups, ins=ins_, outs=outs),
                     reads, [tmp], is_dma=True, own_sem=True)
        self.pending_cc = getattr(self, "pending_cc", [])
        self.pending_cc.append((tmp, list(writes)))
        return i

    def flush_cc(self):
        sc = getattr(self, "cc_scratch", None)
        for tmp, writes in getattr(self, "pending_cc", []):
            if not hasattr(self, "cc_res"):
                self.cc_res = Res()
            self.add("gpsimd", lambda e: e.memset(sc, 0.0), reads=[tmp], writes=list(writes) + [self.cc_res])
        self.pending_cc = []

    def dma(self, out, in_, reads=(), writes=(), is_output=False, eng="sync", extra_deps=(), **kw):
        ins = self.add(eng, lambda e: e.dma_start(out=out, in_=in_, **kw), reads, writes, is_dma=True, extra_deps=extra_deps)
        if is_output:
            self.out_dmas.append(ins)
        return ins


def build_and_emit(nc, sched):
    for e in ENGS:
        cnt = 0
        sems = []
        for ins in sched.streams[e]:
            if ins.is_dma or not ins.signal:
                continue
            ep = cnt // EPOCH
            if ep >= len(sems):
                sems.append(nc.alloc_semaphore(f"s_{e}_{ep}"))
            cnt += 1
            ins.sem = sems[ep]
            ins.val = cnt - ep * EPOCH
    n = sched.n_dma_sems
    dma_sems = [nc.alloc_semaphore(f"s_dma_{i}") for i in range(n)]
    dma_vals = [0] * n
    dmas = []
    for e in ENGS:
        for ins in sched.streams[e]:
            if ins.is_dma:
                dmas.append(ins)
    dmas.sort(key=lambda i: i.order)
    for ins in dmas:
        if ins.sem == "own":
            ins.sem = nc.alloc_semaphore(f"s_cc_{ins.order}")
            ins.val = 1
            continue
        slot = ins.sem
        dma_vals[slot] += ins.inc
        ins.sem = dma_sems[slot]
        ins.val = dma_vals[slot]

    final_waits = list(sched.out_dmas)

    def run_stream(ename, eng):
        for ins in sched.streams[ename]:
            for d in ins.deps:
                eng.wait_ge(d.sem, d.val)
            if ins.is_dma and ins.pre is not None:
                eng.wait_ge(ins.pre.sem, ins.pre.val)
            bi = ins.fn(eng)
            if ins.is_dma:
                bi.then_inc(ins.sem, ins.inc)
            elif ins.signal:
                bi.then_inc(ins.sem, 1)
        if ename == "sync":
            for d in final_waits:
                eng.wait_ge(d.sem, d.val)

    with nc.Block() as block:
        @block.tensor
        def _(eng):
            run_stream("tensor", eng)

        @block.vector
        def _(eng):
            run_stream("vector", eng)

        @block.scalar
        def _(eng):
            run_stream("scalar", eng)

        @block.gpsimd
        def _(eng):
            run_stream("gpsimd", eng)

        @block.sync
        def _(eng):
            run_stream("sync", eng)


class Arena:
    def __init__(self, nc, base=16384, limit=229312):
        self.nc, self.off, self.limit = nc, base, limit
        self.n = 0
        self.peak = base

    def alloc(self, name, shape, dtype):
        nbytes = int(np.prod(shape[1:])) * mybir.dt.size(dtype)
        off = (self.off + 63) // 64 * 64
        assert off + nbytes <= self.limit, f"arena overflow allocating {name} {shape}: {off}+{nbytes} > {self.limit}"
        self.off = off + nbytes
        self.peak = max(self.peak, self.off)
        self.n += 1
        return self.nc.alloc_sbuf_tensor_at(f"ar{self.n}_{name}", shape, dtype, offset=off).ap()

    def mark(self):
        return self.off

    def reset(self, m):
        self.off = m


D = 1024
DFF = 4096
ALPHA = 4 ** 0.25
LN_EPS = 1e-5


def ln_tile(S, nc, A, z_ap, rz, g_bc, b_bc, rgb, out_ap, rout, tmp, rtmp, st, rst, tagid):
    if tmp is None:
        tmp, rtmp = z_ap, rz
    S.dve(lambda e: e.bn_stats(out=st[:, 0:6], in_=z_ap[:, 0:512]), reads=[rz], writes=[rst])
    S.dve(lambda e: e.bn_stats(out=st[:, 6:12], in_=z_ap[:, 512:1024]), reads=[rz], writes=[rst])
    S.dve(lambda e: e.bn_aggr(out=st[:, 12:14], in_=st[:, 0:12]), reads=[rst], writes=[rst])
    S.act(lambda e: e.activation(out=st[:, 14:15], in_=st[:, 13:14], func=AF.Sqrt, bias=A["eps_ln"], scale=1.0),
          reads=[rst], writes=[rst])
    S.dve(lambda e: e.reciprocal(out=st[:, 14:15], in_=st[:, 14:15]), reads=[rst], writes=[rst])
    S.dve(lambda e: e.tensor_scalar(out=st[:, 15:16], in0=st[:, 12:13], scalar1=st[:, 14:15], scalar2=-1.0,
                                    op0=ALU.mult, op1=ALU.mult), reads=[rst], writes=[rst])
    S.act(lambda e: e.activation(out=tmp, in_=z_ap, func=AF.Identity, bias=st[:, 15:16], scale=st[:, 14:15]),
          reads=[rz, rst], writes=[rtmp])
    S.dve(lambda e: e.tensor_tensor(out=tmp, in0=tmp, in1=g_bc, op=ALU.mult), reads=[rtmp, rgb], writes=[rtmp])
    S.pool(lambda e: e.tensor_tensor(out=out_ap, in0=tmp, in1=b_bc, op=ALU.add), reads=[rtmp, rgb], writes=[rout])


def phase_F(nc, S, A, T, yT_dram, w_out_d, ln1g_d, ln1b_d, w_up_d, w_down_d, ln2g_d, ln2b_d,
            hres, rh, out_f32_d=None, out_bf16_d=None, y_gather=None, rout32=None, rout16=None, final_out=True, on_tile_done=None, wo_pre=None):
    NT = T // 128
    NB = T // 512
    al = A["alloc"]
    ident = A["ident_bf"]
    yT = al("yT", [128, 8, T], BF16)
    ry = [Res() for _ in range(NT)]
    lnp = al("lnp", [128, 2, 1024], F32)
    rln = Res()
    if y_gather is None:
        S.dma(yT, yT_dram.rearrange("(k p) t -> p k t", p=128), writes=ry)
    else:
        src, idx, ridx, rsrc = y_gather
        for kc in range(8):
            S.add("gpsimd", lambda e, kc=kc: e.indirect_dma_start(out=yT[:, kc, :], out_offset=None, in_=src[kc // 4],
                  in_offset=bass.IndirectOffsetOnAxis(ap=idx[:, kc:kc + 1], axis=0)), reads=[rsrc[kc // 4], ridx], writes=ry, is_dma=True)
    for i, d in enumerate((ln1g_d, ln1b_d)):
        S.dma(lnp[:, i, :], d.partition_broadcast(128), writes=[rln])
    NS = 4
    wu = [al(f"wu{i}", [128, 8, 1024], BF16) for i in range(2)]
    wd = [al(f"wd{i}", [128, 8, 1024], BF16) for i in range(2)]
    rwu = [Res(), Res()]
    rwd = [Res(), Res()]

    def load_ffn(s):
        b = s % 2
        S.dma(wu[b], w_up_d[:, s * 1024:(s + 1) * 1024].rearrange("(k p) f -> p k f", p=128), writes=[rwu[b]], eng="gpsimd")
        S.dma(wd[b], w_down_d[s * 1024:(s + 1) * 1024, :].rearrange("(k p) n -> p k n", p=128), writes=[rwd[b]], eng="gpsimd")

    if wo_pre is None:
        wo = wd[1]
        rwo = rwd[1]
        S.dma(wo, w_out_d.rearrange("(k p) n -> p k n", p=128), writes=[rwo], eng="gpsimd")
        load_ffn(0)
    else:
        wo, rwo = wo_pre
        load_ffn(0)
        load_ffn(1)
    tmp = [None] * 4
    rtmp = [None] * 4
    st = [al(f"st{i}", [128, 16], F32) for i in range(4)]
    rst = [Res() for _ in range(4)]
    ps = A["psum"]
    rps = A["rpsum"]
    psT = ps[7].bitcast(BF16)
    h1T = yT

    hb3 = [al(f"hb3_{i}", [128, 1024], BF16) for i in range(3)]
    rhb3 = [Res() for _ in range(3)]
    hb = [hb3[0], hb3[1]]
    rhb = [rhb3[0], rhb3[1]]
    def ln1_step(t):
        if t < NT:
            p = t % 2
            for half in range(2):
                bank = 2 * p + half
                for kc in range(8):
                    S.pe(lambda e, kc=kc, half=half, bank=bank, t=t: e.matmul(
                        ps[bank], lhsT=yT[:, kc, t * 128:(t + 1) * 128], rhs=wo[:, kc, half * 512:(half + 1) * 512],
                        start=(kc == 0), stop=(kc == 7)), reads=[ry[t], rwo], writes=[rps[bank]])
                S.dve(lambda e, half=half, bank=bank, t=t: e.scalar_tensor_tensor(
                    out=hres[:, t, half * 512:(half + 1) * 512], in0=hres[:, t, half * 512:(half + 1) * 512],
                    scalar=ALPHA, in1=ps[bank], op0=ALU.mult, op1=ALU.add), reads=[rh[t], rps[bank]], writes=[rh[t]])
            ln_tile(S, nc, A, hres[:, t, :], rh[t], lnp[:, 0, :], lnp[:, 1, :], rln, hres[:, t, :], rh[t],
                    None, None, st[t % 4], rst[t % 4], t)
            S.act(lambda e, t=t: e.activation(out=hb3[t % 3], in_=hres[:, t, :], func=AF.Copy), reads=[rh[t]], writes=[rhb3[t % 3]])
        if t >= 2:
            u = t - 2
            for kc in range(8):
                S.pe(lambda e, kc=kc, u=u: e.transpose(out=psT[:, kc * 128:(kc + 1) * 128], in_=hb3[u % 3][:, kc * 128:(kc + 1) * 128],
                                                       identity=ident), reads=[rhb3[u % 3]], writes=[rps[7]])
            S.act(lambda e, u=u: e.activation(out=h1T[:, :, u * 128:(u + 1) * 128],
                                              in_=psT.rearrange("p (k c) -> p k c", k=8), func=AF.Copy),
                  reads=[rps[7]], writes=[ry[u]])


    n_steps = NT + 2
    prologue = min(n_steps, 6)
    for t in range(prologue):
        ln1_step(t)
    ln1_next = [prologue]

    def ln1_more(k):
        for _ in range(k):
            if ln1_next[0] < n_steps:
                ln1_step(ln1_next[0])
                ln1_next[0] += 1

    if NB < 4:
        ln1_more(n_steps)
    aT = [al("aT", [128, 8, 512], BF16)] * 2
    raT_fc = [Res() for _ in range(8)]
    rl = [al("rl0", [128, 512], BF16)] * 2
    rrl = [Res()] * 2
    cnt = 0
    deferred = []

    def ln2_emit():
        while deferred:
            t = deferred.pop(0)
            p = t % 2
            ln_tile(S, nc, A, hres[:, t, :], rh[t], lnp[:, 0, :], lnp[:, 1, :], rln, hres[:, t, :], rh[t],
                    None, None, st[t % 4], rst[t % 4], t)
            if out_f32_d is not None:
                S.dma(out_f32_d[t * 128:(t + 1) * 128, :], hres[:, t, :], reads=[rh[t]], writes=([rout32[t]] if rout32 else []), is_output=final_out)
            if out_bf16_d is not None:
                S.act(lambda e, t=t, p=p: e.activation(out=hb[p], in_=hres[:, t, :], func=AF.Copy),
                      reads=[rh[t]], writes=[rhb[p]])
                S.dma(out_bf16_d(t) if callable(out_bf16_d) else out_bf16_d[t * 128:(t + 1) * 128, :], hb[p], reads=[rhb[p]],
                      writes=([rout16[t]] if rout16 else []), is_output=final_out)
            if on_tile_done is not None:
                on_tile_done(t)

    for s in range(NS):
        b = s % 2
        for tb in range(NB):
            ab = cnt % 2
            cnt += 1
            for fc in range(8):
                bank = 4 + (fc % 2)
                for kc in range(8):
                    S.pe(lambda e, kc=kc, fc=fc, bank=bank, tb=tb, b=b: e.matmul(
                        ps[bank], lhsT=wu[b][:, kc, fc * 128:(fc + 1) * 128], rhs=h1T[:, kc, tb * 512:(tb + 1) * 512],
                        start=(kc == 0), stop=(kc == 7)),
                        reads=[rwu[b]] + ry[tb * 4:(tb + 1) * 4], writes=[rps[bank]])
                rr = fc % 2
                S.act(lambda e, bank=bank, rr=rr: e.activation(out=rl[rr], in_=ps[bank], func=AF.Relu),
                      reads=[rps[bank]], writes=[rrl[rr]])
                if fc % 2 == 1:
                    S.dve(lambda e, fc=fc, bank=bank, ab=ab, rr=rr: e.tensor_tensor(
                        out=aT[ab][:, fc, :], in0=ps[bank], in1=rl[rr], op=ALU.mult),
                        reads=[rps[bank], rrl[rr]], writes=[raT_fc[fc]])
                else:
                    S.pool(lambda e, fc=fc, ab=ab, rr=rr: e.tensor_tensor(
                        out=aT[ab][:, fc, :], in0=rl[rr], in1=rl[rr], op=ALU.mult),
                        reads=[rrl[rr]], writes=[raT_fc[fc]])
            ln2_emit()
            for tt in range(4):
                t = tb * 4 + tt
                if s == 0:
                    ln1_more(1)
                for half in range(2):
                    bank = (tt % 2) * 2 + half
                    for fc in range(8):
                        S.pe(lambda e, fc=fc, half=half, bank=bank, tt=tt, ab=ab, b=b: e.matmul(
                            ps[bank], lhsT=aT[ab][:, fc, tt * 128:(tt + 1) * 128], rhs=wd[b][:, fc, half * 512:(half + 1) * 512],
                            start=(fc == 0), stop=(fc == 7)), reads=[raT_fc[fc], rwd[b]], writes=[rps[bank]])
                    sl = hres[:, t, half * 512:(half + 1) * 512]
                    if s == 0:
                        S.dve(lambda e, sl=sl, bank=bank: e.scalar_tensor_tensor(
                            out=sl, in0=sl, scalar=ALPHA, in1=ps[bank], op0=ALU.mult, op1=ALU.add),
                            reads=[rh[t], rps[bank]], writes=[rh[t]])
                    else:
                        S.dve(lambda e, sl=sl, bank=bank: e.tensor_tensor(out=sl, in0=sl, in1=ps[bank], op=ALU.add),
                              reads=[rh[t], rps[bank]], writes=[rh[t]])
                if s == NS - 1:
                    deferred.append(t)
        if s == 0:
            ln1_more(n_steps)
            if wo_pre is None:
                load_ffn(1)
            for i, d in enumerate((ln2g_d, ln2b_d)):
                S.dma(lnp[:, i, :], d.partition_broadcast(128), writes=[rln])
        if s + 2 < NS:
            load_ffn(s + 2)
    ln2_emit()


NORM_EPS = 1e-6
SL = {n: i for i, n in enumerate(
    ["A_x", "A_g", "B_q", "B_qsw", "B_k", "B_ksw", "B_g", "C_q", "C_k", "D_q", "D_f", "D_g", "B_v", "C_v", "D_v"])}
NSL = 15
PV = {n: i for i, n in enumerate(
    ["cw0", "cw1", "cw2", "cw3", "cb", "ba", "bx", "lam", "retg", "hgg", "lb0", "lbl", "gam64", "nba", "nbx", "c1", "c1x2", "oml", "lnoml", "tmp0", "tmp1"])}
NPV = 24


class Ctx:
    pass


def setup_M(nc, S, A, SEQ, layer, hin_d, win_d, pvec_d, wab_d, consts):
    c = Ctx()
    c.nc, c.S, c.A, c.SEQ, c.layer = nc, S, A, SEQ, layer
    al = A["alloc"]
    c.NB = SEQ // 512
    c.NT = SEQ // 128
    c.ps, c.rps = A["psum"], A["rpsum"]
    c.hT = al("hT", [128, 8, SEQ], BF16)
    c.win = al("win", [128, 8, NSL * 128], BF16)
    c.K = {}
    c.rK = Res()
    for name, (d, shape, dtype) in consts.items():
        t = al("k_" + name, shape, dtype)
        S.dma(t, d, writes=[c.rK])
        c.K[name] = t
    wv = win_d.rearrange("(k p) n -> p k n", p=128)
    c.rwin_parts = []
    for (a, b) in ((12, 15), (0, 2), (2, 12)):
        r = Res()
        S.dma(c.win[:, :, a * 128:b * 128], wv[:, :, a * 128:b * 128], writes=[r], eng="gpsimd")
        c.rwin_parts.append((a, b, r))
    c.rwin_for = lambda j0, j1: [r for (a, b, r) in c.rwin_parts if a < j1 and j0 < b]
    c.pv = al("pv", [128, NPV], F32)
    c.rpv = Res()
    S.dma(c.pv[:, 0:13], pvec_d, writes=[c.rpv])
    c.wab = al("wab", [128, 2, 128], BF16)
    c.rwab = Res()
    S.dma(c.wab, wab_d, writes=[c.rwab], eng="gpsimd")
    pieces = hin_d if isinstance(hin_d, (list, tuple)) else [(hin_d, 0, SEQ, list(A.get("rhin", [])))]
    stage = [al(f"hstage{i}", [128, 1024], BF16) for i in range(3)]
    rstage = [Res() for _ in range(3)]
    c.rhT_t = [Res() for _ in range(c.NT)]
    c.rhT = lambda a, b: [c.rhT_t[t] for t in range(a // 128, (b + 127) // 128)]
    c.vtm = al("vtm", [128, c.NT, 384], BF16)
    c.rv = [Res() for _ in range(c.NT)]

    def vproj(t):
        bank = t % 2
        for kc in range(8):
            S.pe(lambda e, kc=kc, t=t, bank=bank: e.matmul(
                c.ps[bank][:, 0:384], lhsT=c.hT[:, kc, t * 128:(t + 1) * 128], rhs=c.win[:, kc, 12 * 128:15 * 128],
                start=(kc == 0), stop=(kc == 7)), reads=c.rhT(t * 128, (t + 1) * 128) + c.rwin_for(12, 15), writes=[c.rps[bank]])
        S.act(lambda e, t=t, bank=bank: e.activation(out=c.vtm[:, t, :], in_=c.ps[bank][:, 0:384], func=AF.Copy),
              reads=[c.rps[bank]], writes=[c.rv[t]])

    tiles = []
    for (src, t0, n, rsrc) in pieces:
        for j in range(n // 128):
            tiles.append((src[j * 128:(j + 1) * 128, :], t0 // 128 + j, rsrc))
    for i, (src_t, t, rsrc) in enumerate(tiles):
        sb = i % 3
        S.dma(stage[sb], src_t, reads=rsrc, writes=[rstage[sb]])
        bank = 6 + i % 2
        psT = c.ps[bank].bitcast(BF16)
        for kc in range(8):
            S.pe(lambda e, kc=kc, sb=sb, psT=psT: e.transpose(out=psT[:, kc * 128:(kc + 1) * 128], in_=stage[sb][:, kc * 128:(kc + 1) * 128],
                                                              identity=c.K["ident"]), reads=[rstage[sb], c.rK], writes=[c.rps[bank]])
        ev = S.act if i % 2 == 0 else S.dve
        if i % 2 == 0:
            S.act(lambda e, t=t, psT=psT: e.activation(out=c.hT[:, :, t * 128:(t + 1) * 128], in_=psT.rearrange("p (k c) -> p k c", k=8), func=AF.Copy),
                  reads=[c.rps[bank]], writes=[c.rhT_t[t]])
        else:
            S.dve(lambda e, t=t, psT=psT: e.tensor_copy(out=c.hT[:, :, t * 128:(t + 1) * 128], in_=psT.rearrange("p (k c) -> p k c", k=8)),
                  reads=[c.rps[bank]], writes=[c.rhT_t[t]])
        if i >= 2:
            vproj(tiles[i - 2][1])
    for (src_t, t, rsrc) in tiles[-2:]:
        vproj(t)
    c.pcount = 0
    c.ry_list = []
    c.ry_by_mixer = {}

    def new_yres():
        r = Res()
        c.ry_list.append(r)
        return r
    c.new_yres = new_yres
    return c


def proj_fm(c, slname, blk, nblk=1):
    S = c.S
    j = SL[slname]
    banks = getattr(c, "proj_banks", (0, 1))
    bank = banks[c.pcount % len(banks)]
    c.pcount += 1
    for kc in range(8):
        c.last_proj = S.pe(lambda e, kc=kc, j=j, blk=blk, bank=bank: e.matmul(
            c.ps[bank], lhsT=c.win[:, kc, j * 128:(j + 1) * 128], rhs=c.hT[:, kc, blk * 512:(blk + 1) * 512],
            start=(kc == 0), stop=(kc == 7)), reads=c.rhT(blk * 512, (blk + 1) * 512) + c.rwin_for(j, j + 1), writes=[c.rps[bank]])
    return bank


def small_params(c):
    S, pv, rpv = c.S, c.pv, c.rpv
    col = lambda n: pv[:, PV[n]:PV[n] + 1]
    S.act(lambda e: e.activation(out=col("tmp0"), in_=col("lam"), func=AF.Exp, scale=-1.0), reads=[rpv], writes=[rpv])
    S.act(lambda e: e.activation(out=col("tmp0"), in_=col("tmp0"), func=AF.Ln, bias=c.K["one"], scale=1.0), reads=[rpv, c.rK], writes=[rpv])
    S.dve(lambda e: e.tensor_scalar(out=col("c1"), in0=col("tmp0"), scalar1=-8.0, scalar2=None, op0=ALU.mult), reads=[rpv], writes=[rpv])
    S.dve(lambda e: e.tensor_scalar(out=col("c1x2"), in0=col("tmp0"), scalar1=-16.0, scalar2=None, op0=ALU.mult), reads=[rpv], writes=[rpv])
    if c.layer == 0:
        S.dve(lambda e: e.memset(col("oml"), 1.0), writes=[rpv])
        S.dve(lambda e: e.memset(col("lnoml"), 0.0), writes=[rpv])
    else:
        S.dve(lambda e: e.tensor_tensor(out=col("tmp1"), in0=col("lbl"), in1=col("lb0"), op=ALU.subtract), reads=[rpv], writes=[rpv])
        S.act(lambda e: e.activation(out=col("tmp1"), in_=col("tmp1"), func=AF.Exp), reads=[rpv], writes=[rpv])
        S.act(lambda e: e.activation(out=col("lnoml"), in_=col("tmp1"), func=AF.Ln, bias=c.K["one"], scale=1.0), reads=[rpv, c.rK], writes=[rpv])
        S.dve(lambda e: e.tensor_scalar(out=col("lnoml"), in0=col("lnoml"), scalar1=-1.0, scalar2=None, op0=ALU.mult), reads=[rpv], writes=[rpv])
        S.act(lambda e: e.activation(out=col("oml"), in_=col("lnoml"), func=AF.Exp), reads=[rpv], writes=[rpv])


def mixer_A(c, yT_d):
    S, A, SEQ = c.S, c.A, c.SEQ
    al = A["alloc"]
    pv, rpv = c.pv, c.rpv
    col = lambda n: pv[:, PV[n]:PV[n] + 1]
    NH = 2 if SEQ >= 1024 else 1
    L = SEQ // NH
    NBH = L // 512
    xT = al("A_xT", [128, 3 + SEQ], F32)
    rxT = Res()
    XC = al("A_XC", [128, L], F32); rXC = Res()
    R = al("A_R", [128, L], F32); rR = Res()
    TH = al("A_TH", [128, L], F32); rTH = Res()
    AA = al("A_AA", [128, L], F32); rAA = Res()
    XCB = al("A_XCB", [128, L], BF16); rXCB = Res()
    ii = [al(f"A_ii{i}", [128, 512], F32) for i in range(2)]; rii = [Res(), Res()]
    gg = [al(f"A_gg{i}", [128, 512], F32) for i in range(2)]; rgg = [Res(), Res()]
    yst = [al(f"A_y{i}", [128, 512], BF16) for i in range(2)]; ryst = [Res(), Res()]
    carry = al("A_carry", [128, 1], F32); rcar = Res()
    S.dve(lambda e: e.memset(xT[:, 0:3], 0.0), writes=[rxT])
    S.dve(lambda e: e.memset(carry, 0.0), writes=[rcar])
    ps, rps = c.ps, c.rps
    for hf in range(NH):
        t0 = hf * L
        for b in range(NBH):
            blk = hf * NBH + b
            bank = proj_fm(c, "A_x", blk)
            S.act(lambda e, bank=bank, blk=blk: e.activation(out=xT[:, 3 + blk * 512:3 + (blk + 1) * 512], in_=ps[bank], func=AF.Copy),
                  reads=[rps[bank]], writes=[rxT])
        S.dve(lambda e, t0=t0: e.tensor_scalar(out=XC, in0=xT[:, t0 + 3:t0 + 3 + L], scalar1=col("cw3"), scalar2=col("cb"),
                                               op0=ALU.mult, op1=ALU.add), reads=[rxT, rpv], writes=[rXC])
        for w in range(3):
            S.dve(lambda e, t0=t0, w=w: e.scalar_tensor_tensor(out=XC, in0=xT[:, t0 + w:t0 + w + L], scalar=col(f"cw{w}"), in1=XC,
                                                              op0=ALU.mult, op1=ALU.add), reads=[rxT, rpv, rXC], writes=[rXC])
        S.act(lambda e: e.activation(out=XCB, in_=XC, func=AF.Copy), reads=[rXC], writes=[rXCB])
        for b in range(NBH):
            sl = slice(b * 512, (b + 1) * 512)
            S.pe(lambda e, sl=sl: e.matmul(ps[2], lhsT=c.wab[:, 0, :], rhs=XCB[:, sl], start=True, stop=True),
                 reads=[c.rwab, rXCB], writes=[rps[2]])
            S.pe(lambda e, sl=sl: e.matmul(ps[3], lhsT=c.wab[:, 1, :], rhs=XCB[:, sl], start=True, stop=True),
                 reads=[c.rwab, rXCB], writes=[rps[3]])
            S.act(lambda e, sl=sl: e.activation(out=R[:, sl], in_=ps[2], func=AF.Sigmoid, bias=col("ba"), scale=1.0),
                  reads=[rps[2], rpv], writes=[rR])
            S.act(lambda e, b=b: e.activation(out=ii[b % 2], in_=ps[3], func=AF.Sigmoid, bias=col("bx"), scale=1.0),
                  reads=[rps[3], rpv], writes=[rii[b % 2]])
            S.act(lambda e, sl=sl: e.activation(out=TH[:, sl], in_=R[:, sl], func=AF.Tanh, scale=col("c1")),
                  reads=[rR, rpv], writes=[rTH])
            S.dve(lambda e, sl=sl, b=b: e.tensor_tensor(out=XC[:, sl], in0=XC[:, sl], in1=ii[b % 2], op=ALU.mult),
                  reads=[rXC, rii[b % 2]], writes=[rXC])
        S.act(lambda e: e.activation(out=AA, in_=R, func=AF.Exp, scale=col("c1")), reads=[rR, rpv], writes=[rAA])
        S.act(lambda e: e.activation(out=R, in_=R, func=AF.Exp, scale=col("c1x2")), reads=[rR, rpv], writes=[rR])
        S.dve(lambda e: e.scalar_tensor_tensor(out=TH, in0=R, scalar=1.0, in1=TH, op0=ALU.add, op1=ALU.mult),
              reads=[rR, rTH], writes=[rTH])
        S.act(lambda e: e.activation(out=TH, in_=TH, func=AF.Ln, scale=-1.0), reads=[rTH], writes=[rTH])
        S.act(lambda e: e.activation(out=TH, in_=TH, func=AF.Exp, scale=0.5), reads=[rTH], writes=[rTH])
        S.dve(lambda e: e.tensor_tensor(out=XC, in0=XC, in1=TH, op=ALU.mult), reads=[rXC, rTH], writes=[rXC])
        S.dve(lambda e: e.tensor_tensor_scan(out=R, data0=AA, data1=XC, initial=carry, op0=ALU.mult, op1=ALU.add),
              reads=[rAA, rXC, rcar], writes=[rR])
        S.dve(lambda e: e.tensor_copy(out=carry, in_=R[:, L - 1:L]), reads=[rR], writes=[rcar])
        for b in range(NBH):
            blk = hf * NBH + b
            sl = slice(b * 512, (b + 1) * 512)
            bank = proj_fm(c, "A_g", blk)
            S.act(lambda e, bank=bank, b=b: e.activation(out=gg[b % 2], in_=ps[bank], func=AF.Gelu_apprx_tanh),
                  reads=[rps[bank]], writes=[rgg[b % 2]])
            S.dve(lambda e, sl=sl, b=b: e.tensor_tensor(out=yst[b % 2], in0=gg[b % 2], in1=R[:, sl], op=ALU.mult),
                  reads=[rgg[b % 2], rR], writes=[ryst[b % 2]])
            S.dma(yT_d[:, blk * 512:(blk + 1) * 512], yst[b % 2], reads=[ryst[b % 2]], writes=[c.new_yres()], is_output=True)


def gla(c, tag, qt, rqt, kt, rkt, En, rEn, voff, gslice, gcol, groupnorm, yT_d):
    S, A, SEQ = c.S, c.A, c.SEQ
    al = A["alloc"]
    ps, rps = c.ps, c.rps
    NCH = SEQ // 64
    NB = SEQ // 512
    pv, rpv = c.pv, c.rpv
    ident = c.K["ident"]
    psT = ps[6].bitcast(BF16)
    KVs = al(tag + "_KVs", [128, 64, NCH], F32); rKV = Res()
    Ss = al(tag + "_Ss", [128, 64, NCH], BF16); rSs = Res()
    En0 = al(tag + "_En0", [128, NCH], F32); rEn0 = Res()
    ktm4 = al(tag + "_ktm4", [128, 4, 128], BF16); rktm = Res()
    import os
    STOP = int(os.environ.get("GLA_STOP", "99"))
    if STOP == 0:
        S.dma(yT_d, qt, reads=[rqt], is_output=True)
        return
    for tg in range(NB):
        for i in range(4):
            t = tg * 4 + i
            S.pe(lambda e, i=i, t=t: e.transpose(out=psT[:, i * 128:(i + 1) * 128], in_=kt[:, t * 128:(t + 1) * 128], identity=ident),
                 reads=[rkt, c.rK], writes=[rps[6]])
        S.act(lambda e: e.activation(out=ktm4, in_=psT[:, 0:512].rearrange("p (i c) -> p i c", c=128), func=AF.Copy),
              reads=[rps[6]], writes=[rktm])
        for i in range(4):
            t = tg * 4 + i
            S.pe(lambda e, i=i, t=t: e.matmul(ps[2][:, i * 128:(i + 1) * 128], lhsT=ktm4[0:64, i, :], rhs=c.vtm[0:64, t, voff:voff + 128],
                                             start=True, stop=True), reads=[rktm, c.rv[t]], writes=[rps[2]])
            S.pe(lambda e, i=i, t=t: e.matmul(ps[3][:, i * 128:(i + 1) * 128], lhsT=ktm4[64:128, i, :], rhs=c.vtm[64:128, t, voff:voff + 128],
                                             start=True, stop=True), reads=[rktm, c.rv[t]], writes=[rps[3]])
        for par in range(2):
            for hl in range(2):
                rows = slice(hl * 64, (hl + 1) * 64)
                n0 = 8 * tg + par
                S.dve(lambda e, par=par, hl=hl, rows=rows, n0=n0, tg=tg: e.tensor_tensor(
                    out=KVs[rows, :, n0:8 * tg + 8:2].rearrange("p e n -> p n e"),
                    in0=ps[2 + par][rows, :].rearrange("p (i c) -> p i c", c=128)[:, :, hl * 64:(hl + 1) * 64],
                    in1=En[rows, n0:8 * tg + 8:2].unsqueeze(2).broadcast_to([64, 4, 64]), op=ALU.mult),
                    reads=[rps[2 + par], rEn], writes=[rKV])
    if STOP == 1:
        S.dma(yT_d[:, 0:64 * NCH // 2], KVs.rearrange("p e n -> p (e n)").bitcast(BF16)[:, 0:64 * NCH // 2], reads=[rKV], writes=[c.new_yres()], is_output=True)
        return
    S.dve(lambda e: e.tensor_copy(out=En0, in_=En), reads=[rEn], writes=[rEn0])
    S.dve(lambda e: e.memset(En0[:, 0:1], 0.0), writes=[rEn0])
    rSse = [Res() for _ in range(64)]
    for ee in range(64):
        S.dve(lambda e, ee=ee: e.tensor_tensor_scan(out=Ss[:, ee, :], data0=En0, data1=KVs[:, ee, :],
                                                    initial=0.0, op0=ALU.mult, op1=ALU.add), reads=[rEn0, rKV], writes=[rSse[ee]])
    if STOP == 2:
        S.dma(yT_d[:, 0:64 * NCH], Ss.rearrange("p e n -> p (e n)"), reads=rSse, writes=[c.new_yres()], is_output=True)
        return
    atm = [al(tag + f"_atm{i}", [128, 512], BF16) for i in range(2)]; ratm = [Res(), Res()]
    osb = al(tag + "_osb", [128, 512], F32); rosb = Res()
    ob = al(tag + "_ob", [128, 512], BF16); rob = Res()
    rstd = al(tag + "_rstd", [128, 512], F32); rrstd = Res()
    sg = al(tag + "_sg", [128, 512], F32); rsg = Res()
    yst = [al(tag + f"_y{i}", [128, 512], BF16) for i in range(2)]; ryst = [Res(), Res()]
    osb2 = [osb, al(tag + "_osb1", [128, 512], F32)]; rosb2 = [rosb, Res()]
    sg2 = [sg, al(tag + "_sg1", [128, 512], F32)]; rsg2 = [rsg, Res()]

    def front(tb):
        osb_, rosb_, sg_, rsg_ = osb2[tb % 2], rosb2[tb % 2], sg2[tb % 2], rsg2[tb % 2]
        for i in range(4):
            t = tb * 4 + i
            ts_ = slice(t * 128, (t + 1) * 128)
            for hl in range(2):
                rows = slice(hl * 64, (hl + 1) * 64)
                S.pe(lambda e, i=i, hl=hl, rows=rows, ts_=ts_: e.matmul(ps[2 + hl][:, i * 128:(i + 1) * 128], lhsT=kt[rows, ts_], rhs=qt[rows, ts_],
                                                                      start=True, stop=True), reads=[rkt, rqt], writes=[rps[2 + hl]])
        bank = proj_fm(c, gslice, tb)
        S.act(lambda e, bank=bank: e.activation(out=sg_, in_=ps[bank], func=AF.Silu), reads=[rps[bank]], writes=[rsg_])
        for hl in range(2):
            S.dve(lambda e, hl=hl: e.tensor_tensor(out=atm[hl], in0=ps[2 + hl], in1=c.K["glamask"], op=ALU.mult),
                  reads=[rps[2 + hl], c.rK], writes=[ratm[hl]])
        for i in range(4):
            t = tb * 4 + i
            for hl in range(2):
                rows = slice(hl * 64, (hl + 1) * 64)
                ncs = [n for n in (2 * t, 2 * t + 1) if n > 0]
                S.pe(lambda e, i=i, hl=hl, rows=rows, t=t, ncs=ncs: e.matmul(
                    ps[4][rows, i * 128:(i + 1) * 128], lhsT=c.vtm[:, t, voff + hl * 64:voff + (hl + 1) * 64], rhs=atm[hl][:, i * 128:(i + 1) * 128],
                    start=True, stop=(len(ncs) == 0)), reads=[c.rv[t], ratm[hl]], writes=[rps[4]])
                for n in ncs:
                    cc = n - 2 * t
                    S.pe(lambda e, i=i, hl=hl, rows=rows, n=n, cc=cc, ncs=ncs: e.matmul(
                        ps[4][rows, i * 128 + cc * 64:i * 128 + (cc + 1) * 64], lhsT=Ss[rows, :, n - 1], rhs=qt[rows, n * 64:(n + 1) * 64],
                        start=False, stop=(n == ncs[-1])), reads=rSse + [rqt], writes=[rps[4]])
        S.act(lambda e: e.activation(out=osb_, in_=ps[4], func=AF.Copy), reads=[rps[4]], writes=[rosb_])

    def back(tb):
        osb_, rosb_, sg_, rsg_ = osb2[tb % 2], rosb2[tb % 2], sg2[tb % 2], rsg2[tb % 2]
        if groupnorm:
            S.dve(lambda e: e.tensor_copy(out=ob, in_=osb_), reads=[rosb_], writes=[rob])
            S.pe(lambda e: e.matmul(ps[5], lhsT=c.K["blockmean"], rhs=ob, start=True, stop=True), reads=[rob, c.rK], writes=[rps[5]])
            S.dve(lambda e: e.tensor_tensor(out=osb_, in0=osb_, in1=ps[5], op=ALU.subtract), reads=[rosb_, rps[5]], writes=[rosb_])
        S.act(lambda e: e.activation(out=ob, in_=osb_, func=AF.Square), reads=[rosb_], writes=[rob])
        S.pe(lambda e: e.matmul(ps[5], lhsT=c.K["blockmean"], rhs=ob, start=True, stop=True), reads=[rob, c.rK], writes=[rps[5]])
        S.act(lambda e: e.activation(out=rstd, in_=ps[5], func=AF.Ln, bias=c.K["eps_n"], scale=1.0), reads=[rps[5], c.rK], writes=[rrstd])
        S.act(lambda e: e.activation(out=rstd, in_=rstd, func=AF.Exp, scale=-0.5), reads=[rrstd], writes=[rrstd])
        S.dve(lambda e: e.tensor_tensor(out=osb_, in0=osb_, in1=rstd, op=ALU.mult), reads=[rosb_, rrstd], writes=[rosb_])
        S.dve(lambda e: e.scalar_tensor_tensor(out=yst[tb % 2], in0=osb_, scalar=pv[:, PV[gcol]:PV[gcol] + 1], in1=sg_,
                                               op0=ALU.mult, op1=ALU.mult), reads=[rosb_, rsg_, rpv], writes=[ryst[tb % 2]])
        S.dma(yT_d[:, tb * 512:(tb + 1) * 512], yst[tb % 2], reads=[ryst[tb % 2]], writes=[c.new_yres()], is_output=True)

    for tb in range(NB + 1):
        if tb < NB:
            front(tb)
        if tb >= 1:
            back(tb - 1)


def mixer_B(c, yT_d):
    S, A, SEQ = c.S, c.A, c.SEQ
    al = A["alloc"]
    ps, rps = c.ps, c.rps
    NCH = SEQ // 64
    qt = al("B_qt", [128, SEQ], BF16); rqt = Res()
    kt = al("B_kt", [128, SEQ], BF16); rkt = Res()
    En = al("B_En", [128, NCH], F32); rEn = Res()
    tab = al("B_tab", [128, 4, 512], F32); rtab = Res()
    t1s = [al(f"B_t1{i}", [128, 512], F32) for i in range(2)]; rt1s = [Res(), Res()]
    t2s = [al(f"B_t2{i}", [128, 512], F32) for i in range(2)]; rt2s = [Res(), Res()]
    S.dve(lambda e: e.tensor_copy(out=En, in_=c.pv[:, PV["gam64"]:PV["gam64"] + 1].broadcast_to([128, NCH])), reads=[c.rpv], writes=[rEn])
    c.proj_banks = (0, 1, 2, 3)
    for blk in range(c.NB):
        sl = slice(blk * 512, (blk + 1) * 512)
        S.dma(tab, c.rettab_d[:, :, sl].rearrange("f p t -> p f t"), writes=[rtab])
        for which, dst, rdst, ti in (("B_q", qt, rqt, 0), ("B_k", kt, rkt, 2)):
            t1, rt1, t2, rt2 = t1s[ti // 2], rt1s[ti // 2], t2s[ti // 2], rt2s[ti // 2]
            b1 = proj_fm(c, which, blk)
            b2 = proj_fm(c, which + "sw", blk)
            S.dve(lambda e, b1=b1, ti=ti, t1=t1: e.tensor_tensor(out=t1, in0=ps[b1], in1=tab[:, ti, :], op=ALU.mult), reads=[rps[b1], rtab], writes=[rt1])
            S.dve(lambda e, b2=b2, ti=ti, t2=t2: e.tensor_tensor(out=t2, in0=ps[b2], in1=tab[:, ti + 1, :], op=ALU.mult), reads=[rps[b2], rtab], writes=[rt2])
            S.pool(lambda e, dst=dst, sl=sl, t1=t1, t2=t2: e.tensor_tensor(out=dst[:, sl], in0=t1, in1=t2, op=ALU.add), reads=[rt1, rt2], writes=[rdst])
    c.proj_banks = (0, 1)
    gla(c, "B", qt, rqt, kt, rkt, En, rEn, 0, "B_g", "retg", True, yT_d)


def mixer_D(c, yT_d):
    S, A, SEQ = c.S, c.A, c.SEQ
    al = A["alloc"]
    ps, rps = c.ps, c.rps
    pv, rpv = c.pv, c.rpv
    col = lambda n: pv[:, PV[n]:PV[n] + 1]
    NCH = SEQ // 64
    qt = al("D_qt", [128, SEQ], BF16); rqt = Res()
    kt = al("D_kt", [128, SEQ], BF16); rkt = Res()
    En = al("D_En", [128, NCH], F32); rEn = Res()
    T = [al(f"D_T{i}", [128, 512], F32) for i in range(6)]
    rT = [Res() for _ in range(6)]
    one = c.K["one"]
    c.proj_banks = (0, 1, 2, 3)
    for blk in range(c.NB):
        sl = slice(blk * 512, (blk + 1) * 512)
        bf_ = proj_fm(c, "D_f", blk)
        S.act(lambda e, b=bf_: e.activation(out=T[0], in_=ps[b], func=AF.Exp), reads=[rps[bf_]], writes=[rT[0]])
        S.act(lambda e: e.activation(out=T[0], in_=T[0], func=AF.Ln, bias=one, scale=1.0), reads=[rT[0], c.rK], writes=[rT[0]])
        S.act(lambda e: e.activation(out=T[1], in_=T[0], func=AF.Exp, bias=col("lnoml"), scale=-1.0), reads=[rT[0], rpv], writes=[rT[1]])
        S.act(lambda e: e.activation(out=T[2], in_=T[1], func=AF.Ln, bias=one, scale=-1.0), reads=[rT[1], c.rK], writes=[rT[2]])
        S.dve(lambda e: e.tensor_tensor_scan(out=T[3], data0=c.K["resetmask"], data1=T[2], initial=0.0, op0=ALU.mult, op1=ALU.add),
              reads=[rT[2], c.rK], writes=[rT[3]])
        S.act(lambda e: e.activation(out=T[4], in_=T[3], func=AF.Exp), reads=[rT[3]], writes=[rT[4]])
        S.act(lambda e: e.activation(out=T[5], in_=T[3], func=AF.Exp, scale=-1.0), reads=[rT[3]], writes=[rT[5]])
        bq = proj_fm(c, "D_q", blk)
        S.dve(lambda e, b=bq, sl=sl: e.tensor_tensor(out=qt[:, sl], in0=ps[b], in1=T[4], op=ALU.mult), reads=[rps[bq], rT[4]], writes=[rqt])
        S.pool(lambda e, sl=sl: e.tensor_tensor(out=kt[:, sl], in0=T[1], in1=T[5], op=ALU.mult), reads=[rT[1], rT[5]], writes=[rkt])
        S.dve(lambda e, blk=blk: e.tensor_copy(out=En[:, blk * 8:(blk + 1) * 8], in_=T[4][:, 63:512:64]), reads=[rT[4]], writes=[rEn])
    c.proj_banks = (0, 1)
    gla(c, "D", qt, rqt, kt, rkt, En, rEn, 256, "D_g", "hgg", False, yT_d)


def mixer_C(c, yT_d):
    S, A, SEQ = c.S, c.A, c.SEQ
    al = A["alloc"]
    ps, rps = c.ps, c.rps
    NQ = SEQ // 512
    qT = al("C_qT", [128, SEQ], BF16); rqT = Res()
    kT = al("C_kT", [128, SEQ], BF16); rkT = Res()
    c.proj_banks = (0, 1, 2, 3)
    for blk in range(c.NB):
        sl = slice(blk * 512, (blk + 1) * 512)
        b = proj_fm(c, "C_q", blk)
        S.act(lambda e, b=b, sl=sl: e.activation(out=qT[:, sl], in_=ps[b], func=AF.Copy, scale=0.125), reads=[rps[b]], writes=[rqT])
        b = proj_fm(c, "C_k", blk)
        S.dve(lambda e, b=b, sl=sl: e.tensor_copy(out=kT[:, sl], in_=ps[b]), reads=[rps[b]], writes=[rkT])
    c.proj_banks = (0, 1)
    NE = 4
    eb = [al(f"C_e{i}", [128, 512], BF16) for i in range(NE)]; reb = [Res() for _ in range(NE)]
    msp = [al(f"C_msp{i}", [128, 512], BF16) for i in range(NE)]; rmsp = [Res() for _ in range(NE)]
    exr = [al(f"C_exr{i}", [128, 512], BF16) for i in range(2)]; rexr = [Res() for _ in range(2)]
    wT = [al(f"C_w{i}", [128, 512], BF16) for i in range(NE)]; rwT = [Res() for _ in range(NE)]
    chi = [al(f"C_chi{i}", [1, 512], BF16) for i in range(2)]; rchi = [Res(), Res()]
    clo = [al(f"C_clo{i}", [1, 512], BF16) for i in range(2)]; rclo = [Res(), Res()]
    yst = [al(f"C_y{i}", [128, 512], BF16) for i in range(2)]; ryst = [Res(), Res()]
    negtri, ones_row, cmask = c.K["negtri"], c.K["ones_row"], c.K["cmask"]
    steps = []
    for qb in range(NQ):
        for kb in range(4 * qb + 3, -1, -1):
            for h in range(2):
                steps.append((qb, kb, h))
    N = len(steps)

    def cols(i):
        qb, kb, h = steps[i]
        j = kb - 4 * qb
        return slice(max(j, 0) * 128, 512)

    def stage_Z(i):
        qb, kb, h = steps[i]
        rows = slice(h * 64, (h + 1) * 64)
        cs = cols(i)
        q0 = qb * 512
        S.pe(lambda e: e.matmul(ps[2 + h][:, cs], lhsT=kT[rows, kb * 128:(kb + 1) * 128], rhs=qT[rows, q0 + cs.start:q0 + 512],
                                start=True, stop=True), reads=[rkT, rqT], writes=[rps[2 + h]])
        S.act(lambda e: e.activation(out=eb[i % NE][:, cs], in_=ps[2 + h][:, cs], func=AF.Exp), reads=[rps[2 + h]], writes=[reb[i % NE]])
        j = kb - 4 * qb
        if j >= 0:
            dg = slice(j * 128, (j + 1) * 128)
            S.dve(lambda e: e.tensor_tensor(out=eb[i % NE][:, dg], in0=eb[i % NE][:, dg], in1=cmask[:, j, dg], op=ALU.mult),
                  reads=[reb[i % NE], c.rK], writes=[reb[i % NE]])
        S.act(lambda e: e.activation(out=msp[i % NE][:, cs], in_=eb[i % NE][:, cs], func=AF.Ln, bias=1.0, scale=1.0),
              reads=[reb[i % NE]], writes=[rmsp[i % NE]])

    def stage_R(i):
        qb, kb, h = steps[i]
        first = (kb == 4 * qb + 3)
        cs = cols(i)
        S.pe(lambda e: e.matmul(ps[4 + h][:, cs], lhsT=negtri, rhs=msp[i % NE][:, cs], start=first, stop=False, skip_group_check=True),
             reads=[rmsp[i % NE], c.rK], writes=[rps[4 + h]])
        S.act(lambda e: e.activation(out=exr[i % 2][:, cs], in_=ps[4 + h][:, cs], func=AF.Exp), reads=[rps[4 + h]], writes=[rexr[i % 2]])
        S.dve(lambda e: e.tensor_tensor(out=wT[i % NE][:, cs], in0=eb[i % NE][:, cs], in1=exr[i % 2][:, cs], op=ALU.mult),
              reads=[reb[i % NE], rexr[i % 2]], writes=[rwT[i % NE]])

    def stage_O(i):
        qb, kb, h = steps[i]
        rows = slice(h * 64, (h + 1) * 64)
        ob = 6 + qb % 2
        first = (kb == 4 * qb + 3)
        cs = cols(i)
        if kb > 0:
            S.pe(lambda e: e.matmul(ps[4 + h][:, cs], lhsT=c.K["negcompl"], rhs=msp[i % NE][:, cs], start=False, stop=(kb == 1), skip_group_check=True),
                 reads=[rmsp[i % NE], c.rK], writes=[rps[4 + h]])
        S.pe(lambda e: e.matmul(ps[ob][rows, cs], lhsT=c.vtm[:, kb, 128 + h * 64:128 + (h + 1) * 64], rhs=wT[i % NE][:, cs],
                                start=first, stop=(kb == 0), skip_group_check=True), reads=[c.rv[kb], rwT[i % NE]], writes=[rps[ob]])
        if kb == 0 and h == 1:
            S.act(lambda e: e.activation(out=yst[qb % 2], in_=ps[ob], func=AF.Copy), reads=[rps[ob]], writes=[ryst[qb % 2]])
            S.dma(yT_d[:, qb * 512:(qb + 1) * 512], yst[qb % 2], reads=[ryst[qb % 2]], writes=[c.new_yres()], is_output=True)

    for s in range(-2, N):
        if 0 <= s + 2 < N:
            stage_Z(s + 2)
        if 0 <= s + 1 < N:
            stage_R(s + 1)
        if 0 <= s < N:
            stage_O(s)


def mixer_C2(c, yT_d):
    S, A, SEQ = c.S, c.A, c.SEQ
    al = A["alloc"]
    ps, rps = c.ps, c.rps
    big = A["psum_all"]
    NQ = SEQ // 512
    qT = al("C_qT", [128, SEQ], BF16); rqT = Res()
    kT = al("C_kT", [128, SEQ], BF16); rkT = Res()
    c.proj_banks = (0, 1, 2, 3)
    for blk in range(c.NB):
        sl = slice(blk * 512, (blk + 1) * 512)
        b = proj_fm(c, "C_q", blk)
        S.act(lambda e, b=b, sl=sl: e.activation(out=qT[:, sl], in_=ps[b], func=AF.Copy, scale=0.125), reads=[rps[b]], writes=[rqT])
        b = proj_fm(c, "C_k", blk)
        S.dve(lambda e, b=b, sl=sl: e.tensor_copy(out=kT[:, sl], in_=ps[b]), reads=[rps[b]], writes=[rkT])
    c.proj_banks = (0, 1)
    NE = 4
    eb = [al(f"C_e{i}", [128, 2, 512], BF16) for i in range(NE)]; reb = [Res() for _ in range(NE)]
    msp = [al(f"C_msp{i}", [128, 2, 512], BF16) for i in range(NE)]; rmsp = [Res() for _ in range(NE)]
    exr = [al(f"C_exr{i}", [128, 2, 512], BF16) for i in range(2)]; rexr = [Res() for _ in range(2)]
    wT = [al(f"C_w{i}", [128, 2, 512], BF16) for i in range(NE)]; rwT = [Res() for _ in range(NE)]
    yst = [al(f"C_y{i}", [128, 512], BF16) for i in range(2)]; ryst = [Res(), Res()]
    negtri, negcompl, cmask = c.K["negtri"], c.K["negcompl"], c.K["cmask"]

    def pair(b0):
        return big[:, b0 * 512:(b0 + 2) * 512].rearrange("p (h n) -> p h n", h=2)

    steps = []
    for qb in range(NQ):
        for kb in range(4 * qb + 3, -1, -1):
            steps.append((qb, kb))
    N = len(steps)

    def cols(i):
        qb, kb = steps[i]
        return slice(max(kb - 4 * qb, 0) * 128, 512)

    def stage_Z(i):
        qb, kb = steps[i]
        cs = cols(i)
        q0 = qb * 512
        zb = 0 if i % 2 == 0 else 2
        for h in range(2):
            rows = slice(h * 64, (h + 1) * 64)
            S.pe(lambda e, h=h, rows=rows: e.matmul(ps[zb + h][:, cs], lhsT=kT[rows, kb * 128:(kb + 1) * 128], rhs=qT[rows, q0 + cs.start:q0 + 512],
                                                    start=True, stop=True), reads=[rkT, rqT], writes=[rps[zb + h]])
        S.act(lambda e: e.activation(out=eb[i % NE][:, :, cs], in_=pair(zb)[:, :, cs], func=AF.Exp),
              reads=[rps[zb], rps[zb + 1]], writes=[reb[i % NE]])
        j = kb - 4 * qb
        if j >= 0:
            dg = slice(j * 128, (j + 1) * 128)
            S.dve(lambda e: e.tensor_tensor(out=eb[i % NE][:, :, dg], in0=eb[i % NE][:, :, dg],
                                            in1=cmask[:, j:j + 1, dg].broadcast_to([128, 2, 128]), op=ALU.mult),
                  reads=[reb[i % NE], c.rK], writes=[reb[i % NE]])
        S.act(lambda e: e.activation(out=msp[i % NE][:, :, cs], in_=eb[i % NE][:, :, cs], func=AF.Ln, bias=1.0, scale=1.0),
              reads=[reb[i % NE]], writes=[rmsp[i % NE]])

    def stage_R(i):
        qb, kb = steps[i]
        first = (kb == 4 * qb + 3)
        cs = cols(i)
        for h in range(2):
            S.pe(lambda e, h=h: e.matmul(ps[4 + h][:, cs], lhsT=negtri, rhs=msp[i % NE][:, h, cs], start=first, stop=False, skip_group_check=True),
                 reads=[rmsp[i % NE], c.rK], writes=[rps[4 + h]])
        S.act(lambda e: e.activation(out=exr[i % 2][:, :, cs], in_=pair(4)[:, :, cs], func=AF.Exp),
              reads=[rps[4], rps[5]], writes=[rexr[i % 2]])
        S.dve(lambda e: e.tensor_tensor(out=wT[i % NE][:, :, cs], in0=eb[i % NE][:, :, cs], in1=exr[i % 2][:, :, cs], op=ALU.mult),
              reads=[reb[i % NE], rexr[i % 2]], writes=[rwT[i % NE]])

    def stage_Cmp(i):
        qb, kb = steps[i]
        cs = cols(i)
        if kb > 0:
            for h in range(2):
                S.pe(lambda e, h=h: e.matmul(ps[4 + h][:, cs], lhsT=negcompl, rhs=msp[i % NE][:, h, cs], start=False, stop=(kb == 1), skip_group_check=True),
                     reads=[rmsp[i % NE], c.rK], writes=[rps[4 + h]])

    def stage_O(i):
        qb, kb = steps[i]
        ob = 6 + qb % 2
        first = (kb == 4 * qb + 3)
        cs = cols(i)
        for h in range(2):
            rows = slice(h * 64, (h + 1) * 64)
            S.pe(lambda e, h=h, rows=rows: e.matmul(ps[ob][rows, cs], lhsT=c.vtm[:, kb, 128 + h * 64:128 + (h + 1) * 64], rhs=wT[i % NE][:, h, cs],
                                                    start=first, stop=(kb == 0), skip_group_check=True), reads=[c.rv[kb], rwT[i % NE]], writes=[rps[ob]])
        if kb == 0:
            S.act(lambda e: e.activation(out=yst[qb % 2], in_=ps[ob], func=AF.Copy), reads=[rps[ob]], writes=[ryst[qb % 2]])
            S.dma(yT_d[:, qb * 512:(qb + 1) * 512], yst[qb % 2], reads=[ryst[qb % 2]], writes=[c.new_yres()], is_output=True)

    for s in range(-2, N):
        if 0 <= s + 2 < N:
            stage_Z(s + 2)
        if 0 <= s < N:
            stage_Cmp(s)
        if 0 <= s + 1 < N:
            stage_R(s + 1)
        if 0 <= s < N:
            stage_O(s)


def phase_M(nc, S, A, SEQ, layer, hin_d, win_d, pvec_d, wab_d, consts, rettab_d, yT_d, which="ABDC"):
    ar = A["arena"]
    c = setup_M(nc, S, A, SEQ, layer, hin_d, win_d, pvec_d, wab_d, consts)
    c.rettab_d = rettab_d
    small_params(c)
    m = ar.mark()
    fns = {"A": (mixer_A, 0), "B": (mixer_B, 1), "C": (mixer_C2 if "psum_all" in A else mixer_C, 2), "D": (mixer_D, 3)}
    for i, ch in enumerate(which):
        if i > 0:
            S.barrier(A["bar_scratch"])
            ar.reset(m)
        fn, slot = fns[ch]
        fn(c, yT_d[slot] if isinstance(yT_d, (list, tuple)) else yT_d[slot * 128:(slot + 1) * 128, :])
        c.ry_by_mixer[ch] = c.ry_list
        c.ry_list = []
    return c


BF = ml_dtypes.bfloat16
REF_SLICE = {"A_x": 0, "A_g": 1, "B_q": 2, "B_k": 3, "B_v": 4, "B_g": 5, "C_q": 6, "C_k": 7, "C_v": 8,
             "D_q": 9, "D_f": 10, "D_v": 11, "D_g": 12}
MY_SLICES = ["A_x", "A_g", "B_q", "B_qsw", "B_k", "B_ksw", "B_g", "C_q", "C_k", "D_q", "D_f", "D_g", "B_v", "C_v", "D_v"]


def core_cols(hh):
    p = np.arange(128)
    partner = (p // 64) * 64 + ((p % 64) + 32) % 64
    cols = []
    for n in MY_SLICES:
        sw = n.endswith("sw")
        base = REF_SLICE[n[:-2] if sw else n] * 256 + hh * 128
        cols.append(base + (partner if sw else p))
    return np.concatenate(cols)


def prep_layer_core(inp, l, hh):
    ch = slice(hh * 128, (hh + 1) * 128)
    out = {}
    out["win"] = np.ascontiguousarray(np.asarray(inp["w_in"][l])[:, core_cols(hh)])
    pv = np.zeros((128, 13), np.float32)
    cw = np.asarray(inp["conv_w"][l])
    for w in range(4):
        pv[:, w] = cw[w, ch]
    pv[:, 4] = np.asarray(inp["conv_b"][l])[ch]
    pv[:, 5] = np.asarray(inp["rg_ba"][l]).reshape(-1)[ch]
    pv[:, 6] = np.asarray(inp["rg_bx"][l]).reshape(-1)[ch]
    pv[:, 7] = np.asarray(inp["rg_lambda"][l])[ch]
    pv[:, 8] = np.asarray(inp["ret_norm_g"][l])[ch]
    pv[:, 9] = np.asarray(inp["hgrn_norm_g"][l])[ch]
    pv[:, 10] = np.asarray(inp["hgrn_lb_logits"][0])[ch]
    pv[:, 11] = np.asarray(inp["hgrn_lb_logits"][l])[ch]
    for hl in range(2):
        gam = 1.0 - 2.0 ** (-5.0 - (2 * hh + hl))
        pv[hl * 64:(hl + 1) * 64, 12] = gam ** 64
    out["pvec"] = pv
    wab = np.zeros((128, 2, 128), np.float32)
    for hl in range(2):
        s = slice(hl * 64, (hl + 1) * 64)
        wab[s, 0, s] = np.asarray(inp["rg_wa"][l])[2 * hh + hl]
        wab[s, 1, s] = np.asarray(inp["rg_wx"][l])[2 * hh + hl]
    out["wab"] = wab
    return out


def const_tables(hh, SEQ):
    K = {}
    K["ident"] = np.eye(128).astype(BF)
    j = np.arange(128)
    K["negtri"] = (-(j[:, None] >= j[None, :]).astype(np.float32)).astype(BF)
    K["negcompl"] = (-(j[:, None] < j[None, :]).astype(np.float32)).astype(BF)
    K["ones_row"] = np.ones((1, 128), np.float32).astype(BF)
    K["one"] = np.ones((128, 1), np.float32)
    K["eps_n"] = np.full((128, 1), 1e-6, np.float32)
    q = np.arange(512)
    cm = np.zeros((128, 4, 512), np.float32)
    for jb in range(4):
        cm[:, jb, :] = ((jb * 128 + j)[:, None] < q[None, :])
    K["cmask"] = cm.astype(BF)
    s = np.arange(128)
    gm = ((s[:, None] // 64 == s[None, :] // 64) & (s[:, None] <= s[None, :])).astype(np.float32)
    K["glamask"] = np.tile(gm, (1, 4)).astype(BF)
    K["blockmean"] = ((s[:, None] // 64 == s[None, :] // 64) / 64.0).astype(np.float32).astype(BF)
    rm = np.ones((128, 512), np.float32)
    rm[:, ::64] = 0.0
    K["resetmask"] = rm
    d = np.arange(64)
    inv_freq = (10000.0 ** (-np.arange(0, 64, 2, dtype=np.float32) / 64)).astype(np.float32)
    t = np.arange(SEQ, dtype=np.float32)
    ang = (t[:, None] * inv_freq[None, :]).astype(np.float32)
    cos, sin = np.cos(ang).T, np.sin(ang).T
    tl = (np.arange(SEQ) % 64 + 1).astype(np.float64)
    tabs = np.zeros((4, 128, SEQ), np.float32)
    for hl in range(2):
        gam = 1.0 - 2.0 ** (-5.0 - (2 * hh + hl))
        lg = np.log1p(-2.0 ** (-5.0 - (2 * hh + hl)))
        dq = np.exp(lg * tl)
        dk = np.exp(-lg * tl) / 8.0
        for dd in range(64):
            p = hl * 64 + dd
            c_, s_ = cos[dd % 32], sin[dd % 32]
            sg = -1.0 if dd < 32 else 1.0
            tabs[0, p] = c_ * dq
            tabs[1, p] = sg * s_ * dq
            tabs[2, p] = c_ * dk
            tabs[3, p] = sg * s_ * dk
    K["rettab"] = tabs
    return K


from concourse.bass_utils import run_bass_kernel_spmd

SEQ_FULL = 4096
T_OWN = 2048
SMALLK = ["ident", "negtri", "negcompl", "ones_row", "one", "eps_n", "cmask", "glamask", "blockmean", "resetmask"]


def _mk_A(nc):
    A = {}
    ar = Arena(nc)
    A["arena"] = ar
    A["alloc"] = ar.alloc
    A["psum_all"] = nc.alloc_psum_tensor("psum_all", [128, 6 * 512], F32).ap()
    A["psum"] = [A["psum_all"][:, i * 512:(i + 1) * 512] for i in range(6)] + [nc.alloc_psum_tensor(f"ps{i}", [128, 512], F32).ap() for i in (6, 7)]
    A["rpsum"] = [Res(excl=True) for _ in range(8)]
    A["bar_scratch"] = ar.alloc("bar", [128, 1], F32)
    return A


def _np_dt(a):
    return BF16 if a.dtype == BF else F32


def build_pre():
    nc = bass.Bass("TRN2", target_bir_lowering=False)
    S = Sched(nc)
    dt = lambda n, s, d, k="ExternalInput": nc.dram_tensor(n, s, d, kind=k).ap()
    x_d = dt("x", [T_OWN, 1024], F32)
    g_d = dt("ln_g", [1024], F32)
    b_d = dt("ln_b", [1024], F32)
    h32 = dt("h32", [T_OWN, 1024], F32, "ExternalOutput")
    h16 = dt("h16", [T_OWN, 1024], BF16, "ExternalOutput")
    A = _mk_A(nc)
    al = A["alloc"]
    eps = al("eps", [128, 1], F32)
    reps = Res()
    S.dve(lambda e: e.memset(eps, LN_EPS), writes=[reps])
    A["eps_ln"] = eps
    lnp = al("lnp", [128, 2, 1024], F32); rln = Res()
    if True:
        dmy = al("dmy", [128, 128], BF16); rd = Res()
        S.dve(lambda e: e.memset(dmy, 0.0), writes=[rd])
        S.pe(lambda e: e.matmul(A["psum"][0][:, 0:128], lhsT=dmy, rhs=dmy, start=True, stop=True), reads=[rd], writes=[A["rpsum"][0]])
    S.dma(lnp[:, 0, :], g_d.partition_broadcast(128), writes=[rln])
    S.dma(lnp[:, 1, :], b_d.partition_broadcast(128), writes=[rln])
    NT = T_OWN // 128
    xt = [al(f"xt{i}", [128, 1024], F32) for i in range(2)]; rxt = [Res(), Res()]
    tmp = [al(f"tmp{i}", [128, 1024], F32) for i in range(2)]; rtmp = [Res(), Res()]
    hb = [al(f"hb{i}", [128, 1024], BF16) for i in range(2)]; rhb = [Res(), Res()]
    st = [al(f"st{i}", [128, 16], F32) for i in range(2)]; rst = [Res(), Res()]
    for t in range(NT):
        p = t % 2
        S.dma(xt[p], x_d[t * 128:(t + 1) * 128, :], writes=[rxt[p]])
        ln_tile(S, nc, A, xt[p], rxt[p], lnp[:, 0, :], lnp[:, 1, :], rln, xt[p], rxt[p], tmp[p], rtmp[p], st[p], rst[p], t)
        S.dma(h32[t * 128:(t + 1) * 128, :], xt[p], reads=[rxt[p]], is_output=True)
        S.act(lambda e, p=p: e.activation(out=hb[p], in_=xt[p], func=AF.Copy), reads=[rxt[p]], writes=[rhb[p]])
        S.dma(h16[t * 128:(t + 1) * 128, :], hb[p], reads=[rhb[p]], is_output=True)
    build_and_emit(nc, S)
    return nc


def build_M(layer, Kh):
    nc = bass.Bass("TRN2", target_bir_lowering=False)
    S = Sched(nc)
    dt = lambda n, s, d, k="ExternalInput": nc.dram_tensor(n, s, d, kind=k).ap()
    SEQ = SEQ_FULL
    hin = dt("hin", [SEQ, 1024], BF16)
    win = dt("win", [1024, 1920], F32)
    pvec = dt("pvec", [128, 13], F32)
    wab = dt("wab", [128, 2, 128], F32)
    consts = {}
    for n in SMALLK:
        a = Kh[n]
        consts[n] = (dt("k_" + n, list(a.shape), _np_dt(a)), list(a.shape), _np_dt(a))
    rettab = dt("rettab", [4, 128, SEQ], F32)
    yT = dt("yT", [512, SEQ], BF16, "ExternalOutput")
    A = _mk_A(nc)
    phase_M(nc, S, A, SEQ, layer, hin, win, pvec, wab, consts, rettab, yT)
    build_and_emit(nc, S)
    return nc


def build_F():
    nc = bass.Bass("TRN2", target_bir_lowering=False)
    S = Sched(nc)
    dt = lambda n, s, d, k="ExternalInput": nc.dram_tensor(n, s, d, kind=k).ap()
    T = T_OWN
    yT_d = dt("yT", [1024, T], BF16)
    h_d = dt("h", [T, 1024], F32)
    wout = dt("w_out", [1024, 1024], F32)
    wup = dt("w_up", [1024, 4096], F32)
    wdn = dt("w_down", [4096, 1024], F32)
    l1g, l1b, l2g, l2b = [dt(n, [1024], F32) for n in ("l1g", "l1b", "l2g", "l2b")]
    ident_d = dt("ident", [128, 128], BF16)
    out = dt("h32", [T, 1024], F32, "ExternalOutput")
    outb = dt("h16", [T, 1024], BF16, "ExternalOutput")
    A = _mk_A(nc)
    al = A["alloc"]
    A["ident_bf"] = al("ident", [128, 128], BF16)
    rid = Res()
    S.dma(A["ident_bf"], ident_d, writes=[rid])
    eps = al("eps", [128, 1], F32)
    S.dve(lambda e: e.memset(eps, LN_EPS), writes=[rid])
    A["eps_ln"] = eps
    NT = T // 128
    hres = al("hres", [128, NT, 1024], F32)
    rh = [Res() for _ in range(NT)]
    for t in range(NT):
        S.dma(hres[:, t, :], h_d[t * 128:(t + 1) * 128, :], writes=[rh[t]])
    phase_F(nc, S, A, T, yT_d, wout, l1g, l1b, wup, wdn, l2g, l2b, hres, rh, out_f32_d=out, out_bf16_d=outb)
    build_and_emit(nc, S)
    return nc


def wout_perm():
    idx = []
    for hh in range(2):
        for m in range(4):
            idx.append(m * 256 + hh * 128 + np.arange(128))
    return np.concatenate(idx)


def kernel_unfused(**inputs):
    inp = {k: np.asarray(v) for k, v in inputs.items()}
    x = inp["x"]
    NC = 8
    cores = list(range(NC))
    f32 = np.float32
    Kh = [const_tables(hh, SEQ_FULL) for hh in range(2)]
    nc_pre = build_pre()
    im = []
    for c in cores:
        b, hh = c // 2, c % 2
        im.append({"x": np.ascontiguousarray(x[b, hh * T_OWN:(hh + 1) * T_OWN, :]), "ln_g": inp["ln_in_g"], "ln_b": inp["ln_in_b"]})
    res = run_bass_kernel_spmd(nc_pre, im, core_ids=cores).results
    h32 = [r["h32"] for r in res]
    h16 = [r["h16"] for r in res]
    nc_F = build_F()
    perm = wout_perm()
    for l in range(2):
        nc_M = build_M(l, Kh[0])
        im = []
        for c in cores:
            b, hh = c // 2, c % 2
            pc = prep_layer_core(inp, l, hh)
            d = {"hin": np.concatenate([h16[2 * b], h16[2 * b + 1]], axis=0), "win": pc["win"], "pvec": pc["pvec"], "wab": pc["wab"],
                 "rettab": Kh[hh]["rettab"]}
            for n in SMALLK:
                d["k_" + n] = Kh[hh][n]
            im.append(d)
        res = run_bass_kernel_spmd(nc_M, im, core_ids=cores).results
        yT = [r["yT"] for r in res]
        im = []
        wo = np.ascontiguousarray(inp["w_out"][l][perm, :])
        for c in cores:
            b, hh = c // 2, c % 2
            ya = np.concatenate([yT[2 * b], yT[2 * b + 1]], axis=0)[:, hh * T_OWN:(hh + 1) * T_OWN]
            im.append({"yT": np.ascontiguousarray(ya), "h": h32[c], "w_out": wo, "w_up": inp["w_up"][l], "w_down": inp["w_down"][l],
                       "l1g": inp["ln1_g"][l], "l1b": inp["ln1_b"][l], "l2g": inp["ln2_g"][l], "l2b": inp["ln2_b"][l],
                       "ident": Kh[0]["ident"]})
        res = run_bass_kernel_spmd(nc_F, im, core_ids=cores).results
        h32 = [r["h32"] for r in res]
        h16 = [r["h16"] for r in res]
    out = np.zeros((4, SEQ_FULL, 1024), f32)
    for c in cores:
        b, hh = c // 2, c % 2
        out[b, hh * T_OWN:(hh + 1) * T_OWN, :] = h32[c]
    return out


GROUPS = [[0, 1], [2, 3], [4, 5], [6, 7]]
U32 = mybir.dt.uint32


def build_fused(Kh):
    nc = bass.Bass("TRN2", target_bir_lowering=False)
    S = Sched(nc)
    dt = lambda n, s, d, k="ExternalInput": nc.dram_tensor(n, s, d, kind=k).ap()
    SEQ, T = SEQ_FULL, T_OWN
    NT = T // 128
    x_d = dt("x", [T, 1024], F32)
    g_d = dt("ln_g", [1024], F32)
    b_d = dt("ln_b", [1024], F32)
    gidx_d = dt("gidx", [128, 8], U32)
    L = []
    for l in range(2):
        L.append(dict(
            win=dt(f"win{l}", [1024, 1920], F32), pvec=dt(f"pvec{l}", [128, 13], F32), wab=dt(f"wab{l}", [128, 2, 128], F32),
            wout=dt(f"w_out{l}", [1024, 1024], F32), wup=dt(f"w_up{l}", [1024, 4096], F32), wdn=dt(f"w_down{l}", [4096, 1024], F32),
            l1g=dt(f"l1g{l}", [1024], F32), l1b=dt(f"l1b{l}", [1024], F32), l2g=dt(f"l2g{l}", [1024], F32), l2b=dt(f"l2b{l}", [1024], F32)))
    kd = {}
    for n in SMALLK:
        a = Kh[n]
        kd[n] = (dt("k_" + n, list(a.shape), _np_dt(a)), list(a.shape), _np_dt(a))
    rettab = dt("rettab", [4, 128, SEQ], F32)
    out_d = dt("out", [T, 1024], F32, "ExternalOutput")
    H = T // 2
    hx_loc = [nc.dram_tensor(f"hx_loc{i}", [H, 1024], BF16).ap() for i in range(2)]
    hx_all = [nc.dram_tensor(f"hx_all{i}", [2 * H, 1024], BF16).ap() for i in range(2)]
    y_loc = [nc.dram_tensor(f"y_loc{i}", [256, SEQ], BF16).ap() for i in range(2)]
    y_all = [nc.dram_tensor(f"y_all{i}", [512, SEQ], BF16).ap() for i in range(2)]
    hsp = nc.dram_tensor("hspill", [T, 1024], F32).ap()

    def hx_dst(t):
        i, r = divmod(t * 128, H)
        return hx_loc[i][r:r + 128, :]

    hx_rr = [Res(), Res()]

    def gather_half(i, rhxl):
        S.cc("AllGather", [hx_loc[i]], [hx_all[i]], GROUPS, reads=rhxl[i * (NT // 2):(i + 1) * (NT // 2)], writes=[hx_rr[i]])

    def gather_hx(rhxl, done=()):
        rr = hx_rr
        for i in range(2):
            if i not in done:
                gather_half(i, rhxl)
        return [(hx_all[0][0:H, :], 0, H, [rr[0]]), (hx_all[1][0:H, :], H, H, [rr[1]]),
                (hx_all[0][H:2 * H, :], 2 * H, H, [rr[0]]), (hx_all[1][H:2 * H, :], 3 * H, H, [rr[1]])]
    A = _mk_A(nc)
    ar = A["arena"]
    al = A["alloc"]
    S.cc_scratch = al("ccs", [128, 1], F32)
    eps = al("eps", [128, 1], F32)
    rconst = Res()
    S.dve(lambda e: e.memset(eps, LN_EPS), writes=[rconst])
    A["eps_ln"] = eps
    A["ident_bf"] = al("identF", [128, 128], BF16)
    S.dma(A["ident_bf"], kd["ident"][0], writes=[rconst])
    gidx = al("gidx", [128, 8], U32); rgidx = Res()
    S.dma(gidx, gidx_d, writes=[rgidx])
    base = ar.mark()
    rhsp = [Res() for _ in range(NT)]
    rhxl = [Res() for _ in range(NT)]
    lnp = al("lnp", [128, 2, 1024], F32); rln = Res()
    S.dma(lnp[:, 0, :], g_d.partition_broadcast(128), writes=[rln])
    S.dma(lnp[:, 1, :], b_d.partition_broadcast(128), writes=[rln])
    xt = [al(f"xt{i}", [128, 1024], F32) for i in range(NT)]; rxt = [Res() for _ in range(NT)]
    tmp = [al(f"tmp{i}", [128, 1024], F32) for i in range(4)]; rtmp = [Res() for _ in range(4)]
    hb = [al(f"hb{i}", [128, 1024], BF16) for i in range(8)]; rhb = [Res() for _ in range(8)]
    st = [al(f"st{i}", [128, 16], F32) for i in range(8)]; rst = [Res() for _ in range(8)]
    for t in range(NT):
        S.dma(xt[t], x_d[t * 128:(t + 1) * 128, :], writes=[rxt[t]])
    for t in range(NT):
        p = t % 8
        ln_tile(S, nc, A, xt[t], rxt[t], lnp[:, 0, :], lnp[:, 1, :], rln, xt[t], rxt[t], None, None, st[p], rst[p], t)
        S.act(lambda e, t=t: e.activation(out=hb[t % 8], in_=xt[t], func=AF.Copy), reads=[rxt[t]], writes=[rhb[t % 8]])
        S.dma(hsp[t * 128:(t + 1) * 128, :], xt[t], reads=[rxt[t]], writes=[rhsp[t]])
        S.dma(hx_dst(t), hb[t % 8], reads=[rhb[t % 8]], writes=[rhxl[t]])
        if t == NT // 2 - 1:
            gather_half(0, rhxl)
    hin_pieces = gather_hx(rhxl, done=(0,))
    import os
    NL = int(os.environ.get("FUSE_LAYERS", "2"))
    PH = os.environ.get("FUSE_PHASES", "MF")
    for l in range(NL):
        W = L[l]
        S.barrier(A["bar_scratch"])
        ar.reset(base)
        yslots = [y_loc[0][0:128, :], y_loc[0][128:256, :], y_loc[1][0:128, :], y_loc[1][128:256, :]]
        c = phase_M(nc, S, A, SEQ, l, hin_pieces, W["win"], W["pvec"], W["wab"], kd, rettab, yslots)
        ar.reset(base)
        hres = al("hres", [128, NT, 1024], F32)
        rh = [Res() for _ in range(NT)]
        wo_pre = al("wo_pre", [128, 8, 1024], BF16); rwo_pre = Res()
        S.dma(wo_pre, W["wout"].rearrange("(k p) n -> p k n", p=128), writes=[rwo_pre], eng="gpsimd", extra_deps=[c.last_proj])
        for t in range(NT):
            S.dma(hres[:, t, :], hsp[t * 128:(t + 1) * 128, :], reads=[rhsp[t]], writes=[rh[t]], eng="gpsimd", extra_deps=[c.last_proj])
        ry_all = [Res(), Res()]
        S.cc("AllGather", [y_loc[0]], [y_all[0]], GROUPS, reads=c.ry_by_mixer["A"] + c.ry_by_mixer["B"], writes=[ry_all[0]])
        S.cc("AllGather", [y_loc[1]], [y_all[1]], GROUPS, reads=c.ry_by_mixer["C"] + c.ry_by_mixer["D"], writes=[ry_all[1]])
        if PH == "M":
            continue
        S.barrier(A["bar_scratch"])
        last = (l == NL - 1)
        src = [ya.rearrange("c (h t) -> (c h) t", h=2) for ya in y_all]
        phase_F(nc, S, A, T, None, W["wout"], W["l1g"], W["l1b"], W["wup"], W["wdn"], W["l2g"], W["l2b"], hres, rh,
                out_f32_d=(out_d if last else hsp), out_bf16_d=(None if last else hx_dst),
                y_gather=(src, gidx, rgidx, ry_all), rout32=(None if last else rhsp), rout16=(None if last else rhxl),
                final_out=last, wo_pre=(wo_pre, rwo_pre),
                on_tile_done=(None if last else (lambda t: gather_half(0, rhxl) if t == NT // 2 - 1 else None)))
        if not last:
            hin_pieces = gather_hx(rhxl, done=(0,))
    if NL == 0 or PH == "M":
        S.barrier(A["bar_scratch"])
        ar.reset(base)
        tt = al("tt", [128, 1024], F32); rtt = Res()
        S.dma(tt, hsp[0:128, :], reads=rhsp, writes=[rtt])
        S.dma(out_d[0:128, :], tt, reads=[rtt], is_output=True)
    build_and_emit(nc, S)
    print("fused instr counts", {e: len(S.streams[e]) for e in ENGS}, "arena peak", ar.peak)
    return nc


def wout_perm_fused():
    idx = []
    for grp in ((0, 1), (2, 3)):
        for hh in range(2):
            for m in grp:
                idx.append(m * 256 + hh * 128 + np.arange(128))
    return np.concatenate(idx)


def kernel(**inputs):
    inp = {k: np.asarray(v) for k, v in inputs.items()}
    x = inp["x"]
    cores = list(range(8))
    Kh = [const_tables(hh, SEQ_FULL) for hh in range(2)]
    nc = build_fused(Kh[0])
    perm = wout_perm_fused()
    wo = [np.ascontiguousarray(inp["w_out"][l][perm, :]) for l in range(2)]
    im = []
    for c in cores:
        b, hh = c // 2, c % 2
        d = {"x": np.ascontiguousarray(x[b, hh * T_OWN:(hh + 1) * T_OWN, :]), "ln_g": inp["ln_in_g"], "ln_b": inp["ln_in_b"],
             "rettab": Kh[hh]["rettab"]}
        gi = np.zeros((128, 8), np.uint32)
        for kc in range(8):
            gi[:, kc] = ((kc % 4) * 128 + np.arange(128)) * 2 + hh
        d["gidx"] = gi
        for n in SMALLK:
            d["k_" + n] = Kh[hh][n]
        for l in range(2):
            pc = prep_layer_core(inp, l, hh)
            d[f"win{l}"] = pc["win"]; d[f"pvec{l}"] = pc["pvec"]; d[f"wab{l}"] = pc["wab"]
            d[f"w_out{l}"] = wo[l]; d[f"w_up{l}"] = inp["w_up"][l]; d[f"w_down{l}"] = inp["w_down"][l]
            d[f"l1g{l}"] = inp["ln1_g"][l]; d[f"l1b{l}"] = inp["ln1_b"][l]; d[f"l2g{l}"] = inp["ln2_g"][l]; d[f"l2b{l}"] = inp["ln2_b"][l]
        im.append(d)
    import os
    rr = run_bass_kernel_spmd(nc, im, core_ids=cores, trace=bool(os.environ.get("KERNEL_TRACE")))
    if os.environ.get("KERNEL_TRACE"):
        print("exec_time_ns", rr.exec_time_ns)
    res = rr.results
    out = np.zeros((4, SEQ_FULL, 1024), np.float32)
    for c in cores:
        b, hh = c // 2, c % 2
        out[b, hh * T_OWN:(hh + 1) * T_OWN, :] = res[c]["out"]
    return out
```
